# Optimizing a Trainium2 kernel written in Bass

```python
import jax, jax.numpy as jnp
from jax import lax
import numpy as np

D_MODEL = 2048
BATCH = 2
SEQ = 4096
DEPTH = 4

GRID_W = 64
CTX_LEN = 256
EPS = 1e-6
N_MOD = 6
D_FF = 4 * D_MODEL
N_EVEN = (DEPTH + 1) // 2
N_ODD = DEPTH // 2

D_LRU = D_MODEL // 2
LRU_HEADS = 8
LRU_HEAD_DIM = D_LRU // LRU_HEADS
CONV_W = 4
CONV_LEFT = CONV_W // 2
LRU_C = 8.0

MLA_HEADS = 8
QK_NOPE = 128
QK_ROPE = 64
V_HEAD = 128
Q_LORA = 512
KV_LORA = 512
ROPE_BASE = 10000.0
Q_BLOCK = 128
ATTN_SCALE = (QK_NOPE + QK_ROPE) ** -0.5
D_MLA_OUT = MLA_HEADS * V_HEAD
D_AB_IN = 2 * D_LRU + Q_LORA + KV_LORA + QK_ROPE
D_AB_OUT = D_LRU + D_MLA_OUT

HG_EXPAND = 128
HG_HEADS = D_MODEL // HG_EXPAND
HG_DK = HG_EXPAND
HG_DV = D_MODEL // HG_HEADS
D_HG = HG_HEADS * HG_DK
D_HG_V = HG_HEADS * HG_DV
D_HG_IN = 3 * D_HG + 2 * D_HG_V
HG_CHUNK = 64

kernel_name = "hybrid_rglru_mla_hgrn2_diffusion_block"


def _same(t):
    return t


def _flip_seq(t):
    return jnp.flip(t, axis=1)


def rms_norm(x, w):
    xf = x.astype(jnp.float32)
    y = xf * lax.rsqrt(jnp.mean(xf * xf, axis=-1, keepdims=True) + EPS)
    return (y * w.astype(jnp.float32)).astype(x.dtype)


def modulate(h, shift, scale):
    return h * (1 + scale) + shift


def ada_params(cond, w, b):
    m = jnp.einsum('...d,de->...e', jax.nn.silu(cond), w) + b
    return jnp.split(m[..., None, :], N_MOD, axis=-1)


def squared_relu_mlp(h, w1, w2):
    return jnp.square(jax.nn.relu(h @ w1)) @ w2


def axial_rope_tables(row, col):
    half = QK_ROPE // 2
    inv_freq = ROPE_BASE ** (-jnp.arange(0, half, 2, dtype=jnp.float32) / half)
    ang_r = row.astype(jnp.float32)[:, None] * inv_freq
    ang_c = col.astype(jnp.float32)[:, None] * inv_freq
    ang = jnp.concatenate([ang_r, ang_r, ang_c, ang_c], axis=-1)
    return jnp.cos(ang), jnp.sin(ang)


def apply_rope(x, cos, sin):
    q = QK_ROPE // 4
    rot = jnp.concatenate([-x[..., q:2 * q], x[..., :q], -x[..., 3 * q:], x[..., 2 * q:3 * q]], axis=-1)
    return (x * cos + rot * sin).astype(x.dtype)


def dwconv_centred(u, w, b):
    T = u.shape[1]
    up = jnp.pad(u, ((0, 0), (CONV_LEFT, CONV_W - 1 - CONV_LEFT), (0, 0)))
    taps = [up[:, j:j + T] * w[j] for j in range(CONV_W)]
    return sum(taps[1:], taps[0]) + b


def linear_scan(a, b, h0):
    b = b.at[:, 0].add(a[:, 0] * h0)

    def combine(l, r):
        al, bl = l
        ar, br = r
        return al * ar, ar * bl + br

    _, h = lax.associative_scan(combine, (a, b), axis=1)
    return h


def rglru_gates(u, w_a, b_a, w_x, b_x, lam):
    uh = u.reshape(u.shape[:-1] + (LRU_HEADS, LRU_HEAD_DIM))
    r = jax.nn.sigmoid(jnp.einsum('bthd,hde->bthe', uh, w_a).reshape(u.shape) + b_a)
    i = jax.nn.sigmoid(jnp.einsum('bthd,hde->bthe', uh, w_x).reshape(u.shape) + b_x)
    log_a = -LRU_C * r * jax.nn.softplus(-lam)
    a = jnp.exp(log_a)
    b = jnp.sqrt(-jnp.expm1(2.0 * log_a)) * (i * u)
    return a, b


def rglru_bidir(uc, ux, w_a, b_a, w_x, b_x, lam, need_ctx):
    hc_dirs, hx_dirs = [], []
    for d in range(2):
        rev = _flip_seq if d == 1 else _same
        a, b = rglru_gates(rev(uc), w_a[d], b_a[d], w_x[d], b_x[d], lam[d])
        h_c = linear_scan(a, b, jnp.zeros_like(b[:, 0]))
        a, b = rglru_gates(rev(ux), w_a[d], b_a[d], w_x[d], b_x[d], lam[d])
        h_x = linear_scan(a, b, h_c[:, -1])
        hx_dirs.append(rev(h_x))
        if need_ctx:
            hc_dirs.append(rev(h_c))
    h_c_sum = hc_dirs[0] + hc_dirs[1] if need_ctx else None
    return h_c_sum, hx_dirs[0] + hx_dirs[1]


def mla_queries(pq, norm_w, w_uq):
    B, T = pq.shape[:2]
    q = (rms_norm(pq, norm_w) @ w_uq).reshape(B, T, MLA_HEADS, QK_NOPE + QK_ROPE)
    return q[..., :QK_NOPE], q[..., QK_NOPE:]


def mla_keys_values(pkv, norm_w, w_ukv):
    B, T = pkv.shape[:2]
    kv = (rms_norm(pkv, norm_w) @ w_ukv).reshape(B, T, MLA_HEADS, QK_NOPE + V_HEAD)
    return kv[..., :QK_NOPE], kv[..., QK_NOPE:]


def blocked_attention(qn, qr, kn, kr, v):
    B, T = qn.shape[:2]
    nb = T // Q_BLOCK

    def blocks(t):
        return jnp.swapaxes(t.reshape((B, nb, Q_BLOCK) + t.shape[2:]), 0, 1)

    def attend(blk):
        qn_i, qr_i = blk
        s = jnp.einsum('bqhd,bkhd->bhqk', qn_i, kn) + jnp.einsum('bqhr,bkr->bhqk', qr_i, kr)
        p = jax.nn.softmax(s.astype(jnp.float32) * ATTN_SCALE, axis=-1).astype(v.dtype)
        return jnp.einsum('bhqk,bkhd->bqhd', p, v)

    o = lax.map(attend, (blocks(qn), blocks(qr)))
    return jnp.swapaxes(o, 0, 1).reshape(B, T, D_MLA_OUT)


def split_ab(p):
    return jnp.split(p, [D_LRU, 2 * D_LRU, 2 * D_LRU + Q_LORA, 2 * D_LRU + Q_LORA + KV_LORA], axis=-1)


def mixer_ab(hc, hx, cos, sin, w_in, w_out, conv_w, conv_b, w_a, b_a, w_x, b_x, lam,
             q_norm_w, w_uq, kv_norm_w, w_ukv, need_ctx):
    gate_c, u_c, cq_c, ckv_c, kr_c = split_ab(hc @ w_in)
    gate_x, u_x, cq_x, ckv_x, kr_x = split_ab(hx @ w_in)
    uc = dwconv_centred(u_c, conv_w, conv_b).astype(jnp.float32)
    ux = dwconv_centred(u_x, conv_w, conv_b).astype(jnp.float32)
    h_c, h_x = rglru_bidir(uc, ux, w_a, b_a, w_x, b_x, lam, need_ctx)
    ya_x = (h_x * jax.nn.gelu(gate_x.astype(jnp.float32))).astype(hx.dtype)
    kn_c, v_c = mla_keys_values(ckv_c, kv_norm_w, w_ukv)
    kn_x, v_x = mla_keys_values(ckv_x, kv_norm_w, w_ukv)
    qn_x, qr_x = mla_queries(cq_x, q_norm_w, w_uq)
    qr_x = apply_rope(qr_x, cos[:, None], sin[:, None])
    kr_x = apply_rope(kr_x, cos, sin)
    yb_x = blocked_attention(qn_x, qr_x,
                             jnp.concatenate([kn_x, kn_c], axis=1),
                             jnp.concatenate([kr_x, kr_c], axis=1),
                             jnp.concatenate([v_x, v_c], axis=1))
    out_x = jnp.concatenate([ya_x, yb_x], axis=-1) @ w_out
    if not need_ctx:
        return None, out_x
    ya_c = (h_c * jax.nn.gelu(gate_c.astype(jnp.float32))).astype(hc.dtype)
    qn_c, qr_c = mla_queries(cq_c, q_norm_w, w_uq)
    yb_c = blocked_attention(qn_c, qr_c, kn_c, kr_c, v_c)
    out_c = jnp.concatenate([ya_c, yb_c], axis=-1) @ w_out
    return out_c, out_x


def hgrn2_scan(q, k, logf, v, s0):
    B, T = k.shape[:2]
    nc = T // HG_CHUNK
    with_out = q is not None

    def to_chunks(t):
        return t.reshape(B, nc, HG_CHUNK, HG_HEADS, t.shape[-1]).transpose(1, 0, 3, 2, 4)

    causal = jnp.tril(jnp.ones((HG_CHUNK, HG_CHUNK), dtype=bool))[:, :, None]

    def step(s, blk):
        kc, gc, vc = blk[0], blk[1], blk[2]
        bcum = jnp.cumsum(gc, axis=2)
        b_last = bcum[:, :, -1:]
        s_new = (jnp.exp(b_last[:, :, 0])[..., None] * s
                 + jnp.einsum('bhsd,bhse->bhde', kc * jnp.exp(b_last - bcum), vc))
        if not with_out:
            return s_new, None
        qc = blk[3]
        o_inter = jnp.einsum('bhtd,bhde->bhte', qc * jnp.exp(bcum), s)
        diff = bcum[:, :, :, None, :] - bcum[:, :, None, :, :]
        decay = jnp.exp(jnp.where(causal, diff, -jnp.inf))
        attn = jnp.einsum('bhtd,bhsd,bhtsd->bhts', qc, kc, decay)
        o_intra = jnp.einsum('bhts,bhse->bhte', attn, vc)
        return s_new, o_inter + o_intra

    xs = (to_chunks(k), to_chunks(logf), to_chunks(v)) + ((to_chunks(q),) if with_out else ())
    s_fin, o = lax.scan(step, s0, xs)
    if with_out:
        o = o.transpose(1, 0, 3, 2, 4).reshape(B, T, HG_HEADS, HG_DV)
    return o, s_fin


def hgrn2_project(h, w_in, lb):
    B, T = h.shape[:2]
    p = (h @ w_in).astype(jnp.float32)
    q, f_fwd, f_bwd, i, g = jnp.split(p, [D_HG, 2 * D_HG, 3 * D_HG, 3 * D_HG + D_HG_V], axis=-1)

    def heads(t):
        return t.reshape(B, T, HG_HEADS, -1)

    dirs = []
    for f_raw in (f_fwd, f_bwd):
        f = lb + (1.0 - lb) * jax.nn.sigmoid(f_raw)
        dirs.append((heads(1.0 - f), heads(jnp.log(f))))
    return heads(q), dirs, heads(i), g


def mixer_hgrn2(hc, hx, w_in, lb, norm_w, w_out, need_ctx):
    qc, dc, vc, gc = hgrn2_project(hc, w_in, lb)
    qx, dx, vx, gx = hgrn2_project(hx, w_in, lb)
    B = hx.shape[0]
    s0 = jnp.zeros((B, HG_HEADS, HG_DK, HG_DV), jnp.float32)
    oc_dirs, ox_dirs = [], []
    for d in range(2):
        rev = _flip_seq if d == 1 else _same
        kc, lfc = dc[d]
        kx, lfx = dx[d]
        q_ctx = jax.nn.silu(rev(qc)) if need_ctx else None
        o_c, s_c = hgrn2_scan(q_ctx, rev(kc), rev(lfc), rev(vc), s0)
        o_x, _ = hgrn2_scan(jax.nn.silu(rev(qx)), rev(kx), rev(lfx), rev(vx), s_c)
        ox_dirs.append(rev(o_x))
        if need_ctx:
            oc_dirs.append(rev(o_c))

    def readout(o, g, dtype):
        Bn, T = o.shape[:2]
        o = rms_norm(o, norm_w) * jax.nn.silu(g).reshape(Bn, T, HG_HEADS, HG_DV)
        return o.reshape(Bn, T, D_HG_V).astype(dtype) @ w_out

    out_x = readout(ox_dirs[0] + ox_dirs[1], gx, hx.dtype)
    if not need_ctx:
        return None, out_x
    return readout(oc_dirs[0] + oc_dirs[1], gc, hc.dtype), out_x


def setup_inputs(seed: int = 0) -> dict:
    key = jax.random.key(seed)
    ks = list(jax.random.split(key, 32))

    def nrm(i, shape, scale):
        return scale * jax.random.normal(ks[i], shape, jnp.float32)

    def gain(i, shape):
        return 1.0 + nrm(i, shape, 0.05)

    u = jax.random.uniform(ks[16], (N_EVEN, 2, D_LRU), jnp.float32, 0.9, 0.999)
    a_base = u ** (1.0 / LRU_C)
    return {
        "x": nrm(0, (BATCH, SEQ, D_MODEL), 1.0),
        "c": nrm(1, (BATCH, D_MODEL), 1.0),
        "ctx": nrm(2, (BATCH, CTX_LEN, D_MODEL), 1.0),
        "c_ctx": nrm(3, (D_MODEL,), 1.0),
        "ada_w": nrm(4, (DEPTH, D_MODEL, N_MOD * D_MODEL), 0.5 * D_MODEL ** -0.5),
        "ada_b": nrm(5, (DEPTH, N_MOD * D_MODEL), 0.02),
        "norm_mix_w": gain(6, (DEPTH, D_MODEL)),
        "norm_mlp_w": gain(7, (DEPTH, D_MODEL)),
        "ab_w_in": nrm(8, (N_EVEN, D_MODEL, D_AB_IN), D_MODEL ** -0.5),
        "ab_w_out": nrm(9, (N_EVEN, D_AB_OUT, D_MODEL), D_AB_OUT ** -0.5),
        "lru_conv_w": nrm(10, (N_EVEN, CONV_W, D_LRU), CONV_W ** -0.5),
        "lru_conv_b": nrm(11, (N_EVEN, D_LRU), 0.02),
        "lru_w_a": nrm(12, (N_EVEN, 2, LRU_HEADS, LRU_HEAD_DIM, LRU_HEAD_DIM), LRU_HEAD_DIM ** -0.5),
        "lru_b_a": nrm(13, (N_EVEN, 2, D_LRU), 0.02),
        "lru_w_x": nrm(14, (N_EVEN, 2, LRU_HEADS, LRU_HEAD_DIM, LRU_HEAD_DIM), LRU_HEAD_DIM ** -0.5),
        "lru_b_x": nrm(15, (N_EVEN, 2, D_LRU), 0.02),
        "lru_lambda": jnp.log(a_base) - jnp.log1p(-a_base),
        "mla_q_norm_w": gain(17, (N_EVEN, Q_LORA)),
        "mla_w_uq": nrm(18, (N_EVEN, Q_LORA, MLA_HEADS * (QK_NOPE + QK_ROPE)), Q_LORA ** -0.5),
        "mla_kv_norm_w": gain(19, (N_EVEN, KV_LORA)),
        "mla_w_ukv": nrm(20, (N_EVEN, KV_LORA, MLA_HEADS * (QK_NOPE + V_HEAD)), KV_LORA ** -0.5),
        "hg_w_in": nrm(21, (N_ODD, D_MODEL, D_HG_IN), D_MODEL ** -0.5),
        "hg_lb_logits": nrm(22, (DEPTH, D_HG), 0.1),
        "hg_norm_w": gain(23, (N_ODD, HG_DV)),
        "hg_w_out": nrm(24, (N_ODD, D_HG_V, D_MODEL), D_HG_V ** -0.5),
        "mlp_w1": nrm(25, (DEPTH, D_MODEL, D_FF), D_MODEL ** -0.5),
        "mlp_w2": nrm(26, (DEPTH, D_FF, D_MODEL), D_FF ** -0.5),
        "final_norm_w": gain(27, (D_MODEL,)),
    }


def reference(x, c, ctx, c_ctx, ada_w, ada_b, norm_mix_w, norm_mlp_w, ab_w_in, ab_w_out,
              lru_conv_w, lru_conv_b, lru_w_a, lru_b_a, lru_w_x, lru_b_x, lru_lambda,
              mla_q_norm_w, mla_w_uq, mla_kv_norm_w, mla_w_ukv,
              hg_w_in, hg_lb_logits, hg_norm_w, hg_w_out, mlp_w1, mlp_w2, final_norm_w):
    n_lat = x.shape[1]
    rows_n = n_lat // GRID_W
    row = jnp.repeat(jnp.arange(rows_n), GRID_W)
    col = jnp.tile(jnp.arange(GRID_W), rows_n)
    cos, sin = axial_rope_tables(row, col)
    p_lb = jax.nn.softmax(hg_lb_logits.astype(jnp.float32), axis=0)
    lb_all = jnp.cumsum(p_lb, axis=0) - p_lb[0]
    for l in range(DEPTH):
        last = l == DEPTH - 1
        sh1, sc1, g1, sh2, sc2, g2 = ada_params(c, ada_w[l], ada_b[l])
        csh1, csc1, cg1, csh2, csc2, cg2 = ada_params(c_ctx, ada_w[l], ada_b[l])
        hx = modulate(rms_norm(x, norm_mix_w[l]), sh1, sc1)
        hc = modulate(rms_norm(ctx, norm_mix_w[l]), csh1, csc1)
        if l % 2 == 0:
            e = l // 2
            mc, mx = mixer_ab(hc, hx, cos, sin, ab_w_in[e], ab_w_out[e], lru_conv_w[e], lru_conv_b[e],
                              lru_w_a[e], lru_b_a[e], lru_w_x[e], lru_b_x[e], lru_lambda[e],
                              mla_q_norm_w[e], mla_w_uq[e], mla_kv_norm_w[e], mla_w_ukv[e], not last)
        else:
            o = l // 2
            mc, mx = mixer_hgrn2(hc, hx, hg_w_in[o], lb_all[l], hg_norm_w[o], hg_w_out[o], not last)
        x = x + g1 * mx
        x = x + g2 * squared_relu_mlp(modulate(rms_norm(x, norm_mlp_w[l]), sh2, sc2), mlp_w1[l], mlp_w2[l])
        if not last:
            ctx = ctx + cg1 * mc
            ctx = ctx + cg2 * squared_relu_mlp(modulate(rms_norm(ctx, norm_mlp_w[l]), csh2, csc2),
                                               mlp_w1[l], mlp_w2[l])
    return rms_norm(x, final_norm_w)
```

```python
import numpy as np
from contextlib import ExitStack
import concourse.bass as bass
import concourse.mybir as mybir
from concourse.bass_utils import run_bass_kernel_spmd

F32 = mybir.dt.float32
BF16 = mybir.dt.bfloat16
AF = mybir.ActivationFunctionType
ALU = mybir.AluOpType
AX = mybir.AxisListType

D = 2048
KC = D // 128
B = 2
SEQ = 4096
CTX = 256
DEPTH = 4
DFF = 8192
NCORES = 8
NL = B * SEQ // NCORES
NCX = B * CTX // NCORES
NT = NL + NCX
EPS = 1e-6
D_AB_IN = 3136
D_HG_IN = 10240
TS = SEQ + CTX

SAME_ENG_SYNC = True


class Rec:
    def __init__(self):
        self.calls = []

    def __getattr__(self, name):
        def f(*a, **k):
            self.calls.append((name, a, k))
            return self
        return f


class FW:
    ENGS = ("pe", "dve", "act", "pool", "sp")

    def __init__(self, nc, es, n_dma_sems=48):
        self.nc = nc
        self.es = es
        self.eng = {"pe": nc.tensor, "dve": nc.vector, "act": nc.scalar, "pool": nc.gpsimd, "sp": nc.sync}
        self.sem = {e: es.enter_context(nc.semaphore("s_" + e)) for e in self.ENGS}
        self.cnt = {e: 0 for e in self.ENGS}
        self.prog = {e: [] for e in self.ENGS}
        self.seen = {e: {} for e in self.ENGS}
        self.dsem = [es.enter_context(nc.semaphore("d%d" % i)) for i in range(n_dma_sems)]
        self.dval = [0] * n_dma_sems
        self.drr = 0
        self.last_w = {}
        self.readers = {}
        self.semobj = {}
        self.n_ps = 0

    def sb(self, name, shape, dt):
        return self.es.enter_context(self.nc.sbuf_tensor(name, list(shape), dt))

    def ps(self, name, shape, dt=F32):
        return self.es.enter_context(self.nc.psum_tensor(name, list(shape), dt))

    def _collect(self, eng, reads, writes, nosame=False):
        toks = []
        for k in reads:
            t = self.last_w.get(k)
            if t is not None:
                toks.append(t)
        for k in writes:
            t = self.last_w.get(k)
            if t is not None:
                toks.append(t)
            toks.extend(self.readers.get(k, ()))
        need = {}
        for (sid, val) in toks:
            if ((not SAME_ENG_SYNC) or nosame) and sid == ("e", eng):
                continue
            if self.seen[eng].get(sid, 0) >= val:
                continue
            if need.get(sid, 0) < val:
                need[sid] = val
        for sid, val in need.items():
            self.seen[eng][sid] = val
        return list(need.items())

    def _commit(self, tok, reads, writes):
        for k in writes:
            self.last_w[k] = tok
            self.readers[k] = []
        for k in reads:
            if k in writes:
                continue
            self.readers.setdefault(k, []).append(tok)

    def _semof(self, sid):
        kind, x = sid
        return self.sem[x] if kind == "e" else self.dsem[x]

    def op(self, eng, fn, reads=(), writes=(), nosame=False):
        waits = self._collect(eng, reads, writes, nosame)
        self.cnt[eng] += 1
        tok = (("e", eng), self.cnt[eng])
        rec = Rec()
        fn(rec)
        calls = rec.calls

        def run(e, calls=calls):
            ins = None
            for (name, a, k) in calls:
                ins = getattr(e, name)(*a, **k)
            return ins
        self.prog[eng].append((waits, run, ("e", eng), 1))
        self._commit(tok, reads, writes)
        return tok

    def dma(self, eng, out, in_, reads=(), writes=(), **kw):
        waits = self._collect(eng, reads, writes)
        i = self.drr
        self.drr = (self.drr + 1) % len(self.dsem)
        sid = ("d", i)
        if self.dval[i] > 0 and self.seen[eng].get(sid, 0) < self.dval[i]:
            waits.append((sid, self.dval[i]))
            self.seen[eng][sid] = self.dval[i]
        self.dval[i] += 16
        tok = (sid, self.dval[i])
        self.prog[eng].append((waits, (lambda e, out=out, in_=in_, kw=kw: e.dma_start(out=out, in_=in_, **kw)), sid, 16))
        self._commit(tok, reads, writes)
        return tok

    def barrier(self):
        allw = [(("e", e), self.cnt[e]) for e in self.ENGS if self.cnt[e] > 0]
        allw += [(("d", i), v) for i, v in enumerate(self.dval) if v > 0]
        for e in self.ENGS:
            waits = []
            for (sid, val) in allw:
                if sid == ("e", e):
                    continue
                if self.seen[e].get(sid, 0) < val:
                    waits.append((sid, val))
                    self.seen[e][sid] = val
            if waits:
                self.prog[e].append((waits, None, None, 0))

    def finish(self, eng="sp"):
        waits = []
        for i, v in enumerate(self.dval):
            if v > 0:
                waits.append((("d", i), v))
        self.prog[eng].append((waits, None, None, 0))
        block = self.es.enter_context(self.nc.Block())

        def replay(name):
            def body(e):
                for (waits, fn, sid, inc) in self.prog[name]:
                    for (wsid, val) in waits:
                        e.wait_ge(self._semof(wsid), val)
                    if fn is not None:
                        ins = fn(e)
                        ins.then_inc(self._semof(sid), inc)
            return body

        block.tensor(replay("pe"))
        block.vector(replay("dve"))
        block.scalar(replay("act"))
        block.gpsimd(replay("pool"))
        block.sync(replay("sp"))


def mm_group(fw, out_ps, pairs, reads, writes):
    n = len(pairs)

    def fn(e):
        ins = None
        for i, (l, r) in enumerate(pairs):
            ins = e.matmul(out_ps, l, r, start=(i == 0), stop=(i == n - 1))
        return ins
    return fw.op("pe", fn, reads=reads, writes=writes)


ADA_COLS = 6 * D // NCORES
ADA_J = ADA_COLS // 128


def build_ada():
    nc = bass.Bass("TRN2", target_bir_lowering=False)
    cv = nc.dram_tensor("cv", [128, KC, 3], F32, kind="ExternalInput").ap()
    w = nc.dram_tensor("w", [DEPTH, D, ADA_COLS], F32, kind="ExternalInput").ap()
    bb = nc.dram_tensor("b", [128, DEPTH, ADA_J], F32, kind="ExternalInput").ap()
    out = nc.dram_tensor("out", [128, DEPTH, ADA_J, 3], F32, kind="ExternalOutput").ap()
    with ExitStack() as es:
        fw = FW(nc, es)
        cvt = fw.sb("cvt", [128, KC, 3], F32)
        sg = fw.sb("sg", [128, KC, 3], F32)
        sv = fw.sb("sv", [128, KC, 3], F32)
        bt = fw.sb("bt", [128, DEPTH, ADA_J], F32)
        ot = fw.sb("ot", [128, DEPTH, ADA_J, 3], F32)
        NB = 3
        wt = [fw.sb("wt%d" % i, [128, KC, 512], F32) for i in range(NB)]
        pst = [fw.ps("ps%d" % i, [128, 512], F32) for i in range(4)]
        fw.dma("sp", cvt[:], cv[:, :, :], writes=["cvt"])
        fw.dma("sp", bt[:], bb[:, :, :], writes=["bt"])
        fw.op("act", lambda e: e.activation(out=sg[:], in_=cvt[:], func=AF.Sigmoid), reads=["cvt"], writes=["sg"])
        fw.op("dve", lambda e: e.tensor_tensor(out=sv[:], in0=cvt[:], in1=sg[:], op=ALU.mult), reads=["cvt", "sg"], writes=["sv"])
        it = 0
        for l in range(DEPTH):
            for g in range(ADA_COLS // 512):
                wb = wt[it % NB]
                wk = "wt%d" % (it % NB)
                src = w[l].rearrange("(kc p) n -> p kc n", p=128)[:, :, g * 512:(g + 1) * 512]
                q = "sp" if it % 2 == 0 else "act"
                fw.dma(q, wb[:], src, writes=[wk])
                for jj in range(4):
                    j = g * 4 + jj
                    pi = (it * 4 + jj) % 4
                    pk = "ps%d" % pi
                    mm_group(fw, pst[pi][:, 0:3],
                             [(wb[:, kc, jj * 128:(jj + 1) * 128], sv[:, kc, :]) for kc in range(KC)],
                             reads=[wk, "sv"], writes=[pk])
                    fw.op("dve", lambda e, pi=pi, l=l, j=j: e.tensor_scalar(
                        out=ot[:, l, j, :], in0=pst[pi][:, 0:3], scalar1=bt[:, l, j:j + 1], scalar2=None, op0=ALU.add),
                        reads=[pk, "bt"], writes=["ot"])
                it += 1
        fw.dma("sp", out[:, :, :, :], ot[:], reads=["ot"])
        fw.finish()
    return nc


TT = [(0, 512), (512, 512), (1024, 64)]


def build_tok(post, pre):
    nin = {"ab": D_AB_IN, "hg": D_HG_IN, "final": 0}[pre]
    nc = bass.Bass("TRN2", target_bir_lowering=False)
    xin = nc.dram_tensor("xT", [D, NT], F32, kind="ExternalInput").ap()
    adaL = nc.dram_tensor("adaL", [128, 2, 6, KC], F32, kind="ExternalInput").ap()
    adaC = nc.dram_tensor("adaC", [128, 2, 6, KC], F32, kind="ExternalInput").ap()
    nrm = nc.dram_tensor("nrm", [128, 2, KC], F32, kind="ExternalInput").ap()
    if post:
        yin = nc.dram_tensor("yT", [D, NT], F32, kind="ExternalInput").ap()
        w_out = nc.dram_tensor("w_out", [D, D], F32, kind="ExternalInput").ap()
        w1 = nc.dram_tensor("w1", [D, DFF], F32, kind="ExternalInput").ap()
        w2 = nc.dram_tensor("w2", [DFF, D], F32, kind="ExternalInput").ap()
    if nin:
        w_in = nc.dram_tensor("w_in", [D, nin], F32, kind="ExternalInput").ap()
        pout = nc.dram_tensor("pT", [nin, NT], F32, kind="ExternalOutput").ap()
        xout = nc.dram_tensor("xoT", [D, NT], F32, kind="ExternalOutput").ap()
    else:
        fout = nc.dram_tensor("outT", [D, NT], F32, kind="ExternalOutput").ap()

    with ExitStack() as es:
        fw = FW(nc, es)
        x = fw.sb("x", [128, KC, NT], F32)
        hT = fw.sb("hT", [128, KC, NT], BF16)
        h1 = fw.sb("h1", [128, KC, NT], BF16)
        NWB = 2
        wb = [fw.sb("wb%d" % i, [128, KC, 512], BF16) for i in range(NWB)]
        aL = fw.sb("aL", [128, 2, 6, KC], F32)
        aC = fw.sb("aC", [128, 2, 6, KC], F32)
        nw = fw.sb("nw", [128, 2, KC], F32)
        AL = fw.sb("AL", [128, 2, KC], F32)
        AC = fw.sb("AC", [128, 2, KC], F32)
        ones = fw.sb("ones", [128, 128], BF16)
        rstd = fw.sb("rstd", [128, NT], F32)
        tmpn = [fw.sb("tmpn%d" % i, [128, NT], F32) for i in range(2)]
        stg = [fw.sb("stg%d" % i, [128, NT], F32) for i in range(3)]
        rl = [fw.sb("rl%d" % i, [128, 512], F32) for i in range(2)]
        NPS = 8
        pst = [fw.ps("ps%d" % i, [128, 512], F32) for i in range(NPS)]
        st = {"ps": 0, "wb": 0, "tmpn": 0, "stg": 0, "rl": 0}

        def nxt(name, n):
            i = st[name] % n
            st[name] += 1
            return i

        xk = lambda kc: ("x", kc)
        hk = lambda kc: ("h", kc)
        h1k = lambda kc: ("h1", kc)

        xv = xin.rearrange("(kc p) n -> p kc n", p=128)
        for kc in range(KC):
            fw.dma("sp", x[:, kc, :], xv[:, kc, :], writes=[xk(kc)])
        fw.dma("act", aL[:], adaL[:, :, :, :], writes=["aL"])
        fw.dma("act", aC[:], adaC[:, :, :, :], writes=["aC"])
        fw.dma("act", nw[:], nrm[:, :, :], writes=["nw"])
        fw.op("dve", lambda e: e.memset(ones[:], 1.0), writes=["ones"])
        for (s, idx) in ((0, 4), (1, 1)):
            fw.op("dve", lambda e, s=s, idx=idx: e.scalar_tensor_tensor(
                out=AL[:, s, :], in0=aL[:, s, idx, :], scalar=1.0, in1=nw[:, s, :], op0=ALU.add, op1=ALU.mult),
                reads=["aL", "nw"], writes=["AL"])
            fw.op("dve", lambda e, s=s, idx=idx: e.scalar_tensor_tensor(
                out=AC[:, s, :], in0=aC[:, s, idx, :], scalar=1.0, in1=nw[:, s, :], op0=ALU.add, op1=ALU.mult),
                reads=["aC", "nw"], writes=["AC"])

        def load_w(src_ap, ncols):
            i = nxt("wb", NWB)
            fw.dma("pool", wb[i][:, :, 0:ncols], src_ap.rearrange("(kc p) n -> p kc n", p=128), writes=[("wb", i)])
            return i

        def proj(src, srckey, wsrc, ncols_total, evac):
            c0 = 0
            while c0 < ncols_total:
                gw = min(512, ncols_total - c0)
                wi = load_w(wsrc[:, c0:c0 + gw], gw)
                cc = 0
                while cc < gw:
                    cw = min(128, gw - cc)
                    c = (c0 + cc) // 128
                    for tt, (t0, tw) in enumerate(TT):
                        pi = nxt("ps", NPS)
                        pk = ("ps", pi)
                        mm_group(fw, pst[pi][0:cw, 0:tw],
                                 [(wb[wi][:, kc, cc:cc + cw], src[:, kc, t0:t0 + tw]) for kc in range(KC)],
                                 reads=[("wb", wi)] + [srckey(kc) for kc in range(KC)], writes=[pk])
                        evac(c, cw, tt, pst[pi][0:cw, 0:tw], pk)
                    cc += cw
                c0 += gw

        def norm_mod(s, shift_idx, out_dt_tile, outkey, final=False):
            for kc in range(KC):
                fw.op("act", lambda e, kc=kc: e.activation(out=h1[:, kc, :], in_=x[:, kc, :], func=AF.Square),
                      reads=[xk(kc)], writes=[h1k(kc)])
            for tt, (t0, tw) in enumerate(TT):
                pi = nxt("ps", NPS)
                pk = ("ps", pi)
                mm_group(fw, pst[pi][:, 0:tw], [(ones[:], h1[:, kc, t0:t0 + tw]) for kc in range(KC)],
                         reads=["ones"] + [h1k(kc) for kc in range(KC)], writes=[pk])
                fw.op("act", lambda e, pi=pi, t0=t0, tw=tw: e.activation(
                    out=rstd[:, t0:t0 + tw], in_=pst[pi][:, 0:tw], func=AF.Sqrt, bias=EPS_AP[:, 0:1], scale=1.0 / D),
                    reads=[pk, "eps"], writes=["rstd"])
            fw.op("dve", lambda e: e.reciprocal(out=rstd[:], in_=rstd[:]), reads=["rstd"], writes=["rstd"])
            for kc in range(KC):
                ti = nxt("tmpn", 2)
                tk = ("tmpn", ti)
                fw.op("dve", lambda e, kc=kc, ti=ti: e.tensor_tensor(out=tmpn[ti][:], in0=x[:, kc, :], in1=rstd[:], op=ALU.mult),
                      reads=[xk(kc), "rstd"], writes=[tk])
                if final:
                    si = nxt("stg", 3)
                    sk = ("stg", si)
                    fw.op("act", lambda e, kc=kc, ti=ti, si=si: e.activation(
                        out=stg[si][:], in_=tmpn[ti][:], func=AF.Identity, scale=nw[:, 1, kc:kc + 1]),
                        reads=[tk, "nw"], writes=[sk])
                    fw.dma("sp", fout[kc * 128:(kc + 1) * 128, :], stg[si][:], reads=[sk])
                else:
                    fw.op("act", lambda e, kc=kc, ti=ti: e.activation(
                        out=hT[:, kc, 0:NL], in_=tmpn[ti][:, 0:NL], func=AF.Identity,
                        bias=aL[:, s, shift_idx, kc:kc + 1], scale=AL[:, s, kc:kc + 1]),
                        reads=[tk, "aL", "AL"], writes=[hk(kc)])
                    fw.op("act", lambda e, kc=kc, ti=ti: e.activation(
                        out=hT[:, kc, NL:NT], in_=tmpn[ti][:, NL:NT], func=AF.Identity,
                        bias=aC[:, s, shift_idx, kc:kc + 1], scale=AC[:, s, kc:kc + 1]),
                        reads=[tk, "aC", "AC", hk(kc)], writes=[hk(kc)])

        EPS_AP = fw.sb("epsc", [128, 1], F32)
        fw.op("dve", lambda e: e.memset(EPS_AP[:], EPS), writes=["eps"])

        def resid_evac(gidx):
            def evac(c, cw, tt, ps_ap, pk):
                t0, tw = TT[tt]
                a = aL if tt < 2 else aC
                fw.op("dve", lambda e: e.scalar_tensor_tensor(
                    out=x[:, c, t0:t0 + tw], in0=ps_ap, scalar=a[:, 0, gidx, c:c + 1], in1=x[:, c, t0:t0 + tw],
                    op0=ALU.mult, op1=ALU.add),
                    reads=[pk, "aL", "aC", xk(c)], writes=[xk(c)])
            return evac

        if post:
            yv = yin.rearrange("(kc p) n -> p kc n", p=128)
            for kc in range(KC):
                fw.dma("pool", hT[:, kc, :], yv[:, kc, :], writes=[hk(kc)])
            proj(hT, hk, w_out, D, resid_evac(2))
            norm_mod(0, 3, None, None)
            for q in range(4):
                def evac1(c, cw, tt, ps_ap, pk):
                    t0, tw = TT[tt]
                    ri = nxt("rl", 2)
                    rk = ("rl", ri)
                    fw.op("act", lambda e: e.activation(out=rl[ri][:, 0:tw], in_=ps_ap, func=AF.Relu),
                          reads=[pk], writes=[rk])
                    fw.op("dve", lambda e: e.tensor_tensor(out=h1[:, c, t0:t0 + tw], in0=rl[ri][:, 0:tw], in1=rl[ri][:, 0:tw], op=ALU.mult),
                          reads=[rk], writes=[h1k(c)])
                proj(hT, hk, w1[:, q * 2048:(q + 1) * 2048], 2048, evac1)
                proj(h1, h1k, w2[q * 2048:(q + 1) * 2048, :], D, resid_evac(5))

        if nin:
            xov = xout.rearrange("(kc p) n -> p kc n", p=128)
            for kc in range(KC):
                fw.dma("sp", xov[:, kc, :], x[:, kc, :], reads=[xk(kc)])
            norm_mod(1, 0, None, None)
            cur = {}

            def evac_p(c, cw, tt, ps_ap, pk):
                t0, tw = TT[tt]
                if tt == 0:
                    cur["si"] = nxt("stg", 3)
                si = cur["si"]
                sk = ("stg", si)
                eng = "act" if (c + tt) % 2 == 0 else "dve"
                if eng == "act":
                    fw.op("act", lambda e: e.activation(out=stg[si][0:cw, t0:t0 + tw], in_=ps_ap, func=AF.Copy),
                          reads=[pk], writes=[sk])
                else:
                    fw.op("dve", lambda e: e.tensor_copy(out=stg[si][0:cw, t0:t0 + tw], in_=ps_ap),
                          reads=[pk], writes=[sk])
                if tt == len(TT) - 1:
                    fw.dma("sp", pout[c * 128:c * 128 + cw, :], stg[si][0:cw, :], reads=[sk])
            proj(hT, hk, w_in, nin, evac_p)
        else:
            norm_mod(1, 0, None, None, final=True)
        fw.finish()
    return nc


def core_tok(k):
    return k // 4, k % 4


def to_core_T(lat, ctx):
    outs = []
    for k in range(NCORES):
        b, r = core_tok(k)
        a = np.concatenate([lat[b, r * NL:(r + 1) * NL], ctx[b, r * NCX:(r + 1) * NCX]], axis=0)
        outs.append(np.ascontiguousarray(a.T))
    return outs


def from_core_T(arrs):
    F = arrs[0].shape[0]
    lat = np.empty((B, SEQ, F), arrs[0].dtype)
    ctx = np.empty((B, CTX, F), arrs[0].dtype)
    for k in range(NCORES):
        b, r = core_tok(k)
        lat[b, r * NL:(r + 1) * NL] = arrs[k][:, :NL].T
        ctx[b, r * NCX:(r + 1) * NCX] = arrs[k][:, NL:].T
    return lat, ctx


def vec_pk(v):
    return np.ascontiguousarray(v.reshape(KC, 128).T)


def ada_sets(m, lpost, lpre, v):
    out = np.zeros((128, 2, 6, KC), np.float32)
    for s, l in enumerate((lpost, lpre)):
        if l is None or l < 0 or l >= DEPTH:
            continue
        out[:, s] = m[l, v].reshape(6, KC, 128).transpose(2, 0, 1)
    return out


def ada_host_inputs(c, c_ctx, ada_w, ada_b):
    cvec = np.stack([c[0], c[1], c_ctx], 0)
    cv = np.ascontiguousarray(cvec.reshape(3, KC, 128).transpose(2, 1, 0))
    maps = []
    for k in range(NCORES):
        w = np.ascontiguousarray(ada_w[:, :, k * ADA_COLS:(k + 1) * ADA_COLS])
        b = np.ascontiguousarray(ada_b[:, k * ADA_COLS:(k + 1) * ADA_COLS].reshape(DEPTH, ADA_J, 128).transpose(2, 0, 1))
        maps.append({"cv": cv, "w": w, "b": b})
    return maps


def ada_host_gather(results):
    m = np.zeros((DEPTH, 3, 6 * D), np.float32)
    for k in range(NCORES):
        o = results[k]["out"]
        m[:, :, k * ADA_COLS:(k + 1) * ADA_COLS] = o.transpose(1, 3, 2, 0).reshape(DEPTH, 3, ADA_COLS)
    return m


_NC_CACHE = {}


def get_nc(name, builder, *args):
    key = (name,) + tuple(args)
    if key not in _NC_CACHE:
        _NC_CACHE[key] = builder(*args)
    return _NC_CACHE[key]


def run(nc, maps):
    res = run_bass_kernel_spmd(nc, maps, core_ids=list(range(NCORES)))
    return res.results


ATTN_SCALE = 192.0 ** -0.5
TL = [(i * 512, 512) for i in range(SEQ // 512)] + [(SEQ, CTX)]
NKT = TS // 128
GELU_C = 1.5957691216057308


def build_ab():
    nc = bass.Bass("TRN2", target_bir_lowering=False)
    dgu = nc.dram_tensor("guT", [2, 128, B, TS], F32, kind="ExternalInput").ap()
    dcq = nc.dram_tensor("cqT", [512, B, TS], F32, kind="ExternalInput").ap()
    dckv = nc.dram_tensor("ckvT", [512, B, TS], F32, kind="ExternalInput").ap()
    dkr = nc.dram_tensor("krT", [64, B, TS], F32, kind="ExternalInput").ap()
    dvec = nc.dram_tensor("vecs", [128, 24], F32, kind="ExternalInput").ap()
    dwa = nc.dram_tensor("wax", [2, 2, 128, 128], F32, kind="ExternalInput").ap()
    dwuq = nc.dram_tensor("wuq", [512, 192], F32, kind="ExternalInput").ap()
    dwukv = nc.dram_tensor("wukv", [512, 256], F32, kind="ExternalInput").ap()
    dcs = nc.dram_tensor("cs", [2, 64, SEQ], F32, kind="ExternalInput").ap()
    drm = nc.dram_tensor("rm", [64, 64], F32, kind="ExternalInput").ap()
    dy = nc.dram_tensor("yT", [256, B, TS], F32, kind="ExternalOutput").ap()

    with ExitStack() as es:
        fw = FW(nc, es)
        vec = fw.sb("vecsb", [128, 24], F32)
        c1 = fw.sb("c1", [128, 4], F32)
        nsp = fw.sb("nsp", [128, 2], F32)
        wax = fw.sb("waxb", [128, 4, 128], BF16)
        wuq = fw.sb("wuqb", [128, 4, 192], BF16)
        wukv = fw.sb("wukvb", [128, 4, 256], BF16)
        ones = fw.sb("ones", [128, 128], BF16)
        rm = fw.sb("rmsb", [64, 64], F32)
        NPS = 8
        pst = [fw.ps("ps%d" % i, [128, 512], F32) for i in range(NPS)]
        st = {}

        def nxt(name, n):
            i = st.get(name, 0) % n
            st[name] = st.get(name, 0) + 1
            return i

        fw.dma("sp", vec[:], dvec[:, :], writes=["vec"])
        fw.dma("sp", rm[:], drm[:, :], writes=["rm"])
        fw.dma("pool", wax[:], dwa.rearrange("a d i o -> i (a d) o"), writes=["wax"])
        fw.dma("pool", wuq[:], dwuq.rearrange("(kc p) n -> p kc n", p=128), writes=["wuq"])
        fw.dma("pool", wukv[:], dwukv.rearrange("(kc p) n -> p kc n", p=128), writes=["wukv"])
        fw.op("dve", lambda e: e.memset(ones[:], 1.0), writes=["ones"])
        fw.op("dve", lambda e: e.memset(c1[:, 0:1], 1.0), writes=["c1"])
        fw.op("dve", lambda e: e.memset(c1[:, 1:2], EPS), reads=["c1"], writes=["c1"])
        fw.op("act", lambda e: e.activation(out=nsp[:], in_=vec[:, 9:11], func=AF.Exp, scale=-1.0), reads=["vec"], writes=["nsp"])
        fw.op("act", lambda e: e.activation(out=nsp[:], in_=nsp[:], func=AF.Ln, bias=c1[:, 0:1], scale=1.0), reads=["nsp", "c1"], writes=["nsp"])
        fw.op("dve", lambda e: e.tensor_scalar(out=nsp[:], in0=nsp[:], scalar1=-8.0, scalar2=None, op0=ALU.mult), reads=["nsp"], writes=["nsp"])

        with ExitStack() as es1:
            names = ["tu", "tg", "uc", "t1", "t2", "t3", "hacc"]
            T = {n: es1.enter_context(nc.sbuf_tensor("l_" + n, [128, TS], F32)) for n in names}
            ucb = es1.enter_context(nc.sbuf_tensor("l_ucb", [128, TS], BF16))
            segs = [(0, SEQ), (SEQ, TS)]
            for b in range(B):
                tu, tg, uc, t1, t2, t3, hacc = (T[n] for n in names)
                fw.dma("sp", tg[:], dgu[0, :, b, :], writes=["tg"])
                fw.dma("act", tu[:], dgu[1, :, b, :], writes=["tu"])
                fw.op("dve", lambda e: e.tensor_scalar(out=uc[:], in0=tu[:], scalar1=vec[:, 2:3], scalar2=vec[:, 4:5], op0=ALU.mult, op1=ALU.add),
                      reads=["tu", "vec"], writes=["uc"])
                for (s0, s1) in segs:
                    for (tap, sh) in ((0, 2), (1, 1)):
                        fw.op("dve", lambda e, s0=s0, s1=s1, tap=tap, sh=sh: e.scalar_tensor_tensor(
                            out=uc[:, s0 + sh:s1], in0=tu[:, s0:s1 - sh], scalar=vec[:, tap:tap + 1], in1=uc[:, s0 + sh:s1],
                            op0=ALU.mult, op1=ALU.add), reads=["tu", "vec", "uc"], writes=["uc"])
                    fw.op("dve", lambda e, s0=s0, s1=s1: e.scalar_tensor_tensor(
                        out=uc[:, s0:s1 - 1], in0=tu[:, s0 + 1:s1], scalar=vec[:, 3:4], in1=uc[:, s0:s1 - 1],
                        op0=ALU.mult, op1=ALU.add), reads=["tu", "vec", "uc"], writes=["uc"])
                fw.op("act", lambda e: e.activation(out=ucb[:], in_=uc[:], func=AF.Copy), reads=["uc"], writes=["ucb"])
                for d in range(2):
                    for (t0, tw) in TL:
                        for (ax, dst, dk, bcol) in ((0, t1, "t1", 5 + d), (1, t2, "t2", 7 + d)):
                            pi = nxt("ps", NPS)
                            pk = ("ps", pi)
                            mm_group(fw, pst[pi][:, 0:tw], [(wax[:, ax * 2 + d, :], ucb[:, t0:t0 + tw])],
                                     reads=["wax", "ucb"], writes=[pk])
                            fw.op("act", lambda e, pi=pi, dst=dst, t0=t0, tw=tw, bcol=bcol: e.activation(
                                out=dst[:, t0:t0 + tw], in_=pst[pi][:, 0:tw], func=AF.Sigmoid, bias=vec[:, bcol:bcol + 1], scale=1.0),
                                reads=[pk, "vec"], writes=[dk])
                    fw.op("act", lambda e, d=d: e.activation(out=t1[:], in_=t1[:], func=AF.Exp, scale=nsp[:, d:d + 1]),
                          reads=["t1", "nsp"], writes=["t1"])
                    fw.op("dve", lambda e: e.tensor_tensor(out=t2[:], in0=t2[:], in1=uc[:], op=ALU.mult), reads=["t2", "uc"], writes=["t2"])
                    fw.op("dve", lambda e: e.tensor_tensor(out=t3[:], in0=t1[:], in1=t1[:], op=ALU.mult), reads=["t1"], writes=["t3"])
                    fw.op("act", lambda e: e.activation(out=t3[:], in_=t3[:], func=AF.Sqrt, bias=c1[:, 0:1], scale=-1.0),
                          reads=["t3", "c1"], writes=["t3"])
                    fw.op("dve", lambda e: e.tensor_tensor(out=t2[:], in0=t2[:], in1=t3[:], op=ALU.mult), reads=["t2", "t3"], writes=["t2"])
                    dst = hacc if d == 0 else t3
                    dk = "hacc" if d == 0 else "t3"
                    if d == 0:
                        fw.op("dve", lambda e, dst=dst: e.tensor_tensor_scan(
                            out=dst[:, SEQ:TS], data0=t1[:, SEQ:TS], data1=t2[:, SEQ:TS], initial=0.0, op0=ALU.mult, op1=ALU.add),
                            reads=["t1", "t2"], writes=[dk])
                        fw.op("dve", lambda e, dst=dst: e.tensor_tensor_scan(
                            out=dst[:, 0:SEQ], data0=t1[:, 0:SEQ], data1=t2[:, 0:SEQ], initial=dst[:, TS - 1:TS], op0=ALU.mult, op1=ALU.add),
                            reads=["t1", "t2", dk], writes=[dk])
                    else:
                        fw.op("dve", lambda e, dst=dst: e.tensor_tensor_scan(
                            out=dst[:, SEQ:TS][:, ::-1], data0=t1[:, SEQ:TS][:, ::-1], data1=t2[:, SEQ:TS][:, ::-1],
                            initial=0.0, op0=ALU.mult, op1=ALU.add), reads=["t1", "t2"], writes=[dk])
                        fw.op("dve", lambda e, dst=dst: e.tensor_tensor_scan(
                            out=dst[:, 0:SEQ][:, ::-1], data0=t1[:, 0:SEQ][:, ::-1], data1=t2[:, 0:SEQ][:, ::-1],
                            initial=dst[:, SEQ:SEQ + 1], op0=ALU.mult, op1=ALU.add), reads=["t1", "t2", dk], writes=[dk])
                        fw.op("dve", lambda e: e.tensor_tensor(out=hacc[:], in0=hacc[:], in1=t3[:], op=ALU.add),
                              reads=["hacc", "t3"], writes=["hacc"])
                fw.op("dve", lambda e: e.tensor_tensor(out=t1[:], in0=tg[:], in1=tg[:], op=ALU.mult), reads=["tg", "t1"], writes=["t1"])
                fw.op("dve", lambda e: e.tensor_scalar(out=t1[:], in0=t1[:], scalar1=0.044715, scalar2=1.0, op0=ALU.mult, op1=ALU.add),
                      reads=["t1"], writes=["t1"])
                fw.op("dve", lambda e: e.tensor_tensor(out=t1[:], in0=t1[:], in1=tg[:], op=ALU.mult), reads=["tg", "t1"], writes=["t1"])
                fw.op("act", lambda e: e.activation(out=t1[:], in_=t1[:], func=AF.Sigmoid, scale=GELU_C), reads=["t1"], writes=["t1"])
                fw.op("dve", lambda e: e.tensor_tensor(out=t1[:], in0=t1[:], in1=tg[:], op=ALU.mult), reads=["tg", "t1"], writes=["t1"])
                fw.op("dve", lambda e: e.tensor_tensor(out=t1[:], in0=t1[:], in1=hacc[:], op=ALU.mult), reads=["hacc", "t1"], writes=["t1"])
                fw.dma("sp", dy[0:128, b, :], t1[:], reads=["t1"])
            fw.barrier()

        with ExitStack() as es2:
            def sb2(name, shape, dt):
                return es2.enter_context(nc.sbuf_tensor("m_" + name, list(shape), dt))
            cs = sb2("cssb", [64, 2, SEQ], F32)
            qTn = sb2("qTn", [128, TS], BF16)
            qTr = sb2("qTr", [64, TS], BF16)
            kTn = sb2("kTn", [128, TS], BF16)
            kTr = sb2("kTr", [64, TS], BF16)
            V = sb2("V", [128, NKT, 128], BF16)
            cqt = [sb2("cqt%d" % i, [128, 4, 512], F32) for i in range(2)]
            ckt = [sb2("ckt%d" % i, [128, 4, 512], F32) for i in range(2)]
            krt = [sb2("krt%d" % i, [64, 512], F32) for i in range(2)]
            sq = [sb2("sq%d" % i, [128, 4, 512], BF16) for i in range(2)]
            cqn = [sb2("cqn%d" % i, [128, 4, 512], BF16) for i in range(2)]
            ckn = [sb2("ckn%d" % i, [128, 4, 512], BF16) for i in range(2)]
            rs = [sb2("rs%d" % i, [128, 512], F32) for i in range(2)]
            qrs = [sb2("qrs%d" % i, [64, 512], F32) for i in range(2)]
            ra = [sb2("ra%d" % i, [64, 512], F32) for i in range(2)]
            rb = [sb2("rb%d" % i, [64, 512], F32) for i in range(2)]
            PT = [sb2("PT%d" % i, [128, 512], BF16) for i in range(4)]
            rden = [sb2("rden%d" % i, [128, 512], F32) for i in range(2)]
            ob = [sb2("ob%d" % i, [128, 512], F32) for i in range(2)]
            fw.dma("sp", cs[:], dcs.rearrange("a p t -> p a t"), writes=["cs"])

            def rope(src_ap, srck, t0, tw, dst_ap, dstk):
                pi = nxt("ps", NPS)
                pk = ("ps", pi)
                mm_group(fw, pst[pi][0:64, 0:tw], [(rm[:], src_ap)], reads=["rm", srck], writes=[pk])
                i = nxt("ra", 2)
                fw.op("dve", lambda e: e.tensor_tensor(out=ra[i][:, 0:tw], in0=src_ap, in1=cs[:, 0, t0:t0 + tw], op=ALU.mult),
                      reads=[srck, "cs"], writes=[("ra", i)])
                fw.op("dve", lambda e: e.tensor_tensor(out=rb[i][:, 0:tw], in0=pst[pi][0:64, 0:tw], in1=cs[:, 1, t0:t0 + tw], op=ALU.mult),
                      reads=[pk, "cs"], writes=[("rb", i)])
                fw.op("dve", lambda e: e.tensor_tensor(out=dst_ap, in0=ra[i][:, 0:tw], in1=rb[i][:, 0:tw], op=ALU.add),
                      reads=[("ra", i), ("rb", i)], writes=[dstk])

            for b in range(B):
                for (t0, tw) in TL:
                    is_ctx = t0 >= SEQ
                    i = nxt("tile", 2)
                    fw.dma("sp", cqt[i][:, :, 0:tw], dcq.rearrange("(kc p) b n -> p kc b n", p=128)[:, :, b, t0:t0 + tw], writes=[("cqt", i)])
                    fw.dma("act", ckt[i][:, :, 0:tw], dckv.rearrange("(kc p) b n -> p kc b n", p=128)[:, :, b, t0:t0 + tw], writes=[("ckt", i)])
                    fw.dma("sp", krt[i][:, 0:tw], dkr[:, b, t0:t0 + tw], writes=[("krt", i)])
                    for (src, srck, ncol, dst, dstk) in ((cqt[i], ("cqt", i), 11, cqn[i], ("cqn", i)), (ckt[i], ("ckt", i), 15, ckn[i], ("ckn", i))):
                        j = nxt("sq", 2)
                        fw.op("act", lambda e, src=src, j=j: e.activation(out=sq[j][:, :, 0:tw], in_=src[:, :, 0:tw], func=AF.Square),
                              reads=[srck], writes=[("sq", j)])
                        pi = nxt("ps", NPS)
                        pk = ("ps", pi)
                        mm_group(fw, pst[pi][:, 0:tw], [(ones[:], sq[j][:, kc, 0:tw]) for kc in range(4)],
                                 reads=["ones", ("sq", j)], writes=[pk])
                        r = nxt("rs", 2)
                        fw.op("act", lambda e, pi=pi, r=r: e.activation(out=rs[r][:, 0:tw], in_=pst[pi][:, 0:tw], func=AF.Sqrt,
                                                                      bias=c1[:, 1:2], scale=1.0 / 512), reads=[pk, "c1"], writes=[("rs", r)])
                        fw.op("dve", lambda e, r=r: e.reciprocal(out=rs[r][:, 0:tw], in_=rs[r][:, 0:tw]), reads=[("rs", r)], writes=[("rs", r)])
                        for kc in range(4):
                            fw.op("dve", lambda e, src=src, dst=dst, kc=kc, r=r, ncol=ncol: e.scalar_tensor_tensor(
                                out=dst[:, kc, 0:tw], in0=src[:, kc, 0:tw], scalar=vec[:, ncol + kc:ncol + kc + 1], in1=rs[r][:, 0:tw],
                                op0=ALU.mult, op1=ALU.mult), reads=[srck, "vec", ("rs", r)], writes=[dstk])
                    pi = nxt("ps", NPS); pk = ("ps", pi)
                    mm_group(fw, pst[pi][:, 0:tw], [(wuq[:, kc, 0:128], cqn[i][:, kc, 0:tw]) for kc in range(4)],
                             reads=["wuq", ("cqn", i)], writes=[pk])
                    fw.op("act", lambda e, pi=pi: e.activation(out=qTn[:, t0:t0 + tw], in_=pst[pi][:, 0:tw], func=AF.Copy),
                          reads=[pk], writes=["qTn"])
                    pi = nxt("ps", NPS); pk = ("ps", pi)
                    mm_group(fw, pst[pi][0:64, 0:tw], [(wuq[:, kc, 128:192], cqn[i][:, kc, 0:tw]) for kc in range(4)],
                             reads=["wuq", ("cqn", i)], writes=[pk])
                    if is_ctx:
                        fw.op("act", lambda e, pi=pi: e.activation(out=qTr[:, t0:t0 + tw], in_=pst[pi][0:64, 0:tw], func=AF.Copy),
                              reads=[pk], writes=["qTr"])
                    else:
                        q = nxt("qrs", 2)
                        fw.op("act", lambda e, pi=pi, q=q: e.activation(out=qrs[q][:, 0:tw], in_=pst[pi][0:64, 0:tw], func=AF.Copy),
                              reads=[pk], writes=[("qrs", q)])
                        rope(qrs[q][:, 0:tw], ("qrs", q), t0, tw, qTr[:, t0:t0 + tw], "qTr")
                    pi = nxt("ps", NPS); pk = ("ps", pi)
                    mm_group(fw, pst[pi][:, 0:tw], [(wukv[:, kc, 0:128], ckn[i][:, kc, 0:tw]) for kc in range(4)],
                             reads=["wukv", ("ckn", i)], writes=[pk])
                    fw.op("act", lambda e, pi=pi: e.activation(out=kTn[:, t0:t0 + tw], in_=pst[pi][:, 0:tw], func=AF.Copy),
                          reads=[pk], writes=["kTn"])
                    for sub in range(tw // 128):
                        pi = nxt("ps", NPS); pk = ("ps", pi)
                        mm_group(fw, pst[pi][:, 0:128], [(ckn[i][:, kc, sub * 128:(sub + 1) * 128], wukv[:, kc, 128:256]) for kc in range(4)],
                                 reads=["wukv", ("ckn", i)], writes=[pk])
                        fw.op("dve", lambda e, pi=pi, sub=sub: e.tensor_copy(out=V[:, t0 // 128 + sub, :], in_=pst[pi][:, 0:128]),
                              reads=[pk], writes=["V"])
                    if is_ctx:
                        fw.op("act", lambda e, i=i: e.activation(out=kTr[:, t0:t0 + tw], in_=krt[i][:, 0:tw], func=AF.Copy),
                              reads=[("krt", i)], writes=["kTr"])
                    else:
                        rope(krt[i][:, 0:tw], ("krt", i), t0, tw, kTr[:, t0:t0 + tw], "kTr")
                for (t0, tw) in TL:
                    is_ctx = t0 >= SEQ
                    kts = list(range(SEQ // 128, NKT)) if is_ctx else list(range(NKT))
                    oi = 4 + nxt("po", 2)
                    di = 6 + nxt("pd", 2)
                    ok, dk = ("ps", oi), ("ps", di)
                    for n, kt in enumerate(kts):
                        si = nxt("psS", 4)
                        sk = ("ps", si)
                        mm_group(fw, pst[si][:, 0:tw],
                                 [(kTn[:, kt * 128:(kt + 1) * 128], qTn[:, t0:t0 + tw]), (kTr[:, kt * 128:(kt + 1) * 128], qTr[:, t0:t0 + tw])],
                                 reads=["kTn", "qTn", "kTr", "qTr"], writes=[sk])
                        p = nxt("PT", 4)
                        fw.op("act", lambda e, si=si, p=p: e.activation(out=PT[p][:, 0:tw], in_=pst[si][:, 0:tw], func=AF.Exp, scale=ATTN_SCALE),
                              reads=[sk], writes=[("PT", p)])
                        first, last = (n == 0), (n == len(kts) - 1)
                        fw.op("pe", lambda e, oi=oi, kt=kt, p=p, first=first, last=last: e.matmul(
                            pst[oi][:, 0:tw], V[:, kt, :], PT[p][:, 0:tw], start=first, stop=last),
                            reads=["V", ("PT", p)], writes=[ok], nosame=not first)
                        fw.op("pe", lambda e, di=di, p=p, first=first, last=last: e.matmul(
                            pst[di][:, 0:tw], ones[:], PT[p][:, 0:tw], start=first, stop=last),
                            reads=["ones", ("PT", p)], writes=[dk], nosame=not first)
                    r = nxt("rden", 2)
                    fw.op("dve", lambda e, di=di, r=r: e.reciprocal(out=rden[r][:, 0:tw], in_=pst[di][:, 0:tw]), reads=[dk], writes=[("rden", r)])
                    o = nxt("ob", 2)
                    fw.op("dve", lambda e, oi=oi, r=r, o=o: e.tensor_tensor(out=ob[o][:, 0:tw], in0=pst[oi][:, 0:tw], in1=rden[r][:, 0:tw], op=ALU.mult),
                          reads=[ok, ("rden", r)], writes=[("ob", o)])
                    fw.dma("sp", dy[128:256, b, t0:t0 + tw], ob[o][:, 0:tw], reads=[("ob", o)])
        fw.finish()
    return nc


def rope_consts():
    half = 32
    inv_freq = (10000.0 ** (-np.arange(0, half, 2, dtype=np.float32) / half)).astype(np.float32)
    t = np.arange(SEQ)
    row = (t // 64).astype(np.float32)
    col = (t % 64).astype(np.float32)
    ang_r = row[:, None] * inv_freq
    ang_c = col[:, None] * inv_freq
    ang = np.concatenate([ang_r, ang_r, ang_c, ang_c], axis=-1).astype(np.float32)
    cos = np.cos(ang).astype(np.float32)
    sin = np.sin(ang).astype(np.float32)
    sign = np.ones(64, np.float32)
    sign[0:16] = -1.0
    sign[32:48] = -1.0
    perm = np.concatenate([np.arange(16, 32), np.arange(0, 16), np.arange(48, 64), np.arange(32, 48)])
    rm = np.zeros((64, 64), np.float32)
    rm[perm, np.arange(64)] = 1.0
    cs = np.stack([cos.T, (sin * sign).T], 0)
    return np.ascontiguousarray(cs), rm


def ab_host_inputs(pT, e, P):
    cs, rm = rope_consts()
    maps = []
    cq = np.ascontiguousarray(pT[2048:2560])
    ckv = np.ascontiguousarray(pT[2560:3072])
    kr = np.ascontiguousarray(pT[3072:3136])
    for h in range(NCORES):
        sl = slice(h * 128, (h + 1) * 128)
        gu = np.stack([pT[h * 128:(h + 1) * 128], pT[1024 + h * 128:1024 + (h + 1) * 128]], 0)
        vec = np.zeros((128, 24), np.float32)
        vec[:, 0:4] = P["lru_conv_w"][e][:, sl].T
        vec[:, 4] = P["lru_conv_b"][e][sl]
        vec[:, 5:7] = P["lru_b_a"][e][:, sl].T
        vec[:, 7:9] = P["lru_b_x"][e][:, sl].T
        vec[:, 9:11] = P["lru_lambda"][e][:, sl].T
        vec[:, 11:15] = P["mla_q_norm_w"][e].reshape(4, 128).T
        vec[:, 15:19] = P["mla_kv_norm_w"][e].reshape(4, 128).T
        wax = np.stack([P["lru_w_a"][e][:, h], P["lru_w_x"][e][:, h]], 0)
        maps.append({"guT": np.ascontiguousarray(gu), "cqT": cq, "ckvT": ckv, "krT": kr, "vecs": vec,
                     "wax": np.ascontiguousarray(wax),
                     "wuq": np.ascontiguousarray(P["mla_w_uq"][e][:, h * 192:(h + 1) * 192]),
                     "wukv": np.ascontiguousarray(P["mla_w_ukv"][e][:, h * 256:(h + 1) * 256]),
                     "cs": cs, "rm": rm})
    return maps


def ab_host_gather(results):
    y = np.empty((D, B, TS), np.float32)
    for h in range(NCORES):
        o = results[h]["yT"]
        y[h * 128:(h + 1) * 128] = o[0:128]
        y[1024 + h * 128:1024 + (h + 1) * 128] = o[128:256]
    return y


def tokT_to_full(arrs):
    F = arrs[0].shape[0]
    full = np.empty((F, B, TS), arrs[0].dtype)
    for k in range(NCORES):
        b, r = core_tok(k)
        full[:, b, r * NL:(r + 1) * NL] = arrs[k][:, :NL]
        full[:, b, SEQ + r * NCX:SEQ + (r + 1) * NCX] = arrs[k][:, NL:]
    return full


def full_to_tokT(full):
    outs = []
    for k in range(NCORES):
        b, r = core_tok(k)
        outs.append(np.ascontiguousarray(np.concatenate(
            [full[:, b, r * NL:(r + 1) * NL], full[:, b, SEQ + r * NCX:SEQ + (r + 1) * NCX]], axis=1)))
    return outs


CH = 64
NCH = TS // CH
NPT = TS // 128


def build_hg(layer):
    nc = bass.Bass("TRN2", target_bir_lowering=False)
    dp = nc.dram_tensor("p5", [2, 5, 128, B, TS], F32, kind="ExternalInput").ap()
    dlb = nc.dram_tensor("lbl", [128, 2, DEPTH], F32, kind="ExternalInput").ap()
    dnw = nc.dram_tensor("nw", [128, 1], F32, kind="ExternalInput").ap()
    dmask = nc.dram_tensor("masks", [2, 128, 128], F32, kind="ExternalInput").ap()
    dident = nc.dram_tensor("ident", [128, 128], F32, kind="ExternalInput").ap()
    dy = nc.dram_tensor("yT", [2, 128, B, TS], F32, kind="ExternalOutput").ap()

    with ExitStack() as es:
        fw = FW(nc, es)
        names = ["qs", "iE", "oacc", "kk", "lf", "G", "tmp"]
        T = {n: fw.sb("h_" + n, [128, TS], F32) for n in names}
        qs, iE, oacc, kk, lf, G, tmp = (T[n] for n in names)
        qt = fw.sb("qt", [128, TS], BF16)
        kt = fw.sb("kt", [128, TS], BF16)
        Vt = fw.sb("Vt", [128, NPT, 128], BF16)
        ones_b = fw.sb("ones_b", [128, 512], BF16)
        ones_m = fw.sb("ones_m", [128, 128], BF16)
        masks = fw.sb("masks_sb", [128, 2, 128], F32)
        ident = fw.sb("ident_sb", [128, 128], F32)
        lbl = fw.sb("lbl_sb", [128, 2, DEPTH], F32)
        lbs = fw.sb("lbs", [128, 8], F32)
        nwt = fw.sb("nwt", [128, 1], F32)
        c1 = fw.sb("c1h", [128, 2], F32)
        dec = fw.sb("dec", [128, NCH], F32)
        S = fw.sb("S", [128, 128], F32)
        Sb = fw.sb("Sb", [128, 128], BF16)
        Am = [fw.sb("Am%d" % i, [128, 128], BF16) for i in range(2)]
        khat = [fw.sb("khat%d" % i, [128, 128], BF16) for i in range(2)]
        sqb = [fw.sb("sqb%d" % i, [128, 512], BF16) for i in range(2)]
        rsb = [fw.sb("rsb%d" % i, [128, 512], F32) for i in range(2)]
        NPS = 8
        pst = [fw.ps("ps%d" % i, [128, 512], F32) for i in range(NPS)]
        st = {}

        def nxt(name, n):
            i = st.get(name, 0) % n
            st[name] = st.get(name, 0) + 1
            return i

        def newps():
            pi = nxt("ps", NPS)
            return pi, ("ps", pi)

        fw.dma("sp", masks[:], dmask.rearrange("a s t -> s a t"), writes=["masks"])
        fw.dma("sp", ident[:], dident[:, :], writes=["ident"])
        fw.dma("sp", lbl[:], dlb[:, :, :], writes=["lbl"])
        fw.dma("sp", nwt[:], dnw[:, :], writes=["nwt"])
        fw.op("dve", lambda e: e.memset(ones_b[:], 1.0), writes=["ones_b"])
        fw.op("dve", lambda e: e.memset(ones_m[:], 1.0), writes=["ones_m"])
        fw.op("dve", lambda e: e.memset(c1[:, 0:1], 1.0), writes=["c1"])
        fw.op("dve", lambda e: e.memset(c1[:, 1:2], EPS), reads=["c1"], writes=["c1"])
        fw.op("act", lambda e: e.activation(out=lbl[:], in_=lbl[:], func=AF.Exp), reads=["lbl"], writes=["lbl"])
        fw.op("dve", lambda e: e.tensor_reduce(out=lbs[:, 0:2], in_=lbl[:], axis=AX.X, op=ALU.add), reads=["lbl"], writes=["lbs"])
        fw.op("dve", lambda e: e.tensor_reduce(out=lbs[:, 2:4], in_=lbl[:, :, 1:layer + 1], axis=AX.X, op=ALU.add), reads=["lbl", "lbs"], writes=["lbs"])
        fw.op("dve", lambda e: e.reciprocal(out=lbs[:, 0:2], in_=lbs[:, 0:2]), reads=["lbs"], writes=["lbs"])
        fw.op("dve", lambda e: e.tensor_tensor(out=lbs[:, 4:6], in0=lbs[:, 2:4], in1=lbs[:, 0:2], op=ALU.mult), reads=["lbs"], writes=["lbs"])
        fw.op("dve", lambda e: e.tensor_scalar(out=lbs[:, 6:8], in0=lbs[:, 4:6], scalar1=-1.0, scalar2=1.0, op0=ALU.mult, op1=ALU.add),
              reads=["lbs"], writes=["lbs"])

        def v3(t):
            return t[:].rearrange("p (c j) -> p c j", j=CH)

        for hd in range(2):
            for b in range(B):
                fw.dma("sp", qs[:], dp[hd, 0, :, b, :], writes=["qs"])
                fw.dma("act", iE[:], dp[hd, 3, :, b, :], writes=["iE"])
                fw.op("act", lambda e: e.activation(out=tmp[:], in_=qs[:], func=AF.Sigmoid), reads=["qs"], writes=["tmp"])
                fw.op("dve", lambda e: e.tensor_tensor(out=qs[:], in0=qs[:], in1=tmp[:], op=ALU.mult), reads=["qs", "tmp"], writes=["qs"])
                for pt in range(NPT):
                    pi, pk = newps()
                    mm_group(fw, pst[pi][:, 0:128], [(iE[:, pt * 128:(pt + 1) * 128], ident[:])], reads=["iE", "ident"], writes=[pk])
                    eng = "act" if pt % 2 == 0 else "dve"
                    if eng == "act":
                        fw.op("act", lambda e, pi=pi, pt=pt: e.activation(out=Vt[:, pt, :], in_=pst[pi][:, 0:128], func=AF.Copy), reads=[pk], writes=["Vt"])
                    else:
                        fw.op("dve", lambda e, pi=pi, pt=pt: e.tensor_copy(out=Vt[:, pt, :], in_=pst[pi][:, 0:128]), reads=[pk], writes=["Vt"])
                for d in range(2):
                    fw.dma("sp", kk[:], dp[hd, 1 + d, :, b, :], writes=["kk"])
                    fw.op("act", lambda e: e.activation(out=kk[:], in_=kk[:], func=AF.Sigmoid), reads=["kk"], writes=["kk"])
                    fw.op("dve", lambda e, hd=hd: e.tensor_scalar(out=kk[:], in0=kk[:], scalar1=lbs[:, 6 + hd:7 + hd], scalar2=lbs[:, 4 + hd:5 + hd],
                                                                op0=ALU.mult, op1=ALU.add), reads=["kk", "lbs"], writes=["kk"])
                    fw.op("act", lambda e: e.activation(out=lf[:], in_=kk[:], func=AF.Ln), reads=["kk"], writes=["lf"])
                    fw.op("dve", lambda e: e.tensor_scalar(out=kk[:], in0=kk[:], scalar1=-1.0, scalar2=1.0, op0=ALU.mult, op1=ALU.add),
                          reads=["kk"], writes=["kk"])
                    for blk in range((TS + 511) // 512):
                        c0 = blk * 512
                        cw = min(512, TS - c0)
                        init = 0.0 if blk == 0 else G[:, c0 - 1:c0]
                        fw.op("dve", lambda e, c0=c0, cw=cw, init=init: e.tensor_tensor_scan(
                            out=G[:, c0:c0 + cw], data0=ones_b[:, 0:cw], data1=lf[:, c0:c0 + cw], initial=init, op0=ALU.mult, op1=ALU.add),
                            reads=["ones_b", "lf", "G"], writes=["G"])
                    G3, E3, lf3 = v3(G), v3(iE), v3(lf)
                    if d == 0:
                        fw.op("dve", lambda e: e.tensor_copy(out=E3[:, 0:1, :], in_=G3[:, 0:1, :]), reads=["G", "iE", "Vt"], writes=["iE"])
                        fw.op("dve", lambda e: e.tensor_tensor(out=E3[:, 1:NCH, :], in0=G3[:, 1:NCH, :],
                                                               in1=G3[:, 0:NCH - 1, CH - 1:CH].broadcast_to([128, NCH - 1, CH]), op=ALU.subtract),
                              reads=["G", "iE"], writes=["iE"])
                        last_col = CH - 1
                    else:
                        fw.op("dve", lambda e: e.tensor_tensor(out=E3[:, :, :], in0=G3[:, :, CH - 1:CH].broadcast_to([128, NCH, CH]),
                                                               in1=G3[:, :, :], op=ALU.subtract), reads=["G", "iE", "Vt"], writes=["iE"])
                        fw.op("dve", lambda e: e.tensor_tensor(out=iE[:], in0=iE[:], in1=lf[:], op=ALU.add), reads=["lf", "iE"], writes=["iE"])
                        last_col = 0
                    fw.op("act", lambda e, last_col=last_col: e.activation(out=dec[:, :], in_=E3[:, :, last_col], func=AF.Exp),
                          reads=["iE"], writes=["dec"])
                    fw.op("act", lambda e: e.activation(out=tmp[:], in_=iE[:], func=AF.Exp), reads=["iE"], writes=["tmp"])
                    fw.op("dve", lambda e: e.tensor_tensor(out=qt[:], in0=qs[:], in1=tmp[:], op=ALU.mult), reads=["qs", "tmp"], writes=["qt"])
                    fw.op("act", lambda e: e.activation(out=tmp[:], in_=iE[:], func=AF.Exp, scale=-1.0), reads=["iE", "qt"], writes=["tmp"])
                    fw.op("dve", lambda e: e.tensor_tensor(out=kt[:], in0=kk[:], in1=tmp[:], op=ALU.mult), reads=["kk", "tmp"], writes=["kt"])
                    tmp3 = v3(tmp)
                    fw.op("dve", lambda e, last_col=last_col: e.tensor_tensor(
                        out=tmp3[:, :, :], in0=E3[:, :, last_col:last_col + 1].broadcast_to([128, NCH, CH]), in1=E3[:, :, :], op=ALU.subtract),
                        reads=["iE", "kt"], writes=["tmp"])
                    fw.op("act", lambda e: e.activation(out=tmp[:], in_=tmp[:], func=AF.Exp), reads=["tmp"], writes=["tmp"])
                    fw.op("dve", lambda e: e.tensor_tensor(out=tmp[:], in0=tmp[:], in1=kk[:], op=ALU.mult), reads=["tmp", "kk"], writes=["tmp"])
                    fw.op("dve", lambda e: e.memset(S[:], 0.0), reads=["S"], writes=["S"])
                    fw.op("dve", lambda e: e.memset(Sb[:], 0.0), reads=["Sb"], writes=["Sb"])
                    if d == 0:
                        ptiles = list(range(SEQ // 128, NPT)) + list(range(SEQ // 128))
                    else:
                        ptiles = list(range(NPT - 1, -1, -1))
                    for pt in ptiles:
                        cols = slice(pt * 128, (pt + 1) * 128)
                        pi, pk = newps()
                        mm_group(fw, pst[pi][:, 0:128], [(kt[:, cols], qt[:, cols])], reads=["kt", "qt"], writes=[pk])
                        a = nxt("Am", 2)
                        fw.op("dve", lambda e, pi=pi, a=a, d=d: e.tensor_tensor(out=Am[a][:], in0=pst[pi][:, 0:128], in1=masks[:, d, :], op=ALU.mult),
                              reads=[pk, "masks"], writes=[("Am", a)])
                        pi2, pk2 = newps()
                        mm_group(fw, pst[pi2][:, 0:128], [(tmp[:, cols], ident[:])], reads=["tmp", "ident"], writes=[pk2])
                        kh = nxt("khat", 2)
                        fw.op("act", lambda e, pi2=pi2, kh=kh: e.activation(out=khat[kh][:], in_=pst[pi2][:, 0:128], func=AF.Copy),
                              reads=[pk2], writes=[("khat", kh)])
                        halves = (0, 1) if d == 0 else (1, 0)
                        for hf in halves:
                            c = pt * 2 + hf
                            rows = slice(hf * 64, (hf + 1) * 64)
                            tcols = slice(c * CH, (c + 1) * CH)
                            pi3, pk3 = newps()
                            mm_group(fw, pst[pi3][:, 0:CH],
                                     [(Vt[rows, pt, :], Am[a][rows, hf * 64:(hf + 1) * 64]), (Sb[:], qt[:, tcols])],
                                     reads=["Vt", ("Am", a), "Sb", "qt"], writes=[pk3])
                            if d == 0:
                                fw.op("act", lambda e, pi3=pi3, tcols=tcols: e.activation(out=oacc[:, tcols], in_=pst[pi3][:, 0:CH], func=AF.Copy),
                                      reads=[pk3], writes=["oacc"])
                            else:
                                fw.op("dve", lambda e, pi3=pi3, tcols=tcols: e.tensor_tensor(out=oacc[:, tcols], in0=pst[pi3][:, 0:CH], in1=oacc[:, tcols], op=ALU.add),
                                      reads=[pk3, "oacc"], writes=["oacc"])
                            pi4, pk4 = newps()
                            mm_group(fw, pst[pi4][:, 0:128], [(khat[kh][rows, :], Vt[rows, pt, :])], reads=[("khat", kh), "Vt"], writes=[pk4])
                            fw.op("dve", lambda e, pi4=pi4, c=c: e.scalar_tensor_tensor(
                                out=S[:], in0=S[:], scalar=dec[:, c:c + 1], in1=pst[pi4][:, 0:128], op0=ALU.mult, op1=ALU.add),
                                reads=["S", "dec", pk4], writes=["S"])
                            fw.op("act", lambda e: e.activation(out=Sb[:], in_=S[:], func=AF.Copy), reads=["S"], writes=["Sb"])
                fw.dma("sp", kk[:], dp[hd, 4, :, b, :], writes=["kk"])
                fw.op("act", lambda e: e.activation(out=lf[:], in_=kk[:], func=AF.Sigmoid), reads=["kk"], writes=["lf"])
                fw.op("dve", lambda e: e.tensor_tensor(out=kk[:], in0=kk[:], in1=lf[:], op=ALU.mult), reads=["kk", "lf"], writes=["kk"])
                for (t0, tw) in TL:
                    j = nxt("sqb", 2)
                    fw.op("act", lambda e, j=j, t0=t0, tw=tw: e.activation(out=sqb[j][:, 0:tw], in_=oacc[:, t0:t0 + tw], func=AF.Square),
                          reads=["oacc"], writes=[("sqb", j)])
                    pi, pk = newps()
                    mm_group(fw, pst[pi][:, 0:tw], [(ones_m[:], sqb[j][:, 0:tw])], reads=["ones_m", ("sqb", j)], writes=[pk])
                    r = nxt("rsb", 2)
                    fw.op("act", lambda e, pi=pi, r=r, tw=tw: e.activation(out=rsb[r][:, 0:tw], in_=pst[pi][:, 0:tw], func=AF.Sqrt, bias=c1[:, 1:2], scale=1.0 / 128),
                          reads=[pk, "c1"], writes=[("rsb", r)])
                    fw.op("dve", lambda e, r=r, tw=tw: e.reciprocal(out=rsb[r][:, 0:tw], in_=rsb[r][:, 0:tw]), reads=[("rsb", r)], writes=[("rsb", r)])
                    fw.op("dve", lambda e, r=r, t0=t0, tw=tw: e.scalar_tensor_tensor(
                        out=G[:, t0:t0 + tw], in0=oacc[:, t0:t0 + tw], scalar=nwt[:, 0:1], in1=rsb[r][:, 0:tw], op0=ALU.mult, op1=ALU.mult),
                        reads=["oacc", "nwt", ("rsb", r), "G"], writes=["G"])
                fw.op("dve", lambda e: e.tensor_tensor(out=G[:], in0=G[:], in1=kk[:], op=ALU.mult), reads=["G", "kk"], writes=["G"])
                fw.dma("sp", dy[hd, :, b, :], G[:], reads=["G"])
        fw.finish()
    return nc


def hg_consts():
    s = np.arange(128)[:, None]
    t = np.arange(128)[None, :]
    same = (s // CH) == (t // CH)
    mf = (same & (t >= s)).astype(np.float32)
    mb = (same & (t <= s)).astype(np.float32)
    return np.stack([mf, mb], 0), np.eye(128, dtype=np.float32)


def hg_host_inputs(pT, o, P):
    masks, ident = hg_consts()
    maps = []
    for k in range(NCORES):
        p5 = np.empty((2, 5, 128, B, TS), np.float32)
        lbl = np.empty((128, 2, DEPTH), np.float32)
        for j in range(2):
            hh = 2 * k + j
            for s in range(5):
                p5[j, s] = pT[s * D + hh * 128:s * D + (hh + 1) * 128]
            lbl[:, j, :] = P["hg_lb_logits"][:, hh * 128:(hh + 1) * 128].T
        maps.append({"p5": p5, "lbl": lbl, "nw": np.ascontiguousarray(P["hg_norm_w"][o].reshape(128, 1)),
                     "masks": masks, "ident": ident})
    return maps


def hg_host_gather(results):
    y = np.empty((D, B, TS), np.float32)
    for k in range(NCORES):
        o = results[k]["yT"]
        for j in range(2):
            hh = 2 * k + j
            y[hh * 128:(hh + 1) * 128] = o[j]
    return y


def kernel(**inp):
    P = {k: np.asarray(v, dtype=np.float32) for k, v in inp.items()}
    x, ctx = P["x"], P["ctx"]
    m = ada_host_gather(run(get_nc("ada", build_ada), ada_host_inputs(P["c"], P["c_ctx"], P["ada_w"], P["ada_b"])))
    xs = to_core_T(x, ctx)

    def tok_maps(l_post, l_pre, xs, ys):
        maps = []
        nmlp = vec_pk(P["norm_mlp_w"][l_post]) if l_post is not None else np.zeros((128, KC), np.float32)
        npre = vec_pk(P["norm_mix_w"][l_pre]) if l_pre < DEPTH else vec_pk(P["final_norm_w"])
        nrm = np.ascontiguousarray(np.stack([nmlp, npre], 1))
        for k in range(NCORES):
            b, r = core_tok(k)
            d = {"xT": xs[k], "adaL": ada_sets(m, l_post, l_pre, b), "adaC": ada_sets(m, l_post, l_pre, 2), "nrm": nrm}
            if l_post is not None:
                d["yT"] = ys[k]
                d["w_out"] = P["ab_w_out"][l_post // 2] if l_post % 2 == 0 else P["hg_w_out"][l_post // 2]
                d["w1"] = P["mlp_w1"][l_post]
                d["w2"] = P["mlp_w2"][l_post]
            if l_pre < DEPTH:
                d["w_in"] = P["ab_w_in"][l_pre // 2] if l_pre % 2 == 0 else P["hg_w_in"][l_pre // 2]
            maps.append(d)
        return maps

    res = run(get_nc("tok", build_tok, False, "ab"), tok_maps(None, 0, xs, None))
    for l in range(DEPTH):
        pT = tokT_to_full([r["pT"] for r in res])
        xs = [r["xoT"] for r in res]
        if l % 2 == 0:
            y = ab_host_gather(run(get_nc("ab", build_ab), ab_host_inputs(pT, l // 2, P)))
        else:
            y = hg_host_gather(run(get_nc("hg", build_hg, l), hg_host_inputs(pT, l // 2, P)))
        ys = full_to_tokT(y)
        pre = "final" if l + 1 == DEPTH else ("ab" if (l + 1) % 2 == 0 else "hg")
        res = run(get_nc("tok", build_tok, True, pre), tok_maps(l, l + 1, xs, ys))
    lat, _ = from_core_T([r["outT"] for r in res])
    return np.ascontiguousarray(lat.astype(np.float32))
```

```python
import numpy as np
from contextlib import ExitStack
import concourse.bass as bass
import concourse.mybir as mybir
from concourse.bass_utils import run_bass_kernel_spmd

F32 = mybir.dt.float32
BF16 = mybir.dt.bfloat16
AF = mybir.ActivationFunctionType
ALU = mybir.AluOpType
AX = mybir.AxisListType

D = 2048
KC = D // 128
B = 2
SEQ = 4096
CTX = 256
DEPTH = 4
DFF = 8192
NCORES = 8
NL = B * SEQ // NCORES
NCX = B * CTX // NCORES
NT = NL + NCX
EPS = 1e-6
D_AB_IN = 3136
D_HG_IN = 10240
TS = SEQ + CTX

SAME_ENG_SYNC = True


class Rec:
    def __init__(self):
        self.calls = []

    def __getattr__(self, name):
        def f(*a, **k):
            self.calls.append((name, a, k))
            return self
        return f


class FW:
    ENGS = ("pe", "dve", "act", "pool", "sp")

    def __init__(self, nc, es, n_dma_sems=48):
        self.nc = nc
        self.es = es
        self.eng = {"pe": nc.tensor, "dve": nc.vector, "act": nc.scalar, "pool": nc.gpsimd, "sp": nc.sync}
        self.sem = {e: es.enter_context(nc.semaphore("s_" + e)) for e in self.ENGS}
        self.cnt = {e: 0 for e in self.ENGS}
        self.prog = {e: [] for e in self.ENGS}
        self.seen = {e: {} for e in self.ENGS}
        self.dsem = [es.enter_context(nc.semaphore("d%d" % i)) for i in range(n_dma_sems)]
        self.dval = [0] * n_dma_sems
        self.drr = 0
        self.last_w = {}
        self.readers = {}
        self.semobj = {}
        self.n_ps = 0

    def sb(self, name, shape, dt):
        return self.es.enter_context(self.nc.sbuf_tensor(name, list(shape), dt))

    def ps(self, name, shape, dt=F32):
        return self.es.enter_context(self.nc.psum_tensor(name, list(shape), dt))

    def _collect(self, eng, reads, writes, nosame=False):
        toks = []
        for k in reads:
            t = self.last_w.get(k)
            if t is not None:
                toks.append(t)
        for k in writes:
            t = self.last_w.get(k)
            if t is not None:
                toks.append(t)
            toks.extend(self.readers.get(k, ()))
        need = {}
        for (sid, val) in toks:
            if ((not SAME_ENG_SYNC) or nosame) and sid == ("e", eng):
                continue
            if self.seen[eng].get(sid, 0) >= val:
                continue
            if need.get(sid, 0) < val:
                need[sid] = val
        for sid, val in need.items():
            self.seen[eng][sid] = val
        return list(need.items())

    def _commit(self, tok, reads, writes):
        for k in writes:
            self.last_w[k] = tok
            self.readers[k] = []
        for k in reads:
            if k in writes:
                continue
            self.readers.setdefault(k, []).append(tok)

    def _semof(self, sid):
        kind, x = sid
        return self.sem[x] if kind == "e" else self.dsem[x]

    def op(self, eng, fn, reads=(), writes=(), nosame=False):
        waits = self._collect(eng, reads, writes, nosame)
        self.cnt[eng] += 1
        tok = (("e", eng), self.cnt[eng])
        rec = Rec()
        fn(rec)
        calls = rec.calls

        def run(e, calls=calls):
            ins = None
            for (name, a, k) in calls:
                ins = getattr(e, name)(*a, **k)
            return ins
        self.prog[eng].append((waits, run, ("e", eng), 1))
        self._commit(tok, reads, writes)
        return tok

    def dma(self, eng, out, in_, reads=(), writes=(), **kw):
        waits = self._collect(eng, reads, writes)
        i = self.drr
        self.drr = (self.drr + 1) % len(self.dsem)
        sid = ("d", i)
        if self.dval[i] > 0 and self.seen[eng].get(sid, 0) < self.dval[i]:
            waits.append((sid, self.dval[i]))
            self.seen[eng][sid] = self.dval[i]
        self.dval[i] += 16
        tok = (sid, self.dval[i])
        self.prog[eng].append((waits, (lambda e, out=out, in_=in_, kw=kw: e.dma_start(out=out, in_=in_, **kw)), sid, 16))
        self._commit(tok, reads, writes)
        return tok

    def barrier(self):
        allw = [(("e", e), self.cnt[e]) for e in self.ENGS if self.cnt[e] > 0]
        allw += [(("d", i), v) for i, v in enumerate(self.dval) if v > 0]
        for e in self.ENGS:
            waits = []
            for (sid, val) in allw:
                if sid == ("e", e):
                    continue
                if self.seen[e].get(sid, 0) < val:
                    waits.append((sid, val))
                    self.seen[e][sid] = val
            if waits:
                self.prog[e].append((waits, None, None, 0))

    def finish(self, eng="sp"):
        waits = []
        for i, v in enumerate(self.dval):
            if v > 0:
                waits.append((("d", i), v))
        self.prog[eng].append((waits, None, None, 0))
        block = self.es.enter_context(self.nc.Block())

        def replay(name):
            def body(e):
                for (waits, fn, sid, inc) in self.prog[name]:
                    for (wsid, val) in waits:
                        e.wait_ge(self._semof(wsid), val)
                    if fn is not None:
                        ins = fn(e)
                        ins.then_inc(self._semof(sid), inc)
            return body

        block.tensor(replay("pe"))
        block.vector(replay("dve"))
        block.scalar(replay("act"))
        block.gpsimd(replay("pool"))
        block.sync(replay("sp"))


def mm_group(fw, out_ps, pairs, reads, writes):
    n = len(pairs)

    def fn(e):
        ins = None
        for i, (l, r) in enumerate(pairs):
            ins = e.matmul(out_ps, l, r, start=(i == 0), stop=(i == n - 1))
        return ins
    return fw.op("pe", fn, reads=reads, writes=writes)


ADA_COLS = 6 * D // NCORES
ADA_J = ADA_COLS // 128


def build_ada():
    nc = bass.Bass("TRN2", target_bir_lowering=False)
    cv = nc.dram_tensor("cv", [128, KC, 3], F32, kind="ExternalInput").ap()
    w = nc.dram_tensor("w", [DEPTH, D, ADA_COLS], F32, kind="ExternalInput").ap()
    bb = nc.dram_tensor("b", [128, DEPTH, ADA_J], F32, kind="ExternalInput").ap()
    out = nc.dram_tensor("out", [128, DEPTH, ADA_J, 3], F32, kind="ExternalOutput").ap()
    with ExitStack() as es:
        fw = FW(nc, es)
        cvt = fw.sb("cvt", [128, KC, 3], F32)
        sg = fw.sb("sg", [128, KC, 3], F32)
        sv = fw.sb("sv", [128, KC, 3], F32)
        bt = fw.sb("bt", [128, DEPTH, ADA_J], F32)
        ot = fw.sb("ot", [128, DEPTH, ADA_J, 3], F32)
        NB = 3
        wt = [fw.sb("wt%d" % i, [128, KC, 512], F32) for i in range(NB)]
        pst = [fw.ps("ps%d" % i, [128, 512], F32) for i in range(4)]
        fw.dma("sp", cvt[:], cv[:, :, :], writes=["cvt"])
        fw.dma("sp", bt[:], bb[:, :, :], writes=["bt"])
        fw.op("act", lambda e: e.activation(out=sg[:], in_=cvt[:], func=AF.Sigmoid), reads=["cvt"], writes=["sg"])
        fw.op("dve", lambda e: e.tensor_tensor(out=sv[:], in0=cvt[:], in1=sg[:], op=ALU.mult), reads=["cvt", "sg"], writes=["sv"])
        it = 0
        for l in range(DEPTH):
            for g in range(ADA_COLS // 512):
                wb = wt[it % NB]
                wk = "wt%d" % (it % NB)
                src = w[l].rearrange("(kc p) n -> p kc n", p=128)[:, :, g * 512:(g + 1) * 512]
                q = "sp" if it % 2 == 0 else "act"
                fw.dma(q, wb[:], src, writes=[wk])
                for jj in range(4):
                    j = g * 4 + jj
                    pi = (it * 4 + jj) % 4
                    pk = "ps%d" % pi
                    mm_group(fw, pst[pi][:, 0:3],
                             [(wb[:, kc, jj * 128:(jj + 1) * 128], sv[:, kc, :]) for kc in range(KC)],
                             reads=[wk, "sv"], writes=[pk])
                    fw.op("dve", lambda e, pi=pi, l=l, j=j: e.tensor_scalar(
                        out=ot[:, l, j, :], in0=pst[pi][:, 0:3], scalar1=bt[:, l, j:j + 1], scalar2=None, op0=ALU.add),
                        reads=[pk, "bt"], writes=["ot"])
                it += 1
        fw.dma("sp", out[:, :, :, :], ot[:], reads=["ot"])
        fw.finish()
    return nc


TT = [(0, 512), (512, 512), (1024, 64)]


def build_tok(post, pre):
    nin = {"ab": D_AB_IN, "hg": D_HG_IN, "final": 0}[pre]
    nc = bass.Bass("TRN2", target_bir_lowering=False)
    xin = nc.dram_tensor("xT", [D, NT], F32, kind="ExternalInput").ap()
    adaL = nc.dram_tensor("adaL", [128, 2, 6, KC], F32, kind="ExternalInput").ap()
    adaC = nc.dram_tensor("adaC", [128, 2, 6, KC], F32, kind="ExternalInput").ap()
    nrm = nc.dram_tensor("nrm", [128, 2, KC], F32, kind="ExternalInput").ap()
    if post:
        yin = nc.dram_tensor("yT", [D, NT], F32, kind="ExternalInput").ap()
        w_out = nc.dram_tensor("w_out", [D, D], F32, kind="ExternalInput").ap()
        w1 = nc.dram_tensor("w1", [D, DFF], F32, kind="ExternalInput").ap()
        w2 = nc.dram_tensor("w2", [DFF, D], F32, kind="ExternalInput").ap()
    if nin:
        w_in = nc.dram_tensor("w_in", [D, nin], F32, kind="ExternalInput").ap()
        pout = nc.dram_tensor("pT", [nin, NT], F32, kind="ExternalOutput").ap()
        xout = nc.dram_tensor("xoT", [D, NT], F32, kind="ExternalOutput").ap()
    else:
        fout = nc.dram_tensor("outT", [D, NT], F32, kind="ExternalOutput").ap()

    with ExitStack() as es:
        fw = FW(nc, es)
        x = fw.sb("x", [128, KC, NT], F32)
        hT = fw.sb("hT", [128, KC, NT], BF16)
        h1 = fw.sb("h1", [128, KC, NT], BF16)
        NWB = 2
        wb = [fw.sb("wb%d" % i, [128, KC, 512], BF16) for i in range(NWB)]
        aL = fw.sb("aL", [128, 2, 6, KC], F32)
        aC = fw.sb("aC", [128, 2, 6, KC], F32)
        nw = fw.sb("nw", [128, 2, KC], F32)
        AL = fw.sb("AL", [128, 2, KC], F32)
        AC = fw.sb("AC", [128, 2, KC], F32)
        ones = fw.sb("ones", [128, 128], BF16)
        rstd = fw.sb("rstd", [128, NT], F32)
        tmpn = [fw.sb("tmpn%d" % i, [128, NT], F32) for i in range(2)]
        stg = [fw.sb("stg%d" % i, [128, NT], F32) for i in range(3)]
        rl = [fw.sb("rl%d" % i, [128, 512], F32) for i in range(2)]
        NPS = 8
        pst = [fw.ps("ps%d" % i, [128, 512], F32) for i in range(NPS)]
        st = {"ps": 0, "wb": 0, "tmpn": 0, "stg": 0, "rl": 0}

        def nxt(name, n):
            i = st[name] % n
            st[name] += 1
            return i

        xk = lambda kc: ("x", kc)
        hk = lambda kc: ("h", kc)
        h1k = lambda kc: ("h1", kc)

        xv = xin.rearrange("(kc p) n -> p kc n", p=128)
        for kc in range(KC):
            fw.dma("sp", x[:, kc, :], xv[:, kc, :], writes=[xk(kc)])
        fw.dma("act", aL[:], adaL[:, :, :, :], writes=["aL"])
        fw.dma("act", aC[:], adaC[:, :, :, :], writes=["aC"])
        fw.dma("act", nw[:], nrm[:, :, :], writes=["nw"])
        fw.op("dve", lambda e: e.memset(ones[:], 1.0), writes=["ones"])
        for (s, idx) in ((0, 4), (1, 1)):
            fw.op("dve", lambda e, s=s, idx=idx: e.scalar_tensor_tensor(
                out=AL[:, s, :], in0=aL[:, s, idx, :], scalar=1.0, in1=nw[:, s, :], op0=ALU.add, op1=ALU.mult),
                reads=["aL", "nw"], writes=["AL"])
            fw.op("dve", lambda e, s=s, idx=idx: e.scalar_tensor_tensor(
                out=AC[:, s, :], in0=aC[:, s, idx, :], scalar=1.0, in1=nw[:, s, :], op0=ALU.add, op1=ALU.mult),
                reads=["aC", "nw"], writes=["AC"])

        def load_w(src_ap, ncols):
            i = nxt("wb", NWB)
            fw.dma("pool", wb[i][:, :, 0:ncols], src_ap.rearrange("(kc p) n -> p kc n", p=128), writes=[("wb", i)])
            return i

        def proj(src, srckey, wsrc, ncols_total, evac):
            c0 = 0
            while c0 < ncols_total:
                gw = min(512, ncols_total - c0)
                wi = load_w(wsrc[:, c0:c0 + gw], gw)
                cc = 0
                while cc < gw:
                    cw = min(128, gw - cc)
                    c = (c0 + cc) // 128
                    for tt, (t0, tw) in enumerate(TT):
                        pi = nxt("ps", NPS)
                        pk = ("ps", pi)
                        mm_group(fw, pst[pi][0:cw, 0:tw],
                                 [(wb[wi][:, kc, cc:cc + cw], src[:, kc, t0:t0 + tw]) for kc in range(KC)],
                                 reads=[("wb", wi)] + [srckey(kc) for kc in range(KC)], writes=[pk])
                        evac(c, cw, tt, pst[pi][0:cw, 0:tw], pk)
                    cc += cw
                c0 += gw

        def norm_mod(s, shift_idx, out_dt_tile, outkey, final=False):
            for kc in range(KC):
                fw.op("act", lambda e, kc=kc: e.activation(out=h1[:, kc, :], in_=x[:, kc, :], func=AF.Square),
                      reads=[xk(kc)], writes=[h1k(kc)])
            for tt, (t0, tw) in enumerate(TT):
                pi = nxt("ps", NPS)
                pk = ("ps", pi)
                mm_group(fw, pst[pi][:, 0:tw], [(ones[:], h1[:, kc, t0:t0 + tw]) for kc in range(KC)],
                         reads=["ones"] + [h1k(kc) for kc in range(KC)], writes=[pk])
                fw.op("act", lambda e, pi=pi, t0=t0, tw=tw: e.activation(
                    out=rstd[:, t0:t0 + tw], in_=pst[pi][:, 0:tw], func=AF.Sqrt, bias=EPS_AP[:, 0:1], scale=1.0 / D),
                    reads=[pk, "eps"], writes=["rstd"])
            fw.op("dve", lambda e: e.reciprocal(out=rstd[:], in_=rstd[:]), reads=["rstd"], writes=["rstd"])
            for kc in range(KC):
                ti = nxt("tmpn", 2)
                tk = ("tmpn", ti)
                fw.op("dve", lambda e, kc=kc, ti=ti: e.tensor_tensor(out=tmpn[ti][:], in0=x[:, kc, :], in1=rstd[:], op=ALU.mult),
                      reads=[xk(kc), "rstd"], writes=[tk])
                if final:
                    si = nxt("stg", 3)
                    sk = ("stg", si)
                    fw.op("act", lambda e, kc=kc, ti=ti, si=si: e.activation(
                        out=stg[si][:], in_=tmpn[ti][:], func=AF.Identity, scale=nw[:, 1, kc:kc + 1]),
                        reads=[tk, "nw"], writes=[sk])
                    fw.dma("sp", fout[kc * 128:(kc + 1) * 128, :], stg[si][:], reads=[sk])
                else:
                    fw.op("act", lambda e, kc=kc, ti=ti: e.activation(
                        out=hT[:, kc, 0:NL], in_=tmpn[ti][:, 0:NL], func=AF.Identity,
                        bias=aL[:, s, shift_idx, kc:kc + 1], scale=AL[:, s, kc:kc + 1]),
                        reads=[tk, "aL", "AL"], writes=[hk(kc)])
                    fw.op("act", lambda e, kc=kc, ti=ti: e.activation(
                        out=hT[:, kc, NL:NT], in_=tmpn[ti][:, NL:NT], func=AF.Identity,
                        bias=aC[:, s, shift_idx, kc:kc + 1], scale=AC[:, s, kc:kc + 1]),
                        reads=[tk, "aC", "AC", hk(kc)], writes=[hk(kc)])

        EPS_AP = fw.sb("epsc", [128, 1], F32)
        fw.op("dve", lambda e: e.memset(EPS_AP[:], EPS), writes=["eps"])

        def resid_evac(gidx):
            def evac(c, cw, tt, ps_ap, pk):
                t0, tw = TT[tt]
                a = aL if tt < 2 else aC
                fw.op("dve", lambda e: e.scalar_tensor_tensor(
                    out=x[:, c, t0:t0 + tw], in0=ps_ap, scalar=a[:, 0, gidx, c:c + 1], in1=x[:, c, t0:t0 + tw],
                    op0=ALU.mult, op1=ALU.add),
                    reads=[pk, "aL", "aC", xk(c)], writes=[xk(c)])
            return evac

        if post:
            yv = yin.rearrange("(kc p) n -> p kc n", p=128)
            for kc in range(KC):
                fw.dma("pool", hT[:, kc, :], yv[:, kc, :], writes=[hk(kc)])
            proj(hT, hk, w_out, D, resid_evac(2))
            norm_mod(0, 3, None, None)
            for q in range(4):
                def evac1(c, cw, tt, ps_ap, pk):
                    t0, tw = TT[tt]
                    ri = nxt("rl", 2)
                    rk = ("rl", ri)
                    fw.op("act", lambda e: e.activation(out=rl[ri][:, 0:tw], in_=ps_ap, func=AF.Relu),
                          reads=[pk], writes=[rk])
                    fw.op("dve", lambda e: e.tensor_tensor(out=h1[:, c, t0:t0 + tw], in0=rl[ri][:, 0:tw], in1=rl[ri][:, 0:tw], op=ALU.mult),
                          reads=[rk], writes=[h1k(c)])
                proj(hT, hk, w1[:, q * 2048:(q + 1) * 2048], 2048, evac1)
                proj(h1, h1k, w2[q * 2048:(q + 1) * 2048, :], D, resid_evac(5))

        if nin:
            xov = xout.rearrange("(kc p) n -> p kc n", p=128)
            for kc in range(KC):
                fw.dma("sp", xov[:, kc, :], x[:, kc, :], reads=[xk(kc)])
            norm_mod(1, 0, None, None)
            cur = {}

            def evac_p(c, cw, tt, ps_ap, pk):
                t0, tw = TT[tt]
                if tt == 0:
                    cur["si"] = nxt("stg", 3)
                si = cur["si"]
                sk = ("stg", si)
                eng = "act" if (c + tt) % 2 == 0 else "dve"
                if eng == "act":
                    fw.op("act", lambda e: e.activation(out=stg[si][0:cw, t0:t0 + tw], in_=ps_ap, func=AF.Copy),
                          reads=[pk], writes=[sk])
                else:
                    fw.op("dve", lambda e: e.tensor_copy(out=stg[si][0:cw, t0:t0 + tw], in_=ps_ap),
                          reads=[pk], writes=[sk])
                if tt == len(TT) - 1:
                    fw.dma("sp", pout[c * 128:c * 128 + cw, :], stg[si][0:cw, :], reads=[sk])
            proj(hT, hk, w_in, nin, evac_p)
        else:
            norm_mod(1, 0, None, None, final=True)
        fw.finish()
    return nc


def core_tok(k):
    return k // 4, k % 4


def to_core_T(lat, ctx):
    outs = []
    for k in range(NCORES):
        b, r = core_tok(k)
        a = np.concatenate([lat[b, r * NL:(r + 1) * NL], ctx[b, r * NCX:(r + 1) * NCX]], axis=0)
        outs.append(np.ascontiguousarray(a.T))
    return outs


def from_core_T(arrs):
    F = arrs[0].shape[0]
    lat = np.empty((B, SEQ, F), arrs[0].dtype)
    ctx = np.empty((B, CTX, F), arrs[0].dtype)
    for k in range(NCORES):
        b, r = core_tok(k)
        lat[b, r * NL:(r + 1) * NL] = arrs[k][:, :NL].T
        ctx[b, r * NCX:(r + 1) * NCX] = arrs[k][:, NL:].T
    return lat, ctx


def vec_pk(v):
    return np.ascontiguousarray(v.reshape(KC, 128).T)


def ada_sets(m, lpost, lpre, v):
    out = np.zeros((128, 2, 6, KC), np.float32)
    for s, l in enumerate((lpost, lpre)):
        if l is None or l < 0 or l >= DEPTH:
            continue
        out[:, s] = m[l, v].reshape(6, KC, 128).transpose(2, 0, 1)
    return out


def ada_host_inputs(c, c_ctx, ada_w, ada_b):
    cvec = np.stack([c[0], c[1], c_ctx], 0)
    cv = np.ascontiguousarray(cvec.reshape(3, KC, 128).transpose(2, 1, 0))
    maps = []
    for k in range(NCORES):
        w = np.ascontiguousarray(ada_w[:, :, k * ADA_COLS:(k + 1) * ADA_COLS])
        b = np.ascontiguousarray(ada_b[:, k * ADA_COLS:(k + 1) * ADA_COLS].reshape(DEPTH, ADA_J, 128).transpose(2, 0, 1))
        maps.append({"cv": cv, "w": w, "b": b})
    return maps


def ada_host_gather(results):
    m = np.zeros((DEPTH, 3, 6 * D), np.float32)
    for k in range(NCORES):
        o = results[k]["out"]
        m[:, :, k * ADA_COLS:(k + 1) * ADA_COLS] = o.transpose(1, 3, 2, 0).reshape(DEPTH, 3, ADA_COLS)
    return m


_NC_CACHE = {}


def get_nc(name, builder, *args):
    key = (name,) + tuple(args)
    if key not in _NC_CACHE:
        _NC_CACHE[key] = builder(*args)
    return _NC_CACHE[key]


def run(nc, maps):
    res = run_bass_kernel_spmd(nc, maps, core_ids=list(range(NCORES)))
    return res.results


ATTN_SCALE = 192.0 ** -0.5
TL = [(i * 512, 512) for i in range(SEQ // 512)] + [(SEQ, CTX)]
NKT = TS // 128
GELU_C = 1.5957691216057308


def build_ab():
    nc = bass.Bass("TRN2", target_bir_lowering=False)
    dgu = nc.dram_tensor("guT", [2, 128, B, TS], F32, kind="ExternalInput").ap()
    dcq = nc.dram_tensor("cqT", [512, B, TS], F32, kind="ExternalInput").ap()
    dckv = nc.dram_tensor("ckvT", [512, B, TS], F32, kind="ExternalInput").ap()
    dkr = nc.dram_tensor("krT", [64, B, TS], F32, kind="ExternalInput").ap()
    dvec = nc.dram_tensor("vecs", [128, 24], F32, kind="ExternalInput").ap()
    dwa = nc.dram_tensor("wax", [2, 2, 128, 128], F32, kind="ExternalInput").ap()
    dwuq = nc.dram_tensor("wuq", [512, 192], F32, kind="ExternalInput").ap()
    dwukv = nc.dram_tensor("wukv", [512, 256], F32, kind="ExternalInput").ap()
    dcs = nc.dram_tensor("cs", [2, 64, SEQ], F32, kind="ExternalInput").ap()
    drm = nc.dram_tensor("rm", [64, 64], F32, kind="ExternalInput").ap()
    dy = nc.dram_tensor("yT", [256, B, TS], F32, kind="ExternalOutput").ap()

    with ExitStack() as es:
        fw = FW(nc, es)
        vec = fw.sb("vecsb", [128, 24], F32)
        c1 = fw.sb("c1", [128, 4], F32)
        nsp = fw.sb("nsp", [128, 2], F32)
        wax = fw.sb("waxb", [128, 4, 128], BF16)
        wuq = fw.sb("wuqb", [128, 4, 192], BF16)
        wukv = fw.sb("wukvb", [128, 4, 256], BF16)
        ones = fw.sb("ones", [128, 128], BF16)
        rm = fw.sb("rmsb", [64, 64], F32)
        NPS = 8
        pst = [fw.ps("ps%d" % i, [128, 512], F32) for i in range(NPS)]
        st = {}

        def nxt(name, n):
            i = st.get(name, 0) % n
            st[name] = st.get(name, 0) + 1
            return i

        fw.dma("sp", vec[:], dvec[:, :], writes=["vec"])
        fw.dma("sp", rm[:], drm[:, :], writes=["rm"])
        fw.dma("pool", wax[:], dwa.rearrange("a d i o -> i (a d) o"), writes=["wax"])
        fw.dma("pool", wuq[:], dwuq.rearrange("(kc p) n -> p kc n", p=128), writes=["wuq"])
        fw.dma("pool", wukv[:], dwukv.rearrange("(kc p) n -> p kc n", p=128), writes=["wukv"])
        fw.op("dve", lambda e: e.memset(ones[:], 1.0), writes=["ones"])
        fw.op("dve", lambda e: e.memset(c1[:, 0:1], 1.0), writes=["c1"])
        fw.op("dve", lambda e: e.memset(c1[:, 1:2], EPS), reads=["c1"], writes=["c1"])
        fw.op("act", lambda e: e.activation(out=nsp[:], in_=vec[:, 9:11], func=AF.Exp, scale=-1.0), reads=["vec"], writes=["nsp"])
        fw.op("act", lambda e: e.activation(out=nsp[:], in_=nsp[:], func=AF.Ln, bias=c1[:, 0:1], scale=1.0), reads=["nsp", "c1"], writes=["nsp"])
        fw.op("dve", lambda e: e.tensor_scalar(out=nsp[:], in0=nsp[:], scalar1=-8.0, scalar2=None, op0=ALU.mult), reads=["nsp"], writes=["nsp"])

        with ExitStack() as es1:
            names = ["tu", "tg", "uc", "t1", "t2", "t3", "hacc", "tgel"]
            T = {n: es1.enter_context(nc.sbuf_tensor("l_" + n, [128, TS], F32)) for n in names}
            ucb = es1.enter_context(nc.sbuf_tensor("l_ucb", [128, TS], BF16))
            segs = [(0, SEQ), (SEQ, TS)]
            for b in range(B):
                tu, tg, uc, t1, t2, t3, hacc, tgel = (T[n] for n in names)
                fw.dma("sp", tg[:], dgu[0, :, b, :], writes=["tg"])
                fw.op("pool", lambda e: e.tensor_tensor(out=tgel[:], in0=tg[:], in1=tg[:], op=ALU.mult), reads=["tg"], writes=["tgel"])
                fw.op("pool", lambda e: e.tensor_scalar(out=tgel[:], in0=tgel[:], scalar1=0.044715, scalar2=1.0, op0=ALU.mult, op1=ALU.add),
                      reads=["tgel"], writes=["tgel"])
                fw.op("pool", lambda e: e.tensor_tensor(out=tgel[:], in0=tgel[:], in1=tg[:], op=ALU.mult), reads=["tg", "tgel"], writes=["tgel"])
                fw.op("act", lambda e: e.activation(out=tgel[:], in_=tgel[:], func=AF.Sigmoid, scale=GELU_C), reads=["tgel"], writes=["tgel"])
                fw.op("pool", lambda e: e.tensor_tensor(out=tgel[:], in0=tgel[:], in1=tg[:], op=ALU.mult), reads=["tg", "tgel"], writes=["tgel"])
                fw.dma("act", tu[:], dgu[1, :, b, :], writes=["tu"])
                fw.op("dve", lambda e: e.tensor_scalar(out=uc[:], in0=tu[:], scalar1=vec[:, 2:3], scalar2=vec[:, 4:5], op0=ALU.mult, op1=ALU.add),
                      reads=["tu", "vec"], writes=["uc"])
                for (s0, s1) in segs:
                    for (tap, sh) in ((0, 2), (1, 1)):
                        fw.op("dve", lambda e, s0=s0, s1=s1, tap=tap, sh=sh: e.scalar_tensor_tensor(
                            out=uc[:, s0 + sh:s1], in0=tu[:, s0:s1 - sh], scalar=vec[:, tap:tap + 1], in1=uc[:, s0 + sh:s1],
                            op0=ALU.mult, op1=ALU.add), reads=["tu", "vec", "uc"], writes=["uc"])
                    fw.op("dve", lambda e, s0=s0, s1=s1: e.scalar_tensor_tensor(
                        out=uc[:, s0:s1 - 1], in0=tu[:, s0 + 1:s1], scalar=vec[:, 3:4], in1=uc[:, s0:s1 - 1],
                        op0=ALU.mult, op1=ALU.add), reads=["tu", "vec", "uc"], writes=["uc"])
                fw.op("act", lambda e: e.activation(out=ucb[:], in_=uc[:], func=AF.Copy), reads=["uc"], writes=["ucb"])
                for d in range(2):
                    for (t0, tw) in TL:
                        for (ax, dst, dk, bcol) in ((0, t1, "t1", 5 + d), (1, t2, "t2", 7 + d)):
                            pi = nxt("ps", NPS)
                            pk = ("ps", pi)
                            mm_group(fw, pst[pi][:, 0:tw], [(wax[:, ax * 2 + d, :], ucb[:, t0:t0 + tw])],
                                     reads=["wax", "ucb"], writes=[pk])
                            fw.op("act", lambda e, pi=pi, dst=dst, t0=t0, tw=tw, bcol=bcol: e.activation(
                                out=dst[:, t0:t0 + tw], in_=pst[pi][:, 0:tw], func=AF.Sigmoid, bias=vec[:, bcol:bcol + 1], scale=1.0),
                                reads=[pk, "vec"], writes=[dk])
                    fw.op("act", lambda e, d=d: e.activation(out=t1[:], in_=t1[:], func=AF.Exp, scale=nsp[:, d:d + 1]),
                          reads=["t1", "nsp"], writes=["t1"])
                    fw.op("pool", lambda e: e.tensor_tensor(out=t2[:], in0=t2[:], in1=uc[:], op=ALU.mult), reads=["t2", "uc"], writes=["t2"])
                    fw.op("dve", lambda e: e.tensor_tensor(out=t3[:], in0=t1[:], in1=t1[:], op=ALU.mult), reads=["t1"], writes=["t3"])
                    fw.op("act", lambda e: e.activation(out=t3[:], in_=t3[:], func=AF.Sqrt, bias=c1[:, 0:1], scale=-1.0),
                          reads=["t3", "c1"], writes=["t3"])
                    fw.op("dve", lambda e: e.tensor_tensor(out=t2[:], in0=t2[:], in1=t3[:], op=ALU.mult), reads=["t2", "t3"], writes=["t2"])
                    dst = hacc if d == 0 else t3
                    dk = "hacc" if d == 0 else "t3"
                    if d == 0:
                        fw.op("dve", lambda e, dst=dst: e.tensor_tensor_scan(
                            out=dst[:, SEQ:TS], data0=t1[:, SEQ:TS], data1=t2[:, SEQ:TS], initial=0.0, op0=ALU.mult, op1=ALU.add),
                            reads=["t1", "t2"], writes=[dk])
                        fw.op("dve", lambda e, dst=dst: e.tensor_tensor_scan(
                            out=dst[:, 0:SEQ], data0=t1[:, 0:SEQ], data1=t2[:, 0:SEQ], initial=dst[:, TS - 1:TS], op0=ALU.mult, op1=ALU.add),
                            reads=["t1", "t2", dk], writes=[dk])
                    else:
                        fw.op("dve", lambda e, dst=dst: e.tensor_tensor_scan(
                            out=dst[:, SEQ:TS][:, ::-1], data0=t1[:, SEQ:TS][:, ::-1], data1=t2[:, SEQ:TS][:, ::-1],
                            initial=0.0, op0=ALU.mult, op1=ALU.add), reads=["t1", "t2"], writes=[dk])
                        fw.op("dve", lambda e, dst=dst: e.tensor_tensor_scan(
                            out=dst[:, 0:SEQ][:, ::-1], data0=t1[:, 0:SEQ][:, ::-1], data1=t2[:, 0:SEQ][:, ::-1],
                            initial=dst[:, SEQ:SEQ + 1], op0=ALU.mult, op1=ALU.add), reads=["t1", "t2", dk], writes=[dk])
                        fw.op("dve", lambda e: e.tensor_tensor(out=hacc[:], in0=hacc[:], in1=t3[:], op=ALU.add),
                              reads=["hacc", "t3"], writes=["hacc"])
                fw.op("dve", lambda e: e.tensor_tensor(out=t1[:], in0=tgel[:], in1=hacc[:], op=ALU.mult), reads=["hacc", "tgel", "t1"], writes=["t1"])
                fw.dma("sp", dy[0:128, b, :], t1[:], reads=["t1"])
            fw.barrier()

        with ExitStack() as es2:
            def sb2(name, shape, dt):
                return es2.enter_context(nc.sbuf_tensor("m_" + name, list(shape), dt))
            cs = sb2("cssb", [64, 2, SEQ], F32)
            qTn = sb2("qTn", [128, TS], BF16)
            qTr = sb2("qTr", [64, TS], BF16)
            kTn = sb2("kTn", [128, TS], BF16)
            kTr = sb2("kTr", [64, TS], BF16)
            V = sb2("V", [128, NKT, 128], BF16)
            cqt = [sb2("cqt%d" % i, [128, 4, 512], F32) for i in range(2)]
            ckt = [sb2("ckt%d" % i, [128, 4, 512], F32) for i in range(2)]
            krt = [sb2("krt%d" % i, [64, 512], F32) for i in range(2)]
            sq = [sb2("sq%d" % i, [128, 4, 512], BF16) for i in range(2)]
            cqn = [sb2("cqn%d" % i, [128, 4, 512], BF16) for i in range(2)]
            ckn = [sb2("ckn%d" % i, [128, 4, 512], BF16) for i in range(2)]
            rs = [sb2("rs%d" % i, [128, 512], F32) for i in range(2)]
            qrs = [sb2("qrs%d" % i, [64, 512], F32) for i in range(2)]
            ra = [sb2("ra%d" % i, [64, 512], F32) for i in range(2)]
            rb = [sb2("rb%d" % i, [64, 512], F32) for i in range(2)]
            PT = [sb2("PT%d" % i, [128, 512], BF16) for i in range(6)]
            accD = [sb2("accD%d" % i, [128, 512], F32) for i in range(2)]
            accP = [sb2("accP%d" % i, [128, 512], F32) for i in range(2)]
            ones_f = sb2("ones_f", [128, 128], F32)
            fw.op("dve", lambda e: e.memset(ones_f[:], 1.0), writes=["ones_f"])
            rden = [sb2("rden%d" % i, [128, 512], F32) for i in range(2)]
            ob = [sb2("ob%d" % i, [128, 512], F32) for i in range(2)]
            fw.dma("sp", cs[:], dcs.rearrange("a p t -> p a t"), writes=["cs"])

            def rope(src_ap, srck, t0, tw, dst_ap, dstk):
                pi = nxt("ps", NPS)
                pk = ("ps", pi)
                mm_group(fw, pst[pi][0:64, 0:tw], [(rm[:], src_ap)], reads=["rm", srck], writes=[pk])
                i = nxt("ra", 2)
                fw.op("dve", lambda e: e.tensor_tensor(out=ra[i][:, 0:tw], in0=src_ap, in1=cs[:, 0, t0:t0 + tw], op=ALU.mult),
                      reads=[srck, "cs"], writes=[("ra", i)])
                fw.op("dve", lambda e: e.tensor_tensor(out=rb[i][:, 0:tw], in0=pst[pi][0:64, 0:tw], in1=cs[:, 1, t0:t0 + tw], op=ALU.mult),
                      reads=[pk, "cs"], writes=[("rb", i)])
                fw.op("dve", lambda e: e.tensor_tensor(out=dst_ap, in0=ra[i][:, 0:tw], in1=rb[i][:, 0:tw], op=ALU.add),
                      reads=[("ra", i), ("rb", i)], writes=[dstk])

            for b in range(B):
                for (t0, tw) in TL:
                    is_ctx = t0 >= SEQ
                    i = nxt("tile", 2)
                    fw.dma("sp", cqt[i][:, :, 0:tw], dcq.rearrange("(kc p) b n -> p kc b n", p=128)[:, :, b, t0:t0 + tw], writes=[("cqt", i)])
                    fw.dma("act", ckt[i][:, :, 0:tw], dckv.rearrange("(kc p) b n -> p kc b n", p=128)[:, :, b, t0:t0 + tw], writes=[("ckt", i)])
                    fw.dma("sp", krt[i][:, 0:tw], dkr[:, b, t0:t0 + tw], writes=[("krt", i)])
                    for (src, srck, ncol, dst, dstk) in ((cqt[i], ("cqt", i), 11, cqn[i], ("cqn", i)), (ckt[i], ("ckt", i), 15, ckn[i], ("ckn", i))):
                        j = nxt("sq", 2)
                        fw.op("act", lambda e, src=src, j=j: e.activation(out=sq[j][:, :, 0:tw], in_=src[:, :, 0:tw], func=AF.Square),
                              reads=[srck], writes=[("sq", j)])
                        pi = nxt("ps", NPS)
                        pk = ("ps", pi)
                        mm_group(fw, pst[pi][:, 0:tw], [(ones[:], sq[j][:, kc, 0:tw]) for kc in range(4)],
                                 reads=["ones", ("sq", j)], writes=[pk])
                        r = nxt("rs", 2)
                        fw.op("act", lambda e, pi=pi, r=r: e.activation(out=rs[r][:, 0:tw], in_=pst[pi][:, 0:tw], func=AF.Sqrt,
                                                                      bias=c1[:, 1:2], scale=1.0 / 512), reads=[pk, "c1"], writes=[("rs", r)])
                        fw.op("dve", lambda e, r=r: e.reciprocal(out=rs[r][:, 0:tw], in_=rs[r][:, 0:tw]), reads=[("rs", r)], writes=[("rs", r)])
                        for kc in range(4):
                            fw.op("dve", lambda e, src=src, dst=dst, kc=kc, r=r, ncol=ncol: e.scalar_tensor_tensor(
                                out=dst[:, kc, 0:tw], in0=src[:, kc, 0:tw], scalar=vec[:, ncol + kc:ncol + kc + 1], in1=rs[r][:, 0:tw],
                                op0=ALU.mult, op1=ALU.mult), reads=[srck, "vec", ("rs", r)], writes=[dstk])
                    pi = nxt("ps", NPS); pk = ("ps", pi)
                    mm_group(fw, pst[pi][:, 0:tw], [(wuq[:, kc, 0:128], cqn[i][:, kc, 0:tw]) for kc in range(4)],
                             reads=["wuq", ("cqn", i)], writes=[pk])
                    fw.op("act", lambda e, pi=pi: e.activation(out=qTn[:, t0:t0 + tw], in_=pst[pi][:, 0:tw], func=AF.Copy),
                          reads=[pk], writes=["qTn"])
                    pi = nxt("ps", NPS); pk = ("ps", pi)
                    mm_group(fw, pst[pi][0:64, 0:tw], [(wuq[:, kc, 128:192], cqn[i][:, kc, 0:tw]) for kc in range(4)],
                             reads=["wuq", ("cqn", i)], writes=[pk])
                    if is_ctx:
                        fw.op("act", lambda e, pi=pi: e.activation(out=qTr[:, t0:t0 + tw], in_=pst[pi][0:64, 0:tw], func=AF.Copy),
                              reads=[pk], writes=["qTr"])
                    else:
                        q = nxt("qrs", 2)
                        fw.op("act", lambda e, pi=pi, q=q: e.activation(out=qrs[q][:, 0:tw], in_=pst[pi][0:64, 0:tw], func=AF.Copy),
                              reads=[pk], writes=[("qrs", q)])
                        rope(qrs[q][:, 0:tw], ("qrs", q), t0, tw, qTr[:, t0:t0 + tw], "qTr")
                    pi = nxt("ps", NPS); pk = ("ps", pi)
                    mm_group(fw, pst[pi][:, 0:tw], [(wukv[:, kc, 0:128], ckn[i][:, kc, 0:tw]) for kc in range(4)],
                             reads=["wukv", ("ckn", i)], writes=[pk])
                    fw.op("act", lambda e, pi=pi: e.activation(out=kTn[:, t0:t0 + tw], in_=pst[pi][:, 0:tw], func=AF.Copy),
                          reads=[pk], writes=["kTn"])
                    for sub in range(tw // 128):
                        pi = nxt("ps", NPS); pk = ("ps", pi)
                        mm_group(fw, pst[pi][:, 0:128], [(ckn[i][:, kc, sub * 128:(sub + 1) * 128], wukv[:, kc, 128:256]) for kc in range(4)],
                                 reads=["wukv", ("ckn", i)], writes=[pk])
                        fw.op("dve", lambda e, pi=pi, sub=sub: e.tensor_copy(out=V[:, t0 // 128 + sub, :], in_=pst[pi][:, 0:128]),
                              reads=[pk], writes=["V"])
                    if is_ctx:
                        fw.op("act", lambda e, i=i: e.activation(out=kTr[:, t0:t0 + tw], in_=krt[i][:, 0:tw], func=AF.Copy),
                              reads=[("krt", i)], writes=["kTr"])
                    else:
                        rope(krt[i][:, 0:tw], ("krt", i), t0, tw, kTr[:, t0:t0 + tw], "kTr")
                SKEW = 2
                for (t0, tw) in TL:
                    is_ctx = t0 >= SEQ
                    kts = list(range(SEQ // 128, NKT)) if is_ctx else list(range(NKT))
                    oi = 4 + nxt("po", 2)
                    di = 6 + nxt("pd", 2)
                    ok, dk = ("ps", oi), ("ps", di)
                    ai = nxt("acc", 2)
                    aD, aP = accD[ai], accP[ai]
                    aDk, aPk = ("accD", ai), ("accP", ai)
                    pend = []

                    def consume(item):
                        n, kt, p = item
                        first, last = (n == 0), (n == len(kts) - 1)
                        fw.op("pe", lambda e: e.matmul(pst[oi][:, 0:tw], V[:, kt, :], PT[p][:, 0:tw], start=first, stop=last),
                              reads=["V", ("PT", p)], writes=[ok], nosame=not first)
                        if n % 2 == 0:
                            if n == 0:
                                fw.op("dve", lambda e: e.tensor_copy(out=aD[:, 0:tw], in_=PT[p][:, 0:tw]), reads=[("PT", p)], writes=[aDk])
                            else:
                                fw.op("dve", lambda e: e.tensor_tensor(out=aD[:, 0:tw], in0=aD[:, 0:tw], in1=PT[p][:, 0:tw], op=ALU.add),
                                      reads=[("PT", p), aDk], writes=[aDk])
                        else:
                            if n == 1:
                                fw.op("pool", lambda e: e.tensor_copy(out=aP[:, 0:tw], in_=PT[p][:, 0:tw]), reads=[("PT", p)], writes=[aPk])
                            else:
                                fw.op("pool", lambda e: e.tensor_tensor(out=aP[:, 0:tw], in0=aP[:, 0:tw], in1=PT[p][:, 0:tw], op=ALU.add),
                                      reads=[("PT", p), aPk], writes=[aPk])

                    for n, kt in enumerate(kts):
                        si = nxt("psS", 4)
                        sk = ("ps", si)
                        mm_group(fw, pst[si][:, 0:tw],
                                 [(kTn[:, kt * 128:(kt + 1) * 128], qTn[:, t0:t0 + tw]), (kTr[:, kt * 128:(kt + 1) * 128], qTr[:, t0:t0 + tw])],
                                 reads=["kTn", "qTn", "kTr", "qTr"], writes=[sk])
                        p = nxt("PT", 6)
                        fw.op("act", lambda e, si=si, p=p: e.activation(out=PT[p][:, 0:tw], in_=pst[si][:, 0:tw], func=AF.Exp, scale=ATTN_SCALE),
                              reads=[sk], writes=[("PT", p)])
                        pend.append((n, kt, p))
                        if len(pend) > SKEW:
                            consume(pend.pop(0))
                    while pend:
                        consume(pend.pop(0))
                    fw.op("dve", lambda e: e.tensor_tensor(out=aD[:, 0:tw], in0=aD[:, 0:tw], in1=aP[:, 0:tw], op=ALU.add),
                          reads=[aDk, aPk], writes=[aDk])
                    mm_group(fw, pst[di][:, 0:tw], [(ones_f[:], aD[:, 0:tw])], reads=["ones_f", aDk], writes=[dk])
                    r = nxt("rden", 2)
                    fw.op("dve", lambda e, r=r: e.reciprocal(out=rden[r][:, 0:tw], in_=pst[di][:, 0:tw]), reads=[dk], writes=[("rden", r)])
                    o = nxt("ob", 2)
                    fw.op("dve", lambda e, r=r, o=o: e.tensor_tensor(out=ob[o][:, 0:tw], in0=pst[oi][:, 0:tw], in1=rden[r][:, 0:tw], op=ALU.mult),
                          reads=[ok, ("rden", r)], writes=[("ob", o)])
                    fw.dma("sp", dy[128:256, b, t0:t0 + tw], ob[o][:, 0:tw], reads=[("ob", o)])
        fw.finish()
    return nc


def rope_consts():
    half = 32
    inv_freq = (10000.0 ** (-np.arange(0, half, 2, dtype=np.float32) / half)).astype(np.float32)
    t = np.arange(SEQ)
    row = (t // 64).astype(np.float32)
    col = (t % 64).astype(np.float32)
    ang_r = row[:, None] * inv_freq
    ang_c = col[:, None] * inv_freq
    ang = np.concatenate([ang_r, ang_r, ang_c, ang_c], axis=-1).astype(np.float32)
    cos = np.cos(ang).astype(np.float32)
    sin = np.sin(ang).astype(np.float32)
    sign = np.ones(64, np.float32)
    sign[0:16] = -1.0
    sign[32:48] = -1.0
    perm = np.concatenate([np.arange(16, 32), np.arange(0, 16), np.arange(48, 64), np.arange(32, 48)])
    rm = np.zeros((64, 64), np.float32)
    rm[perm, np.arange(64)] = 1.0
    cs = np.stack([cos.T, (sin * sign).T], 0)
    return np.ascontiguousarray(cs), rm


def ab_host_inputs(pT, e, P):
    cs, rm = rope_consts()
    maps = []
    cq = np.ascontiguousarray(pT[2048:2560])
    ckv = np.ascontiguousarray(pT[2560:3072])
    kr = np.ascontiguousarray(pT[3072:3136])
    for h in range(NCORES):
        sl = slice(h * 128, (h + 1) * 128)
        gu = np.stack([pT[h * 128:(h + 1) * 128], pT[1024 + h * 128:1024 + (h + 1) * 128]], 0)
        vec = np.zeros((128, 24), np.float32)
        vec[:, 0:4] = P["lru_conv_w"][e][:, sl].T
        vec[:, 4] = P["lru_conv_b"][e][sl]
        vec[:, 5:7] = P["lru_b_a"][e][:, sl].T
        vec[:, 7:9] = P["lru_b_x"][e][:, sl].T
        vec[:, 9:11] = P["lru_lambda"][e][:, sl].T
        vec[:, 11:15] = P["mla_q_norm_w"][e].reshape(4, 128).T
        vec[:, 15:19] = P["mla_kv_norm_w"][e].reshape(4, 128).T
        wax = np.stack([P["lru_w_a"][e][:, h], P["lru_w_x"][e][:, h]], 0)
        maps.append({"guT": np.ascontiguousarray(gu), "cqT": cq, "ckvT": ckv, "krT": kr, "vecs": vec,
                     "wax": np.ascontiguousarray(wax),
                     "wuq": np.ascontiguousarray(P["mla_w_uq"][e][:, h * 192:(h + 1) * 192]),
                     "wukv": np.ascontiguousarray(P["mla_w_ukv"][e][:, h * 256:(h + 1) * 256]),
                     "cs": cs, "rm": rm})
    return maps


def ab_host_gather(results):
    y = np.empty((D, B, TS), np.float32)
    for h in range(NCORES):
        o = results[h]["yT"]
        y[h * 128:(h + 1) * 128] = o[0:128]
        y[1024 + h * 128:1024 + (h + 1) * 128] = o[128:256]
    return y


def tokT_to_full(arrs):
    F = arrs[0].shape[0]
    full = np.empty((F, B, TS), arrs[0].dtype)
    for k in range(NCORES):
        b, r = core_tok(k)
        full[:, b, r * NL:(r + 1) * NL] = arrs[k][:, :NL]
        full[:, b, SEQ + r * NCX:SEQ + (r + 1) * NCX] = arrs[k][:, NL:]
    return full


def full_to_tokT(full):
    outs = []
    for k in range(NCORES):
        b, r = core_tok(k)
        outs.append(np.ascontiguousarray(np.concatenate(
            [full[:, b, r * NL:(r + 1) * NL], full[:, b, SEQ + r * NCX:SEQ + (r + 1) * NCX]], axis=1)))
    return outs


CH = 64
NCH = TS // CH
NPT = TS // 128


def build_hg(layer):
    nc = bass.Bass("TRN2", target_bir_lowering=False)
    dp = nc.dram_tensor("p5", [2, 5, 128, B, TS], F32, kind="ExternalInput").ap()
    dlb = nc.dram_tensor("lbl", [128, 2, DEPTH], F32, kind="ExternalInput").ap()
    dnw = nc.dram_tensor("nw", [128, 1], F32, kind="ExternalInput").ap()
    dmask = nc.dram_tensor("masks", [2, 128, 128], F32, kind="ExternalInput").ap()
    dident = nc.dram_tensor("ident", [128, 128], F32, kind="ExternalInput").ap()
    dy = nc.dram_tensor("yT", [2, 128, B, TS], F32, kind="ExternalOutput").ap()

    with ExitStack() as es:
        fw = FW(nc, es)
        names = ["qs", "iE", "oacc", "kk", "lf", "G", "tmp"]
        T = {n: fw.sb("h_" + n, [128, TS], F32) for n in names}
        qs, iE, oacc, kk, lf, G, tmp = (T[n] for n in names)
        qt = fw.sb("qt", [128, TS], BF16)
        kt = fw.sb("kt", [128, TS], BF16)
        Vt = fw.sb("Vt", [128, NPT, 128], BF16)
        ones_b = fw.sb("ones_b", [128, 512], BF16)
        ones_m = fw.sb("ones_m", [128, 128], BF16)
        masks = fw.sb("masks_sb", [128, 2, 128], F32)
        ident = fw.sb("ident_sb", [128, 128], F32)
        lbl = fw.sb("lbl_sb", [128, 2, DEPTH], F32)
        lbs = fw.sb("lbs", [128, 8], F32)
        nwt = fw.sb("nwt", [128, 1], F32)
        c1 = fw.sb("c1h", [128, 2], F32)
        dec = fw.sb("dec", [128, NCH], F32)
        S = fw.sb("S", [128, 128], F32)
        Sb = fw.sb("Sb", [128, 128], BF16)
        Am = [fw.sb("Am%d" % i, [128, 128], BF16) for i in range(2)]
        khat = [fw.sb("khat%d" % i, [128, 128], BF16) for i in range(2)]
        sqb = [fw.sb("sqb%d" % i, [128, 512], BF16) for i in range(2)]
        rsb = [fw.sb("rsb%d" % i, [128, 512], F32) for i in range(2)]
        NPS = 8
        pst = [fw.ps("ps%d" % i, [128, 512], F32) for i in range(NPS)]
        st = {}

        def nxt(name, n):
            i = st.get(name, 0) % n
            st[name] = st.get(name, 0) + 1
            return i

        def newps():
            pi = nxt("ps", NPS)
            return pi, ("ps", pi)

        fw.dma("sp", masks[:], dmask.rearrange("a s t -> s a t"), writes=["masks"])
        fw.dma("sp", ident[:], dident[:, :], writes=["ident"])
        fw.dma("sp", lbl[:], dlb[:, :, :], writes=["lbl"])
        fw.dma("sp", nwt[:], dnw[:, :], writes=["nwt"])
        fw.op("dve", lambda e: e.memset(ones_b[:], 1.0), writes=["ones_b"])
        fw.op("dve", lambda e: e.memset(ones_m[:], 1.0), writes=["ones_m"])
        fw.op("dve", lambda e: e.memset(c1[:, 0:1], 1.0), writes=["c1"])
        fw.op("dve", lambda e: e.memset(c1[:, 1:2], EPS), reads=["c1"], writes=["c1"])
        fw.op("act", lambda e: e.activation(out=lbl[:], in_=lbl[:], func=AF.Exp), reads=["lbl"], writes=["lbl"])
        fw.op("dve", lambda e: e.tensor_reduce(out=lbs[:, 0:2], in_=lbl[:], axis=AX.X, op=ALU.add), reads=["lbl"], writes=["lbs"])
        fw.op("dve", lambda e: e.tensor_reduce(out=lbs[:, 2:4], in_=lbl[:, :, 1:layer + 1], axis=AX.X, op=ALU.add), reads=["lbl", "lbs"], writes=["lbs"])
        fw.op("dve", lambda e: e.reciprocal(out=lbs[:, 0:2], in_=lbs[:, 0:2]), reads=["lbs"], writes=["lbs"])
        fw.op("dve", lambda e: e.tensor_tensor(out=lbs[:, 4:6], in0=lbs[:, 2:4], in1=lbs[:, 0:2], op=ALU.mult), reads=["lbs"], writes=["lbs"])
        fw.op("dve", lambda e: e.tensor_scalar(out=lbs[:, 6:8], in0=lbs[:, 4:6], scalar1=-1.0, scalar2=1.0, op0=ALU.mult, op1=ALU.add),
              reads=["lbs"], writes=["lbs"])

        def v3(t):
            return t[:].rearrange("p (c j) -> p c j", j=CH)

        for hd in range(2):
            for b in range(B):
                fw.dma("sp", qs[:], dp[hd, 0, :, b, :], writes=["qs"])
                fw.dma("act", iE[:], dp[hd, 3, :, b, :], writes=["iE"])
                fw.op("act", lambda e: e.activation(out=tmp[:], in_=qs[:], func=AF.Sigmoid), reads=["qs"], writes=["tmp"])
                fw.op("dve", lambda e: e.tensor_tensor(out=qs[:], in0=qs[:], in1=tmp[:], op=ALU.mult), reads=["qs", "tmp"], writes=["qs"])
                for pt in range(NPT):
                    pi, pk = newps()
                    mm_group(fw, pst[pi][:, 0:128], [(iE[:, pt * 128:(pt + 1) * 128], ident[:])], reads=["iE", "ident"], writes=[pk])
                    eng = "act" if pt % 2 == 0 else "dve"
                    if eng == "act":
                        fw.op("act", lambda e, pi=pi, pt=pt: e.activation(out=Vt[:, pt, :], in_=pst[pi][:, 0:128], func=AF.Copy), reads=[pk], writes=["Vt"])
                    else:
                        fw.op("dve", lambda e, pi=pi, pt=pt: e.tensor_copy(out=Vt[:, pt, :], in_=pst[pi][:, 0:128]), reads=[pk], writes=["Vt"])
                for d in range(2):
                    fw.dma("sp", kk[:], dp[hd, 1 + d, :, b, :], writes=["kk"])
                    fw.op("act", lambda e: e.activation(out=kk[:], in_=kk[:], func=AF.Sigmoid), reads=["kk"], writes=["kk"])
                    fw.op("dve", lambda e, hd=hd: e.tensor_scalar(out=kk[:], in0=kk[:], scalar1=lbs[:, 6 + hd:7 + hd], scalar2=lbs[:, 4 + hd:5 + hd],
                                                                op0=ALU.mult, op1=ALU.add), reads=["kk", "lbs"], writes=["kk"])
                    fw.op("act", lambda e: e.activation(out=lf[:], in_=kk[:], func=AF.Ln), reads=["kk"], writes=["lf"])
                    fw.op("dve", lambda e: e.tensor_scalar(out=kk[:], in0=kk[:], scalar1=-1.0, scalar2=1.0, op0=ALU.mult, op1=ALU.add),
                          reads=["kk"], writes=["kk"])
                    for blk in range((TS + 511) // 512):
                        c0 = blk * 512
                        cw = min(512, TS - c0)
                        init = 0.0 if blk == 0 else G[:, c0 - 1:c0]
                        fw.op("dve", lambda e, c0=c0, cw=cw, init=init: e.tensor_tensor_scan(
                            out=G[:, c0:c0 + cw], data0=ones_b[:, 0:cw], data1=lf[:, c0:c0 + cw], initial=init, op0=ALU.mult, op1=ALU.add),
                            reads=["ones_b", "lf", "G"], writes=["G"])
                    G3, E3, lf3 = v3(G), v3(iE), v3(lf)
                    if d == 0:
                        fw.op("dve", lambda e: e.tensor_copy(out=E3[:, 0:1, :], in_=G3[:, 0:1, :]), reads=["G", "iE", "Vt"], writes=["iE"])
                        fw.op("dve", lambda e: e.tensor_tensor(out=E3[:, 1:NCH, :], in0=G3[:, 1:NCH, :],
                                                               in1=G3[:, 0:NCH - 1, CH - 1:CH].broadcast_to([128, NCH - 1, CH]), op=ALU.subtract),
                              reads=["G", "iE"], writes=["iE"])
                        last_col = CH - 1
                    else:
                        fw.op("dve", lambda e: e.tensor_tensor(out=E3[:, :, :], in0=G3[:, :, CH - 1:CH].broadcast_to([128, NCH, CH]),
                                                               in1=G3[:, :, :], op=ALU.subtract), reads=["G", "iE", "Vt"], writes=["iE"])
                        fw.op("dve", lambda e: e.tensor_tensor(out=iE[:], in0=iE[:], in1=lf[:], op=ALU.add), reads=["lf", "iE"], writes=["iE"])
                        last_col = 0
                    fw.op("act", lambda e, last_col=last_col: e.activation(out=dec[:, :], in_=E3[:, :, last_col], func=AF.Exp),
                          reads=["iE"], writes=["dec"])
                    fw.op("act", lambda e: e.activation(out=tmp[:], in_=iE[:], func=AF.Exp), reads=["iE"], writes=["tmp"])
                    fw.op("dve", lambda e: e.tensor_tensor(out=qt[:], in0=qs[:], in1=tmp[:], op=ALU.mult), reads=["qs", "tmp"], writes=["qt"])
                    fw.op("act", lambda e: e.activation(out=tmp[:], in_=iE[:], func=AF.Exp, scale=-1.0), reads=["iE", "qt"], writes=["tmp"])
                    fw.op("dve", lambda e: e.tensor_tensor(out=kt[:], in0=kk[:], in1=tmp[:], op=ALU.mult), reads=["kk", "tmp"], writes=["kt"])
                    tmp3 = v3(tmp)
                    fw.op("dve", lambda e, last_col=last_col: e.tensor_tensor(
                        out=tmp3[:, :, :], in0=E3[:, :, last_col:last_col + 1].broadcast_to([128, NCH, CH]), in1=E3[:, :, :], op=ALU.subtract),
                        reads=["iE", "kt"], writes=["tmp"])
                    fw.op("act", lambda e: e.activation(out=tmp[:], in_=tmp[:], func=AF.Exp), reads=["tmp"], writes=["tmp"])
                    fw.op("dve", lambda e: e.tensor_tensor(out=tmp[:], in0=tmp[:], in1=kk[:], op=ALU.mult), reads=["tmp", "kk"], writes=["tmp"])
                    fw.op("dve", lambda e: e.memset(S[:], 0.0), reads=["S"], writes=["S"])
                    fw.op("dve", lambda e: e.memset(Sb[:], 0.0), reads=["Sb"], writes=["Sb"])
                    if d == 0:
                        ptiles = list(range(SEQ // 128, NPT)) + list(range(SEQ // 128))
                    else:
                        ptiles = list(range(NPT - 1, -1, -1))
                    for pt in ptiles:
                        cols = slice(pt * 128, (pt + 1) * 128)
                        pi, pk = newps()
                        mm_group(fw, pst[pi][:, 0:128], [(kt[:, cols], qt[:, cols])], reads=["kt", "qt"], writes=[pk])
                        a = nxt("Am", 2)
                        fw.op("dve", lambda e, pi=pi, a=a, d=d: e.tensor_tensor(out=Am[a][:], in0=pst[pi][:, 0:128], in1=masks[:, d, :], op=ALU.mult),
                              reads=[pk, "masks"], writes=[("Am", a)])
                        pi2, pk2 = newps()
                        mm_group(fw, pst[pi2][:, 0:128], [(tmp[:, cols], ident[:])], reads=["tmp", "ident"], writes=[pk2])
                        kh = nxt("khat", 2)
                        fw.op("act", lambda e, pi2=pi2, kh=kh: e.activation(out=khat[kh][:], in_=pst[pi2][:, 0:128], func=AF.Copy),
                              reads=[pk2], writes=[("khat", kh)])
                        halves = (0, 1) if d == 0 else (1, 0)
                        for hf in halves:
                            c = pt * 2 + hf
                            rows = slice(hf * 64, (hf + 1) * 64)
                            tcols = slice(c * CH, (c + 1) * CH)
                            pi3, pk3 = newps()
                            mm_group(fw, pst[pi3][:, 0:CH],
                                     [(Vt[rows, pt, :], Am[a][rows, hf * 64:(hf + 1) * 64]), (Sb[:], qt[:, tcols])],
                                     reads=["Vt", ("Am", a), "Sb", "qt"], writes=[pk3])
                            if d == 0:
                                fw.op("act", lambda e, pi3=pi3, tcols=tcols: e.activation(out=oacc[:, tcols], in_=pst[pi3][:, 0:CH], func=AF.Copy),
                                      reads=[pk3], writes=["oacc"])
                            else:
                                fw.op("dve", lambda e, pi3=pi3, tcols=tcols: e.tensor_tensor(out=oacc[:, tcols], in0=pst[pi3][:, 0:CH], in1=oacc[:, tcols], op=ALU.add),
                                      reads=[pk3, "oacc"], writes=["oacc"])
                            pi4, pk4 = newps()
                            mm_group(fw, pst[pi4][:, 0:128], [(khat[kh][rows, :], Vt[rows, pt, :])], reads=[("khat", kh), "Vt"], writes=[pk4])
                            fw.op("dve", lambda e, pi4=pi4, c=c: e.scalar_tensor_tensor(
                                out=S[:], in0=S[:], scalar=dec[:, c:c + 1], in1=pst[pi4][:, 0:128], op0=ALU.mult, op1=ALU.add),
                                reads=["S", "dec", pk4], writes=["S"])
                            fw.op("act", lambda e: e.activation(out=Sb[:], in_=S[:], func=AF.Copy), reads=["S"], writes=["Sb"])
                fw.dma("sp", kk[:], dp[hd, 4, :, b, :], writes=["kk"])
                fw.op("act", lambda e: e.activation(out=lf[:], in_=kk[:], func=AF.Sigmoid), reads=["kk"], writes=["lf"])
                fw.op("dve", lambda e: e.tensor_tensor(out=kk[:], in0=kk[:], in1=lf[:], op=ALU.mult), reads=["kk", "lf"], writes=["kk"])
                for (t0, tw) in TL:
                    j = nxt("sqb", 2)
                    fw.op("act", lambda e, j=j, t0=t0, tw=tw: e.activation(out=sqb[j][:, 0:tw], in_=oacc[:, t0:t0 + tw], func=AF.Square),
                          reads=["oacc"], writes=[("sqb", j)])
                    pi, pk = newps()
                    mm_group(fw, pst[pi][:, 0:tw], [(ones_m[:], sqb[j][:, 0:tw])], reads=["ones_m", ("sqb", j)], writes=[pk])
                    r = nxt("rsb", 2)
                    fw.op("act", lambda e, pi=pi, r=r, tw=tw: e.activation(out=rsb[r][:, 0:tw], in_=pst[pi][:, 0:tw], func=AF.Sqrt, bias=c1[:, 1:2], scale=1.0 / 128),
                          reads=[pk, "c1"], writes=[("rsb", r)])
                    fw.op("dve", lambda e, r=r, tw=tw: e.reciprocal(out=rsb[r][:, 0:tw], in_=rsb[r][:, 0:tw]), reads=[("rsb", r)], writes=[("rsb", r)])
                    fw.op("dve", lambda e, r=r, t0=t0, tw=tw: e.scalar_tensor_tensor(
                        out=G[:, t0:t0 + tw], in0=oacc[:, t0:t0 + tw], scalar=nwt[:, 0:1], in1=rsb[r][:, 0:tw], op0=ALU.mult, op1=ALU.mult),
                        reads=["oacc", "nwt", ("rsb", r), "G"], writes=["G"])
                fw.op("dve", lambda e: e.tensor_tensor(out=G[:], in0=G[:], in1=kk[:], op=ALU.mult), reads=["G", "kk"], writes=["G"])
                fw.dma("sp", dy[hd, :, b, :], G[:], reads=["G"])
        fw.finish()
    return nc


def hg_consts():
    s = np.arange(128)[:, None]
    t = np.arange(128)[None, :]
    same = (s // CH) == (t // CH)
    mf = (same & (t >= s)).astype(np.float32)
    mb = (same & (t <= s)).astype(np.float32)
    return np.stack([mf, mb], 0), np.eye(128, dtype=np.float32)


def hg_host_inputs(pT, o, P):
    masks, ident = hg_consts()
    maps = []
    for k in range(NCORES):
        p5 = np.empty((2, 5, 128, B, TS), np.float32)
        lbl = np.empty((128, 2, DEPTH), np.float32)
        for j in range(2):
            hh = 2 * k + j
            for s in range(5):
                p5[j, s] = pT[s * D + hh * 128:s * D + (hh + 1) * 128]
            lbl[:, j, :] = P["hg_lb_logits"][:, hh * 128:(hh + 1) * 128].T
        maps.append({"p5": p5, "lbl": lbl, "nw": np.ascontiguousarray(P["hg_norm_w"][o].reshape(128, 1)),
                     "masks": masks, "ident": ident})
    return maps


def hg_host_gather(results):
    y = np.empty((D, B, TS), np.float32)
    for k in range(NCORES):
        o = results[k]["yT"]
        for j in range(2):
            hh = 2 * k + j
            y[hh * 128:(hh + 1) * 128] = o[j]
    return y


def kernel(**inp):
    P = {k: np.asarray(v, dtype=np.float32) for k, v in inp.items()}
    x, ctx = P["x"], P["ctx"]
    m = ada_host_gather(run(get_nc("ada", build_ada), ada_host_inputs(P["c"], P["c_ctx"], P["ada_w"], P["ada_b"])))
    xs = to_core_T(x, ctx)

    def tok_maps(l_post, l_pre, xs, ys):
        maps = []
        nmlp = vec_pk(P["norm_mlp_w"][l_post]) if l_post is not None else np.zeros((128, KC), np.float32)
        npre = vec_pk(P["norm_mix_w"][l_pre]) if l_pre < DEPTH else vec_pk(P["final_norm_w"])
        nrm = np.ascontiguousarray(np.stack([nmlp, npre], 1))
        for k in range(NCORES):
            b, r = core_tok(k)
            d = {"xT": xs[k], "adaL": ada_sets(m, l_post, l_pre, b), "adaC": ada_sets(m, l_post, l_pre, 2), "nrm": nrm}
            if l_post is not None:
                d["yT"] = ys[k]
                d["w_out"] = P["ab_w_out"][l_post // 2] if l_post % 2 == 0 else P["hg_w_out"][l_post // 2]
                d["w1"] = P["mlp_w1"][l_post]
                d["w2"] = P["mlp_w2"][l_post]
            if l_pre < DEPTH:
                d["w_in"] = P["ab_w_in"][l_pre // 2] if l_pre % 2 == 0 else P["hg_w_in"][l_pre // 2]
            maps.append(d)
        return maps

    res = run(get_nc("tok", build_tok, False, "ab"), tok_maps(None, 0, xs, None))
    for l in range(DEPTH):
        pT = tokT_to_full([r["pT"] for r in res])
        xs = [r["xoT"] for r in res]
        if l % 2 == 0:
            y = ab_host_gather(run(get_nc("ab", build_ab), ab_host_inputs(pT, l // 2, P)))
        else:
            y = hg_host_gather(run(get_nc("hg", build_hg, l), hg_host_inputs(pT, l // 2, P)))
        ys = full_to_tokT(y)
        pre = "final" if l + 1 == DEPTH else ("ab" if (l + 1) % 2 == 0 else "hg")
        res = run(get_nc("tok", build_tok, True, pre), tok_maps(l, l + 1, xs, ys))
    lat, _ = from_core_T([r["outT"] for r in res])
    return np.ascontiguousarray(lat.astype(np.float32))
```

```python
import numpy as np
from contextlib import ExitStack
import concourse.bass as bass
import concourse.mybir as mybir
from concourse.bass_utils import run_bass_kernel_spmd

F32 = mybir.dt.float32
BF16 = mybir.dt.bfloat16
AF = mybir.ActivationFunctionType
ALU = mybir.AluOpType
AX = mybir.AxisListType

D = 2048
KC = D // 128
B = 2
SEQ = 4096
CTX = 256
DEPTH = 4
DFF = 8192
NCORES = 8
NL = B * SEQ // NCORES
NCX = B * CTX // NCORES
NT = NL + NCX
EPS = 1e-6
D_AB_IN = 3136
D_HG_IN = 10240
TS = SEQ + CTX

SAME_ENG_SYNC = True


class Rec:
    def __init__(self):
        self.calls = []

    def __getattr__(self, name):
        def f(*a, **k):
            self.calls.append((name, a, k))
            return self
        return f


class FW:
    ENGS = ("pe", "dve", "act", "pool", "sp")

    def __init__(self, nc, es, n_dma_sems=48):
        self.nc = nc
        self.es = es
        self.eng = {"pe": nc.tensor, "dve": nc.vector, "act": nc.scalar, "pool": nc.gpsimd, "sp": nc.sync}
        self.sem = {e: es.enter_context(nc.semaphore("s_" + e)) for e in self.ENGS}
        self.cnt = {e: 0 for e in self.ENGS}
        self.prog = {e: [] for e in self.ENGS}
        self.seen = {e: {} for e in self.ENGS}
        self.dsem = [es.enter_context(nc.semaphore("d%d" % i)) for i in range(n_dma_sems)]
        self.dval = [0] * n_dma_sems
        self.drr = 0
        self.last_w = {}
        self.readers = {}
        self.semobj = {}
        self.n_ps = 0

    def sb(self, name, shape, dt):
        return self.es.enter_context(self.nc.sbuf_tensor(name, list(shape), dt))

    def ps(self, name, shape, dt=F32):
        return self.es.enter_context(self.nc.psum_tensor(name, list(shape), dt))

    def _collect(self, eng, reads, writes, nosame=False):
        toks = []
        for k in reads:
            t = self.last_w.get(k)
            if t is not None:
                toks.append(t)
        for k in writes:
            t = self.last_w.get(k)
            if t is not None:
                toks.append(t)
            toks.extend(self.readers.get(k, ()))
        need = {}
        for (sid, val) in toks:
            if ((not SAME_ENG_SYNC) or nosame) and sid == ("e", eng):
                continue
            if self.seen[eng].get(sid, 0) >= val:
                continue
            if need.get(sid, 0) < val:
                need[sid] = val
        for sid, val in need.items():
            self.seen[eng][sid] = val
        return list(need.items())

    def _commit(self, tok, reads, writes):
        for k in writes:
            self.last_w[k] = tok
            self.readers[k] = []
        for k in reads:
            if k in writes:
                continue
            self.readers.setdefault(k, []).append(tok)

    def _semof(self, sid):
        kind, x = sid
        return self.sem[x] if kind == "e" else self.dsem[x]

    def op(self, eng, fn, reads=(), writes=(), nosame=False):
        waits = self._collect(eng, reads, writes, nosame)
        self.cnt[eng] += 1
        tok = (("e", eng), self.cnt[eng])
        rec = Rec()
        fn(rec)
        calls = rec.calls

        def run(e, calls=calls):
            ins = None
            for (name, a, k) in calls:
                ins = getattr(e, name)(*a, **k)
            return ins
        self.prog[eng].append((waits, run, ("e", eng), 1))
        self._commit(tok, reads, writes)
        return tok

    def dma(self, eng, out, in_, reads=(), writes=(), **kw):
        waits = self._collect(eng, reads, writes)
        i = self.drr
        self.drr = (self.drr + 1) % len(self.dsem)
        sid = ("d", i)
        if self.dval[i] > 0 and self.seen[eng].get(sid, 0) < self.dval[i]:
            waits.append((sid, self.dval[i]))
            self.seen[eng][sid] = self.dval[i]
        self.dval[i] += 16
        tok = (sid, self.dval[i])
        self.prog[eng].append((waits, (lambda e, out=out, in_=in_, kw=kw: e.dma_start(out=out, in_=in_, **kw)), sid, 16))
        self._commit(tok, reads, writes)
        return tok

    def barrier(self):
        allw = [(("e", e), self.cnt[e]) for e in self.ENGS if self.cnt[e] > 0]
        allw += [(("d", i), v) for i, v in enumerate(self.dval) if v > 0]
        for e in self.ENGS:
            waits = []
            for (sid, val) in allw:
                if sid == ("e", e):
                    continue
                if self.seen[e].get(sid, 0) < val:
                    waits.append((sid, val))
                    self.seen[e][sid] = val
            if waits:
                self.prog[e].append((waits, None, None, 0))

    def finish(self, eng="sp"):
        waits = []
        for i, v in enumerate(self.dval):
            if v > 0:
                waits.append((("d", i), v))
        self.prog[eng].append((waits, None, None, 0))
        block = self.es.enter_context(self.nc.Block())

        def replay(name):
            def body(e):
                for (waits, fn, sid, inc) in self.prog[name]:
                    for (wsid, val) in waits:
                        e.wait_ge(self._semof(wsid), val)
                    if fn is not None:
                        ins = fn(e)
                        ins.then_inc(self._semof(sid), inc)
            return body

        block.tensor(replay("pe"))
        block.vector(replay("dve"))
        block.scalar(replay("act"))
        block.gpsimd(replay("pool"))
        block.sync(replay("sp"))


def mm_group(fw, out_ps, pairs, reads, writes):
    n = len(pairs)

    def fn(e):
        ins = None
        for i, (l, r) in enumerate(pairs):
            ins = e.matmul(out_ps, l, r, start=(i == 0), stop=(i == n - 1))
        return ins
    return fw.op("pe", fn, reads=reads, writes=writes)


ADA_COLS = 6 * D // NCORES
ADA_J = ADA_COLS // 128


def build_ada():
    nc = bass.Bass("TRN2", target_bir_lowering=False)
    cv = nc.dram_tensor("cv", [128, KC, 3], F32, kind="ExternalInput").ap()
    w = nc.dram_tensor("w", [DEPTH, D, ADA_COLS], F32, kind="ExternalInput").ap()
    bb = nc.dram_tensor("b", [128, DEPTH, ADA_J], F32, kind="ExternalInput").ap()
    out = nc.dram_tensor("out", [128, DEPTH, ADA_J, 3], F32, kind="ExternalOutput").ap()
    with ExitStack() as es:
        fw = FW(nc, es)
        cvt = fw.sb("cvt", [128, KC, 3], F32)
        sg = fw.sb("sg", [128, KC, 3], F32)
        sv = fw.sb("sv", [128, KC, 3], F32)
        bt = fw.sb("bt", [128, DEPTH, ADA_J], F32)
        ot = fw.sb("ot", [128, DEPTH, ADA_J, 3], F32)
        NB = 3
        wt = [fw.sb("wt%d" % i, [128, KC, 512], F32) for i in range(NB)]
        pst = [fw.ps("ps%d" % i, [128, 512], F32) for i in range(4)]
        fw.dma("sp", cvt[:], cv[:, :, :], writes=["cvt"])
        fw.dma("sp", bt[:], bb[:, :, :], writes=["bt"])
        fw.op("act", lambda e: e.activation(out=sg[:], in_=cvt[:], func=AF.Sigmoid), reads=["cvt"], writes=["sg"])
        fw.op("dve", lambda e: e.tensor_tensor(out=sv[:], in0=cvt[:], in1=sg[:], op=ALU.mult), reads=["cvt", "sg"], writes=["sv"])
        it = 0
        for l in range(DEPTH):
            for g in range(ADA_COLS // 512):
                wb = wt[it % NB]
                wk = "wt%d" % (it % NB)
                src = w[l].rearrange("(kc p) n -> p kc n", p=128)[:, :, g * 512:(g + 1) * 512]
                q = "sp" if it % 2 == 0 else "act"
                fw.dma(q, wb[:], src, writes=[wk])
                for jj in range(4):
                    j = g * 4 + jj
                    pi = (it * 4 + jj) % 4
                    pk = "ps%d" % pi
                    mm_group(fw, pst[pi][:, 0:3],
                             [(wb[:, kc, jj * 128:(jj + 1) * 128], sv[:, kc, :]) for kc in range(KC)],
                             reads=[wk, "sv"], writes=[pk])
                    fw.op("dve", lambda e, pi=pi, l=l, j=j: e.tensor_scalar(
                        out=ot[:, l, j, :], in0=pst[pi][:, 0:3], scalar1=bt[:, l, j:j + 1], scalar2=None, op0=ALU.add),
                        reads=[pk, "bt"], writes=["ot"])
                it += 1
        fw.dma("sp", out[:, :, :, :], ot[:], reads=["ot"])
        fw.finish()
    return nc


TT = [(0, 512), (512, 512), (1024, 64)]


def build_tok(post, pre):
    nin = {"ab": D_AB_IN, "hg": D_HG_IN, "final": 0}[pre]
    nc = bass.Bass("TRN2", target_bir_lowering=False)
    xin = nc.dram_tensor("xT", [D, NT], F32, kind="ExternalInput").ap()
    adaL = nc.dram_tensor("adaL", [128, 2, 6, KC], F32, kind="ExternalInput").ap()
    adaC = nc.dram_tensor("adaC", [128, 2, 6, KC], F32, kind="ExternalInput").ap()
    nrm = nc.dram_tensor("nrm", [128, 2, KC], F32, kind="ExternalInput").ap()
    if post:
        yin = nc.dram_tensor("yT", [D, NT], F32, kind="ExternalInput").ap()
        w_out = nc.dram_tensor("w_out", [D, D], F32, kind="ExternalInput").ap()
        w1 = nc.dram_tensor("w1", [D, DFF], F32, kind="ExternalInput").ap()
        w2 = nc.dram_tensor("w2", [DFF, D], F32, kind="ExternalInput").ap()
    if nin:
        w_in = nc.dram_tensor("w_in", [D, nin], F32, kind="ExternalInput").ap()
        pout = nc.dram_tensor("pT", [nin, NT], F32, kind="ExternalOutput").ap()
        xout = nc.dram_tensor("xoT", [D, NT], F32, kind="ExternalOutput").ap()
    else:
        fout = nc.dram_tensor("outT", [D, NT], F32, kind="ExternalOutput").ap()

    with ExitStack() as es:
        fw = FW(nc, es)
        x = fw.sb("x", [128, KC, NT], F32)
        hT = fw.sb("hT", [128, KC, NT], BF16)
        h1 = fw.sb("h1", [128, KC, NT], BF16)
        NWB = 2
        wb = [fw.sb("wb%d" % i, [128, KC, 512], BF16) for i in range(NWB)]
        aL = fw.sb("aL", [128, 2, 6, KC], F32)
        aC = fw.sb("aC", [128, 2, 6, KC], F32)
        nw = fw.sb("nw", [128, 2, KC], F32)
        AL = fw.sb("AL", [128, 2, KC], F32)
        AC = fw.sb("AC", [128, 2, KC], F32)
        ones = fw.sb("ones", [128, 128], BF16)
        rstd = fw.sb("rstd", [128, NT], F32)
        tmpn = [fw.sb("tmpn%d" % i, [128, NT], F32) for i in range(2)]
        stg = [fw.sb("stg%d" % i, [128, NT], F32) for i in range(3)]
        rl = [fw.sb("rl%d" % i, [128, 512], F32) for i in range(2)]
        NPS = 8
        pst = [fw.ps("ps%d" % i, [128, 512], F32) for i in range(NPS)]
        st = {"ps": 0, "wb": 0, "tmpn": 0, "stg": 0, "rl": 0}

        def nxt(name, n):
            i = st[name] % n
            st[name] += 1
            return i

        xk = lambda kc: ("x", kc)
        hk = lambda kc: ("h", kc)
        h1k = lambda kc: ("h1", kc)

        xv = xin.rearrange("(kc p) n -> p kc n", p=128)
        for kc in range(KC):
            fw.dma("sp", x[:, kc, :], xv[:, kc, :], writes=[xk(kc)])
        fw.dma("act", aL[:], adaL[:, :, :, :], writes=["aL"])
        fw.dma("act", aC[:], adaC[:, :, :, :], writes=["aC"])
        fw.dma("act", nw[:], nrm[:, :, :], writes=["nw"])
        fw.op("dve", lambda e: e.memset(ones[:], 1.0), writes=["ones"])
        for (s, idx) in ((0, 4), (1, 1)):
            fw.op("dve", lambda e, s=s, idx=idx: e.scalar_tensor_tensor(
                out=AL[:, s, :], in0=aL[:, s, idx, :], scalar=1.0, in1=nw[:, s, :], op0=ALU.add, op1=ALU.mult),
                reads=["aL", "nw"], writes=["AL"])
            fw.op("dve", lambda e, s=s, idx=idx: e.scalar_tensor_tensor(
                out=AC[:, s, :], in0=aC[:, s, idx, :], scalar=1.0, in1=nw[:, s, :], op0=ALU.add, op1=ALU.mult),
                reads=["aC", "nw"], writes=["AC"])

        def load_w(src_ap, ncols):
            i = nxt("wb", NWB)
            fw.dma("pool", wb[i][:, :, 0:ncols], src_ap.rearrange("(kc p) n -> p kc n", p=128), writes=[("wb", i)])
            return i

        def proj(src, srckey, wsrc, ncols_total, evac):
            c0 = 0
            while c0 < ncols_total:
                gw = min(512, ncols_total - c0)
                wi = load_w(wsrc[:, c0:c0 + gw], gw)
                cc = 0
                while cc < gw:
                    cw = min(128, gw - cc)
                    c = (c0 + cc) // 128
                    for tt, (t0, tw) in enumerate(TT):
                        pi = nxt("ps", NPS)
                        pk = ("ps", pi)
                        mm_group(fw, pst[pi][0:cw, 0:tw],
                                 [(wb[wi][:, kc, cc:cc + cw], src[:, kc, t0:t0 + tw]) for kc in range(KC)],
                                 reads=[("wb", wi)] + [srckey(kc) for kc in range(KC)], writes=[pk])
                        evac(c, cw, tt, pst[pi][0:cw, 0:tw], pk)
                    cc += cw
                c0 += gw

        def norm_mod(s, shift_idx, out_dt_tile, outkey, final=False):
            for kc in range(KC):
                fw.op("act", lambda e, kc=kc: e.activation(out=h1[:, kc, :], in_=x[:, kc, :], func=AF.Square),
                      reads=[xk(kc)], writes=[h1k(kc)])
            for tt, (t0, tw) in enumerate(TT):
                pi = nxt("ps", NPS)
                pk = ("ps", pi)
                mm_group(fw, pst[pi][:, 0:tw], [(ones[:], h1[:, kc, t0:t0 + tw]) for kc in range(KC)],
                         reads=["ones"] + [h1k(kc) for kc in range(KC)], writes=[pk])
                fw.op("act", lambda e, pi=pi, t0=t0, tw=tw: e.activation(
                    out=rstd[:, t0:t0 + tw], in_=pst[pi][:, 0:tw], func=AF.Sqrt, bias=EPS_AP[:, 0:1], scale=1.0 / D),
                    reads=[pk, "eps"], writes=["rstd"])
            fw.op("dve", lambda e: e.reciprocal(out=rstd[:], in_=rstd[:]), reads=["rstd"], writes=["rstd"])
            for kc in range(KC):
                ti = nxt("tmpn", 2)
                tk = ("tmpn", ti)
                fw.op("dve", lambda e, kc=kc, ti=ti: e.tensor_tensor(out=tmpn[ti][:], in0=x[:, kc, :], in1=rstd[:], op=ALU.mult),
                      reads=[xk(kc), "rstd"], writes=[tk])
                if final:
                    si = nxt("stg", 3)
                    sk = ("stg", si)
                    fw.op("act", lambda e, kc=kc, ti=ti, si=si: e.activation(
                        out=stg[si][:], in_=tmpn[ti][:], func=AF.Identity, scale=nw[:, 1, kc:kc + 1]),
                        reads=[tk, "nw"], writes=[sk])
                    fw.dma("sp", fout[kc * 128:(kc + 1) * 128, :], stg[si][:], reads=[sk])
                else:
                    fw.op("act", lambda e, kc=kc, ti=ti: e.activation(
                        out=hT[:, kc, 0:NL], in_=tmpn[ti][:, 0:NL], func=AF.Identity,
                        bias=aL[:, s, shift_idx, kc:kc + 1], scale=AL[:, s, kc:kc + 1]),
                        reads=[tk, "aL", "AL"], writes=[hk(kc)])
                    fw.op("act", lambda e, kc=kc, ti=ti: e.activation(
                        out=hT[:, kc, NL:NT], in_=tmpn[ti][:, NL:NT], func=AF.Identity,
                        bias=aC[:, s, shift_idx, kc:kc + 1], scale=AC[:, s, kc:kc + 1]),
                        reads=[tk, "aC", "AC", hk(kc)], writes=[hk(kc)])

        EPS_AP = fw.sb("epsc", [128, 1], F32)
        fw.op("dve", lambda e: e.memset(EPS_AP[:], EPS), writes=["eps"])

        def resid_evac(gidx):
            def evac(c, cw, tt, ps_ap, pk):
                t0, tw = TT[tt]
                a = aL if tt < 2 else aC
                fw.op("dve", lambda e: e.scalar_tensor_tensor(
                    out=x[:, c, t0:t0 + tw], in0=ps_ap, scalar=a[:, 0, gidx, c:c + 1], in1=x[:, c, t0:t0 + tw],
                    op0=ALU.mult, op1=ALU.add),
                    reads=[pk, "aL", "aC", xk(c)], writes=[xk(c)])
            return evac

        if post:
            yv = yin.rearrange("(kc p) n -> p kc n", p=128)
            for kc in range(KC):
                fw.dma("pool", hT[:, kc, :], yv[:, kc, :], writes=[hk(kc)])
            proj(hT, hk, w_out, D, resid_evac(2))
            norm_mod(0, 3, None, None)
            for q in range(4):
                def evac1(c, cw, tt, ps_ap, pk):
                    t0, tw = TT[tt]
                    ri = nxt("rl", 2)
                    rk = ("rl", ri)
                    fw.op("act", lambda e: e.activation(out=rl[ri][:, 0:tw], in_=ps_ap, func=AF.Relu),
                          reads=[pk], writes=[rk])
                    fw.op("dve", lambda e: e.tensor_tensor(out=h1[:, c, t0:t0 + tw], in0=rl[ri][:, 0:tw], in1=rl[ri][:, 0:tw], op=ALU.mult),
                          reads=[rk], writes=[h1k(c)])
                proj(hT, hk, w1[:, q * 2048:(q + 1) * 2048], 2048, evac1)
                proj(h1, h1k, w2[q * 2048:(q + 1) * 2048, :], D, resid_evac(5))

        if nin:
            xov = xout.rearrange("(kc p) n -> p kc n", p=128)
            for kc in range(KC):
                fw.dma("sp", xov[:, kc, :], x[:, kc, :], reads=[xk(kc)])
            norm_mod(1, 0, None, None)
            cur = {}

            def evac_p(c, cw, tt, ps_ap, pk):
                t0, tw = TT[tt]
                if tt == 0:
                    cur["si"] = nxt("stg", 3)
                si = cur["si"]
                sk = ("stg", si)
                eng = "act" if (c + tt) % 2 == 0 else "dve"
                if eng == "act":
                    fw.op("act", lambda e: e.activation(out=stg[si][0:cw, t0:t0 + tw], in_=ps_ap, func=AF.Copy),
                          reads=[pk], writes=[sk])
                else:
                    fw.op("dve", lambda e: e.tensor_copy(out=stg[si][0:cw, t0:t0 + tw], in_=ps_ap),
                          reads=[pk], writes=[sk])
                if tt == len(TT) - 1:
                    fw.dma("sp", pout[c * 128:c * 128 + cw, :], stg[si][0:cw, :], reads=[sk])
            proj(hT, hk, w_in, nin, evac_p)
        else:
            norm_mod(1, 0, None, None, final=True)
        fw.finish()
    return nc


def core_tok(k):
    return k // 4, k % 4


def to_core_T(lat, ctx):
    outs = []
    for k in range(NCORES):
        b, r = core_tok(k)
        a = np.concatenate([lat[b, r * NL:(r + 1) * NL], ctx[b, r * NCX:(r + 1) * NCX]], axis=0)
        outs.append(np.ascontiguousarray(a.T))
    return outs


def from_core_T(arrs):
    F = arrs[0].shape[0]
    lat = np.empty((B, SEQ, F), arrs[0].dtype)
    ctx = np.empty((B, CTX, F), arrs[0].dtype)
    for k in range(NCORES):
        b, r = core_tok(k)
        lat[b, r * NL:(r + 1) * NL] = arrs[k][:, :NL].T
        ctx[b, r * NCX:(r + 1) * NCX] = arrs[k][:, NL:].T
    return lat, ctx


def vec_pk(v):
    return np.ascontiguousarray(v.reshape(KC, 128).T)


def ada_sets(m, lpost, lpre, v):
    out = np.zeros((128, 2, 6, KC), np.float32)
    for s, l in enumerate((lpost, lpre)):
        if l is None or l < 0 or l >= DEPTH:
            continue
        out[:, s] = m[l, v].reshape(6, KC, 128).transpose(2, 0, 1)
    return out


def ada_host_inputs(c, c_ctx, ada_w, ada_b):
    cvec = np.stack([c[0], c[1], c_ctx], 0)
    cv = np.ascontiguousarray(cvec.reshape(3, KC, 128).transpose(2, 1, 0))
    maps = []
    for k in range(NCORES):
        w = np.ascontiguousarray(ada_w[:, :, k * ADA_COLS:(k + 1) * ADA_COLS])
        b = np.ascontiguousarray(ada_b[:, k * ADA_COLS:(k + 1) * ADA_COLS].reshape(DEPTH, ADA_J, 128).transpose(2, 0, 1))
        maps.append({"cv": cv, "w": w, "b": b})
    return maps


def ada_host_gather(results):
    m = np.zeros((DEPTH, 3, 6 * D), np.float32)
    for k in range(NCORES):
        o = results[k]["out"]
        m[:, :, k * ADA_COLS:(k + 1) * ADA_COLS] = o.transpose(1, 3, 2, 0).reshape(DEPTH, 3, ADA_COLS)
    return m


_NC_CACHE = {}


def get_nc(name, builder, *args):
    key = (name,) + tuple(args)
    if key not in _NC_CACHE:
        _NC_CACHE[key] = builder(*args)
    return _NC_CACHE[key]


def run(nc, maps):
    res = run_bass_kernel_spmd(nc, maps, core_ids=list(range(NCORES)))
    return res.results


ATTN_SCALE = 192.0 ** -0.5
TL = [(i * 512, 512) for i in range(SEQ // 512)] + [(SEQ, CTX)]
NKT = TS // 128
GELU_C = 1.5957691216057308


def build_ab():
    nc = bass.Bass("TRN2", target_bir_lowering=False)
    dgu = nc.dram_tensor("guT", [2, 128, B, TS], F32, kind="ExternalInput").ap()
    dcq = nc.dram_tensor("cqT", [512, B, TS], F32, kind="ExternalInput").ap()
    dckv = nc.dram_tensor("ckvT", [512, B, TS], F32, kind="ExternalInput").ap()
    dkr = nc.dram_tensor("krT", [64, B, TS], F32, kind="ExternalInput").ap()
    dvec = nc.dram_tensor("vecs", [128, 24], F32, kind="ExternalInput").ap()
    dwa = nc.dram_tensor("wax", [2, 2, 128, 128], F32, kind="ExternalInput").ap()
    dwuq = nc.dram_tensor("wuq", [512, 192], F32, kind="ExternalInput").ap()
    dwukv = nc.dram_tensor("wukv", [512, 256], F32, kind="ExternalInput").ap()
    dcs = nc.dram_tensor("cs", [2, 64, SEQ], F32, kind="ExternalInput").ap()
    drm = nc.dram_tensor("rm", [64, 64], F32, kind="ExternalInput").ap()
    dy = nc.dram_tensor("yT", [256, B, TS], F32, kind="ExternalOutput").ap()

    with ExitStack() as es:
        fw = FW(nc, es)
        vec = fw.sb("vecsb", [128, 24], F32)
        c1 = fw.sb("c1", [128, 4], F32)
        nsp = fw.sb("nsp", [128, 2], F32)
        wax = fw.sb("waxb", [128, 4, 128], BF16)
        wuq = fw.sb("wuqb", [128, 4, 192], BF16)
        wukv = fw.sb("wukvb", [128, 4, 256], BF16)
        ones = fw.sb("ones", [128, 128], BF16)
        rm = fw.sb("rmsb", [64, 64], F32)
        NPS = 8
        pst = [fw.ps("ps%d" % i, [128, 512], F32) for i in range(NPS)]
        st = {}

        def nxt(name, n):
            i = st.get(name, 0) % n
            st[name] = st.get(name, 0) + 1
            return i

        fw.dma("sp", vec[:], dvec[:, :], writes=["vec"])
        fw.dma("sp", rm[:], drm[:, :], writes=["rm"])
        fw.dma("pool", wax[:], dwa.rearrange("a d i o -> i (a d) o"), writes=["wax"])
        fw.dma("pool", wuq[:], dwuq.rearrange("(kc p) n -> p kc n", p=128), writes=["wuq"])
        fw.dma("pool", wukv[:], dwukv.rearrange("(kc p) n -> p kc n", p=128), writes=["wukv"])
        fw.op("dve", lambda e: e.memset(ones[:], 1.0), writes=["ones"])
        fw.op("dve", lambda e: e.memset(c1[:, 0:1], 1.0), writes=["c1"])
        fw.op("dve", lambda e: e.memset(c1[:, 1:2], EPS), reads=["c1"], writes=["c1"])
        fw.op("act", lambda e: e.activation(out=nsp[:], in_=vec[:, 9:11], func=AF.Exp, scale=-1.0), reads=["vec"], writes=["nsp"])
        fw.op("act", lambda e: e.activation(out=nsp[:], in_=nsp[:], func=AF.Ln, bias=c1[:, 0:1], scale=1.0), reads=["nsp", "c1"], writes=["nsp"])
        fw.op("dve", lambda e: e.tensor_scalar(out=nsp[:], in0=nsp[:], scalar1=-8.0, scalar2=None, op0=ALU.mult), reads=["nsp"], writes=["nsp"])

        with ExitStack() as es1:
            names = ["tu", "tg", "uc", "t1", "t2", "t3", "hacc", "tgel"]
            T = {n: es1.enter_context(nc.sbuf_tensor("l_" + n, [128, TS], F32)) for n in names}
            ucb = es1.enter_context(nc.sbuf_tensor("l_ucb", [128, TS], BF16))
            segs = [(0, SEQ), (SEQ, TS)]
            for b in range(B):
                tu, tg, uc, t1, t2, t3, hacc, tgel = (T[n] for n in names)
                fw.dma("sp", tg[:], dgu[0, :, b, :], writes=["tg"])
                fw.op("pool", lambda e: e.tensor_tensor(out=tgel[:], in0=tg[:], in1=tg[:], op=ALU.mult), reads=["tg"], writes=["tgel"])
                fw.op("pool", lambda e: e.tensor_scalar(out=tgel[:], in0=tgel[:], scalar1=0.044715, scalar2=1.0, op0=ALU.mult, op1=ALU.add),
                      reads=["tgel"], writes=["tgel"])
                fw.op("pool", lambda e: e.tensor_tensor(out=tgel[:], in0=tgel[:], in1=tg[:], op=ALU.mult), reads=["tg", "tgel"], writes=["tgel"])
                fw.op("act", lambda e: e.activation(out=tgel[:], in_=tgel[:], func=AF.Sigmoid, scale=GELU_C), reads=["tgel"], writes=["tgel"])
                fw.op("pool", lambda e: e.tensor_tensor(out=tgel[:], in0=tgel[:], in1=tg[:], op=ALU.mult), reads=["tg", "tgel"], writes=["tgel"])
                fw.dma("act", tu[:], dgu[1, :, b, :], writes=["tu"])
                fw.op("dve", lambda e: e.tensor_scalar(out=uc[:], in0=tu[:], scalar1=vec[:, 2:3], scalar2=vec[:, 4:5], op0=ALU.mult, op1=ALU.add),
                      reads=["tu", "vec"], writes=["uc"])
                for (s0, s1) in segs:
                    for (tap, sh) in ((0, 2), (1, 1)):
                        fw.op("dve", lambda e, s0=s0, s1=s1, tap=tap, sh=sh: e.scalar_tensor_tensor(
                            out=uc[:, s0 + sh:s1], in0=tu[:, s0:s1 - sh], scalar=vec[:, tap:tap + 1], in1=uc[:, s0 + sh:s1],
                            op0=ALU.mult, op1=ALU.add), reads=["tu", "vec", "uc"], writes=["uc"])
                    fw.op("dve", lambda e, s0=s0, s1=s1: e.scalar_tensor_tensor(
                        out=uc[:, s0:s1 - 1], in0=tu[:, s0 + 1:s1], scalar=vec[:, 3:4], in1=uc[:, s0:s1 - 1],
                        op0=ALU.mult, op1=ALU.add), reads=["tu", "vec", "uc"], writes=["uc"])
                fw.op("act", lambda e: e.activation(out=ucb[:], in_=uc[:], func=AF.Copy), reads=["uc"], writes=["ucb"])
                for d in range(2):
                    for (t0, tw) in TL:
                        for (ax, dst, dk, bcol) in ((0, t1, "t1", 5 + d), (1, t2, "t2", 7 + d)):
                            pi = nxt("ps", NPS)
                            pk = ("ps", pi)
                            mm_group(fw, pst[pi][:, 0:tw], [(wax[:, ax * 2 + d, :], ucb[:, t0:t0 + tw])],
                                     reads=["wax", "ucb"], writes=[pk])
                            fw.op("act", lambda e, pi=pi, dst=dst, t0=t0, tw=tw, bcol=bcol: e.activation(
                                out=dst[:, t0:t0 + tw], in_=pst[pi][:, 0:tw], func=AF.Sigmoid, bias=vec[:, bcol:bcol + 1], scale=1.0),
                                reads=[pk, "vec"], writes=[dk])
                    fw.op("act", lambda e, d=d: e.activation(out=t1[:], in_=t1[:], func=AF.Exp, scale=nsp[:, d:d + 1]),
                          reads=["t1", "nsp"], writes=["t1"])
                    fw.op("pool", lambda e: e.tensor_tensor(out=t2[:], in0=t2[:], in1=uc[:], op=ALU.mult), reads=["t2", "uc"], writes=["t2"])
                    fw.op("dve", lambda e: e.tensor_tensor(out=t3[:], in0=t1[:], in1=t1[:], op=ALU.mult), reads=["t1"], writes=["t3"])
                    fw.op("act", lambda e: e.activation(out=t3[:], in_=t3[:], func=AF.Sqrt, bias=c1[:, 0:1], scale=-1.0),
                          reads=["t3", "c1"], writes=["t3"])
                    fw.op("dve", lambda e: e.tensor_tensor(out=t2[:], in0=t2[:], in1=t3[:], op=ALU.mult), reads=["t2", "t3"], writes=["t2"])
                    dst = hacc if d == 0 else t3
                    dk = "hacc" if d == 0 else "t3"
                    if d == 0:
                        fw.op("dve", lambda e, dst=dst: e.tensor_tensor_scan(
                            out=dst[:, SEQ:TS], data0=t1[:, SEQ:TS], data1=t2[:, SEQ:TS], initial=0.0, op0=ALU.mult, op1=ALU.add),
                            reads=["t1", "t2"], writes=[dk])
                        fw.op("dve", lambda e, dst=dst: e.tensor_tensor_scan(
                            out=dst[:, 0:SEQ], data0=t1[:, 0:SEQ], data1=t2[:, 0:SEQ], initial=dst[:, TS - 1:TS], op0=ALU.mult, op1=ALU.add),
                            reads=["t1", "t2", dk], writes=[dk])
                    else:
                        fw.op("dve", lambda e, dst=dst: e.tensor_tensor_scan(
                            out=dst[:, SEQ:TS][:, ::-1], data0=t1[:, SEQ:TS][:, ::-1], data1=t2[:, SEQ:TS][:, ::-1],
                            initial=0.0, op0=ALU.mult, op1=ALU.add), reads=["t1", "t2"], writes=[dk])
                        fw.op("dve", lambda e, dst=dst: e.tensor_tensor_scan(
                            out=dst[:, 0:SEQ][:, ::-1], data0=t1[:, 0:SEQ][:, ::-1], data1=t2[:, 0:SEQ][:, ::-1],
                            initial=dst[:, SEQ:SEQ + 1], op0=ALU.mult, op1=ALU.add), reads=["t1", "t2", dk], writes=[dk])
                        fw.op("dve", lambda e: e.tensor_tensor(out=hacc[:], in0=hacc[:], in1=t3[:], op=ALU.add),
                              reads=["hacc", "t3"], writes=["hacc"])
                fw.op("dve", lambda e: e.tensor_tensor(out=t1[:], in0=tgel[:], in1=hacc[:], op=ALU.mult), reads=["hacc", "tgel", "t1"], writes=["t1"])
                fw.dma("sp", dy[0:128, b, :], t1[:], reads=["t1"])
            fw.barrier()

        with ExitStack() as es2:
            def sb2(name, shape, dt):
                return es2.enter_context(nc.sbuf_tensor("m_" + name, list(shape), dt))
            cs = sb2("cssb", [64, 2, SEQ], F32)
            qTn = sb2("qTn", [128, TS], BF16)
            qTr = sb2("qTr", [64, TS], BF16)
            kTn = sb2("kTn", [128, TS], BF16)
            kTr = sb2("kTr", [64, TS], BF16)
            V = sb2("V", [128, NKT, 128], BF16)
            cqt = [sb2("cqt%d" % i, [128, 4, 512], F32) for i in range(2)]
            ckt = [sb2("ckt%d" % i, [128, 4, 512], F32) for i in range(2)]
            krt = [sb2("krt%d" % i, [64, 512], F32) for i in range(2)]
            sq = [sb2("sq%d" % i, [128, 4, 512], BF16) for i in range(2)]
            cqn = [sb2("cqn%d" % i, [128, 4, 512], BF16) for i in range(2)]
            ckn = [sb2("ckn%d" % i, [128, 4, 512], BF16) for i in range(2)]
            rs = [sb2("rs%d" % i, [128, 512], F32) for i in range(2)]
            qrs = [sb2("qrs%d" % i, [64, 512], F32) for i in range(2)]
            ra = [sb2("ra%d" % i, [64, 512], F32) for i in range(2)]
            rb = [sb2("rb%d" % i, [64, 512], F32) for i in range(2)]
            PT = [sb2("PT%d" % i, [128, 512], BF16) for i in range(6)]
            accD = [sb2("accD%d" % i, [128, 512], F32) for i in range(2)]
            accP = [sb2("accP%d" % i, [128, 512], F32) for i in range(2)]
            ones_f = sb2("ones_f", [128, 128], F32)
            fw.op("dve", lambda e: e.memset(ones_f[:], 1.0), writes=["ones_f"])
            rden = [sb2("rden%d" % i, [128, 512], F32) for i in range(2)]
            ob = [sb2("ob%d" % i, [128, 512], F32) for i in range(2)]
            fw.dma("sp", cs[:], dcs.rearrange("a p t -> p a t"), writes=["cs"])

            def rope(src_ap, srck, t0, tw, dst_ap, dstk):
                pi = nxt("ps", NPS)
                pk = ("ps", pi)
                mm_group(fw, pst[pi][0:64, 0:tw], [(rm[:], src_ap)], reads=["rm", srck], writes=[pk])
                i = nxt("ra", 2)
                fw.op("dve", lambda e: e.tensor_tensor(out=ra[i][:, 0:tw], in0=src_ap, in1=cs[:, 0, t0:t0 + tw], op=ALU.mult),
                      reads=[srck, "cs"], writes=[("ra", i)])
                fw.op("dve", lambda e: e.tensor_tensor(out=rb[i][:, 0:tw], in0=pst[pi][0:64, 0:tw], in1=cs[:, 1, t0:t0 + tw], op=ALU.mult),
                      reads=[pk, "cs"], writes=[("rb", i)])
                fw.op("dve", lambda e: e.tensor_tensor(out=dst_ap, in0=ra[i][:, 0:tw], in1=rb[i][:, 0:tw], op=ALU.add),
                      reads=[("ra", i), ("rb", i)], writes=[dstk])

            for b in range(B):
                for (t0, tw) in TL:
                    is_ctx = t0 >= SEQ
                    i = nxt("tile", 2)
                    fw.dma("sp", cqt[i][:, :, 0:tw], dcq.rearrange("(kc p) b n -> p kc b n", p=128)[:, :, b, t0:t0 + tw], writes=[("cqt", i)])
                    fw.dma("act", ckt[i][:, :, 0:tw], dckv.rearrange("(kc p) b n -> p kc b n", p=128)[:, :, b, t0:t0 + tw], writes=[("ckt", i)])
                    fw.dma("sp", krt[i][:, 0:tw], dkr[:, b, t0:t0 + tw], writes=[("krt", i)])
                    for (src, srck, ncol, dst, dstk) in ((cqt[i], ("cqt", i), 11, cqn[i], ("cqn", i)), (ckt[i], ("ckt", i), 15, ckn[i], ("ckn", i))):
                        j = nxt("sq", 2)
                        fw.op("act", lambda e, src=src, j=j: e.activation(out=sq[j][:, :, 0:tw], in_=src[:, :, 0:tw], func=AF.Square),
                              reads=[srck], writes=[("sq", j)])
                        pi = nxt("ps", NPS)
                        pk = ("ps", pi)
                        mm_group(fw, pst[pi][:, 0:tw], [(ones[:], sq[j][:, kc, 0:tw]) for kc in range(4)],
                                 reads=["ones", ("sq", j)], writes=[pk])
                        r = nxt("rs", 2)
                        fw.op("act", lambda e, pi=pi, r=r: e.activation(out=rs[r][:, 0:tw], in_=pst[pi][:, 0:tw], func=AF.Sqrt,
                                                                      bias=c1[:, 1:2], scale=1.0 / 512), reads=[pk, "c1"], writes=[("rs", r)])
                        fw.op("dve", lambda e, r=r: e.reciprocal(out=rs[r][:, 0:tw], in_=rs[r][:, 0:tw]), reads=[("rs", r)], writes=[("rs", r)])
                        for kc in range(4):
                            fw.op("dve", lambda e, src=src, dst=dst, kc=kc, r=r, ncol=ncol: e.scalar_tensor_tensor(
                                out=dst[:, kc, 0:tw], in0=src[:, kc, 0:tw], scalar=vec[:, ncol + kc:ncol + kc + 1], in1=rs[r][:, 0:tw],
                                op0=ALU.mult, op1=ALU.mult), reads=[srck, "vec", ("rs", r)], writes=[dstk])
                    pi = nxt("ps", NPS); pk = ("ps", pi)
                    mm_group(fw, pst[pi][:, 0:tw], [(wuq[:, kc, 0:128], cqn[i][:, kc, 0:tw]) for kc in range(4)],
                             reads=["wuq", ("cqn", i)], writes=[pk])
                    fw.op("act", lambda e, pi=pi: e.activation(out=qTn[:, t0:t0 + tw], in_=pst[pi][:, 0:tw], func=AF.Copy),
                          reads=[pk], writes=["qTn"])
                    pi = nxt("ps", NPS); pk = ("ps", pi)
                    mm_group(fw, pst[pi][0:64, 0:tw], [(wuq[:, kc, 128:192], cqn[i][:, kc, 0:tw]) for kc in range(4)],
                             reads=["wuq", ("cqn", i)], writes=[pk])
                    if is_ctx:
                        fw.op("act", lambda e, pi=pi: e.activation(out=qTr[:, t0:t0 + tw], in_=pst[pi][0:64, 0:tw], func=AF.Copy),
                              reads=[pk], writes=["qTr"])
                    else:
                        q = nxt("qrs", 2)
                        fw.op("act", lambda e, pi=pi, q=q: e.activation(out=qrs[q][:, 0:tw], in_=pst[pi][0:64, 0:tw], func=AF.Copy),
                              reads=[pk], writes=[("qrs", q)])
                        rope(qrs[q][:, 0:tw], ("qrs", q), t0, tw, qTr[:, t0:t0 + tw], "qTr")
                    pi = nxt("ps", NPS); pk = ("ps", pi)
                    mm_group(fw, pst[pi][:, 0:tw], [(wukv[:, kc, 0:128], ckn[i][:, kc, 0:tw]) for kc in range(4)],
                             reads=["wukv", ("ckn", i)], writes=[pk])
                    fw.op("act", lambda e, pi=pi: e.activation(out=kTn[:, t0:t0 + tw], in_=pst[pi][:, 0:tw], func=AF.Copy),
                          reads=[pk], writes=["kTn"])
                    for sub in range(tw // 128):
                        pi = nxt("ps", NPS); pk = ("ps", pi)
                        mm_group(fw, pst[pi][:, 0:128], [(ckn[i][:, kc, sub * 128:(sub + 1) * 128], wukv[:, kc, 128:256]) for kc in range(4)],
                                 reads=["wukv", ("ckn", i)], writes=[pk])
                        fw.op("dve", lambda e, pi=pi, sub=sub: e.tensor_copy(out=V[:, t0 // 128 + sub, :], in_=pst[pi][:, 0:128]),
                              reads=[pk], writes=["V"])
                    if is_ctx:
                        fw.op("act", lambda e, i=i: e.activation(out=kTr[:, t0:t0 + tw], in_=krt[i][:, 0:tw], func=AF.Copy),
                              reads=[("krt", i)], writes=["kTr"])
                    else:
                        rope(krt[i][:, 0:tw], ("krt", i), t0, tw, kTr[:, t0:t0 + tw], "kTr")
                SKEW = 2
                for (t0, tw) in TL:
                    is_ctx = t0 >= SEQ
                    kts = list(range(SEQ // 128, NKT)) if is_ctx else list(range(NKT))
                    oi = 4 + nxt("po", 2)
                    di = 6 + nxt("pd", 2)
                    ok, dk = ("ps", oi), ("ps", di)
                    ai = nxt("acc", 2)
                    aD, aP = accD[ai], accP[ai]
                    aDk, aPk = ("accD", ai), ("accP", ai)
                    pend = []

                    def consume(item):
                        n, kt, p = item
                        first, last = (n == 0), (n == len(kts) - 1)
                        fw.op("pe", lambda e: e.matmul(pst[oi][:, 0:tw], V[:, kt, :], PT[p][:, 0:tw], start=first, stop=last),
                              reads=["V", ("PT", p)], writes=[ok], nosame=not first)
                        if n % 2 == 0:
                            if n == 0:
                                fw.op("dve", lambda e: e.tensor_copy(out=aD[:, 0:tw], in_=PT[p][:, 0:tw]), reads=[("PT", p)], writes=[aDk])
                            else:
                                fw.op("dve", lambda e: e.tensor_tensor(out=aD[:, 0:tw], in0=aD[:, 0:tw], in1=PT[p][:, 0:tw], op=ALU.add),
                                      reads=[("PT", p), aDk], writes=[aDk])
                        else:
                            if n == 1:
                                fw.op("pool", lambda e: e.tensor_copy(out=aP[:, 0:tw], in_=PT[p][:, 0:tw]), reads=[("PT", p)], writes=[aPk])
                            else:
                                fw.op("pool", lambda e: e.tensor_tensor(out=aP[:, 0:tw], in0=aP[:, 0:tw], in1=PT[p][:, 0:tw], op=ALU.add),
                                      reads=[("PT", p), aPk], writes=[aPk])

                    for n, kt in enumerate(kts):
                        si = nxt("psS", 4)
                        sk = ("ps", si)
                        mm_group(fw, pst[si][:, 0:tw],
                                 [(kTn[:, kt * 128:(kt + 1) * 128], qTn[:, t0:t0 + tw]), (kTr[:, kt * 128:(kt + 1) * 128], qTr[:, t0:t0 + tw])],
                                 reads=["kTn", "qTn", "kTr", "qTr"], writes=[sk])
                        p = nxt("PT", 6)
                        fw.op("act", lambda e, si=si, p=p: e.activation(out=PT[p][:, 0:tw], in_=pst[si][:, 0:tw], func=AF.Exp, scale=ATTN_SCALE),
                              reads=[sk], writes=[("PT", p)])
                        pend.append((n, kt, p))
                        if len(pend) > SKEW:
                            consume(pend.pop(0))
                    while pend:
                        consume(pend.pop(0))
                    fw.op("dve", lambda e: e.tensor_tensor(out=aD[:, 0:tw], in0=aD[:, 0:tw], in1=aP[:, 0:tw], op=ALU.add),
                          reads=[aDk, aPk], writes=[aDk])
                    mm_group(fw, pst[di][:, 0:tw], [(ones_f[:], aD[:, 0:tw])], reads=["ones_f", aDk], writes=[dk])
                    r = nxt("rden", 2)
                    fw.op("dve", lambda e, r=r: e.reciprocal(out=rden[r][:, 0:tw], in_=pst[di][:, 0:tw]), reads=[dk], writes=[("rden", r)])
                    o = nxt("ob", 2)
                    fw.op("dve", lambda e, r=r, o=o: e.tensor_tensor(out=ob[o][:, 0:tw], in0=pst[oi][:, 0:tw], in1=rden[r][:, 0:tw], op=ALU.mult),
                          reads=[ok, ("rden", r)], writes=[("ob", o)])
                    fw.dma("sp", dy[128:256, b, t0:t0 + tw], ob[o][:, 0:tw], reads=[("ob", o)])
        fw.finish()
    return nc


def rope_consts():
    half = 32
    inv_freq = (10000.0 ** (-np.arange(0, half, 2, dtype=np.float32) / half)).astype(np.float32)
    t = np.arange(SEQ)
    row = (t // 64).astype(np.float32)
    col = (t % 64).astype(np.float32)
    ang_r = row[:, None] * inv_freq
    ang_c = col[:, None] * inv_freq
    ang = np.concatenate([ang_r, ang_r, ang_c, ang_c], axis=-1).astype(np.float32)
    cos = np.cos(ang).astype(np.float32)
    sin = np.sin(ang).astype(np.float32)
    sign = np.ones(64, np.float32)
    sign[0:16] = -1.0
    sign[32:48] = -1.0
    perm = np.concatenate([np.arange(16, 32), np.arange(0, 16), np.arange(48, 64), np.arange(32, 48)])
    rm = np.zeros((64, 64), np.float32)
    rm[perm, np.arange(64)] = 1.0
    cs = np.stack([cos.T, (sin * sign).T], 0)
    return np.ascontiguousarray(cs), rm


def ab_host_inputs(pT, e, P):
    cs, rm = rope_consts()
    maps = []
    cq = np.ascontiguousarray(pT[2048:2560])
    ckv = np.ascontiguousarray(pT[2560:3072])
    kr = np.ascontiguousarray(pT[3072:3136])
    for h in range(NCORES):
        sl = slice(h * 128, (h + 1) * 128)
        gu = np.stack([pT[h * 128:(h + 1) * 128], pT[1024 + h * 128:1024 + (h + 1) * 128]], 0)
        vec = np.zeros((128, 24), np.float32)
        vec[:, 0:4] = P["lru_conv_w"][e][:, sl].T
        vec[:, 4] = P["lru_conv_b"][e][sl]
        vec[:, 5:7] = P["lru_b_a"][e][:, sl].T
        vec[:, 7:9] = P["lru_b_x"][e][:, sl].T
        vec[:, 9:11] = P["lru_lambda"][e][:, sl].T
        vec[:, 11:15] = P["mla_q_norm_w"][e].reshape(4, 128).T
        vec[:, 15:19] = P["mla_kv_norm_w"][e].reshape(4, 128).T
        wax = np.stack([P["lru_w_a"][e][:, h], P["lru_w_x"][e][:, h]], 0)
        maps.append({"guT": np.ascontiguousarray(gu), "cqT": cq, "ckvT": ckv, "krT": kr, "vecs": vec,
                     "wax": np.ascontiguousarray(wax),
                     "wuq": np.ascontiguousarray(P["mla_w_uq"][e][:, h * 192:(h + 1) * 192]),
                     "wukv": np.ascontiguousarray(P["mla_w_ukv"][e][:, h * 256:(h + 1) * 256]),
                     "cs": cs, "rm": rm})
    return maps


def ab_host_gather(results):
    y = np.empty((D, B, TS), np.float32)
    for h in range(NCORES):
        o = results[h]["yT"]
        y[h * 128:(h + 1) * 128] = o[0:128]
        y[1024 + h * 128:1024 + (h + 1) * 128] = o[128:256]
    return y


def tokT_to_full(arrs):
    F = arrs[0].shape[0]
    full = np.empty((F, B, TS), arrs[0].dtype)
    for k in range(NCORES):
        b, r = core_tok(k)
        full[:, b, r * NL:(r + 1) * NL] = arrs[k][:, :NL]
        full[:, b, SEQ + r * NCX:SEQ + (r + 1) * NCX] = arrs[k][:, NL:]
    return full


def full_to_tokT(full):
    outs = []
    for k in range(NCORES):
        b, r = core_tok(k)
        outs.append(np.ascontiguousarray(np.concatenate(
            [full[:, b, r * NL:(r + 1) * NL], full[:, b, SEQ + r * NCX:SEQ + (r + 1) * NCX]], axis=1)))
    return outs


CH = 64
NCH = TS // CH
NPT = TS // 128


def build_hg(layer):
    nc = bass.Bass("TRN2", target_bir_lowering=False)
    dp = nc.dram_tensor("p5", [2, 5, 128, B, TS], F32, kind="ExternalInput").ap()
    dlb = nc.dram_tensor("lbl", [128, 2, DEPTH], F32, kind="ExternalInput").ap()
    dnw = nc.dram_tensor("nw", [128, 1], F32, kind="ExternalInput").ap()
    dmask = nc.dram_tensor("masks", [2, 128, 128], F32, kind="ExternalInput").ap()
    dident = nc.dram_tensor("ident", [128, 128], F32, kind="ExternalInput").ap()
    dy = nc.dram_tensor("yT", [2, 128, B, TS], F32, kind="ExternalOutput").ap()

    with ExitStack() as es:
        fw = FW(nc, es)
        names = ["qs", "iE", "oacc", "kk", "lf", "G", "tmp"]
        T = {n: fw.sb("h_" + n, [128, TS], F32) for n in names}
        qs, iE, oacc, kk, lf, G, tmp = (T[n] for n in names)
        qt = [fw.sb("qt%d" % d, [128, TS], BF16) for d in range(2)]
        kt = [fw.sb("kt%d" % d, [128, TS], BF16) for d in range(2)]
        kh = [fw.sb("kh%d" % d, [128, TS], BF16) for d in range(2)]
        Vt = fw.sb("Vt", [128, NPT, 128], BF16)
        ones_b = fw.sb("ones_b", [128, 1088], BF16)
        ones_m = fw.sb("ones_m", [128, 128], BF16)
        masks = fw.sb("masks_sb", [128, 2, 128], F32)
        ident = fw.sb("ident_sb", [128, 128], F32)
        identb = fw.sb("identb_sb", [128, 128], BF16)
        lbl = fw.sb("lbl_sb", [128, 2, DEPTH], F32)
        lbs = fw.sb("lbs", [128, 8], F32)
        nwt = fw.sb("nwt", [128, 1], F32)
        c1 = fw.sb("c1h", [128, 2], F32)
        dec = [fw.sb("dec%d" % d, [128, NCH], F32) for d in range(2)]
        S = [fw.sb("S%d" % d, [128, 128], F32) for d in range(2)]
        Sb = [fw.sb("Sb%d" % d, [128, 128], BF16) for d in range(2)]
        Am = [fw.sb("Am%d" % i, [128, 128], BF16) for i in range(4)]
        khat = [fw.sb("khat%d" % i, [128, 128], BF16) for i in range(4)]
        sqb = [fw.sb("sqb%d" % i, [128, 512], BF16) for i in range(2)]
        rsb = [fw.sb("rsb%d" % i, [128, 512], F32) for i in range(2)]
        NPS = 8
        pst = [fw.ps("ps%d" % i, [128, 512], F32) for i in range(NPS)]
        st = {}

        def nxt(name, n):
            i = st.get(name, 0) % n
            st[name] = st.get(name, 0) + 1
            return i

        def newps():
            pi = nxt("ps", NPS)
            return pi, ("ps", pi)

        fw.dma("sp", masks[:], dmask.rearrange("a s t -> s a t"), writes=["masks"])
        fw.dma("sp", ident[:], dident[:, :], writes=["ident"])
        fw.dma("sp", lbl[:], dlb[:, :, :], writes=["lbl"])
        fw.dma("sp", nwt[:], dnw[:, :], writes=["nwt"])
        fw.op("act", lambda e: e.activation(out=identb[:], in_=ident[:], func=AF.Copy), reads=["ident"], writes=["identb"])
        fw.op("dve", lambda e: e.memset(ones_b[:], 1.0), writes=["ones_b"])
        fw.op("dve", lambda e: e.memset(ones_m[:], 1.0), writes=["ones_m"])
        fw.op("dve", lambda e: e.memset(c1[:, 0:1], 1.0), writes=["c1"])
        fw.op("dve", lambda e: e.memset(c1[:, 1:2], EPS), reads=["c1"], writes=["c1"])
        fw.op("act", lambda e: e.activation(out=lbl[:], in_=lbl[:], func=AF.Exp), reads=["lbl"], writes=["lbl"])
        fw.op("dve", lambda e: e.tensor_reduce(out=lbs[:, 0:2], in_=lbl[:], axis=AX.X, op=ALU.add), reads=["lbl"], writes=["lbs"])
        fw.op("dve", lambda e: e.tensor_reduce(out=lbs[:, 2:4], in_=lbl[:, :, 1:layer + 1], axis=AX.X, op=ALU.add), reads=["lbl", "lbs"], writes=["lbs"])
        fw.op("dve", lambda e: e.reciprocal(out=lbs[:, 0:2], in_=lbs[:, 0:2]), reads=["lbs"], writes=["lbs"])
        fw.op("dve", lambda e: e.tensor_tensor(out=lbs[:, 4:6], in0=lbs[:, 2:4], in1=lbs[:, 0:2], op=ALU.mult), reads=["lbs"], writes=["lbs"])
        fw.op("dve", lambda e: e.tensor_scalar(out=lbs[:, 6:8], in0=lbs[:, 4:6], scalar1=-1.0, scalar2=1.0, op0=ALU.mult, op1=ALU.add),
              reads=["lbs"], writes=["lbs"])

        def v3(t):
            return t[:].rearrange("p (c j) -> p c j", j=CH)

        NBK = 4
        BW = TS // NBK
        CPB = BW // CH

        def K(name, k):
            return (name, k)

        def allk(name):
            return [(name, k) for k in range(NBK)]

        def blk_of(col):
            return col // BW

        def setup(hd, b, d):
            QT, KT, KH, DEC = "qt%d" % d, "kt%d" % d, "kh%d" % d, "dec%d" % d
            G3, E3, tmp3 = v3(G), v3(iE), v3(tmp)
            last_col = CH - 1 if d == 0 else 0
            stages = []

            def blockbody(k, emit):
                c0 = k * BW
                cs = slice(c0, c0 + BW)
                ch = slice(k * CPB, (k + 1) * CPB)
                emit(fw.dma, "sp" if k % 2 == 0 else "act", kk[:, cs], dp[hd, 1 + d, :, b, cs], writes=[K("kk", k)])
                emit(fw.op, "act", lambda e: e.activation(out=kk[:, cs], in_=kk[:, cs], func=AF.Sigmoid), reads=[K("kk", k)], writes=[K("kk", k)])
                emit(fw.op, "pool", lambda e: e.tensor_scalar(out=kk[:, cs], in0=kk[:, cs], scalar1=lbs[:, 6 + hd:7 + hd], scalar2=lbs[:, 4 + hd:5 + hd],
                                                        op0=ALU.mult, op1=ALU.add), reads=[K("kk", k), "lbs"], writes=[K("kk", k)])
                emit(fw.op, "act", lambda e: e.activation(out=lf[:, cs], in_=kk[:, cs], func=AF.Ln), reads=[K("kk", k)], writes=[K("lf", k)])
                emit(fw.op, "pool", lambda e: e.tensor_scalar(out=kk[:, cs], in0=kk[:, cs], scalar1=-1.0, scalar2=1.0, op0=ALU.mult, op1=ALU.add),
                      reads=[K("kk", k)], writes=[K("kk", k)])
                init = 0.0 if k == 0 else G[:, c0 - 1:c0]
                emit(fw.op, "dve", lambda e: e.tensor_tensor_scan(out=G[:, cs], data0=ones_b[:, 0:BW], data1=lf[:, cs], initial=init,
                                                            op0=ALU.mult, op1=ALU.add),
                      reads=["ones_b", K("lf", k)] + ([K("G", k - 1)] if k > 0 else []), writes=[K("G", k)])
                if d == 0:
                    if k == 0:
                        emit(fw.op, "dve", lambda e: e.tensor_copy(out=E3[:, 0:1, :], in_=G3[:, 0:1, :]), reads=[K("G", 0)], writes=[K("iE", 0)])
                        emit(fw.op, "dve", lambda e: e.tensor_tensor(out=E3[:, 1:CPB, :], in0=G3[:, 1:CPB, :],
                                                               in1=G3[:, 0:CPB - 1, CH - 1:CH].broadcast_to([128, CPB - 1, CH]), op=ALU.subtract),
                              reads=[K("G", 0), K("iE", 0)], writes=[K("iE", 0)])
                    else:
                        emit(fw.op, "dve", lambda e: e.tensor_tensor(out=E3[:, ch, :], in0=G3[:, ch, :],
                                                               in1=G3[:, k * CPB - 1:(k + 1) * CPB - 1, CH - 1:CH].broadcast_to([128, CPB, CH]), op=ALU.subtract),
                              reads=[K("G", k), K("G", k - 1)], writes=[K("iE", k)])
                else:
                    emit(fw.op, "dve", lambda e: e.tensor_tensor(out=E3[:, ch, :], in0=G3[:, ch, CH - 1:CH].broadcast_to([128, CPB, CH]),
                                                           in1=G3[:, ch, :], op=ALU.subtract), reads=[K("G", k)], writes=[K("iE", k)])
                    emit(fw.op, "pool", lambda e: e.tensor_tensor(out=iE[:, cs], in0=iE[:, cs], in1=lf[:, cs], op=ALU.add),
                          reads=[K("lf", k), K("iE", k)], writes=[K("iE", k)])
                emit(fw.op, "act", lambda e: e.activation(out=dec[d][:, ch], in_=E3[:, ch, last_col], func=AF.Exp), reads=[K("iE", k)], writes=[K(DEC, k)])
                emit(fw.op, "act", lambda e: e.activation(out=tmp[:, cs], in_=iE[:, cs], func=AF.Exp), reads=[K("iE", k)], writes=[K("tmp", k)])
                emit(fw.op, "dve", lambda e: e.tensor_tensor(out=qt[d][:, cs], in0=qs[:, cs], in1=tmp[:, cs], op=ALU.mult),
                      reads=[K("qs", k), K("tmp", k)], writes=[K(QT, k)])
                emit(fw.op, "act", lambda e: e.activation(out=tmp[:, cs], in_=iE[:, cs], func=AF.Exp, scale=-1.0), reads=[K("iE", k)], writes=[K("tmp", k)])
                emit(fw.op, "pool", lambda e: e.tensor_tensor(out=kt[d][:, cs], in0=kk[:, cs], in1=tmp[:, cs], op=ALU.mult),
                      reads=[K("kk", k), K("tmp", k)], writes=[K(KT, k)])
                emit(fw.op, "dve", lambda e: e.tensor_tensor(out=tmp3[:, ch, :], in0=E3[:, ch, last_col:last_col + 1].broadcast_to([128, CPB, CH]),
                                                       in1=E3[:, ch, :], op=ALU.subtract), reads=[K("iE", k)], writes=[K("tmp", k)])
                emit(fw.op, "act", lambda e: e.activation(out=tmp[:, cs], in_=tmp[:, cs], func=AF.Exp), reads=[K("tmp", k)], writes=[K("tmp", k)])
                emit(fw.op, "dve", lambda e: e.tensor_tensor(out=kh[d][:, cs], in0=tmp[:, cs], in1=kk[:, cs], op=ALU.mult),
                      reads=[K("tmp", k), K("kk", k)], writes=[K(KH, k)])


            per_block = []
            for k in range(NBK):
                lst = []
                blockbody(k, lambda f, *a, **kw: lst.append((f, a, kw)))
                per_block.append(lst)
            nst = max(len(l) for l in per_block)
            for j in range(nst):
                for k in range(NBK):
                    if j < len(per_block[k]):
                        f, a, kw = per_block[k][j]
                        f(*a, **kw)

        def pair_pre(d, pt):
            bks = sorted({blk_of(pt * 128), blk_of(pt * 128 + 127)})
            QT = [K("qt%d" % d, x) for x in bks]
            KT = [K("kt%d" % d, x) for x in bks]
            KH = [K("kh%d" % d, x) for x in bks]
            cols = slice(pt * 128, (pt + 1) * 128)
            pi, pk = newps()
            mm_group(fw, pst[pi][:, 0:128], [(kt[d][:, cols], qt[d][:, cols])], reads=KT + QT, writes=[pk])
            a = nxt("Am", 4)
            fw.op("dve", lambda e: e.tensor_tensor(out=Am[a][:], in0=pst[pi][:, 0:128], in1=masks[:, d, :], op=ALU.mult),
                  reads=[pk, "masks"], writes=[("Am", a)])
            pi2, pk2 = newps()
            mm_group(fw, pst[pi2][:, 0:128], [(kh[d][:, cols], identb[:])], reads=KH + ["identb"], writes=[pk2])
            kx = nxt("khat", 4)
            fw.op("act", lambda e: e.activation(out=khat[kx][:], in_=pst[pi2][:, 0:128], func=AF.Copy), reads=[pk2], writes=[("khat", kx)])
            return a, kx

        def half_step(d, pt, hf, a, kx):
            c = pt * 2 + hf
            bk = blk_of(c * CH)
            QT, DEC, SK, SBK = K("qt%d" % d, bk), K("dec%d" % d, bk), ("S", d), ("Sb", d)
            rows = slice(hf * 64, (hf + 1) * 64)
            tcols = slice(c * CH, (c + 1) * CH)
            pi3, pk3 = newps()
            mm_group(fw, pst[pi3][:, 0:CH], [(Vt[rows, pt, :], Am[a][rows, hf * 64:(hf + 1) * 64]), (Sb[d][:], qt[d][:, tcols])],
                     reads=["Vt", ("Am", a), SBK, QT], writes=[pk3])
            fw.op("dve", lambda e: e.tensor_tensor(out=oacc[:, tcols], in0=pst[pi3][:, 0:CH], in1=oacc[:, tcols], op=ALU.add),
                  reads=[pk3, "oacc"], writes=["oacc"])
            pi4, pk4 = newps()
            mm_group(fw, pst[pi4][:, 0:128], [(khat[kx][rows, :], Vt[rows, pt, :])], reads=[("khat", kx), "Vt"], writes=[pk4])
            fw.op("dve", lambda e: e.scalar_tensor_tensor(out=S[d][:], in0=S[d][:], scalar=dec[d][:, c:c + 1], in1=pst[pi4][:, 0:128],
                                                          op0=ALU.mult, op1=ALU.add), reads=[SK, DEC, pk4], writes=[SK])
            fw.op("act", lambda e: e.activation(out=Sb[d][:], in_=S[d][:], func=AF.Copy), reads=[SK], writes=[SBK])

        ptiles = [list(range(SEQ // 128, NPT)) + list(range(SEQ // 128)), list(range(NPT - 1, -1, -1))]
        for hd in range(2):
            for b in range(B):
                for k in range(NBK):
                    cs = slice(k * BW, (k + 1) * BW)
                    fw.dma("sp", qs[:, cs], dp[hd, 0, :, b, cs], writes=[K("qs", k)])
                    fw.dma("act", iE[:, cs], dp[hd, 3, :, b, cs], writes=[K("iE", k)])
                    fw.op("act", lambda e: e.activation(out=tmp[:, cs], in_=qs[:, cs], func=AF.Sigmoid), reads=[K("qs", k)], writes=[K("tmp", k)])
                    fw.op("pool", lambda e: e.tensor_tensor(out=qs[:, cs], in0=qs[:, cs], in1=tmp[:, cs], op=ALU.mult),
                          reads=[K("qs", k), K("tmp", k)], writes=[K("qs", k)])
                    fw.op("act", lambda e: e.activation(out=kh[1][:, cs], in_=iE[:, cs], func=AF.Copy), reads=[K("iE", k)], writes=[K("kh1", k)])
                for pt in range(NPT):
                    pi, pk = newps()
                    mm_group(fw, pst[pi][:, 0:128], [(kh[1][:, pt * 128:(pt + 1) * 128], identb[:])],
                             reads=[K("kh1", blk_of(pt * 128)), K("kh1", blk_of(pt * 128 + 127)), "identb"], writes=[pk])
                    if pt % 2 == 0:
                        fw.op("act", lambda e, pi=pi, pt=pt: e.activation(out=Vt[:, pt, :], in_=pst[pi][:, 0:128], func=AF.Copy), reads=[pk], writes=["Vt"])
                    else:
                        fw.op("dve", lambda e, pi=pi, pt=pt: e.tensor_copy(out=Vt[:, pt, :], in_=pst[pi][:, 0:128]), reads=[pk], writes=["Vt"])
                fw.op("pool", lambda e: e.memset(oacc[:], 0.0), reads=["oacc"], writes=["oacc"])
                for d in range(2):
                    setup(hd, b, d)
                    fw.op("dve", lambda e, d=d: e.memset(S[d][:], 0.0), reads=[("S", d)], writes=[("S", d)])
                    fw.op("dve", lambda e, d=d: e.memset(Sb[d][:], 0.0), reads=[("Sb", d)], writes=[("Sb", d)])
                for step in range(NPT):
                    pre = [pair_pre(d, ptiles[d][step]) for d in range(2)]
                    for hi in range(2):
                        for d in range(2):
                            hf = (0, 1)[hi] if d == 0 else (1, 0)[hi]
                            half_step(d, ptiles[d][step], hf, *pre[d])
                fw.dma("sp", kk[:], dp[hd, 4, :, b, :], writes=allk("kk"))
                fw.op("act", lambda e: e.activation(out=lf[:], in_=kk[:], func=AF.Sigmoid), reads=allk("kk"), writes=allk("lf"))
                fw.op("pool", lambda e: e.tensor_tensor(out=kk[:], in0=kk[:], in1=lf[:], op=ALU.mult), reads=allk("kk") + allk("lf"), writes=allk("kk"))
                for (t0, tw) in TL:
                    j = nxt("sqb", 2)
                    fw.op("act", lambda e, j=j, t0=t0, tw=tw: e.activation(out=sqb[j][:, 0:tw], in_=oacc[:, t0:t0 + tw], func=AF.Square),
                          reads=["oacc"], writes=[("sqb", j)])
                    pi, pk = newps()
                    mm_group(fw, pst[pi][:, 0:tw], [(ones_m[:], sqb[j][:, 0:tw])], reads=["ones_m", ("sqb", j)], writes=[pk])
                    r = nxt("rsb", 2)
                    fw.op("act", lambda e, pi=pi, r=r, tw=tw: e.activation(out=rsb[r][:, 0:tw], in_=pst[pi][:, 0:tw], func=AF.Sqrt, bias=c1[:, 1:2], scale=1.0 / 128),
                          reads=[pk, "c1"], writes=[("rsb", r)])
                    fw.op("dve", lambda e, r=r, tw=tw: e.reciprocal(out=rsb[r][:, 0:tw], in_=rsb[r][:, 0:tw]), reads=[("rsb", r)], writes=[("rsb", r)])
                    fw.op("dve", lambda e, r=r, t0=t0, tw=tw: e.scalar_tensor_tensor(
                        out=G[:, t0:t0 + tw], in0=oacc[:, t0:t0 + tw], scalar=nwt[:, 0:1], in1=rsb[r][:, 0:tw], op0=ALU.mult, op1=ALU.mult),
                        reads=["oacc", "nwt", ("rsb", r)] + allk("G"), writes=allk("G"))
                fw.op("pool", lambda e: e.tensor_tensor(out=G[:], in0=G[:], in1=kk[:], op=ALU.mult), reads=allk("G") + allk("kk"), writes=allk("G"))
                fw.dma("sp", dy[hd, :, b, :], G[:], reads=allk("G"))
        fw.finish()
    return nc


def hg_consts():
    s = np.arange(128)[:, None]
    t = np.arange(128)[None, :]
    same = (s // CH) == (t // CH)
    mf = (same & (t >= s)).astype(np.float32)
    mb = (same & (t <= s)).astype(np.float32)
    return np.stack([mf, mb], 0), np.eye(128, dtype=np.float32)


def hg_host_inputs(pT, o, P):
    masks, ident = hg_consts()
    maps = []
    for k in range(NCORES):
        p5 = np.empty((2, 5, 128, B, TS), np.float32)
        lbl = np.empty((128, 2, DEPTH), np.float32)
        for j in range(2):
            hh = 2 * k + j
            for s in range(5):
                p5[j, s] = pT[s * D + hh * 128:s * D + (hh + 1) * 128]
            lbl[:, j, :] = P["hg_lb_logits"][:, hh * 128:(hh + 1) * 128].T
        maps.append({"p5": p5, "lbl": lbl, "nw": np.ascontiguousarray(P["hg_norm_w"][o].reshape(128, 1)),
                     "masks": masks, "ident": ident})
    return maps


def hg_host_gather(results):
    y = np.empty((D, B, TS), np.float32)
    for k in range(NCORES):
        o = results[k]["yT"]
        for j in range(2):
            hh = 2 * k + j
            y[hh * 128:(hh + 1) * 128] = o[j]
    return y


def kernel(**inp):
    P = {k: np.asarray(v, dtype=np.float32) for k, v in inp.items()}
    x, ctx = P["x"], P["ctx"]
    m = ada_host_gather(run(get_nc("ada", build_ada), ada_host_inputs(P["c"], P["c_ctx"], P["ada_w"], P["ada_b"])))
    xs = to_core_T(x, ctx)

    def tok_maps(l_post, l_pre, xs, ys):
        maps = []
        nmlp = vec_pk(P["norm_mlp_w"][l_post]) if l_post is not None else np.zeros((128, KC), np.float32)
        npre = vec_pk(P["norm_mix_w"][l_pre]) if l_pre < DEPTH else vec_pk(P["final_norm_w"])
        nrm = np.ascontiguousarray(np.stack([nmlp, npre], 1))
        for k in range(NCORES):
            b, r = core_tok(k)
            d = {"xT": xs[k], "adaL": ada_sets(m, l_post, l_pre, b), "adaC": ada_sets(m, l_post, l_pre, 2), "nrm": nrm}
            if l_post is not None:
                d["yT"] = ys[k]
                d["w_out"] = P["ab_w_out"][l_post // 2] if l_post % 2 == 0 else P["hg_w_out"][l_post // 2]
                d["w1"] = P["mlp_w1"][l_post]
                d["w2"] = P["mlp_w2"][l_post]
            if l_pre < DEPTH:
                d["w_in"] = P["ab_w_in"][l_pre // 2] if l_pre % 2 == 0 else P["hg_w_in"][l_pre // 2]
            maps.append(d)
        return maps

    res = run(get_nc("tok", build_tok, False, "ab"), tok_maps(None, 0, xs, None))
    for l in range(DEPTH):
        pT = tokT_to_full([r["pT"] for r in res])
        xs = [r["xoT"] for r in res]
        if l % 2 == 0:
            y = ab_host_gather(run(get_nc("ab", build_ab), ab_host_inputs(pT, l // 2, P)))
        else:
            y = hg_host_gather(run(get_nc("hg", build_hg, l), hg_host_inputs(pT, l // 2, P)))
        ys = full_to_tokT(y)
        pre = "final" if l + 1 == DEPTH else ("ab" if (l + 1) % 2 == 0 else "hg")
        res = run(get_nc("tok", build_tok, True, pre), tok_maps(l, l + 1, xs, ys))
    lat, _ = from_core_T([r["outT"] for r in res])
    return np.ascontiguousarray(lat.astype(np.float32))
```

```python
import numpy as np
from contextlib import ExitStack
import concourse.bass as bass
import concourse.mybir as mybir
from concourse.bass_utils import run_bass_kernel_spmd

F32 = mybir.dt.float32
BF16 = mybir.dt.bfloat16
AF = mybir.ActivationFunctionType
ALU = mybir.AluOpType
AX = mybir.AxisListType

D = 2048
KC = D // 128
B = 2
SEQ = 4096
CTX = 256
DEPTH = 4
DFF = 8192
NCORES = 8
NL = B * SEQ // NCORES
NCX = B * CTX // NCORES
NT = NL + NCX
EPS = 1e-6
D_AB_IN = 3136
D_HG_IN = 10240
TS = SEQ + CTX

SAME_ENG_SYNC = True


class Rec:
    def __init__(self):
        self.calls = []

    def __getattr__(self, name):
        def f(*a, **k):
            self.calls.append((name, a, k))
            return self
        return f


class FW:
    ENGS = ("pe", "dve", "act", "pool", "sp")

    def __init__(self, nc, es, n_dma_sems=48):
        self.nc = nc
        self.es = es
        self.eng = {"pe": nc.tensor, "dve": nc.vector, "act": nc.scalar, "pool": nc.gpsimd, "sp": nc.sync}
        self.sem = {e: es.enter_context(nc.semaphore("s_" + e)) for e in self.ENGS}
        self.cnt = {e: 0 for e in self.ENGS}
        self.prog = {e: [] for e in self.ENGS}
        self.seen = {e: {} for e in self.ENGS}
        self.dsem = [es.enter_context(nc.semaphore("d%d" % i)) for i in range(n_dma_sems)]
        self.dval = [0] * n_dma_sems
        self.drr = 0
        self.last_w = {}
        self.readers = {}
        self.semobj = {}
        self.n_ps = 0

    def sb(self, name, shape, dt):
        return self.es.enter_context(self.nc.sbuf_tensor(name, list(shape), dt))

    def ps(self, name, shape, dt=F32):
        return self.es.enter_context(self.nc.psum_tensor(name, list(shape), dt))

    def _collect(self, eng, reads, writes, nosame=False):
        toks = []
        for k in reads:
            t = self.last_w.get(k)
            if t is not None:
                toks.append(t)
        for k in writes:
            t = self.last_w.get(k)
            if t is not None:
                toks.append(t)
            toks.extend(self.readers.get(k, ()))
        need = {}
        for (sid, val) in toks:
            if ((not SAME_ENG_SYNC) or nosame) and sid == ("e", eng):
                continue
            if self.seen[eng].get(sid, 0) >= val:
                continue
            if need.get(sid, 0) < val:
                need[sid] = val
        for sid, val in need.items():
            self.seen[eng][sid] = val
        return list(need.items())

    def _commit(self, tok, reads, writes):
        for k in writes:
            self.last_w[k] = tok
            self.readers[k] = []
        for k in reads:
            if k in writes:
                continue
            self.readers.setdefault(k, []).append(tok)

    def _semof(self, sid):
        kind, x = sid
        return self.sem[x] if kind == "e" else self.dsem[x]

    def op(self, eng, fn, reads=(), writes=(), nosame=False):
        waits = self._collect(eng, reads, writes, nosame)
        self.cnt[eng] += 1
        tok = (("e", eng), self.cnt[eng])
        rec = Rec()
        fn(rec)
        calls = rec.calls

        def run(e, calls=calls):
            ins = None
            for (name, a, k) in calls:
                ins = getattr(e, name)(*a, **k)
            return ins
        self.prog[eng].append((waits, run, ("e", eng), 1))
        self._commit(tok, reads, writes)
        return tok

    def dma(self, eng, out, in_, reads=(), writes=(), **kw):
        waits = self._collect(eng, reads, writes)
        i = self.drr
        self.drr = (self.drr + 1) % len(self.dsem)
        sid = ("d", i)
        if self.dval[i] > 0 and self.seen[eng].get(sid, 0) < self.dval[i]:
            waits.append((sid, self.dval[i]))
            self.seen[eng][sid] = self.dval[i]
        self.dval[i] += 16
        tok = (sid, self.dval[i])
        self.prog[eng].append((waits, (lambda e, out=out, in_=in_, kw=kw: e.dma_start(out=out, in_=in_, **kw)), sid, 16))
        self._commit(tok, reads, writes)
        return tok

    def barrier(self):
        allw = [(("e", e), self.cnt[e]) for e in self.ENGS if self.cnt[e] > 0]
        allw += [(("d", i), v) for i, v in enumerate(self.dval) if v > 0]
        for e in self.ENGS:
            waits = []
            for (sid, val) in allw:
                if sid == ("e", e):
                    continue
                if self.seen[e].get(sid, 0) < val:
                    waits.append((sid, val))
                    self.seen[e][sid] = val
            if waits:
                self.prog[e].append((waits, None, None, 0))

    def finish(self, eng="sp"):
        waits = []
        for i, v in enumerate(self.dval):
            if v > 0:
                waits.append((("d", i), v))
        self.prog[eng].append((waits, None, None, 0))
        block = self.es.enter_context(self.nc.Block())

        def replay(name):
            def body(e):
                for (waits, fn, sid, inc) in self.prog[name]:
                    for (wsid, val) in waits:
                        e.wait_ge(self._semof(wsid), val)
                    if fn is not None:
                        ins = fn(e)
                        ins.then_inc(self._semof(sid), inc)
            return body

        block.tensor(replay("pe"))
        block.vector(replay("dve"))
        block.scalar(replay("act"))
        block.gpsimd(replay("pool"))
        block.sync(replay("sp"))


def mm_group(fw, out_ps, pairs, reads, writes):
    n = len(pairs)

    def fn(e):
        ins = None
        for i, (l, r) in enumerate(pairs):
            ins = e.matmul(out_ps, l, r, start=(i == 0), stop=(i == n - 1))
        return ins
    return fw.op("pe", fn, reads=reads, writes=writes)


ADA_COLS = 6 * D // NCORES
ADA_J = ADA_COLS // 128


def build_ada():
    nc = bass.Bass("TRN2", target_bir_lowering=False)
    cv = nc.dram_tensor("cv", [128, KC, 3], F32, kind="ExternalInput").ap()
    w = nc.dram_tensor("w", [DEPTH, D, ADA_COLS], F32, kind="ExternalInput").ap()
    bb = nc.dram_tensor("b", [128, DEPTH, ADA_J], F32, kind="ExternalInput").ap()
    out = nc.dram_tensor("out", [128, DEPTH, ADA_J, 3], F32, kind="ExternalOutput").ap()
    with ExitStack() as es:
        fw = FW(nc, es)
        cvt = fw.sb("cvt", [128, KC, 3], F32)
        sg = fw.sb("sg", [128, KC, 3], F32)
        sv = fw.sb("sv", [128, KC, 3], F32)
        bt = fw.sb("bt", [128, DEPTH, ADA_J], F32)
        ot = fw.sb("ot", [128, DEPTH, ADA_J, 3], F32)
        NB = 3
        wt = [fw.sb("wt%d" % i, [128, KC, 512], F32) for i in range(NB)]
        pst = [fw.ps("ps%d" % i, [128, 512], F32) for i in range(4)]
        fw.dma("sp", cvt[:], cv[:, :, :], writes=["cvt"])
        fw.dma("sp", bt[:], bb[:, :, :], writes=["bt"])
        fw.op("act", lambda e: e.activation(out=sg[:], in_=cvt[:], func=AF.Sigmoid), reads=["cvt"], writes=["sg"])
        fw.op("dve", lambda e: e.tensor_tensor(out=sv[:], in0=cvt[:], in1=sg[:], op=ALU.mult), reads=["cvt", "sg"], writes=["sv"])
        it = 0
        for l in range(DEPTH):
            for g in range(ADA_COLS // 512):
                wb = wt[it % NB]
                wk = "wt%d" % (it % NB)
                src = w[l].rearrange("(kc p) n -> p kc n", p=128)[:, :, g * 512:(g + 1) * 512]
                q = "sp" if it % 2 == 0 else "act"
                fw.dma(q, wb[:], src, writes=[wk])
                for jj in range(4):
                    j = g * 4 + jj
                    pi = (it * 4 + jj) % 4
                    pk = "ps%d" % pi
                    mm_group(fw, pst[pi][:, 0:3],
                             [(wb[:, kc, jj * 128:(jj + 1) * 128], sv[:, kc, :]) for kc in range(KC)],
                             reads=[wk, "sv"], writes=[pk])
                    fw.op("dve", lambda e, pi=pi, l=l, j=j: e.tensor_scalar(
                        out=ot[:, l, j, :], in0=pst[pi][:, 0:3], scalar1=bt[:, l, j:j + 1], scalar2=None, op0=ALU.add),
                        reads=[pk, "bt"], writes=["ot"])
                it += 1
        fw.dma("sp", out[:, :, :, :], ot[:], reads=["ot"])
        fw.finish()
    return nc


TT = [(0, 512), (512, 512), (1024, 64)]


def build_tok(post, pre):
    nin = {"ab": D_AB_IN, "hg": D_HG_IN, "final": 0}[pre]
    nc = bass.Bass("TRN2", target_bir_lowering=False)
    xin = nc.dram_tensor("xT", [D, NT], F32, kind="ExternalInput").ap()
    adaL = nc.dram_tensor("adaL", [128, 2, 6, KC], F32, kind="ExternalInput").ap()
    adaC = nc.dram_tensor("adaC", [128, 2, 6, KC], F32, kind="ExternalInput").ap()
    nrm = nc.dram_tensor("nrm", [128, 2, KC], F32, kind="ExternalInput").ap()
    if post:
        yin = nc.dram_tensor("yT", [D, NT], F32, kind="ExternalInput").ap()
        w_out = nc.dram_tensor("w_out", [D, D], F32, kind="ExternalInput").ap()
        w1 = nc.dram_tensor("w1", [D, DFF], F32, kind="ExternalInput").ap()
        w2 = nc.dram_tensor("w2", [DFF, D], F32, kind="ExternalInput").ap()
    if nin:
        w_in = nc.dram_tensor("w_in", [D, nin], F32, kind="ExternalInput").ap()
        pout = nc.dram_tensor("pT", [nin, NT], F32, kind="ExternalOutput").ap()
        xout = nc.dram_tensor("xoT", [D, NT], F32, kind="ExternalOutput").ap()
    else:
        fout = nc.dram_tensor("outT", [D, NT], F32, kind="ExternalOutput").ap()

    with ExitStack() as es:
        fw = FW(nc, es)
        x = fw.sb("x", [128, KC, NT], F32)
        hT = fw.sb("hT", [128, KC, NT], BF16)
        h1 = fw.sb("h1", [128, KC, NT], BF16)
        NWB = 2
        wb = [fw.sb("wb%d" % i, [128, KC, 512], BF16) for i in range(NWB)]
        aL = fw.sb("aL", [128, 2, 6, KC], F32)
        aC = fw.sb("aC", [128, 2, 6, KC], F32)
        nw = fw.sb("nw", [128, 2, KC], F32)
        AL = fw.sb("AL", [128, 2, KC], F32)
        AC = fw.sb("AC", [128, 2, KC], F32)
        ones = fw.sb("ones", [128, 128], BF16)
        rstd = fw.sb("rstd", [128, NT], F32)
        tmpn = [fw.sb("tmpn%d" % i, [128, NT], F32) for i in range(2)]
        stg = [fw.sb("stg%d" % i, [128, NT], F32) for i in range(3)]
        rl = [fw.sb("rl%d" % i, [128, 512], F32) for i in range(2)]
        NPS = 8
        pst = [fw.ps("ps%d" % i, [128, 512], F32) for i in range(NPS)]
        st = {"ps": 0, "wb": 0, "tmpn": 0, "stg": 0, "rl": 0}

        def nxt(name, n):
            i = st[name] % n
            st[name] += 1
            return i

        xk = lambda kc: ("x", kc)
        hk = lambda kc: ("h", kc)
        h1k = lambda kc: ("h1", kc)

        xv = xin.rearrange("(kc p) n -> p kc n", p=128)
        for kc in range(KC):
            fw.dma("sp", x[:, kc, :], xv[:, kc, :], writes=[xk(kc)])
        fw.dma("act", aL[:], adaL[:, :, :, :], writes=["aL"])
        fw.dma("act", aC[:], adaC[:, :, :, :], writes=["aC"])
        fw.dma("act", nw[:], nrm[:, :, :], writes=["nw"])
        fw.op("dve", lambda e: e.memset(ones[:], 1.0), writes=["ones"])
        for (s, idx) in ((0, 4), (1, 1)):
            fw.op("dve", lambda e, s=s, idx=idx: e.scalar_tensor_tensor(
                out=AL[:, s, :], in0=aL[:, s, idx, :], scalar=1.0, in1=nw[:, s, :], op0=ALU.add, op1=ALU.mult),
                reads=["aL", "nw"], writes=["AL"])
            fw.op("dve", lambda e, s=s, idx=idx: e.scalar_tensor_tensor(
                out=AC[:, s, :], in0=aC[:, s, idx, :], scalar=1.0, in1=nw[:, s, :], op0=ALU.add, op1=ALU.mult),
                reads=["aC", "nw"], writes=["AC"])

        def load_w(src_ap, ncols):
            i = nxt("wb", NWB)
            fw.dma("pool", wb[i][:, :, 0:ncols], src_ap.rearrange("(kc p) n -> p kc n", p=128), writes=[("wb", i)])
            return i

        def proj(src, srckey, wsrc, ncols_total, evac):
            c0 = 0
            while c0 < ncols_total:
                gw = min(512, ncols_total - c0)
                wi = load_w(wsrc[:, c0:c0 + gw], gw)
                cc = 0
                while cc < gw:
                    cw = min(128, gw - cc)
                    c = (c0 + cc) // 128
                    for tt, (t0, tw) in enumerate(TT):
                        pi = nxt("ps", NPS)
                        pk = ("ps", pi)
                        mm_group(fw, pst[pi][0:cw, 0:tw],
                                 [(wb[wi][:, kc, cc:cc + cw], src[:, kc, t0:t0 + tw]) for kc in range(KC)],
                                 reads=[("wb", wi)] + [srckey(kc) for kc in range(KC)], writes=[pk])
                        evac(c, cw, tt, pst[pi][0:cw, 0:tw], pk)
                    cc += cw
                c0 += gw

        def norm_mod(s, shift_idx, out_dt_tile, outkey, final=False):
            for kc in range(KC):
                fw.op("act", lambda e, kc=kc: e.activation(out=h1[:, kc, :], in_=x[:, kc, :], func=AF.Square),
                      reads=[xk(kc)], writes=[h1k(kc)])
            for tt, (t0, tw) in enumerate(TT):
                pi = nxt("ps", NPS)
                pk = ("ps", pi)
                mm_group(fw, pst[pi][:, 0:tw], [(ones[:], h1[:, kc, t0:t0 + tw]) for kc in range(KC)],
                         reads=["ones"] + [h1k(kc) for kc in range(KC)], writes=[pk])
                fw.op("act", lambda e, pi=pi, t0=t0, tw=tw: e.activation(
                    out=rstd[:, t0:t0 + tw], in_=pst[pi][:, 0:tw], func=AF.Sqrt, bias=EPS_AP[:, 0:1], scale=1.0 / D),
                    reads=[pk, "eps"], writes=["rstd"])
            fw.op("dve", lambda e: e.reciprocal(out=rstd[:], in_=rstd[:]), reads=["rstd"], writes=["rstd"])
            for kc in range(KC):
                ti = nxt("tmpn", 2)
                tk = ("tmpn", ti)
                fw.op("dve", lambda e, kc=kc, ti=ti: e.tensor_tensor(out=tmpn[ti][:], in0=x[:, kc, :], in1=rstd[:], op=ALU.mult),
                      reads=[xk(kc), "rstd"], writes=[tk])
                if final:
                    si = nxt("stg", 3)
                    sk = ("stg", si)
                    fw.op("act", lambda e, kc=kc, ti=ti, si=si: e.activation(
                        out=stg[si][:], in_=tmpn[ti][:], func=AF.Identity, scale=nw[:, 1, kc:kc + 1]),
                        reads=[tk, "nw"], writes=[sk])
                    fw.dma("sp", fout[kc * 128:(kc + 1) * 128, :], stg[si][:], reads=[sk])
                else:
                    fw.op("act", lambda e, kc=kc, ti=ti: e.activation(
                        out=hT[:, kc, 0:NL], in_=tmpn[ti][:, 0:NL], func=AF.Identity,
                        bias=aL[:, s, shift_idx, kc:kc + 1], scale=AL[:, s, kc:kc + 1]),
                        reads=[tk, "aL", "AL"], writes=[hk(kc)])
                    fw.op("act", lambda e, kc=kc, ti=ti: e.activation(
                        out=hT[:, kc, NL:NT], in_=tmpn[ti][:, NL:NT], func=AF.Identity,
                        bias=aC[:, s, shift_idx, kc:kc + 1], scale=AC[:, s, kc:kc + 1]),
                        reads=[tk, "aC", "AC", hk(kc)], writes=[hk(kc)])

        EPS_AP = fw.sb("epsc", [128, 1], F32)
        fw.op("dve", lambda e: e.memset(EPS_AP[:], EPS), writes=["eps"])

        def resid_evac(gidx):
            def evac(c, cw, tt, ps_ap, pk):
                t0, tw = TT[tt]
                a = aL if tt < 2 else aC
                fw.op("dve", lambda e: e.scalar_tensor_tensor(
                    out=x[:, c, t0:t0 + tw], in0=ps_ap, scalar=a[:, 0, gidx, c:c + 1], in1=x[:, c, t0:t0 + tw],
                    op0=ALU.mult, op1=ALU.add),
                    reads=[pk, "aL", "aC", xk(c)], writes=[xk(c)])
            return evac

        if post:
            yv = yin.rearrange("(kc p) n -> p kc n", p=128)
            for kc in range(KC):
                fw.dma("pool", hT[:, kc, :], yv[:, kc, :], writes=[hk(kc)])
            proj(hT, hk, w_out, D, resid_evac(2))
            norm_mod(0, 3, None, None)
            for q in range(4):
                def evac1(c, cw, tt, ps_ap, pk):
                    t0, tw = TT[tt]
                    ri = nxt("rl", 2)
                    rk = ("rl", ri)
                    fw.op("act", lambda e: e.activation(out=rl[ri][:, 0:tw], in_=ps_ap, func=AF.Relu),
                          reads=[pk], writes=[rk])
                    fw.op("dve", lambda e: e.tensor_tensor(out=h1[:, c, t0:t0 + tw], in0=rl[ri][:, 0:tw], in1=rl[ri][:, 0:tw], op=ALU.mult),
                          reads=[rk], writes=[h1k(c)])
                proj(hT, hk, w1[:, q * 2048:(q + 1) * 2048], 2048, evac1)
                proj(h1, h1k, w2[q * 2048:(q + 1) * 2048, :], D, resid_evac(5))

        if nin:
            xov = xout.rearrange("(kc p) n -> p kc n", p=128)
            for kc in range(KC):
                fw.dma("sp", xov[:, kc, :], x[:, kc, :], reads=[xk(kc)])
            norm_mod(1, 0, None, None)
            cur = {}

            def evac_p(c, cw, tt, ps_ap, pk):
                t0, tw = TT[tt]
                if tt == 0:
                    cur["si"] = nxt("stg", 3)
                si = cur["si"]
                sk = ("stg", si)
                eng = "act" if (c + tt) % 2 == 0 else "dve"
                if eng == "act":
                    fw.op("act", lambda e: e.activation(out=stg[si][0:cw, t0:t0 + tw], in_=ps_ap, func=AF.Copy),
                          reads=[pk], writes=[sk])
                else:
                    fw.op("dve", lambda e: e.tensor_copy(out=stg[si][0:cw, t0:t0 + tw], in_=ps_ap),
                          reads=[pk], writes=[sk])
                if tt == len(TT) - 1:
                    fw.dma("sp", pout[c * 128:c * 128 + cw, :], stg[si][0:cw, :], reads=[sk])
            proj(hT, hk, w_in, nin, evac_p)
        else:
            norm_mod(1, 0, None, None, final=True)
        fw.finish()
    return nc


def core_tok(k):
    return k // 4, k % 4


def to_core_T(lat, ctx):
    outs = []
    for k in range(NCORES):
        b, r = core_tok(k)
        a = np.concatenate([lat[b, r * NL:(r + 1) * NL], ctx[b, r * NCX:(r + 1) * NCX]], axis=0)
        outs.append(np.ascontiguousarray(a.T))
    return outs


def from_core_T(arrs):
    F = arrs[0].shape[0]
    lat = np.empty((B, SEQ, F), arrs[0].dtype)
    ctx = np.empty((B, CTX, F), arrs[0].dtype)
    for k in range(NCORES):
        b, r = core_tok(k)
        lat[b, r * NL:(r + 1) * NL] = arrs[k][:, :NL].T
        ctx[b, r * NCX:(r + 1) * NCX] = arrs[k][:, NL:].T
    return lat, ctx


def vec_pk(v):
    return np.ascontiguousarray(v.reshape(KC, 128).T)


def ada_sets(m, lpost, lpre, v):
    out = np.zeros((128, 2, 6, KC), np.float32)
    for s, l in enumerate((lpost, lpre)):
        if l is None or l < 0 or l >= DEPTH:
            continue
        out[:, s] = m[l, v].reshape(6, KC, 128).transpose(2, 0, 1)
    return out


def ada_host_inputs(c, c_ctx, ada_w, ada_b):
    cvec = np.stack([c[0], c[1], c_ctx], 0)
    cv = np.ascontiguousarray(cvec.reshape(3, KC, 128).transpose(2, 1, 0))
    maps = []
    for k in range(NCORES):
        w = np.ascontiguousarray(ada_w[:, :, k * ADA_COLS:(k + 1) * ADA_COLS])
        b = np.ascontiguousarray(ada_b[:, k * ADA_COLS:(k + 1) * ADA_COLS].reshape(DEPTH, ADA_J, 128).transpose(2, 0, 1))
        maps.append({"cv": cv, "w": w, "b": b})
    return maps


def ada_host_gather(results):
    m = np.zeros((DEPTH, 3, 6 * D), np.float32)
    for k in range(NCORES):
        o = results[k]["out"]
        m[:, :, k * ADA_COLS:(k + 1) * ADA_COLS] = o.transpose(1, 3, 2, 0).reshape(DEPTH, 3, ADA_COLS)
    return m


_NC_CACHE = {}


def get_nc(name, builder, *args):
    key = (name,) + tuple(args)
    if key not in _NC_CACHE:
        _NC_CACHE[key] = builder(*args)
    return _NC_CACHE[key]


def run(nc, maps):
    res = run_bass_kernel_spmd(nc, maps, core_ids=list(range(NCORES)))
    return res.results


ATTN_SCALE = 192.0 ** -0.5
TL = [(i * 512, 512) for i in range(SEQ // 512)] + [(SEQ, CTX)]
NKT = TS // 128
GELU_C = 1.5957691216057308


def build_ab():
    nc = bass.Bass("TRN2", target_bir_lowering=False)
    dgu = nc.dram_tensor("guT", [2, 128, B, TS], F32, kind="ExternalInput").ap()
    dcq = nc.dram_tensor("cqT", [512, B, TS], F32, kind="ExternalInput").ap()
    dckv = nc.dram_tensor("ckvT", [512, B, TS], F32, kind="ExternalInput").ap()
    dkr = nc.dram_tensor("krT", [64, B, TS], F32, kind="ExternalInput").ap()
    dvec = nc.dram_tensor("vecs", [128, 24], F32, kind="ExternalInput").ap()
    dwa = nc.dram_tensor("wax", [2, 2, 128, 128], F32, kind="ExternalInput").ap()
    dwuq = nc.dram_tensor("wuq", [512, 192], F32, kind="ExternalInput").ap()
    dwukv = nc.dram_tensor("wukv", [512, 256], F32, kind="ExternalInput").ap()
    dcs = nc.dram_tensor("cs", [2, 64, SEQ], F32, kind="ExternalInput").ap()
    drm = nc.dram_tensor("rm", [64, 64], F32, kind="ExternalInput").ap()
    dy = nc.dram_tensor("yT", [256, B, TS], F32, kind="ExternalOutput").ap()

    with ExitStack() as es:
        fw = FW(nc, es)
        vec = fw.sb("vecsb", [128, 24], F32)
        c1 = fw.sb("c1", [128, 4], F32)
        nsp = fw.sb("nsp", [128, 2], F32)
        wax = fw.sb("waxb", [128, 4, 128], BF16)
        wuq = fw.sb("wuqb", [128, 4, 192], BF16)
        wukv = fw.sb("wukvb", [128, 4, 256], BF16)
        ones = fw.sb("ones", [128, 128], BF16)
        rm = fw.sb("rmsb", [64, 64], F32)
        NPS = 8
        pst = [fw.ps("ps%d" % i, [128, 512], F32) for i in range(NPS)]
        st = {}

        def nxt(name, n):
            i = st.get(name, 0) % n
            st[name] = st.get(name, 0) + 1
            return i

        fw.dma("sp", vec[:], dvec[:, :], writes=["vec"])
        fw.dma("sp", rm[:], drm[:, :], writes=["rm"])
        fw.dma("pool", wax[:], dwa.rearrange("a d i o -> i (a d) o"), writes=["wax"])
        fw.dma("pool", wuq[:], dwuq.rearrange("(kc p) n -> p kc n", p=128), writes=["wuq"])
        fw.dma("pool", wukv[:], dwukv.rearrange("(kc p) n -> p kc n", p=128), writes=["wukv"])
        fw.op("dve", lambda e: e.memset(ones[:], 1.0), writes=["ones"])
        fw.op("dve", lambda e: e.memset(c1[:, 0:1], 1.0), writes=["c1"])
        fw.op("dve", lambda e: e.memset(c1[:, 1:2], EPS), reads=["c1"], writes=["c1"])
        fw.op("act", lambda e: e.activation(out=nsp[:], in_=vec[:, 9:11], func=AF.Exp, scale=-1.0), reads=["vec"], writes=["nsp"])
        fw.op("act", lambda e: e.activation(out=nsp[:], in_=nsp[:], func=AF.Ln, bias=c1[:, 0:1], scale=1.0), reads=["nsp", "c1"], writes=["nsp"])
        fw.op("dve", lambda e: e.tensor_scalar(out=nsp[:], in0=nsp[:], scalar1=-8.0, scalar2=None, op0=ALU.mult), reads=["nsp"], writes=["nsp"])

        with ExitStack() as es1:
            names = ["tu", "tg", "uc", "t1", "t2", "t3", "hacc", "tgel"]
            T = {n: es1.enter_context(nc.sbuf_tensor("l_" + n, [128, TS], F32)) for n in names}
            ucb = es1.enter_context(nc.sbuf_tensor("l_ucb", [128, TS], BF16))
            BL = [(0, 1024), (1024, 2048), (2048, 3072), (3072, 4096), (SEQ, TS)]
            NBL = len(BL)

            def K(name, k):
                return (name, k)

            def stage_major(bodyfn):
                per_block = []
                for k in range(NBL):
                    lst = []
                    bodyfn(k, lambda f, *a, **kw: lst.append((f, a, kw)))
                    per_block.append(lst)
                for j in range(max(len(l) for l in per_block)):
                    for k in range(NBL):
                        if j < len(per_block[k]):
                            f, a_, kw = per_block[k][j]
                            f(*a_, **kw)

            for b in range(B):
                tu, tg, uc, t1, t2, t3, hacc, tgel = (T[n] for n in names)

                def conv_body(k, emit):
                    c0, cE = BL[k]
                    s0, s1 = (0, SEQ) if k < 4 else (SEQ, TS)
                    cs = slice(c0, cE)
                    emit(fw.dma, "sp", tg[:, cs], dgu[0, :, b, cs], writes=[K("tg", k)])
                    emit(fw.dma, "act", tu[:, cs], dgu[1, :, b, cs], writes=[K("tu", k)])
                    emit(fw.op, "pool", lambda e: e.tensor_tensor(out=tgel[:, cs], in0=tg[:, cs], in1=tg[:, cs], op=ALU.mult),
                         reads=[K("tg", k)], writes=[K("tgel", k)])
                    emit(fw.op, "pool", lambda e: e.tensor_scalar(out=tgel[:, cs], in0=tgel[:, cs], scalar1=0.044715, scalar2=1.0, op0=ALU.mult, op1=ALU.add),
                         reads=[K("tgel", k)], writes=[K("tgel", k)])
                    emit(fw.op, "pool", lambda e: e.tensor_tensor(out=tgel[:, cs], in0=tgel[:, cs], in1=tg[:, cs], op=ALU.mult),
                         reads=[K("tg", k), K("tgel", k)], writes=[K("tgel", k)])
                    emit(fw.op, "act", lambda e: e.activation(out=tgel[:, cs], in_=tgel[:, cs], func=AF.Sigmoid, scale=GELU_C),
                         reads=[K("tgel", k)], writes=[K("tgel", k)])
                    emit(fw.op, "pool", lambda e: e.tensor_tensor(out=tgel[:, cs], in0=tgel[:, cs], in1=tg[:, cs], op=ALU.mult),
                         reads=[K("tg", k), K("tgel", k)], writes=[K("tgel", k)])

                def conv_body2(k, emit):
                    c0, cE = BL[k]
                    s0, s1 = (0, SEQ) if k < 4 else (SEQ, TS)
                    cs = slice(c0, cE)
                    left = [K("tu", k - 1)] if c0 > s0 else []
                    right = [K("tu", k + 1)] if cE < s1 else []
                    emit(fw.op, "dve", lambda e: e.tensor_scalar(out=uc[:, cs], in0=tu[:, cs], scalar1=vec[:, 2:3], scalar2=vec[:, 4:5], op0=ALU.mult, op1=ALU.add),
                         reads=[K("tu", k), "vec"], writes=[K("uc", k)])
                    for (tap, sh) in ((0, 2), (1, 1)):
                        a0 = max(c0, s0 + sh)
                        emit(fw.op, "dve", lambda e, a0=a0, tap=tap, sh=sh: e.scalar_tensor_tensor(
                            out=uc[:, a0:cE], in0=tu[:, a0 - sh:cE - sh], scalar=vec[:, tap:tap + 1], in1=uc[:, a0:cE], op0=ALU.mult, op1=ALU.add),
                            reads=[K("tu", k), "vec", K("uc", k)] + left, writes=[K("uc", k)])
                    b1 = min(cE, s1 - 1)
                    emit(fw.op, "dve", lambda e: e.scalar_tensor_tensor(
                        out=uc[:, c0:b1], in0=tu[:, c0 + 1:b1 + 1], scalar=vec[:, 3:4], in1=uc[:, c0:b1], op0=ALU.mult, op1=ALU.add),
                        reads=[K("tu", k), "vec", K("uc", k)] + right, writes=[K("uc", k)])
                    emit(fw.op, "act", lambda e: e.activation(out=ucb[:, cs], in_=uc[:, cs], func=AF.Copy), reads=[K("uc", k)], writes=[K("ucb", k)])

                stage_major(conv_body)
                stage_major(conv_body2)
                for d in range(2):
                    def gate_body(k, emit):
                        c0, cE = BL[k]
                        cs = slice(c0, cE)
                        tiles = [(t0, min(512, cE - t0)) for t0 in range(c0, cE, 512)]
                        for (t0, tw) in tiles:
                            for (ax, dst, dn, bcol) in ((0, t1, "t1", 5 + d), (1, t2, "t2", 7 + d)):
                                def one(t0=t0, tw=tw, ax=ax, dst=dst, dn=dn, bcol=bcol):
                                    pi = nxt("ps", NPS)
                                    pk = ("ps", pi)
                                    mm_group(fw, pst[pi][:, 0:tw], [(wax[:, ax * 2 + d, :], ucb[:, t0:t0 + tw])],
                                             reads=["wax", K("ucb", k)], writes=[pk])
                                    fw.op("act", lambda e: e.activation(out=dst[:, t0:t0 + tw], in_=pst[pi][:, 0:tw], func=AF.Sigmoid,
                                                                        bias=vec[:, bcol:bcol + 1], scale=1.0),
                                          reads=[pk, "vec"], writes=[K(dn, k)])
                                emit(one)
                        emit(fw.op, "act", lambda e: e.activation(out=t1[:, cs], in_=t1[:, cs], func=AF.Exp, scale=nsp[:, d:d + 1]),
                             reads=[K("t1", k), "nsp"], writes=[K("t1", k)])
                        emit(fw.op, "pool", lambda e: e.tensor_tensor(out=t2[:, cs], in0=t2[:, cs], in1=uc[:, cs], op=ALU.mult),
                             reads=[K("t2", k), K("uc", k)], writes=[K("t2", k)])
                        emit(fw.op, "dve", lambda e: e.tensor_tensor(out=t3[:, cs], in0=t1[:, cs], in1=t1[:, cs], op=ALU.mult),
                             reads=[K("t1", k)], writes=[K("t3", k)])
                        emit(fw.op, "act", lambda e: e.activation(out=t3[:, cs], in_=t3[:, cs], func=AF.Sqrt, bias=c1[:, 0:1], scale=-1.0),
                             reads=[K("t3", k), "cE"], writes=[K("t3", k)])
                        emit(fw.op, "dve", lambda e: e.tensor_tensor(out=t2[:, cs], in0=t2[:, cs], in1=t3[:, cs], op=ALU.mult),
                             reads=[K("t2", k), K("t3", k)], writes=[K("t2", k)])
                    stage_major(gate_body)
                    dst = hacc if d == 0 else t3
                    dn = "hacc" if d == 0 else "t3"
                    order = [4, 0, 1, 2, 3] if d == 0 else [4, 3, 2, 1, 0]
                    prev = None
                    for k in order:
                        c0, cE = BL[k]
                        if d == 0:
                            init = 0.0 if prev is None else dst[:, BL[prev][1] - 1:BL[prev][1]]
                            o_ap, a_ap, b_ap = dst[:, c0:cE], t1[:, c0:cE], t2[:, c0:cE]
                        else:
                            init = 0.0 if prev is None else dst[:, BL[prev][0]:BL[prev][0] + 1]
                            o_ap, a_ap, b_ap = dst[:, c0:cE][:, ::-1], t1[:, c0:cE][:, ::-1], t2[:, c0:cE][:, ::-1]
                        fw.op("dve", lambda e, o_ap=o_ap, a_ap=a_ap, b_ap=b_ap, init=init: e.tensor_tensor_scan(
                            out=o_ap, data0=a_ap, data1=b_ap, initial=init, op0=ALU.mult, op1=ALU.add),
                            reads=[K("t1", k), K("t2", k)] + ([K(dn, prev)] if prev is not None else []), writes=[K(dn, k)])
                        if d == 1:
                            fw.op("pool", lambda e, c0=c0, cE=cE: e.tensor_tensor(out=hacc[:, c0:cE], in0=hacc[:, c0:cE], in1=t3[:, c0:cE], op=ALU.add),
                                  reads=[K("hacc", k), K("t3", k)], writes=[K("hacc", k)])
                        prev = k

                def out_body(k, emit):
                    c0, cE = BL[k]
                    cs = slice(c0, cE)
                    emit(fw.op, "dve", lambda e: e.tensor_tensor(out=t1[:, cs], in0=tgel[:, cs], in1=hacc[:, cs], op=ALU.mult),
                         reads=[K("hacc", k), K("tgel", k), K("t1", k)], writes=[K("t1", k)])
                    emit(fw.dma, "sp", dy[0:128, b, cs], t1[:, cs], reads=[K("t1", k)])
                stage_major(out_body)
            fw.barrier()

        with ExitStack() as es2:
            def sb2(name, shape, dt):
                return es2.enter_context(nc.sbuf_tensor("m_" + name, list(shape), dt))
            cs = sb2("cssb", [64, 2, SEQ], F32)
            qTn = sb2("qTn", [128, TS], BF16)
            qTr = sb2("qTr", [64, TS], BF16)
            kTn = sb2("kTn", [128, TS], BF16)
            kTr = sb2("kTr", [64, TS], BF16)
            V = sb2("V", [128, NKT, 128], BF16)
            cqt = [sb2("cqt%d" % i, [128, 4, 512], F32) for i in range(2)]
            ckt = [sb2("ckt%d" % i, [128, 4, 512], F32) for i in range(2)]
            krt = [sb2("krt%d" % i, [64, 512], F32) for i in range(2)]
            sq = [sb2("sq%d" % i, [128, 4, 512], BF16) for i in range(2)]
            cqn = [sb2("cqn%d" % i, [128, 4, 512], BF16) for i in range(2)]
            ckn = [sb2("ckn%d" % i, [128, 4, 512], BF16) for i in range(2)]
            rs = [sb2("rs%d" % i, [128, 512], F32) for i in range(2)]
            qrs = [sb2("qrs%d" % i, [64, 512], F32) for i in range(2)]
            ra = [sb2("ra%d" % i, [64, 512], F32) for i in range(2)]
            rb = [sb2("rb%d" % i, [64, 512], F32) for i in range(2)]
            PT = [sb2("PT%d" % i, [128, 512], BF16) for i in range(6)]
            accD = [sb2("accD%d" % i, [128, 512], F32) for i in range(2)]
            accP = [sb2("accP%d" % i, [128, 512], F32) for i in range(2)]
            ones_f = sb2("ones_f", [128, 128], F32)
            fw.op("dve", lambda e: e.memset(ones_f[:], 1.0), writes=["ones_f"])
            rden = [sb2("rden%d" % i, [128, 512], F32) for i in range(2)]
            ob = [sb2("ob%d" % i, [128, 512], F32) for i in range(2)]
            fw.dma("sp", cs[:], dcs.rearrange("a p t -> p a t"), writes=["cs"])

            def rope(src_ap, srck, t0, tw, dst_ap, dstk):
                pi = nxt("ps", NPS)
                pk = ("ps", pi)
                mm_group(fw, pst[pi][0:64, 0:tw], [(rm[:], src_ap)], reads=["rm", srck], writes=[pk])
                i = nxt("ra", 2)
                fw.op("dve", lambda e: e.tensor_tensor(out=ra[i][:, 0:tw], in0=src_ap, in1=cs[:, 0, t0:t0 + tw], op=ALU.mult),
                      reads=[srck, "cs"], writes=[("ra", i)])
                fw.op("dve", lambda e: e.tensor_tensor(out=rb[i][:, 0:tw], in0=pst[pi][0:64, 0:tw], in1=cs[:, 1, t0:t0 + tw], op=ALU.mult),
                      reads=[pk, "cs"], writes=[("rb", i)])
                fw.op("dve", lambda e: e.tensor_tensor(out=dst_ap, in0=ra[i][:, 0:tw], in1=rb[i][:, 0:tw], op=ALU.add),
                      reads=[("ra", i), ("rb", i)], writes=[dstk])

            for b in range(B):
                for (t0, tw) in TL:
                    is_ctx = t0 >= SEQ
                    i = nxt("tile", 2)
                    fw.dma("sp", cqt[i][:, :, 0:tw], dcq.rearrange("(kc p) b n -> p kc b n", p=128)[:, :, b, t0:t0 + tw], writes=[("cqt", i)])
                    fw.dma("act", ckt[i][:, :, 0:tw], dckv.rearrange("(kc p) b n -> p kc b n", p=128)[:, :, b, t0:t0 + tw], writes=[("ckt", i)])
                    fw.dma("sp", krt[i][:, 0:tw], dkr[:, b, t0:t0 + tw], writes=[("krt", i)])
                    for (src, srck, ncol, dst, dstk) in ((cqt[i], ("cqt", i), 11, cqn[i], ("cqn", i)), (ckt[i], ("ckt", i), 15, ckn[i], ("ckn", i))):
                        j = nxt("sq", 2)
                        fw.op("act", lambda e, src=src, j=j: e.activation(out=sq[j][:, :, 0:tw], in_=src[:, :, 0:tw], func=AF.Square),
                              reads=[srck], writes=[("sq", j)])
                        pi = nxt("ps", NPS)
                        pk = ("ps", pi)
                        mm_group(fw, pst[pi][:, 0:tw], [(ones[:], sq[j][:, kc, 0:tw]) for kc in range(4)],
                                 reads=["ones", ("sq", j)], writes=[pk])
                        r = nxt("rs", 2)
                        fw.op("act", lambda e, pi=pi, r=r: e.activation(out=rs[r][:, 0:tw], in_=pst[pi][:, 0:tw], func=AF.Ln,
                                                                      bias=c1[:, 1:2], scale=1.0 / 512), reads=[pk, "c1"], writes=[("rs", r)])
                        fw.op("act", lambda e, r=r: e.activation(out=rs[r][:, 0:tw], in_=rs[r][:, 0:tw], func=AF.Exp, scale=-0.5),
                              reads=[("rs", r)], writes=[("rs", r)])
                        for kc in range(4):
                            fw.op("dve", lambda e, src=src, dst=dst, kc=kc, r=r, ncol=ncol: e.scalar_tensor_tensor(
                                out=dst[:, kc, 0:tw], in0=src[:, kc, 0:tw], scalar=vec[:, ncol + kc:ncol + kc + 1], in1=rs[r][:, 0:tw],
                                op0=ALU.mult, op1=ALU.mult), reads=[srck, "vec", ("rs", r)], writes=[dstk])
                    pi = nxt("ps", NPS); pk = ("ps", pi)
                    mm_group(fw, pst[pi][:, 0:tw], [(wuq[:, kc, 0:128], cqn[i][:, kc, 0:tw]) for kc in range(4)],
                             reads=["wuq", ("cqn", i)], writes=[pk])
                    fw.op("act", lambda e, pi=pi: e.activation(out=qTn[:, t0:t0 + tw], in_=pst[pi][:, 0:tw], func=AF.Copy),
                          reads=[pk], writes=["qTn"])
                    pi = nxt("ps", NPS); pk = ("ps", pi)
                    mm_group(fw, pst[pi][0:64, 0:tw], [(wuq[:, kc, 128:192], cqn[i][:, kc, 0:tw]) for kc in range(4)],
                             reads=["wuq", ("cqn", i)], writes=[pk])
                    if is_ctx:
                        fw.op("act", lambda e, pi=pi: e.activation(out=qTr[:, t0:t0 + tw], in_=pst[pi][0:64, 0:tw], func=AF.Copy),
                              reads=[pk], writes=["qTr"])
                    else:
                        q = nxt("qrs", 2)
                        fw.op("act", lambda e, pi=pi, q=q: e.activation(out=qrs[q][:, 0:tw], in_=pst[pi][0:64, 0:tw], func=AF.Copy),
                              reads=[pk], writes=[("qrs", q)])
                        rope(qrs[q][:, 0:tw], ("qrs", q), t0, tw, qTr[:, t0:t0 + tw], "qTr")
                    pi = nxt("ps", NPS); pk = ("ps", pi)
                    mm_group(fw, pst[pi][:, 0:tw], [(wukv[:, kc, 0:128], ckn[i][:, kc, 0:tw]) for kc in range(4)],
                             reads=["wukv", ("ckn", i)], writes=[pk])
                    fw.op("act", lambda e, pi=pi: e.activation(out=kTn[:, t0:t0 + tw], in_=pst[pi][:, 0:tw], func=AF.Copy),
                          reads=[pk], writes=["kTn"])
                    for sub in range(tw // 128):
                        pi = nxt("ps", NPS); pk = ("ps", pi)
                        mm_group(fw, pst[pi][:, 0:128], [(ckn[i][:, kc, sub * 128:(sub + 1) * 128], wukv[:, kc, 128:256]) for kc in range(4)],
                                 reads=["wukv", ("ckn", i)], writes=[pk])
                        fw.op("dve", lambda e, pi=pi, sub=sub: e.tensor_copy(out=V[:, t0 // 128 + sub, :], in_=pst[pi][:, 0:128]),
                              reads=[pk], writes=["V"])
                    if is_ctx:
                        fw.op("act", lambda e, i=i: e.activation(out=kTr[:, t0:t0 + tw], in_=krt[i][:, 0:tw], func=AF.Copy),
                              reads=[("krt", i)], writes=["kTr"])
                    else:
                        rope(krt[i][:, 0:tw], ("krt", i), t0, tw, kTr[:, t0:t0 + tw], "kTr")
                SKEW = 2
                for (t0, tw) in TL:
                    is_ctx = t0 >= SEQ
                    kts = list(range(SEQ // 128, NKT)) if is_ctx else list(range(NKT))
                    oi = 4 + nxt("po", 2)
                    di = 6 + nxt("pd", 2)
                    ok, dk = ("ps", oi), ("ps", di)
                    ai = nxt("acc", 2)
                    aD, aP = accD[ai], accP[ai]
                    aDk, aPk = ("accD", ai), ("accP", ai)
                    pend = []

                    def consume(item):
                        n, kt, p = item
                        first, last = (n == 0), (n == len(kts) - 1)
                        fw.op("pe", lambda e: e.matmul(pst[oi][:, 0:tw], V[:, kt, :], PT[p][:, 0:tw], start=first, stop=last),
                              reads=["V", ("PT", p)], writes=[ok], nosame=not first)
                        if n % 2 == 0:
                            if n == 0:
                                fw.op("dve", lambda e: e.tensor_copy(out=aD[:, 0:tw], in_=PT[p][:, 0:tw]), reads=[("PT", p)], writes=[aDk])
                            else:
                                fw.op("dve", lambda e: e.tensor_tensor(out=aD[:, 0:tw], in0=aD[:, 0:tw], in1=PT[p][:, 0:tw], op=ALU.add),
                                      reads=[("PT", p), aDk], writes=[aDk])
                        else:
                            if n == 1:
                                fw.op("pool", lambda e: e.tensor_copy(out=aP[:, 0:tw], in_=PT[p][:, 0:tw]), reads=[("PT", p)], writes=[aPk])
                            else:
                                fw.op("pool", lambda e: e.tensor_tensor(out=aP[:, 0:tw], in0=aP[:, 0:tw], in1=PT[p][:, 0:tw], op=ALU.add),
                                      reads=[("PT", p), aPk], writes=[aPk])

                    for n, kt in enumerate(kts):
                        si = nxt("psS", 4)
                        sk = ("ps", si)
                        mm_group(fw, pst[si][:, 0:tw],
                                 [(kTn[:, kt * 128:(kt + 1) * 128], qTn[:, t0:t0 + tw]), (kTr[:, kt * 128:(kt + 1) * 128], qTr[:, t0:t0 + tw])],
                                 reads=["kTn", "qTn", "kTr", "qTr"], writes=[sk])
                        p = nxt("PT", 6)
                        fw.op("act", lambda e, si=si, p=p: e.activation(out=PT[p][:, 0:tw], in_=pst[si][:, 0:tw], func=AF.Exp, scale=ATTN_SCALE),
                              reads=[sk], writes=[("PT", p)])
                        pend.append((n, kt, p))
                        if len(pend) > SKEW:
                            consume(pend.pop(0))
                    while pend:
                        consume(pend.pop(0))
                    fw.op("dve", lambda e: e.tensor_tensor(out=aD[:, 0:tw], in0=aD[:, 0:tw], in1=aP[:, 0:tw], op=ALU.add),
                          reads=[aDk, aPk], writes=[aDk])
                    mm_group(fw, pst[di][:, 0:tw], [(ones_f[:], aD[:, 0:tw])], reads=["ones_f", aDk], writes=[dk])
                    r = nxt("rden", 2)
                    fw.op("dve", lambda e, r=r: e.reciprocal(out=rden[r][:, 0:tw], in_=pst[di][:, 0:tw]), reads=[dk], writes=[("rden", r)])
                    o = nxt("ob", 2)
                    fw.op("dve", lambda e, r=r, o=o: e.tensor_tensor(out=ob[o][:, 0:tw], in0=pst[oi][:, 0:tw], in1=rden[r][:, 0:tw], op=ALU.mult),
                          reads=[ok, ("rden", r)], writes=[("ob", o)])
                    fw.dma("sp", dy[128:256, b, t0:t0 + tw], ob[o][:, 0:tw], reads=[("ob", o)])
        fw.finish()
    return nc


def rope_consts():
    half = 32
    inv_freq = (10000.0 ** (-np.arange(0, half, 2, dtype=np.float32) / half)).astype(np.float32)
    t = np.arange(SEQ)
    row = (t // 64).astype(np.float32)
    col = (t % 64).astype(np.float32)
    ang_r = row[:, None] * inv_freq
    ang_c = col[:, None] * inv_freq
    ang = np.concatenate([ang_r, ang_r, ang_c, ang_c], axis=-1).astype(np.float32)
    cos = np.cos(ang).astype(np.float32)
    sin = np.sin(ang).astype(np.float32)
    sign = np.ones(64, np.float32)
    sign[0:16] = -1.0
    sign[32:48] = -1.0
    perm = np.concatenate([np.arange(16, 32), np.arange(0, 16), np.arange(48, 64), np.arange(32, 48)])
    rm = np.zeros((64, 64), np.float32)
    rm[perm, np.arange(64)] = 1.0
    cs = np.stack([cos.T, (sin * sign).T], 0)
    return np.ascontiguousarray(cs), rm


def ab_host_inputs(pT, e, P):
    cs, rm = rope_consts()
    maps = []
    cq = np.ascontiguousarray(pT[2048:2560])
    ckv = np.ascontiguousarray(pT[2560:3072])
    kr = np.ascontiguousarray(pT[3072:3136])
    for h in range(NCORES):
        sl = slice(h * 128, (h + 1) * 128)
        gu = np.stack([pT[h * 128:(h + 1) * 128], pT[1024 + h * 128:1024 + (h + 1) * 128]], 0)
        vec = np.zeros((128, 24), np.float32)
        vec[:, 0:4] = P["lru_conv_w"][e][:, sl].T
        vec[:, 4] = P["lru_conv_b"][e][sl]
        vec[:, 5:7] = P["lru_b_a"][e][:, sl].T
        vec[:, 7:9] = P["lru_b_x"][e][:, sl].T
        vec[:, 9:11] = P["lru_lambda"][e][:, sl].T
        vec[:, 11:15] = P["mla_q_norm_w"][e].reshape(4, 128).T
        vec[:, 15:19] = P["mla_kv_norm_w"][e].reshape(4, 128).T
        wax = np.stack([P["lru_w_a"][e][:, h], P["lru_w_x"][e][:, h]], 0)
        maps.append({"guT": np.ascontiguousarray(gu), "cqT": cq, "ckvT": ckv, "krT": kr, "vecs": vec,
                     "wax": np.ascontiguousarray(wax),
                     "wuq": np.ascontiguousarray(P["mla_w_uq"][e][:, h * 192:(h + 1) * 192]),
                     "wukv": np.ascontiguousarray(P["mla_w_ukv"][e][:, h * 256:(h + 1) * 256]),
                     "cs": cs, "rm": rm})
    return maps


def ab_host_gather(results):
    y = np.empty((D, B, TS), np.float32)
    for h in range(NCORES):
        o = results[h]["yT"]
        y[h * 128:(h + 1) * 128] = o[0:128]
        y[1024 + h * 128:1024 + (h + 1) * 128] = o[128:256]
    return y


def tokT_to_full(arrs):
    F = arrs[0].shape[0]
    full = np.empty((F, B, TS), arrs[0].dtype)
    for k in range(NCORES):
        b, r = core_tok(k)
        full[:, b, r * NL:(r + 1) * NL] = arrs[k][:, :NL]
        full[:, b, SEQ + r * NCX:SEQ + (r + 1) * NCX] = arrs[k][:, NL:]
    return full


def full_to_tokT(full):
    outs = []
    for k in range(NCORES):
        b, r = core_tok(k)
        outs.append(np.ascontiguousarray(np.concatenate(
            [full[:, b, r * NL:(r + 1) * NL], full[:, b, SEQ + r * NCX:SEQ + (r + 1) * NCX]], axis=1)))
    return outs


CH = 64
NCH = TS // CH
NPT = TS // 128


def build_hg(layer):
    nc = bass.Bass("TRN2", target_bir_lowering=False)
    dp = nc.dram_tensor("p5", [2, 5, 128, B, TS], F32, kind="ExternalInput").ap()
    dlb = nc.dram_tensor("lbl", [128, 2, DEPTH], F32, kind="ExternalInput").ap()
    dnw = nc.dram_tensor("nw", [128, 1], F32, kind="ExternalInput").ap()
    dmask = nc.dram_tensor("masks", [2, 128, 128], F32, kind="ExternalInput").ap()
    dident = nc.dram_tensor("ident", [128, 128], F32, kind="ExternalInput").ap()
    dy = nc.dram_tensor("yT", [2, 128, B, TS], F32, kind="ExternalOutput").ap()

    with ExitStack() as es:
        fw = FW(nc, es)
        names = ["qs", "iE", "oacc", "kk", "lf", "G", "tmp"]
        T = {n: fw.sb("h_" + n, [128, TS], F32) for n in names}
        qs, iE, oacc, kk, lf, G, tmp = (T[n] for n in names)
        qt = [fw.sb("qt%d" % d, [128, TS], BF16) for d in range(2)]
        kt = [fw.sb("kt%d" % d, [128, TS], BF16) for d in range(2)]
        kh = [fw.sb("kh%d" % d, [128, TS], BF16) for d in range(2)]
        Vt = fw.sb("Vt", [128, NPT, 128], BF16)
        ones_b = fw.sb("ones_b", [128, 1088], BF16)
        ones_m = fw.sb("ones_m", [128, 128], BF16)
        masks = fw.sb("masks_sb", [128, 2, 128], F32)
        ident = fw.sb("ident_sb", [128, 128], F32)
        identb = fw.sb("identb_sb", [128, 128], BF16)
        lbl = fw.sb("lbl_sb", [128, 2, DEPTH], F32)
        lbs = fw.sb("lbs", [128, 8], F32)
        nwt = fw.sb("nwt", [128, 1], F32)
        c1 = fw.sb("c1h", [128, 2], F32)
        dec = [fw.sb("dec%d" % d, [128, NCH], F32) for d in range(2)]
        S = [fw.sb("S%d" % d, [128, 128], F32) for d in range(2)]
        Sb = [fw.sb("Sb%d" % d, [128, 128], BF16) for d in range(2)]
        Am = [fw.sb("Am%d" % i, [128, 128], BF16) for i in range(4)]
        khat = [fw.sb("khat%d" % i, [128, 128], BF16) for i in range(4)]
        sqb = [fw.sb("sqb%d" % i, [128, 512], BF16) for i in range(2)]
        rsb = [fw.sb("rsb%d" % i, [128, 512], F32) for i in range(2)]
        NPS = 8
        pst = [fw.ps("ps%d" % i, [128, 512], F32) for i in range(NPS)]
        st = {}

        def nxt(name, n):
            i = st.get(name, 0) % n
            st[name] = st.get(name, 0) + 1
            return i

        def newps():
            pi = nxt("ps", NPS)
            return pi, ("ps", pi)

        fw.dma("sp", masks[:], dmask.rearrange("a s t -> s a t"), writes=["masks"])
        fw.dma("sp", ident[:], dident[:, :], writes=["ident"])
        fw.dma("sp", lbl[:], dlb[:, :, :], writes=["lbl"])
        fw.dma("sp", nwt[:], dnw[:, :], writes=["nwt"])
        fw.op("act", lambda e: e.activation(out=identb[:], in_=ident[:], func=AF.Copy), reads=["ident"], writes=["identb"])
        fw.op("dve", lambda e: e.memset(ones_b[:], 1.0), writes=["ones_b"])
        fw.op("dve", lambda e: e.memset(ones_m[:], 1.0), writes=["ones_m"])
        fw.op("dve", lambda e: e.memset(c1[:, 0:1], 1.0), writes=["c1"])
        fw.op("dve", lambda e: e.memset(c1[:, 1:2], EPS), reads=["c1"], writes=["c1"])
        fw.op("act", lambda e: e.activation(out=lbl[:], in_=lbl[:], func=AF.Exp), reads=["lbl"], writes=["lbl"])
        fw.op("dve", lambda e: e.tensor_reduce(out=lbs[:, 0:2], in_=lbl[:], axis=AX.X, op=ALU.add), reads=["lbl"], writes=["lbs"])
        fw.op("dve", lambda e: e.tensor_reduce(out=lbs[:, 2:4], in_=lbl[:, :, 1:layer + 1], axis=AX.X, op=ALU.add), reads=["lbl", "lbs"], writes=["lbs"])
        fw.op("dve", lambda e: e.reciprocal(out=lbs[:, 0:2], in_=lbs[:, 0:2]), reads=["lbs"], writes=["lbs"])
        fw.op("dve", lambda e: e.tensor_tensor(out=lbs[:, 4:6], in0=lbs[:, 2:4], in1=lbs[:, 0:2], op=ALU.mult), reads=["lbs"], writes=["lbs"])
        fw.op("dve", lambda e: e.tensor_scalar(out=lbs[:, 6:8], in0=lbs[:, 4:6], scalar1=-1.0, scalar2=1.0, op0=ALU.mult, op1=ALU.add),
              reads=["lbs"], writes=["lbs"])

        def v3(t):
            return t[:].rearrange("p (c j) -> p c j", j=CH)

        NBK = 4
        BW = TS // NBK
        CPB = BW // CH

        def K(name, k):
            return (name, k)

        def allk(name):
            return [(name, k) for k in range(NBK)]

        def blk_of(col):
            return col // BW

        def setup(hd, b, d):
            QT, KT, KH, DEC = "qt%d" % d, "kt%d" % d, "kh%d" % d, "dec%d" % d
            G3, E3, tmp3 = v3(G), v3(iE), v3(tmp)
            last_col = CH - 1 if d == 0 else 0
            stages = []

            def blockbody(k, emit):
                c0 = k * BW
                cs = slice(c0, c0 + BW)
                ch = slice(k * CPB, (k + 1) * CPB)
                emit(fw.dma, "sp" if k % 2 == 0 else "act", kk[:, cs], dp[hd, 1 + d, :, b, cs], writes=[K("kk", k)])
                emit(fw.op, "act", lambda e: e.activation(out=kk[:, cs], in_=kk[:, cs], func=AF.Sigmoid), reads=[K("kk", k)], writes=[K("kk", k)])
                emit(fw.op, "pool", lambda e: e.tensor_scalar(out=kk[:, cs], in0=kk[:, cs], scalar1=lbs[:, 6 + hd:7 + hd], scalar2=lbs[:, 4 + hd:5 + hd],
                                                        op0=ALU.mult, op1=ALU.add), reads=[K("kk", k), "lbs"], writes=[K("kk", k)])
                emit(fw.op, "act", lambda e: e.activation(out=lf[:, cs], in_=kk[:, cs], func=AF.Ln), reads=[K("kk", k)], writes=[K("lf", k)])
                emit(fw.op, "pool", lambda e: e.tensor_scalar(out=kk[:, cs], in0=kk[:, cs], scalar1=-1.0, scalar2=1.0, op0=ALU.mult, op1=ALU.add),
                      reads=[K("kk", k)], writes=[K("kk", k)])
                init = 0.0 if k == 0 else G[:, c0 - 1:c0]
                emit(fw.op, "dve", lambda e: e.tensor_tensor_scan(out=G[:, cs], data0=ones_b[:, 0:BW], data1=lf[:, cs], initial=init,
                                                            op0=ALU.mult, op1=ALU.add),
                      reads=["ones_b", K("lf", k)] + ([K("G", k - 1)] if k > 0 else []), writes=[K("G", k)])
                if d == 0:
                    if k == 0:
                        emit(fw.op, "dve", lambda e: e.tensor_copy(out=E3[:, 0:1, :], in_=G3[:, 0:1, :]), reads=[K("G", 0)], writes=[K("iE", 0)])
                        emit(fw.op, "dve", lambda e: e.tensor_tensor(out=E3[:, 1:CPB, :], in0=G3[:, 1:CPB, :],
                                                               in1=G3[:, 0:CPB - 1, CH - 1:CH].broadcast_to([128, CPB - 1, CH]), op=ALU.subtract),
                              reads=[K("G", 0), K("iE", 0)], writes=[K("iE", 0)])
                    else:
                        emit(fw.op, "dve", lambda e: e.tensor_tensor(out=E3[:, ch, :], in0=G3[:, ch, :],
                                                               in1=G3[:, k * CPB - 1:(k + 1) * CPB - 1, CH - 1:CH].broadcast_to([128, CPB, CH]), op=ALU.subtract),
                              reads=[K("G", k), K("G", k - 1)], writes=[K("iE", k)])
                else:
                    emit(fw.op, "dve", lambda e: e.tensor_tensor(out=E3[:, ch, :], in0=G3[:, ch, CH - 1:CH].broadcast_to([128, CPB, CH]),
                                                           in1=G3[:, ch, :], op=ALU.subtract), reads=[K("G", k)], writes=[K("iE", k)])
                    emit(fw.op, "pool", lambda e: e.tensor_tensor(out=iE[:, cs], in0=iE[:, cs], in1=lf[:, cs], op=ALU.add),
                          reads=[K("lf", k), K("iE", k)], writes=[K("iE", k)])
                emit(fw.op, "act", lambda e: e.activation(out=dec[d][:, ch], in_=E3[:, ch, last_col], func=AF.Exp), reads=[K("iE", k)], writes=[K(DEC, k)])
                emit(fw.op, "act", lambda e: e.activation(out=tmp[:, cs], in_=iE[:, cs], func=AF.Exp), reads=[K("iE", k)], writes=[K("tmp", k)])
                emit(fw.op, "dve", lambda e: e.tensor_tensor(out=qt[d][:, cs], in0=qs[:, cs], in1=tmp[:, cs], op=ALU.mult),
                      reads=[K("qs", k), K("tmp", k)], writes=[K(QT, k)])
                emit(fw.op, "act", lambda e: e.activation(out=tmp[:, cs], in_=iE[:, cs], func=AF.Exp, scale=-1.0), reads=[K("iE", k)], writes=[K("tmp", k)])
                emit(fw.op, "pool", lambda e: e.tensor_tensor(out=kt[d][:, cs], in0=kk[:, cs], in1=tmp[:, cs], op=ALU.mult),
                      reads=[K("kk", k), K("tmp", k)], writes=[K(KT, k)])
                emit(fw.op, "dve", lambda e: e.tensor_tensor(out=tmp3[:, ch, :], in0=E3[:, ch, last_col:last_col + 1].broadcast_to([128, CPB, CH]),
                                                       in1=E3[:, ch, :], op=ALU.subtract), reads=[K("iE", k)], writes=[K("tmp", k)])
                emit(fw.op, "act", lambda e: e.activation(out=tmp[:, cs], in_=tmp[:, cs], func=AF.Exp), reads=[K("tmp", k)], writes=[K("tmp", k)])
                emit(fw.op, "dve", lambda e: e.tensor_tensor(out=kh[d][:, cs], in0=tmp[:, cs], in1=kk[:, cs], op=ALU.mult),
                      reads=[K("tmp", k), K("kk", k)], writes=[K(KH, k)])


            per_block = []
            for k in range(NBK):
                lst = []
                blockbody(k, lambda f, *a, **kw: lst.append((f, a, kw)))
                per_block.append(lst)
            nst = max(len(l) for l in per_block)
            for j in range(nst):
                for k in range(NBK):
                    if j < len(per_block[k]):
                        f, a, kw = per_block[k][j]
                        f(*a, **kw)

        def pair_pre(d, pt):
            bks = sorted({blk_of(pt * 128), blk_of(pt * 128 + 127)})
            QT = [K("qt%d" % d, x) for x in bks]
            KT = [K("kt%d" % d, x) for x in bks]
            KH = [K("kh%d" % d, x) for x in bks]
            cols = slice(pt * 128, (pt + 1) * 128)
            pi, pk = newps()
            mm_group(fw, pst[pi][:, 0:128], [(kt[d][:, cols], qt[d][:, cols])], reads=KT + QT, writes=[pk])
            a = nxt("Am", 4)
            fw.op("dve", lambda e: e.tensor_tensor(out=Am[a][:], in0=pst[pi][:, 0:128], in1=masks[:, d, :], op=ALU.mult),
                  reads=[pk, "masks"], writes=[("Am", a)])
            pi2, pk2 = newps()
            mm_group(fw, pst[pi2][:, 0:128], [(kh[d][:, cols], identb[:])], reads=KH + ["identb"], writes=[pk2])
            kx = nxt("khat", 4)
            fw.op("act", lambda e: e.activation(out=khat[kx][:], in_=pst[pi2][:, 0:128], func=AF.Copy), reads=[pk2], writes=[("khat", kx)])
            return a, kx

        def half_step(d, pt, hf, a, kx):
            c = pt * 2 + hf
            bk = blk_of(c * CH)
            QT, DEC, SK, SBK = K("qt%d" % d, bk), K("dec%d" % d, bk), ("S", d), ("Sb", d)
            rows = slice(hf * 64, (hf + 1) * 64)
            tcols = slice(c * CH, (c + 1) * CH)
            pi3, pk3 = newps()
            mm_group(fw, pst[pi3][:, 0:CH], [(Vt[rows, pt, :], Am[a][rows, hf * 64:(hf + 1) * 64]), (Sb[d][:], qt[d][:, tcols])],
                     reads=["Vt", ("Am", a), SBK, QT], writes=[pk3])
            fw.op("dve", lambda e: e.tensor_tensor(out=oacc[:, tcols], in0=pst[pi3][:, 0:CH], in1=oacc[:, tcols], op=ALU.add),
                  reads=[pk3, "oacc"], writes=["oacc"])
            pi4, pk4 = newps()
            mm_group(fw, pst[pi4][:, 0:128], [(khat[kx][rows, :], Vt[rows, pt, :])], reads=[("khat", kx), "Vt"], writes=[pk4])
            fw.op("dve", lambda e: e.scalar_tensor_tensor(out=S[d][:], in0=S[d][:], scalar=dec[d][:, c:c + 1], in1=pst[pi4][:, 0:128],
                                                          op0=ALU.mult, op1=ALU.add), reads=[SK, DEC, pk4], writes=[SK])
            fw.op("act", lambda e: e.activation(out=Sb[d][:], in_=S[d][:], func=AF.Copy), reads=[SK], writes=[SBK])

        ptiles = [list(range(SEQ // 128, NPT)) + list(range(SEQ // 128)), list(range(NPT - 1, -1, -1))]
        for hd in range(2):
            for b in range(B):
                for k in range(NBK):
                    cs = slice(k * BW, (k + 1) * BW)
                    fw.dma("sp", qs[:, cs], dp[hd, 0, :, b, cs], writes=[K("qs", k)])
                    fw.dma("act", iE[:, cs], dp[hd, 3, :, b, cs], writes=[K("iE", k)])
                    fw.op("act", lambda e: e.activation(out=tmp[:, cs], in_=qs[:, cs], func=AF.Sigmoid), reads=[K("qs", k)], writes=[K("tmp", k)])
                    fw.op("pool", lambda e: e.tensor_tensor(out=qs[:, cs], in0=qs[:, cs], in1=tmp[:, cs], op=ALU.mult),
                          reads=[K("qs", k), K("tmp", k)], writes=[K("qs", k)])
                    fw.op("act", lambda e: e.activation(out=kh[1][:, cs], in_=iE[:, cs], func=AF.Copy), reads=[K("iE", k)], writes=[K("kh1", k)])
                for pt in range(NPT):
                    pi, pk = newps()
                    mm_group(fw, pst[pi][:, 0:128], [(kh[1][:, pt * 128:(pt + 1) * 128], identb[:])],
                             reads=[K("kh1", blk_of(pt * 128)), K("kh1", blk_of(pt * 128 + 127)), "identb"], writes=[pk])
                    if pt % 2 == 0:
                        fw.op("act", lambda e, pi=pi, pt=pt: e.activation(out=Vt[:, pt, :], in_=pst[pi][:, 0:128], func=AF.Copy), reads=[pk], writes=["Vt"])
                    else:
                        fw.op("dve", lambda e, pi=pi, pt=pt: e.tensor_copy(out=Vt[:, pt, :], in_=pst[pi][:, 0:128]), reads=[pk], writes=["Vt"])
                fw.op("pool", lambda e: e.memset(oacc[:], 0.0), reads=["oacc"], writes=["oacc"])
                for d in range(2):
                    setup(hd, b, d)
                    fw.op("dve", lambda e, d=d: e.memset(S[d][:], 0.0), reads=[("S", d)], writes=[("S", d)])
                    fw.op("dve", lambda e, d=d: e.memset(Sb[d][:], 0.0), reads=[("Sb", d)], writes=[("Sb", d)])
                for step in range(NPT):
                    pre = [pair_pre(d, ptiles[d][step]) for d in range(2)]
                    for hi in range(2):
                        for d in range(2):
                            hf = (0, 1)[hi] if d == 0 else (1, 0)[hi]
                            half_step(d, ptiles[d][step], hf, *pre[d])
                fw.dma("sp", kk[:], dp[hd, 4, :, b, :], writes=allk("kk"))
                fw.op("act", lambda e: e.activation(out=lf[:], in_=kk[:], func=AF.Sigmoid), reads=allk("kk"), writes=allk("lf"))
                fw.op("pool", lambda e: e.tensor_tensor(out=kk[:], in0=kk[:], in1=lf[:], op=ALU.mult), reads=allk("kk") + allk("lf"), writes=allk("kk"))
                for (t0, tw) in TL:
                    j = nxt("sqb", 2)
                    fw.op("act", lambda e, j=j, t0=t0, tw=tw: e.activation(out=sqb[j][:, 0:tw], in_=oacc[:, t0:t0 + tw], func=AF.Square),
                          reads=["oacc"], writes=[("sqb", j)])
                    pi, pk = newps()
                    mm_group(fw, pst[pi][:, 0:tw], [(ones_m[:], sqb[j][:, 0:tw])], reads=["ones_m", ("sqb", j)], writes=[pk])
                    r = nxt("rsb", 2)
                    fw.op("act", lambda e, pi=pi, r=r, tw=tw: e.activation(out=rsb[r][:, 0:tw], in_=pst[pi][:, 0:tw], func=AF.Ln, bias=c1[:, 1:2], scale=1.0 / 128),
                          reads=[pk, "c1"], writes=[("rsb", r)])
                    fw.op("act", lambda e, r=r, tw=tw: e.activation(out=rsb[r][:, 0:tw], in_=rsb[r][:, 0:tw], func=AF.Exp, scale=-0.5),
                          reads=[("rsb", r)], writes=[("rsb", r)])
                    fw.op("dve", lambda e, r=r, t0=t0, tw=tw: e.scalar_tensor_tensor(
                        out=G[:, t0:t0 + tw], in0=oacc[:, t0:t0 + tw], scalar=nwt[:, 0:1], in1=rsb[r][:, 0:tw], op0=ALU.mult, op1=ALU.mult),
                        reads=["oacc", "nwt", ("rsb", r)] + allk("G"), writes=allk("G"))
                fw.op("pool", lambda e: e.tensor_tensor(out=G[:], in0=G[:], in1=kk[:], op=ALU.mult), reads=allk("G") + allk("kk"), writes=allk("G"))
                fw.dma("sp", dy[hd, :, b, :], G[:], reads=allk("G"))
        fw.finish()
    return nc


def hg_consts():
    s = np.arange(128)[:, None]
    t = np.arange(128)[None, :]
    same = (s // CH) == (t // CH)
    mf = (same & (t >= s)).astype(np.float32)
    mb = (same & (t <= s)).astype(np.float32)
    return np.stack([mf, mb], 0), np.eye(128, dtype=np.float32)


def hg_host_inputs(pT, o, P):
    masks, ident = hg_consts()
    maps = []
    for k in range(NCORES):
        p5 = np.empty((2, 5, 128, B, TS), np.float32)
        lbl = np.empty((128, 2, DEPTH), np.float32)
        for j in range(2):
            hh = 2 * k + j
            for s in range(5):
                p5[j, s] = pT[s * D + hh * 128:s * D + (hh + 1) * 128]
            lbl[:, j, :] = P["hg_lb_logits"][:, hh * 128:(hh + 1) * 128].T
        maps.append({"p5": p5, "lbl": lbl, "nw": np.ascontiguousarray(P["hg_norm_w"][o].reshape(128, 1)),
                     "masks": masks, "ident": ident})
    return maps


def hg_host_gather(results):
    y = np.empty((D, B, TS), np.float32)
    for k in range(NCORES):
        o = results[k]["yT"]
        for j in range(2):
            hh = 2 * k + j
            y[hh * 128:(hh + 1) * 128] = o[j]
    return y


def kernel(**inp):
    P = {k: np.asarray(v, dtype=np.float32) for k, v in inp.items()}
    x, ctx = P["x"], P["ctx"]
    m = ada_host_gather(run(get_nc("ada", build_ada), ada_host_inputs(P["c"], P["c_ctx"], P["ada_w"], P["ada_b"])))
    xs = to_core_T(x, ctx)

    def tok_maps(l_post, l_pre, xs, ys):
        maps = []
        nmlp = vec_pk(P["norm_mlp_w"][l_post]) if l_post is not None else np.zeros((128, KC), np.float32)
        npre = vec_pk(P["norm_mix_w"][l_pre]) if l_pre < DEPTH else vec_pk(P["final_norm_w"])
        nrm = np.ascontiguousarray(np.stack([nmlp, npre], 1))
        for k in range(NCORES):
            b, r = core_tok(k)
            d = {"xT": xs[k], "adaL": ada_sets(m, l_post, l_pre, b), "adaC": ada_sets(m, l_post, l_pre, 2), "nrm": nrm}
            if l_post is not None:
                d["yT"] = ys[k]
                d["w_out"] = P["ab_w_out"][l_post // 2] if l_post % 2 == 0 else P["hg_w_out"][l_post // 2]
                d["w1"] = P["mlp_w1"][l_post]
                d["w2"] = P["mlp_w2"][l_post]
            if l_pre < DEPTH:
                d["w_in"] = P["ab_w_in"][l_pre // 2] if l_pre % 2 == 0 else P["hg_w_in"][l_pre // 2]
            maps.append(d)
        return maps

    res = run(get_nc("tok", build_tok, False, "ab"), tok_maps(None, 0, xs, None))
    for l in range(DEPTH):
        pT = tokT_to_full([r["pT"] for r in res])
        xs = [r["xoT"] for r in res]
        if l % 2 == 0:
            y = ab_host_gather(run(get_nc("ab", build_ab), ab_host_inputs(pT, l // 2, P)))
        else:
            y = hg_host_gather(run(get_nc("hg", build_hg, l), hg_host_inputs(pT, l // 2, P)))
        ys = full_to_tokT(y)
        pre = "final" if l + 1 == DEPTH else ("ab" if (l + 1) % 2 == 0 else "hg")
        res = run(get_nc("tok", build_tok, True, pre), tok_maps(l, l + 1, xs, ys))
    lat, _ = from_core_T([r["outT"] for r in res])
    return np.ascontiguousarray(lat.astype(np.float32))
```

```python
import numpy as np
from contextlib import ExitStack
import concourse.bass as bass
import concourse.mybir as mybir
from concourse.bass_utils import run_bass_kernel_spmd

F32 = mybir.dt.float32
BF16 = mybir.dt.bfloat16
AF = mybir.ActivationFunctionType
ALU = mybir.AluOpType
AX = mybir.AxisListType

D = 2048
KC = D // 128
B = 2
SEQ = 4096
CTX = 256
DEPTH = 4
DFF = 8192
NCORES = 8
NL = B * SEQ // NCORES
NCX = B * CTX // NCORES
NT = NL + NCX
EPS = 1e-6
D_AB_IN = 3136
D_HG_IN = 10240
TS = SEQ + CTX

SAME_ENG_SYNC = True


class Rec:
    def __init__(self):
        self.calls = []

    def __getattr__(self, name):
        def f(*a, **k):
            self.calls.append((name, a, k))
            return self
        return f


class FW:
    ENGS = ("pe", "dve", "act", "pool", "sp")

    def __init__(self, nc, es, n_dma_sems=48):
        self.nc = nc
        self.es = es
        self.eng = {"pe": nc.tensor, "dve": nc.vector, "act": nc.scalar, "pool": nc.gpsimd, "sp": nc.sync}
        self.sem = {e: es.enter_context(nc.semaphore("s_" + e)) for e in self.ENGS}
        self.cnt = {e: 0 for e in self.ENGS}
        self.prog = {e: [] for e in self.ENGS}
        self.seen = {e: {} for e in self.ENGS}
        self.dsem = [es.enter_context(nc.semaphore("d%d" % i)) for i in range(n_dma_sems)]
        self.dval = [0] * n_dma_sems
        self.drr = 0
        self.last_w = {}
        self.readers = {}
        self.semobj = {}
        self.n_ps = 0

    def sb(self, name, shape, dt):
        return self.es.enter_context(self.nc.sbuf_tensor(name, list(shape), dt))

    def ps(self, name, shape, dt=F32):
        return self.es.enter_context(self.nc.psum_tensor(name, list(shape), dt))

    def _collect(self, eng, reads, writes, nosame=False):
        toks = []
        for k in reads:
            t = self.last_w.get(k)
            if t is not None:
                toks.append(t)
        for k in writes:
            t = self.last_w.get(k)
            if t is not None:
                toks.append(t)
            toks.extend(self.readers.get(k, ()))
        need = {}
        for (sid, val) in toks:
            if ((not SAME_ENG_SYNC) or nosame) and sid == ("e", eng):
                continue
            if self.seen[eng].get(sid, 0) >= val:
                continue
            if need.get(sid, 0) < val:
                need[sid] = val
        for sid, val in need.items():
            self.seen[eng][sid] = val
        return list(need.items())

    def _commit(self, tok, reads, writes):
        for k in writes:
            self.last_w[k] = tok
            self.readers[k] = []
        for k in reads:
            if k in writes:
                continue
            self.readers.setdefault(k, []).append(tok)

    def _semof(self, sid):
        kind, x = sid
        return self.sem[x] if kind == "e" else self.dsem[x]

    def op(self, eng, fn, reads=(), writes=(), nosame=False):
        waits = self._collect(eng, reads, writes, nosame)
        self.cnt[eng] += 1
        tok = (("e", eng), self.cnt[eng])
        rec = Rec()
        fn(rec)
        calls = rec.calls

        def run(e, calls=calls):
            ins = None
            for (name, a, k) in calls:
                ins = getattr(e, name)(*a, **k)
            return ins
        self.prog[eng].append((waits, run, ("e", eng), 1))
        self._commit(tok, reads, writes)
        return tok

    def dma(self, eng, out, in_, reads=(), writes=(), **kw):
        waits = self._collect(eng, reads, writes)
        i = self.drr
        self.drr = (self.drr + 1) % len(self.dsem)
        sid = ("d", i)
        if self.dval[i] > 0 and self.seen[eng].get(sid, 0) < self.dval[i]:
            waits.append((sid, self.dval[i]))
            self.seen[eng][sid] = self.dval[i]
        self.dval[i] += 16
        tok = (sid, self.dval[i])
        self.prog[eng].append((waits, (lambda e, out=out, in_=in_, kw=kw: e.dma_start(out=out, in_=in_, **kw)), sid, 16))
        self._commit(tok, reads, writes)
        return tok

    def barrier(self):
        allw = [(("e", e), self.cnt[e]) for e in self.ENGS if self.cnt[e] > 0]
        allw += [(("d", i), v) for i, v in enumerate(self.dval) if v > 0]
        for e in self.ENGS:
            waits = []
            for (sid, val) in allw:
                if sid == ("e", e):
                    continue
                if self.seen[e].get(sid, 0) < val:
                    waits.append((sid, val))
                    self.seen[e][sid] = val
            if waits:
                self.prog[e].append((waits, None, None, 0))

    def finish(self, eng="sp"):
        waits = []
        for i, v in enumerate(self.dval):
            if v > 0:
                waits.append((("d", i), v))
        self.prog[eng].append((waits, None, None, 0))
        block = self.es.enter_context(self.nc.Block())

        def replay(name):
            def body(e):
                for (waits, fn, sid, inc) in self.prog[name]:
                    for (wsid, val) in waits:
                        e.wait_ge(self._semof(wsid), val)
                    if fn is not None:
                        ins = fn(e)
                        ins.then_inc(self._semof(sid), inc)
            return body

        block.tensor(replay("pe"))
        block.vector(replay("dve"))
        block.scalar(replay("act"))
        block.gpsimd(replay("pool"))
        block.sync(replay("sp"))


def mm_group(fw, out_ps, pairs, reads, writes):
    n = len(pairs)

    def fn(e):
        ins = None
        for i, (l, r) in enumerate(pairs):
            ins = e.matmul(out_ps, l, r, start=(i == 0), stop=(i == n - 1))
        return ins
    return fw.op("pe", fn, reads=reads, writes=writes)


ADA_COLS = 6 * D // NCORES
ADA_J = ADA_COLS // 128


def build_ada():
    nc = bass.Bass("TRN2", target_bir_lowering=False)
    cv = nc.dram_tensor("cv", [128, KC, 3], F32, kind="ExternalInput").ap()
    w = nc.dram_tensor("w", [DEPTH, D, ADA_COLS], F32, kind="ExternalInput").ap()
    bb = nc.dram_tensor("b", [128, DEPTH, ADA_J], F32, kind="ExternalInput").ap()
    out = nc.dram_tensor("out", [128, DEPTH, ADA_J, 3], F32, kind="ExternalOutput").ap()
    with ExitStack() as es:
        fw = FW(nc, es)
        cvt = fw.sb("cvt", [128, KC, 3], F32)
        sg = fw.sb("sg", [128, KC, 3], F32)
        sv = fw.sb("sv", [128, KC, 3], F32)
        bt = fw.sb("bt", [128, DEPTH, ADA_J], F32)
        ot = fw.sb("ot", [128, DEPTH, ADA_J, 3], F32)
        NB = 4
        wt = [fw.sb("wt%d" % i, [128, KC, 512], F32) for i in range(NB)]
        pst = [fw.ps("ps%d" % i, [128, 512], F32) for i in range(4)]
        fw.dma("sp", cvt[:], cv[:, :, :], writes=["cvt"])
        fw.dma("sp", bt[:], bb[:, :, :], writes=["bt"])
        fw.op("act", lambda e: e.activation(out=sg[:], in_=cvt[:], func=AF.Sigmoid), reads=["cvt"], writes=["sg"])
        fw.op("dve", lambda e: e.tensor_tensor(out=sv[:], in0=cvt[:], in1=sg[:], op=ALU.mult), reads=["cvt", "sg"], writes=["sv"])
        it = 0
        for l in range(DEPTH):
            for g in range(ADA_COLS // 512):
                wb = wt[it % NB]
                wk = "wt%d" % (it % NB)
                src = w[l].rearrange("(kc p) n -> p kc n", p=128)[:, :, g * 512:(g + 1) * 512]
                q = ("sp", "act", "pool")[it % 3]
                fw.dma(q, wb[:, 0:KC // 2, :], src[:, 0:KC // 2, :], writes=[wk])
                fw.dma(("act", "pool", "sp")[it % 3], wb[:, KC // 2:KC, :], src[:, KC // 2:KC, :], reads=[wk], writes=[wk])
                for jj in range(4):
                    j = g * 4 + jj
                    pi = (it * 4 + jj) % 4
                    pk = "ps%d" % pi
                    mm_group(fw, pst[pi][:, 0:3],
                             [(wb[:, kc, jj * 128:(jj + 1) * 128], sv[:, kc, :]) for kc in range(KC)],
                             reads=[wk, "sv"], writes=[pk])
                    fw.op("dve", lambda e, pi=pi, l=l, j=j: e.tensor_scalar(
                        out=ot[:, l, j, :], in0=pst[pi][:, 0:3], scalar1=bt[:, l, j:j + 1], scalar2=None, op0=ALU.add),
                        reads=[pk, "bt"], writes=["ot"])
                it += 1
        fw.dma("sp", out[:, :, :, :], ot[:], reads=["ot"])
        fw.finish()
    return nc


TT = [(0, 512), (512, 512), (1024, 64)]


def build_tok(post, pre):
    nin = {"ab": D_AB_IN, "hg": D_HG_IN, "final": 0}[pre]
    nc = bass.Bass("TRN2", target_bir_lowering=False)
    xin = nc.dram_tensor("xT", [D, NT], F32, kind="ExternalInput").ap()
    adaL = nc.dram_tensor("adaL", [128, 2, 6, KC], F32, kind="ExternalInput").ap()
    adaC = nc.dram_tensor("adaC", [128, 2, 6, KC], F32, kind="ExternalInput").ap()
    nrm = nc.dram_tensor("nrm", [128, 2, KC], F32, kind="ExternalInput").ap()
    if post:
        yin = nc.dram_tensor("yT", [D, NT], BF16, kind="ExternalInput").ap()
        w_out = nc.dram_tensor("w_out", [D, D], F32, kind="ExternalInput").ap()
        w1 = nc.dram_tensor("w1", [D, DFF], F32, kind="ExternalInput").ap()
        w2 = nc.dram_tensor("w2", [DFF, D], F32, kind="ExternalInput").ap()
    if nin:
        w_in = nc.dram_tensor("w_in", [D, nin], F32, kind="ExternalInput").ap()
        pout = nc.dram_tensor("pT", [nin, NT], F32, kind="ExternalOutput").ap()
        xout = nc.dram_tensor("xoT", [D, NT], F32, kind="ExternalOutput").ap()
    else:
        fout = nc.dram_tensor("outT", [D, NT], F32, kind="ExternalOutput").ap()

    with ExitStack() as es:
        fw = FW(nc, es)
        x = fw.sb("x", [128, KC, NT], F32)
        hT = fw.sb("hT", [128, KC, NT], BF16)
        h1 = fw.sb("h1", [128, KC, NT], BF16)
        NWB = 2
        wb = [fw.sb("wb%d" % i, [128, KC, 512], BF16) for i in range(NWB)]
        aL = fw.sb("aL", [128, 2, 6, KC], F32)
        aC = fw.sb("aC", [128, 2, 6, KC], F32)
        nw = fw.sb("nw", [128, 2, KC], F32)
        AL = fw.sb("AL", [128, 2, KC], F32)
        AC = fw.sb("AC", [128, 2, KC], F32)
        ones = fw.sb("ones", [128, 128], BF16)
        rstd = fw.sb("rstd", [128, NT], F32)
        tmpn = [fw.sb("tmpn%d" % i, [128, NT], F32) for i in range(2)]
        stg = [fw.sb("stg%d" % i, [128, NT], F32) for i in range(3)]
        rl = [fw.sb("rl%d" % i, [128, 512], F32) for i in range(2)]
        NPS = 8
        pst = [fw.ps("ps%d" % i, [128, 512], F32) for i in range(NPS)]
        st = {"ps": 0, "wb": 0, "tmpn": 0, "stg": 0, "rl": 0}

        def nxt(name, n):
            i = st[name] % n
            st[name] += 1
            return i

        xk = lambda kc: ("x", kc)
        hk = lambda kc: ("h", kc)
        h1k = lambda kc: ("h1", kc)

        if post:
            yv = yin.rearrange("(kc p) n -> p kc n", p=128)
            for kc in range(KC):
                fw.dma("sp" if kc % 2 == 0 else "act", hT[:, kc, :], yv[:, kc, :], writes=[hk(kc)])
        xv = xin.rearrange("(kc p) n -> p kc n", p=128)
        for kc in range(KC):
            fw.dma("sp" if kc % 2 == 0 else "act", x[:, kc, :], xv[:, kc, :], writes=[xk(kc)])
        fw.dma("act", aL[:], adaL[:, :, :, :], writes=["aL"])
        fw.dma("act", aC[:], adaC[:, :, :, :], writes=["aC"])
        fw.dma("act", nw[:], nrm[:, :, :], writes=["nw"])
        fw.op("dve", lambda e: e.memset(ones[:], 1.0), writes=["ones"])
        for (s, idx) in ((0, 4), (1, 1)):
            fw.op("dve", lambda e, s=s, idx=idx: e.scalar_tensor_tensor(
                out=AL[:, s, :], in0=aL[:, s, idx, :], scalar=1.0, in1=nw[:, s, :], op0=ALU.add, op1=ALU.mult),
                reads=["aL", "nw"], writes=["AL"])
            fw.op("dve", lambda e, s=s, idx=idx: e.scalar_tensor_tensor(
                out=AC[:, s, :], in0=aC[:, s, idx, :], scalar=1.0, in1=nw[:, s, :], op0=ALU.add, op1=ALU.mult),
                reads=["aC", "nw"], writes=["AC"])

        def load_w(src_ap, ncols):
            i = nxt("wb", NWB)
            fw.dma("pool", wb[i][:, :, 0:ncols], src_ap.rearrange("(kc p) n -> p kc n", p=128), writes=[("wb", i)])
            return i

        def proj(src, srckey, wsrc, ncols_total, evac):
            c0 = 0
            while c0 < ncols_total:
                gw = min(512, ncols_total - c0)
                wi = load_w(wsrc[:, c0:c0 + gw], gw)
                cc = 0
                while cc < gw:
                    cw = min(128, gw - cc)
                    c = (c0 + cc) // 128
                    for tt, (t0, tw) in enumerate(TT):
                        pi = nxt("ps", NPS)
                        pk = ("ps", pi)
                        mm_group(fw, pst[pi][0:cw, 0:tw],
                                 [(wb[wi][:, kc, cc:cc + cw], src[:, kc, t0:t0 + tw]) for kc in range(KC)],
                                 reads=[("wb", wi)] + [srckey(kc) for kc in range(KC)], writes=[pk])
                        evac(c, cw, tt, pst[pi][0:cw, 0:tw], pk)
                    cc += cw
                c0 += gw

        def norm_mod(s, shift_idx, out_dt_tile, outkey, final=False):
            for kc in range(KC):
                fw.op("act", lambda e, kc=kc: e.activation(out=h1[:, kc, :], in_=x[:, kc, :], func=AF.Square),
                      reads=[xk(kc)], writes=[h1k(kc)])
            for tt, (t0, tw) in enumerate(TT):
                pi = nxt("ps", NPS)
                pk = ("ps", pi)
                mm_group(fw, pst[pi][:, 0:tw], [(ones[:], h1[:, kc, t0:t0 + tw]) for kc in range(KC)],
                         reads=["ones"] + [h1k(kc) for kc in range(KC)], writes=[pk])
                fw.op("act", lambda e, pi=pi, t0=t0, tw=tw: e.activation(
                    out=rstd[:, t0:t0 + tw], in_=pst[pi][:, 0:tw], func=AF.Ln, bias=EPS_AP[:, 0:1], scale=1.0 / D),
                    reads=[pk, "eps"], writes=["rstd"])
            fw.op("act", lambda e: e.activation(out=rstd[:], in_=rstd[:], func=AF.Exp, scale=-0.5), reads=["rstd"], writes=["rstd"])
            for kc in range(KC):
                ti = nxt("tmpn", 2)
                tk = ("tmpn", ti)
                fw.op("dve", lambda e, kc=kc, ti=ti: e.tensor_tensor(out=tmpn[ti][:], in0=x[:, kc, :], in1=rstd[:], op=ALU.mult),
                      reads=[xk(kc), "rstd"], writes=[tk])
                if final:
                    si = nxt("stg", 3)
                    sk = ("stg", si)
                    fw.op("act", lambda e, kc=kc, ti=ti, si=si: e.activation(
                        out=stg[si][:], in_=tmpn[ti][:], func=AF.Identity, scale=nw[:, 1, kc:kc + 1]),
                        reads=[tk, "nw"], writes=[sk])
                    fw.dma("sp", fout[kc * 128:(kc + 1) * 128, :], stg[si][:], reads=[sk])
                else:
                    fw.op("act", lambda e, kc=kc, ti=ti: e.activation(
                        out=hT[:, kc, 0:NL], in_=tmpn[ti][:, 0:NL], func=AF.Identity,
                        bias=aL[:, s, shift_idx, kc:kc + 1], scale=AL[:, s, kc:kc + 1]),
                        reads=[tk, "aL", "AL"], writes=[hk(kc)])
                    fw.op("act", lambda e, kc=kc, ti=ti: e.activation(
                        out=hT[:, kc, NL:NT], in_=tmpn[ti][:, NL:NT], func=AF.Identity,
                        bias=aC[:, s, shift_idx, kc:kc + 1], scale=AC[:, s, kc:kc + 1]),
                        reads=[tk, "aC", "AC", hk(kc)], writes=[hk(kc)])

        EPS_AP = fw.sb("epsc", [128, 1], F32)
        fw.op("dve", lambda e: e.memset(EPS_AP[:], EPS), writes=["eps"])

        def resid_evac(gidx):
            def evac(c, cw, tt, ps_ap, pk):
                t0, tw = TT[tt]
                a = aL if tt < 2 else aC
                fw.op("dve", lambda e: e.scalar_tensor_tensor(
                    out=x[:, c, t0:t0 + tw], in0=ps_ap, scalar=a[:, 0, gidx, c:c + 1], in1=x[:, c, t0:t0 + tw],
                    op0=ALU.mult, op1=ALU.add),
                    reads=[pk, "aL", "aC", xk(c)], writes=[xk(c)])
            return evac

        if post:
            proj(hT, hk, w_out, D, resid_evac(2))
            norm_mod(0, 3, None, None)
            for q in range(4):
                def evac1(c, cw, tt, ps_ap, pk):
                    t0, tw = TT[tt]
                    ri = nxt("rl", 2)
                    rk = ("rl", ri)
                    fw.op("act", lambda e: e.activation(out=rl[ri][:, 0:tw], in_=ps_ap, func=AF.Relu),
                          reads=[pk], writes=[rk])
                    fw.op("dve", lambda e: e.tensor_tensor(out=h1[:, c, t0:t0 + tw], in0=rl[ri][:, 0:tw], in1=rl[ri][:, 0:tw], op=ALU.mult),
                          reads=[rk], writes=[h1k(c)])
                proj(hT, hk, w1[:, q * 2048:(q + 1) * 2048], 2048, evac1)
                proj(h1, h1k, w2[q * 2048:(q + 1) * 2048, :], D, resid_evac(5))

        if nin:
            xov = xout.rearrange("(kc p) n -> p kc n", p=128)
            for kc in range(KC):
                fw.dma("sp", xov[:, kc, :], x[:, kc, :], reads=[xk(kc)])
            norm_mod(1, 0, None, None)
            cur = {}

            def evac_p(c, cw, tt, ps_ap, pk):
                t0, tw = TT[tt]
                if tt == 0:
                    cur["si"] = nxt("stg", 3)
                si = cur["si"]
                sk = ("stg", si)
                eng = "act" if (c + tt) % 2 == 0 else "dve"
                if eng == "act":
                    fw.op("act", lambda e: e.activation(out=stg[si][0:cw, t0:t0 + tw], in_=ps_ap, func=AF.Copy),
                          reads=[pk], writes=[sk])
                else:
                    fw.op("dve", lambda e: e.tensor_copy(out=stg[si][0:cw, t0:t0 + tw], in_=ps_ap),
                          reads=[pk], writes=[sk])
                if tt == len(TT) - 1:
                    fw.dma("sp", pout[c * 128:c * 128 + cw, :], stg[si][0:cw, :], reads=[sk])
            proj(hT, hk, w_in, nin, evac_p)
        else:
            norm_mod(1, 0, None, None, final=True)
        fw.finish()
    return nc


def core_tok(k):
    return k // 4, k % 4


def to_core_T(lat, ctx):
    outs = []
    for k in range(NCORES):
        b, r = core_tok(k)
        a = np.concatenate([lat[b, r * NL:(r + 1) * NL], ctx[b, r * NCX:(r + 1) * NCX]], axis=0)
        outs.append(np.ascontiguousarray(a.T))
    return outs


def from_core_T(arrs):
    F = arrs[0].shape[0]
    lat = np.empty((B, SEQ, F), arrs[0].dtype)
    ctx = np.empty((B, CTX, F), arrs[0].dtype)
    for k in range(NCORES):
        b, r = core_tok(k)
        lat[b, r * NL:(r + 1) * NL] = arrs[k][:, :NL].T
        ctx[b, r * NCX:(r + 1) * NCX] = arrs[k][:, NL:].T
    return lat, ctx


def vec_pk(v):
    return np.ascontiguousarray(v.reshape(KC, 128).T)


def ada_sets(m, lpost, lpre, v):
    out = np.zeros((128, 2, 6, KC), np.float32)
    for s, l in enumerate((lpost, lpre)):
        if l is None or l < 0 or l >= DEPTH:
            continue
        out[:, s] = m[l, v].reshape(6, KC, 128).transpose(2, 0, 1)
    return out


def ada_host_inputs(c, c_ctx, ada_w, ada_b):
    cvec = np.stack([c[0], c[1], c_ctx], 0)
    cv = np.ascontiguousarray(cvec.reshape(3, KC, 128).transpose(2, 1, 0))
    maps = []
    for k in range(NCORES):
        w = np.ascontiguousarray(ada_w[:, :, k * ADA_COLS:(k + 1) * ADA_COLS])
        b = np.ascontiguousarray(ada_b[:, k * ADA_COLS:(k + 1) * ADA_COLS].reshape(DEPTH, ADA_J, 128).transpose(2, 0, 1))
        maps.append({"cv": cv, "w": w, "b": b})
    return maps


def ada_host_gather(results):
    m = np.zeros((DEPTH, 3, 6 * D), np.float32)
    for k in range(NCORES):
        o = results[k]["out"]
        m[:, :, k * ADA_COLS:(k + 1) * ADA_COLS] = o.transpose(1, 3, 2, 0).reshape(DEPTH, 3, ADA_COLS)
    return m


_NC_CACHE = {}


def get_nc(name, builder, *args):
    key = (name,) + tuple(args)
    if key not in _NC_CACHE:
        _NC_CACHE[key] = builder(*args)
    return _NC_CACHE[key]


def run(nc, maps):
    res = run_bass_kernel_spmd(nc, maps, core_ids=list(range(NCORES)))
    return res.results


ATTN_SCALE = 192.0 ** -0.5
TL = [(i * 512, 512) for i in range(SEQ // 512)] + [(SEQ, CTX)]
NKT = TS // 128
GELU_C = 1.5957691216057308


def build_ab():
    nc = bass.Bass("TRN2", target_bir_lowering=False)
    dgu = nc.dram_tensor("guT", [2, 128, B, TS], F32, kind="ExternalInput").ap()
    dcq = nc.dram_tensor("cqT", [512, B, TS], F32, kind="ExternalInput").ap()
    dckv = nc.dram_tensor("ckvT", [512, B, TS], F32, kind="ExternalInput").ap()
    dkr = nc.dram_tensor("krT", [64, B, TS], F32, kind="ExternalInput").ap()
    dvec = nc.dram_tensor("vecs", [128, 24], F32, kind="ExternalInput").ap()
    dwa = nc.dram_tensor("wax", [2, 2, 128, 128], F32, kind="ExternalInput").ap()
    dwuq = nc.dram_tensor("wuq", [512, 192], F32, kind="ExternalInput").ap()
    dwukv = nc.dram_tensor("wukv", [512, 256], F32, kind="ExternalInput").ap()
    dcs = nc.dram_tensor("cs", [2, 64, SEQ], F32, kind="ExternalInput").ap()
    drm = nc.dram_tensor("rm", [64, 64], F32, kind="ExternalInput").ap()
    dy = nc.dram_tensor("yT", [256, B, TS], BF16, kind="ExternalOutput").ap()

    with ExitStack() as es:
        fw = FW(nc, es)
        vec = fw.sb("vecsb", [128, 24], F32)
        c1 = fw.sb("c1", [128, 4], F32)
        nsp = fw.sb("nsp", [128, 2], F32)
        wax = fw.sb("waxb", [128, 4, 128], BF16)
        wuq = fw.sb("wuqb", [128, 4, 192], BF16)
        wukv = fw.sb("wukvb", [128, 4, 256], BF16)
        ones = fw.sb("ones", [128, 128], BF16)
        rm = fw.sb("rmsb", [64, 64], F32)
        NPS = 8
        pst = [fw.ps("ps%d" % i, [128, 512], F32) for i in range(NPS)]
        st = {}

        def nxt(name, n):
            i = st.get(name, 0) % n
            st[name] = st.get(name, 0) + 1
            return i

        fw.dma("sp", vec[:], dvec[:, :], writes=["vec"])
        fw.dma("sp", rm[:], drm[:, :], writes=["rm"])
        fw.dma("pool", wax[:], dwa.rearrange("a d i o -> i (a d) o"), writes=["wax"])
        fw.dma("pool", wuq[:], dwuq.rearrange("(kc p) n -> p kc n", p=128), writes=["wuq"])
        fw.dma("pool", wukv[:], dwukv.rearrange("(kc p) n -> p kc n", p=128), writes=["wukv"])
        fw.op("dve", lambda e: e.memset(ones[:], 1.0), writes=["ones"])
        fw.op("dve", lambda e: e.memset(c1[:, 0:1], 1.0), writes=["c1"])
        fw.op("dve", lambda e: e.memset(c1[:, 1:2], EPS), reads=["c1"], writes=["c1"])
        fw.op("act", lambda e: e.activation(out=nsp[:], in_=vec[:, 9:11], func=AF.Exp, scale=-1.0), reads=["vec"], writes=["nsp"])
        fw.op("act", lambda e: e.activation(out=nsp[:], in_=nsp[:], func=AF.Ln, bias=c1[:, 0:1], scale=1.0), reads=["nsp", "c1"], writes=["nsp"])
        fw.op("dve", lambda e: e.tensor_scalar(out=nsp[:], in0=nsp[:], scalar1=-8.0, scalar2=None, op0=ALU.mult), reads=["nsp"], writes=["nsp"])

        with ExitStack() as es1:
            names = ["tu", "tg", "uc", "t1", "t2", "t3", "hacc", "tgel"]
            T = {n: es1.enter_context(nc.sbuf_tensor("l_" + n, [128, TS], F32)) for n in names}
            ucb = es1.enter_context(nc.sbuf_tensor("l_ucb", [128, TS], BF16))
            BL = [(0, 1024), (1024, 2048), (2048, 3072), (3072, 4096), (SEQ, TS)]
            NBL = len(BL)

            def K(name, k):
                return (name, k)

            def stage_major(bodyfn):
                per_block = []
                for k in range(NBL):
                    lst = []
                    bodyfn(k, lambda f, *a, **kw: lst.append((f, a, kw)))
                    per_block.append(lst)
                for j in range(max(len(l) for l in per_block)):
                    for k in range(NBL):
                        if j < len(per_block[k]):
                            f, a_, kw = per_block[k][j]
                            f(*a_, **kw)

            for b in range(B):
                tu, tg, uc, t1, t2, t3, hacc, tgel = (T[n] for n in names)

                def conv_body(k, emit):
                    c0, cE = BL[k]
                    s0, s1 = (0, SEQ) if k < 4 else (SEQ, TS)
                    cs = slice(c0, cE)
                    emit(fw.dma, "sp", tg[:, cs], dgu[0, :, b, cs], writes=[K("tg", k)])
                    emit(fw.dma, "act", tu[:, cs], dgu[1, :, b, cs], writes=[K("tu", k)])
                    emit(fw.op, "pool", lambda e: e.tensor_tensor(out=tgel[:, cs], in0=tg[:, cs], in1=tg[:, cs], op=ALU.mult),
                         reads=[K("tg", k)], writes=[K("tgel", k)])
                    emit(fw.op, "pool", lambda e: e.tensor_scalar(out=tgel[:, cs], in0=tgel[:, cs], scalar1=0.044715, scalar2=1.0, op0=ALU.mult, op1=ALU.add),
                         reads=[K("tgel", k)], writes=[K("tgel", k)])
                    emit(fw.op, "pool", lambda e: e.tensor_tensor(out=tgel[:, cs], in0=tgel[:, cs], in1=tg[:, cs], op=ALU.mult),
                         reads=[K("tg", k), K("tgel", k)], writes=[K("tgel", k)])
                    emit(fw.op, "act", lambda e: e.activation(out=tgel[:, cs], in_=tgel[:, cs], func=AF.Sigmoid, scale=GELU_C),
                         reads=[K("tgel", k)], writes=[K("tgel", k)])
                    emit(fw.op, "pool", lambda e: e.tensor_tensor(out=tgel[:, cs], in0=tgel[:, cs], in1=tg[:, cs], op=ALU.mult),
                         reads=[K("tg", k), K("tgel", k)], writes=[K("tgel", k)])

                def conv_body2(k, emit):
                    c0, cE = BL[k]
                    s0, s1 = (0, SEQ) if k < 4 else (SEQ, TS)
                    cs = slice(c0, cE)
                    left = [K("tu", k - 1)] if c0 > s0 else []
                    right = [K("tu", k + 1)] if cE < s1 else []
                    emit(fw.op, "dve", lambda e: e.tensor_scalar(out=uc[:, cs], in0=tu[:, cs], scalar1=vec[:, 2:3], scalar2=vec[:, 4:5], op0=ALU.mult, op1=ALU.add),
                         reads=[K("tu", k), "vec"], writes=[K("uc", k)])
                    for (tap, sh) in ((0, 2), (1, 1)):
                        a0 = max(c0, s0 + sh)
                        emit(fw.op, "dve", lambda e, a0=a0, tap=tap, sh=sh: e.scalar_tensor_tensor(
                            out=uc[:, a0:cE], in0=tu[:, a0 - sh:cE - sh], scalar=vec[:, tap:tap + 1], in1=uc[:, a0:cE], op0=ALU.mult, op1=ALU.add),
                            reads=[K("tu", k), "vec", K("uc", k)] + left, writes=[K("uc", k)])
                    b1 = min(cE, s1 - 1)
                    emit(fw.op, "dve", lambda e: e.scalar_tensor_tensor(
                        out=uc[:, c0:b1], in0=tu[:, c0 + 1:b1 + 1], scalar=vec[:, 3:4], in1=uc[:, c0:b1], op0=ALU.mult, op1=ALU.add),
                        reads=[K("tu", k), "vec", K("uc", k)] + right, writes=[K("uc", k)])
                    emit(fw.op, "act", lambda e: e.activation(out=ucb[:, cs], in_=uc[:, cs], func=AF.Copy), reads=[K("uc", k)], writes=[K("ucb", k)])

                stage_major(conv_body)
                stage_major(conv_body2)
                for d in range(2):
                    def gate_body(k, emit):
                        c0, cE = BL[k]
                        cs = slice(c0, cE)
                        tiles = [(t0, min(512, cE - t0)) for t0 in range(c0, cE, 512)]
                        for (t0, tw) in tiles:
                            for (ax, dst, dn, bcol) in ((0, t1, "t1", 5 + d), (1, t2, "t2", 7 + d)):
                                def one(t0=t0, tw=tw, ax=ax, dst=dst, dn=dn, bcol=bcol):
                                    pi = nxt("ps", NPS)
                                    pk = ("ps", pi)
                                    mm_group(fw, pst[pi][:, 0:tw], [(wax[:, ax * 2 + d, :], ucb[:, t0:t0 + tw])],
                                             reads=["wax", K("ucb", k)], writes=[pk])
                                    fw.op("act", lambda e: e.activation(out=dst[:, t0:t0 + tw], in_=pst[pi][:, 0:tw], func=AF.Sigmoid,
                                                                        bias=vec[:, bcol:bcol + 1], scale=1.0),
                                          reads=[pk, "vec"], writes=[K(dn, k)])
                                emit(one)
                        emit(fw.op, "act", lambda e: e.activation(out=t1[:, cs], in_=t1[:, cs], func=AF.Exp, scale=nsp[:, d:d + 1]),
                             reads=[K("t1", k), "nsp"], writes=[K("t1", k)])
                        emit(fw.op, "pool", lambda e: e.tensor_tensor(out=t2[:, cs], in0=t2[:, cs], in1=uc[:, cs], op=ALU.mult),
                             reads=[K("t2", k), K("uc", k)], writes=[K("t2", k)])
                        emit(fw.op, "dve", lambda e: e.tensor_tensor(out=t3[:, cs], in0=t1[:, cs], in1=t1[:, cs], op=ALU.mult),
                             reads=[K("t1", k)], writes=[K("t3", k)])
                        emit(fw.op, "act", lambda e: e.activation(out=t3[:, cs], in_=t3[:, cs], func=AF.Sqrt, bias=c1[:, 0:1], scale=-1.0),
                             reads=[K("t3", k), "cE"], writes=[K("t3", k)])
                        emit(fw.op, "dve", lambda e: e.tensor_tensor(out=t2[:, cs], in0=t2[:, cs], in1=t3[:, cs], op=ALU.mult),
                             reads=[K("t2", k), K("t3", k)], writes=[K("t2", k)])
                    stage_major(gate_body)
                    dst = hacc if d == 0 else t3
                    dn = "hacc" if d == 0 else "t3"
                    order = [4, 0, 1, 2, 3] if d == 0 else [4, 3, 2, 1, 0]
                    prev = None
                    for k in order:
                        c0, cE = BL[k]
                        if d == 0:
                            init = 0.0 if prev is None else dst[:, BL[prev][1] - 1:BL[prev][1]]
                            o_ap, a_ap, b_ap = dst[:, c0:cE], t1[:, c0:cE], t2[:, c0:cE]
                        else:
                            init = 0.0 if prev is None else dst[:, BL[prev][0]:BL[prev][0] + 1]
                            o_ap, a_ap, b_ap = dst[:, c0:cE][:, ::-1], t1[:, c0:cE][:, ::-1], t2[:, c0:cE][:, ::-1]
                        fw.op("dve", lambda e, o_ap=o_ap, a_ap=a_ap, b_ap=b_ap, init=init: e.tensor_tensor_scan(
                            out=o_ap, data0=a_ap, data1=b_ap, initial=init, op0=ALU.mult, op1=ALU.add),
                            reads=[K("t1", k), K("t2", k)] + ([K(dn, prev)] if prev is not None else []), writes=[K(dn, k)])
                        if d == 1:
                            fw.op("pool", lambda e, c0=c0, cE=cE: e.tensor_tensor(out=hacc[:, c0:cE], in0=hacc[:, c0:cE], in1=t3[:, c0:cE], op=ALU.add),
                                  reads=[K("hacc", k), K("t3", k)], writes=[K("hacc", k)])
                        prev = k

                def out_body(k, emit):
                    c0, cE = BL[k]
                    cs = slice(c0, cE)
                    emit(fw.op, "dve", lambda e: e.tensor_tensor(out=ucb[:, cs], in0=tgel[:, cs], in1=hacc[:, cs], op=ALU.mult),
                         reads=[K("hacc", k), K("tgel", k), K("ucb", k)], writes=[K("ucb", k)])
                    emit(fw.dma, "sp", dy[0:128, b, cs], ucb[:, cs], reads=[K("ucb", k)])
                stage_major(out_body)
            fw.barrier()

        with ExitStack() as es2:
            def sb2(name, shape, dt):
                return es2.enter_context(nc.sbuf_tensor("m_" + name, list(shape), dt))
            cs = sb2("cssb", [64, 2, SEQ], F32)
            qTn = sb2("qTn", [128, TS], BF16)
            qTr = sb2("qTr", [64, TS], BF16)
            kTn = sb2("kTn", [128, TS], BF16)
            kTr = sb2("kTr", [64, TS], BF16)
            V = sb2("V", [128, NKT, 128], BF16)
            cqt = [sb2("cqt%d" % i, [128, 4, 512], F32) for i in range(2)]
            ckt = [sb2("ckt%d" % i, [128, 4, 512], F32) for i in range(2)]
            krt = [sb2("krt%d" % i, [64, 512], F32) for i in range(2)]
            sq = [sb2("sq%d" % i, [128, 4, 512], BF16) for i in range(2)]
            cqn = [sb2("cqn%d" % i, [128, 4, 512], BF16) for i in range(2)]
            ckn = [sb2("ckn%d" % i, [128, 4, 512], BF16) for i in range(2)]
            rs = [sb2("rs%d" % i, [128, 512], F32) for i in range(2)]
            qrs = [sb2("qrs%d" % i, [64, 512], F32) for i in range(2)]
            ra = [sb2("ra%d" % i, [64, 512], F32) for i in range(2)]
            rb = [sb2("rb%d" % i, [64, 512], F32) for i in range(2)]
            PT = [sb2("PT%d" % i, [128, 512], BF16) for i in range(6)]
            accD = [sb2("accD%d" % i, [128, 512], F32) for i in range(2)]
            accP = [sb2("accP%d" % i, [128, 512], F32) for i in range(2)]
            ones_f = sb2("ones_f", [128, 128], F32)
            fw.op("dve", lambda e: e.memset(ones_f[:], 1.0), writes=["ones_f"])
            rden = [sb2("rden%d" % i, [128, 512], F32) for i in range(2)]
            ob = [sb2("ob%d" % i, [128, 512], BF16) for i in range(2)]
            fw.dma("sp", cs[:], dcs.rearrange("a p t -> p a t"), writes=["cs"])

            def rope(src_ap, srck, t0, tw, dst_ap, dstk):
                pi = nxt("ps", NPS)
                pk = ("ps", pi)
                mm_group(fw, pst[pi][0:64, 0:tw], [(rm[:], src_ap)], reads=["rm", srck], writes=[pk])
                i = nxt("ra", 2)
                fw.op("dve", lambda e: e.tensor_tensor(out=ra[i][:, 0:tw], in0=src_ap, in1=cs[:, 0, t0:t0 + tw], op=ALU.mult),
                      reads=[srck, "cs"], writes=[("ra", i)])
                fw.op("dve", lambda e: e.tensor_tensor(out=rb[i][:, 0:tw], in0=pst[pi][0:64, 0:tw], in1=cs[:, 1, t0:t0 + tw], op=ALU.mult),
                      reads=[pk, "cs"], writes=[("rb", i)])
                fw.op("dve", lambda e: e.tensor_tensor(out=dst_ap, in0=ra[i][:, 0:tw], in1=rb[i][:, 0:tw], op=ALU.add),
                      reads=[("ra", i), ("rb", i)], writes=[dstk])

            for b in range(B):
                for (t0, tw) in TL:
                    is_ctx = t0 >= SEQ
                    i = nxt("tile", 2)
                    fw.dma("sp", cqt[i][:, :, 0:tw], dcq.rearrange("(kc p) b n -> p kc b n", p=128)[:, :, b, t0:t0 + tw], writes=[("cqt", i)])
                    fw.dma("act", ckt[i][:, :, 0:tw], dckv.rearrange("(kc p) b n -> p kc b n", p=128)[:, :, b, t0:t0 + tw], writes=[("ckt", i)])
                    fw.dma("sp", krt[i][:, 0:tw], dkr[:, b, t0:t0 + tw], writes=[("krt", i)])
                    for (src, srck, ncol, dst, dstk) in ((cqt[i], ("cqt", i), 11, cqn[i], ("cqn", i)), (ckt[i], ("ckt", i), 15, ckn[i], ("ckn", i))):
                        j = nxt("sq", 2)
                        fw.op("act", lambda e, src=src, j=j: e.activation(out=sq[j][:, :, 0:tw], in_=src[:, :, 0:tw], func=AF.Square),
                              reads=[srck], writes=[("sq", j)])
                        pi = nxt("ps", NPS)
                        pk = ("ps", pi)
                        mm_group(fw, pst[pi][:, 0:tw], [(ones[:], sq[j][:, kc, 0:tw]) for kc in range(4)],
                                 reads=["ones", ("sq", j)], writes=[pk])
                        r = nxt("rs", 2)
                        fw.op("act", lambda e, pi=pi, r=r: e.activation(out=rs[r][:, 0:tw], in_=pst[pi][:, 0:tw], func=AF.Ln,
                                                                      bias=c1[:, 1:2], scale=1.0 / 512), reads=[pk, "c1"], writes=[("rs", r)])
                        fw.op("act", lambda e, r=r: e.activation(out=rs[r][:, 0:tw], in_=rs[r][:, 0:tw], func=AF.Exp, scale=-0.5),
                              reads=[("rs", r)], writes=[("rs", r)])
                        for kc in range(4):
                            fw.op("dve", lambda e, src=src, dst=dst, kc=kc, r=r, ncol=ncol: e.scalar_tensor_tensor(
                                out=dst[:, kc, 0:tw], in0=src[:, kc, 0:tw], scalar=vec[:, ncol + kc:ncol + kc + 1], in1=rs[r][:, 0:tw],
                                op0=ALU.mult, op1=ALU.mult), reads=[srck, "vec", ("rs", r)], writes=[dstk])
                    pi = nxt("ps", NPS); pk = ("ps", pi)
                    mm_group(fw, pst[pi][:, 0:tw], [(wuq[:, kc, 0:128], cqn[i][:, kc, 0:tw]) for kc in range(4)],
                             reads=["wuq", ("cqn", i)], writes=[pk])
                    fw.op("act", lambda e, pi=pi: e.activation(out=qTn[:, t0:t0 + tw], in_=pst[pi][:, 0:tw], func=AF.Copy),
                          reads=[pk], writes=["qTn"])
                    pi = nxt("ps", NPS); pk = ("ps", pi)
                    mm_group(fw, pst[pi][0:64, 0:tw], [(wuq[:, kc, 128:192], cqn[i][:, kc, 0:tw]) for kc in range(4)],
                             reads=["wuq", ("cqn", i)], writes=[pk])
                    if is_ctx:
                        fw.op("act", lambda e, pi=pi: e.activation(out=qTr[:, t0:t0 + tw], in_=pst[pi][0:64, 0:tw], func=AF.Copy),
                              reads=[pk], writes=["qTr"])
                    else:
                        q = nxt("qrs", 2)
                        fw.op("act", lambda e, pi=pi, q=q: e.activation(out=qrs[q][:, 0:tw], in_=pst[pi][0:64, 0:tw], func=AF.Copy),
                              reads=[pk], writes=[("qrs", q)])
                        rope(qrs[q][:, 0:tw], ("qrs", q), t0, tw, qTr[:, t0:t0 + tw], "qTr")
                    pi = nxt("ps", NPS); pk = ("ps", pi)
                    mm_group(fw, pst[pi][:, 0:tw], [(wukv[:, kc, 0:128], ckn[i][:, kc, 0:tw]) for kc in range(4)],
                             reads=["wukv", ("ckn", i)], writes=[pk])
                    fw.op("act", lambda e, pi=pi: e.activation(out=kTn[:, t0:t0 + tw], in_=pst[pi][:, 0:tw], func=AF.Copy),
                          reads=[pk], writes=["kTn"])
                    for sub in range(tw // 128):
                        pi = nxt("ps", NPS); pk = ("ps", pi)
                        mm_group(fw, pst[pi][:, 0:128], [(ckn[i][:, kc, sub * 128:(sub + 1) * 128], wukv[:, kc, 128:256]) for kc in range(4)],
                                 reads=["wukv", ("ckn", i)], writes=[pk])
                        fw.op("dve", lambda e, pi=pi, sub=sub: e.tensor_copy(out=V[:, t0 // 128 + sub, :], in_=pst[pi][:, 0:128]),
                              reads=[pk], writes=["V"])
                    if is_ctx:
                        fw.op("act", lambda e, i=i: e.activation(out=kTr[:, t0:t0 + tw], in_=krt[i][:, 0:tw], func=AF.Copy),
                              reads=[("krt", i)], writes=["kTr"])
                    else:
                        rope(krt[i][:, 0:tw], ("krt", i), t0, tw, kTr[:, t0:t0 + tw], "kTr")
                SKEW = 2
                for (t0, tw) in TL:
                    is_ctx = t0 >= SEQ
                    kts = list(range(SEQ // 128, NKT)) if is_ctx else list(range(NKT))
                    oi = 4 + nxt("po", 2)
                    di = 6 + nxt("pd", 2)
                    ok, dk = ("ps", oi), ("ps", di)
                    ai = nxt("acc", 2)
                    aD, aP = accD[ai], accP[ai]
                    aDk, aPk = ("accD", ai), ("accP", ai)
                    pend = []

                    def consume(item):
                        n, kt, p = item
                        first, last = (n == 0), (n == len(kts) - 1)
                        fw.op("pe", lambda e: e.matmul(pst[oi][:, 0:tw], V[:, kt, :], PT[p][:, 0:tw], start=first, stop=last),
                              reads=["V", ("PT", p)], writes=[ok], nosame=not first)
                        if n % 2 == 0:
                            if n == 0:
                                fw.op("dve", lambda e: e.tensor_copy(out=aD[:, 0:tw], in_=PT[p][:, 0:tw]), reads=[("PT", p)], writes=[aDk])
                            else:
                                fw.op("dve", lambda e: e.tensor_tensor(out=aD[:, 0:tw], in0=aD[:, 0:tw], in1=PT[p][:, 0:tw], op=ALU.add),
                                      reads=[("PT", p), aDk], writes=[aDk])
                        else:
                            if n == 1:
                                fw.op("pool", lambda e: e.tensor_copy(out=aP[:, 0:tw], in_=PT[p][:, 0:tw]), reads=[("PT", p)], writes=[aPk])
                            else:
                                fw.op("pool", lambda e: e.tensor_tensor(out=aP[:, 0:tw], in0=aP[:, 0:tw], in1=PT[p][:, 0:tw], op=ALU.add),
                                      reads=[("PT", p), aPk], writes=[aPk])

                    for n, kt in enumerate(kts):
                        si = nxt("psS", 4)
                        sk = ("ps", si)
                        mm_group(fw, pst[si][:, 0:tw],
                                 [(kTn[:, kt * 128:(kt + 1) * 128], qTn[:, t0:t0 + tw]), (kTr[:, kt * 128:(kt + 1) * 128], qTr[:, t0:t0 + tw])],
                                 reads=["kTn", "qTn", "kTr", "qTr"], writes=[sk])
                        p = nxt("PT", 6)
                        fw.op("act", lambda e, si=si, p=p: e.activation(out=PT[p][:, 0:tw], in_=pst[si][:, 0:tw], func=AF.Exp, scale=ATTN_SCALE),
                              reads=[sk], writes=[("PT", p)])
                        pend.append((n, kt, p))
                        if len(pend) > SKEW:
                            consume(pend.pop(0))
                    while pend:
                        consume(pend.pop(0))
                    fw.op("dve", lambda e: e.tensor_tensor(out=aD[:, 0:tw], in0=aD[:, 0:tw], in1=aP[:, 0:tw], op=ALU.add),
                          reads=[aDk, aPk], writes=[aDk])
                    mm_group(fw, pst[di][:, 0:tw], [(ones_f[:], aD[:, 0:tw])], reads=["ones_f", aDk], writes=[dk])
                    r = nxt("rden", 2)
                    fw.op("dve", lambda e, r=r: e.reciprocal(out=rden[r][:, 0:tw], in_=pst[di][:, 0:tw]), reads=[dk], writes=[("rden", r)])
                    o = nxt("ob", 2)
                    fw.op("dve", lambda e, r=r, o=o: e.tensor_tensor(out=ob[o][:, 0:tw], in0=pst[oi][:, 0:tw], in1=rden[r][:, 0:tw], op=ALU.mult),
                          reads=[ok, ("rden", r)], writes=[("ob", o)])
                    fw.dma("sp", dy[128:256, b, t0:t0 + tw], ob[o][:, 0:tw], reads=[("ob", o)])
        fw.finish()
    return nc


def rope_consts():
    half = 32
    inv_freq = (10000.0 ** (-np.arange(0, half, 2, dtype=np.float32) / half)).astype(np.float32)
    t = np.arange(SEQ)
    row = (t // 64).astype(np.float32)
    col = (t % 64).astype(np.float32)
    ang_r = row[:, None] * inv_freq
    ang_c = col[:, None] * inv_freq
    ang = np.concatenate([ang_r, ang_r, ang_c, ang_c], axis=-1).astype(np.float32)
    cos = np.cos(ang).astype(np.float32)
    sin = np.sin(ang).astype(np.float32)
    sign = np.ones(64, np.float32)
    sign[0:16] = -1.0
    sign[32:48] = -1.0
    perm = np.concatenate([np.arange(16, 32), np.arange(0, 16), np.arange(48, 64), np.arange(32, 48)])
    rm = np.zeros((64, 64), np.float32)
    rm[perm, np.arange(64)] = 1.0
    cs = np.stack([cos.T, (sin * sign).T], 0)
    return np.ascontiguousarray(cs), rm


def ab_host_inputs(pT, e, P):
    cs, rm = rope_consts()
    maps = []
    cq = np.ascontiguousarray(pT[2048:2560])
    ckv = np.ascontiguousarray(pT[2560:3072])
    kr = np.ascontiguousarray(pT[3072:3136])
    for h in range(NCORES):
        sl = slice(h * 128, (h + 1) * 128)
        gu = np.stack([pT[h * 128:(h + 1) * 128], pT[1024 + h * 128:1024 + (h + 1) * 128]], 0)
        vec = np.zeros((128, 24), np.float32)
        vec[:, 0:4] = P["lru_conv_w"][e][:, sl].T
        vec[:, 4] = P["lru_conv_b"][e][sl]
        vec[:, 5:7] = P["lru_b_a"][e][:, sl].T
        vec[:, 7:9] = P["lru_b_x"][e][:, sl].T
        vec[:, 9:11] = P["lru_lambda"][e][:, sl].T
        vec[:, 11:15] = P["mla_q_norm_w"][e].reshape(4, 128).T
        vec[:, 15:19] = P["mla_kv_norm_w"][e].reshape(4, 128).T
        wax = np.stack([P["lru_w_a"][e][:, h], P["lru_w_x"][e][:, h]], 0)
        maps.append({"guT": np.ascontiguousarray(gu), "cqT": cq, "ckvT": ckv, "krT": kr, "vecs": vec,
                     "wax": np.ascontiguousarray(wax),
                     "wuq": np.ascontiguousarray(P["mla_w_uq"][e][:, h * 192:(h + 1) * 192]),
                     "wukv": np.ascontiguousarray(P["mla_w_ukv"][e][:, h * 256:(h + 1) * 256]),
                     "cs": cs, "rm": rm})
    return maps


def ab_host_gather(results):
    y = np.empty((D, B, TS), results[0]["yT"].dtype)
    for h in range(NCORES):
        o = results[h]["yT"]
        y[h * 128:(h + 1) * 128] = o[0:128]
        y[1024 + h * 128:1024 + (h + 1) * 128] = o[128:256]
    return y


def tokT_to_full(arrs):
    F = arrs[0].shape[0]
    full = np.empty((F, B, TS), arrs[0].dtype)
    for k in range(NCORES):
        b, r = core_tok(k)
        full[:, b, r * NL:(r + 1) * NL] = arrs[k][:, :NL]
        full[:, b, SEQ + r * NCX:SEQ + (r + 1) * NCX] = arrs[k][:, NL:]
    return full


def full_to_tokT(full):
    outs = []
    for k in range(NCORES):
        b, r = core_tok(k)
        outs.append(np.ascontiguousarray(np.concatenate(
            [full[:, b, r * NL:(r + 1) * NL], full[:, b, SEQ + r * NCX:SEQ + (r + 1) * NCX]], axis=1)))
    return outs


CH = 64
NCH = TS // CH
NPT = TS // 128


def build_hg(layer):
    nc = bass.Bass("TRN2", target_bir_lowering=False)
    dp = nc.dram_tensor("p5", [2, 5, 128, B, TS], F32, kind="ExternalInput").ap()
    dlb = nc.dram_tensor("lbl", [128, 2, DEPTH], F32, kind="ExternalInput").ap()
    dnw = nc.dram_tensor("nw", [128, 1], F32, kind="ExternalInput").ap()
    dmask = nc.dram_tensor("masks", [2, 128, 128], F32, kind="ExternalInput").ap()
    dident = nc.dram_tensor("ident", [128, 128], F32, kind="ExternalInput").ap()
    dy = nc.dram_tensor("yT", [2, 128, B, TS], BF16, kind="ExternalOutput").ap()

    with ExitStack() as es:
        fw = FW(nc, es)
        names = ["qs", "iE", "oacc", "kk", "lf", "G", "tmp"]
        T = {n: fw.sb("h_" + n, [128, TS], F32) for n in names}
        qs, iE, oacc, kk, lf, G, tmp = (T[n] for n in names)
        qt = [fw.sb("qt%d" % d, [128, TS], BF16) for d in range(2)]
        kt = [fw.sb("kt%d" % d, [128, TS], BF16) for d in range(2)]
        kh = [fw.sb("kh%d" % d, [128, TS], BF16) for d in range(2)]
        Vt = fw.sb("Vt", [128, NPT, 128], BF16)
        ones_b = fw.sb("ones_b", [128, 1088], BF16)
        ones_m = fw.sb("ones_m", [128, 128], BF16)
        masks = fw.sb("masks_sb", [128, 2, 128], F32)
        ident = fw.sb("ident_sb", [128, 128], F32)
        identb = fw.sb("identb_sb", [128, 128], BF16)
        lbl = fw.sb("lbl_sb", [128, 2, DEPTH], F32)
        lbs = fw.sb("lbs", [128, 8], F32)
        nwt = fw.sb("nwt", [128, 1], F32)
        c1 = fw.sb("c1h", [128, 2], F32)
        dec = [fw.sb("dec%d" % d, [128, NCH], F32) for d in range(2)]
        S = [fw.sb("S%d" % d, [128, 128], F32) for d in range(2)]
        Sb = [fw.sb("Sb%d" % d, [128, 128], BF16) for d in range(2)]
        Am = [fw.sb("Am%d" % i, [128, 128], BF16) for i in range(4)]
        khat = [fw.sb("khat%d" % i, [128, 128], BF16) for i in range(4)]
        sqb = [fw.sb("sqb%d" % i, [128, 512], BF16) for i in range(2)]
        rsb = [fw.sb("rsb%d" % i, [128, 512], F32) for i in range(2)]
        NPS = 8
        pst = [fw.ps("ps%d" % i, [128, 512], F32) for i in range(NPS)]
        st = {}

        def nxt(name, n):
            i = st.get(name, 0) % n
            st[name] = st.get(name, 0) + 1
            return i

        def newps():
            pi = nxt("ps", NPS)
            return pi, ("ps", pi)

        fw.dma("sp", masks[:], dmask.rearrange("a s t -> s a t"), writes=["masks"])
        fw.dma("sp", ident[:], dident[:, :], writes=["ident"])
        fw.dma("sp", lbl[:], dlb[:, :, :], writes=["lbl"])
        fw.dma("sp", nwt[:], dnw[:, :], writes=["nwt"])
        fw.op("act", lambda e: e.activation(out=identb[:], in_=ident[:], func=AF.Copy), reads=["ident"], writes=["identb"])
        fw.op("dve", lambda e: e.memset(ones_b[:], 1.0), writes=["ones_b"])
        fw.op("dve", lambda e: e.memset(ones_m[:], 1.0), writes=["ones_m"])
        fw.op("dve", lambda e: e.memset(c1[:, 0:1], 1.0), writes=["c1"])
        fw.op("dve", lambda e: e.memset(c1[:, 1:2], EPS), reads=["c1"], writes=["c1"])
        fw.op("act", lambda e: e.activation(out=lbl[:], in_=lbl[:], func=AF.Exp), reads=["lbl"], writes=["lbl"])
        fw.op("dve", lambda e: e.tensor_reduce(out=lbs[:, 0:2], in_=lbl[:], axis=AX.X, op=ALU.add), reads=["lbl"], writes=["lbs"])
        fw.op("dve", lambda e: e.tensor_reduce(out=lbs[:, 2:4], in_=lbl[:, :, 1:layer + 1], axis=AX.X, op=ALU.add), reads=["lbl", "lbs"], writes=["lbs"])
        fw.op("dve", lambda e: e.reciprocal(out=lbs[:, 0:2], in_=lbs[:, 0:2]), reads=["lbs"], writes=["lbs"])
        fw.op("dve", lambda e: e.tensor_tensor(out=lbs[:, 4:6], in0=lbs[:, 2:4], in1=lbs[:, 0:2], op=ALU.mult), reads=["lbs"], writes=["lbs"])
        fw.op("dve", lambda e: e.tensor_scalar(out=lbs[:, 6:8], in0=lbs[:, 4:6], scalar1=-1.0, scalar2=1.0, op0=ALU.mult, op1=ALU.add),
              reads=["lbs"], writes=["lbs"])

        def v3(t):
            return t[:].rearrange("p (c j) -> p c j", j=CH)

        NBK = 4
        BW = TS // NBK
        CPB = BW // CH

        def K(name, k):
            return (name, k)

        def allk(name):
            return [(name, k) for k in range(NBK)]

        def blk_of(col):
            return col // BW

        def setup(hd, b, d):
            QT, KT, KH, DEC = "qt%d" % d, "kt%d" % d, "kh%d" % d, "dec%d" % d
            G3, E3, tmp3 = v3(G), v3(iE), v3(tmp)
            last_col = CH - 1 if d == 0 else 0
            stages = []

            def blockbody(k, emit):
                c0 = k * BW
                cs = slice(c0, c0 + BW)
                ch = slice(k * CPB, (k + 1) * CPB)
                emit(fw.dma, "sp" if k % 2 == 0 else "act", kk[:, cs], dp[hd, 1 + d, :, b, cs], writes=[K("kk", k)])
                emit(fw.op, "act", lambda e: e.activation(out=kk[:, cs], in_=kk[:, cs], func=AF.Sigmoid), reads=[K("kk", k)], writes=[K("kk", k)])
                emit(fw.op, "pool", lambda e: e.tensor_scalar(out=kk[:, cs], in0=kk[:, cs], scalar1=lbs[:, 6 + hd:7 + hd], scalar2=lbs[:, 4 + hd:5 + hd],
                                                        op0=ALU.mult, op1=ALU.add), reads=[K("kk", k), "lbs"], writes=[K("kk", k)])
                emit(fw.op, "act", lambda e: e.activation(out=lf[:, cs], in_=kk[:, cs], func=AF.Ln), reads=[K("kk", k)], writes=[K("lf", k)])
                emit(fw.op, "pool", lambda e: e.tensor_scalar(out=kk[:, cs], in0=kk[:, cs], scalar1=-1.0, scalar2=1.0, op0=ALU.mult, op1=ALU.add),
                      reads=[K("kk", k)], writes=[K("kk", k)])
                init = 0.0 if k == 0 else G[:, c0 - 1:c0]
                emit(fw.op, "dve", lambda e: e.tensor_tensor_scan(out=G[:, cs], data0=ones_b[:, 0:BW], data1=lf[:, cs], initial=init,
                                                            op0=ALU.mult, op1=ALU.add),
                      reads=["ones_b", K("lf", k)] + ([K("G", k - 1)] if k > 0 else []), writes=[K("G", k)])
                if d == 0:
                    if k == 0:
                        emit(fw.op, "dve", lambda e: e.tensor_copy(out=E3[:, 0:1, :], in_=G3[:, 0:1, :]), reads=[K("G", 0)], writes=[K("iE", 0)])
                        emit(fw.op, "dve", lambda e: e.tensor_tensor(out=E3[:, 1:CPB, :], in0=G3[:, 1:CPB, :],
                                                               in1=G3[:, 0:CPB - 1, CH - 1:CH].broadcast_to([128, CPB - 1, CH]), op=ALU.subtract),
                              reads=[K("G", 0), K("iE", 0)], writes=[K("iE", 0)])
                    else:
                        emit(fw.op, "dve", lambda e: e.tensor_tensor(out=E3[:, ch, :], in0=G3[:, ch, :],
                                                               in1=G3[:, k * CPB - 1:(k + 1) * CPB - 1, CH - 1:CH].broadcast_to([128, CPB, CH]), op=ALU.subtract),
                              reads=[K("G", k), K("G", k - 1)], writes=[K("iE", k)])
                else:
                    emit(fw.op, "dve", lambda e: e.tensor_tensor(out=E3[:, ch, :], in0=G3[:, ch, CH - 1:CH].broadcast_to([128, CPB, CH]),
                                                           in1=G3[:, ch, :], op=ALU.subtract), reads=[K("G", k)], writes=[K("iE", k)])
                    emit(fw.op, "pool", lambda e: e.tensor_tensor(out=iE[:, cs], in0=iE[:, cs], in1=lf[:, cs], op=ALU.add),
                          reads=[K("lf", k), K("iE", k)], writes=[K("iE", k)])
                emit(fw.op, "act", lambda e: e.activation(out=dec[d][:, ch], in_=E3[:, ch, last_col], func=AF.Exp), reads=[K("iE", k)], writes=[K(DEC, k)])
                emit(fw.op, "act", lambda e: e.activation(out=tmp[:, cs], in_=iE[:, cs], func=AF.Exp), reads=[K("iE", k)], writes=[K("tmp", k)])
                emit(fw.op, "dve", lambda e: e.tensor_tensor(out=qt[d][:, cs], in0=qs[:, cs], in1=tmp[:, cs], op=ALU.mult),
                      reads=[K("qs", k), K("tmp", k)], writes=[K(QT, k)])
                emit(fw.op, "act", lambda e: e.activation(out=tmp[:, cs], in_=iE[:, cs], func=AF.Exp, scale=-1.0), reads=[K("iE", k)], writes=[K("tmp", k)])
                emit(fw.op, "pool", lambda e: e.tensor_tensor(out=kt[d][:, cs], in0=kk[:, cs], in1=tmp[:, cs], op=ALU.mult),
                      reads=[K("kk", k), K("tmp", k)], writes=[K(KT, k)])
                emit(fw.op, "dve", lambda e: e.tensor_tensor(out=tmp3[:, ch, :], in0=E3[:, ch, last_col:last_col + 1].broadcast_to([128, CPB, CH]),
                                                       in1=E3[:, ch, :], op=ALU.subtract), reads=[K("iE", k)], writes=[K("tmp", k)])
                emit(fw.op, "act", lambda e: e.activation(out=tmp[:, cs], in_=tmp[:, cs], func=AF.Exp), reads=[K("tmp", k)], writes=[K("tmp", k)])
                emit(fw.op, "dve", lambda e: e.tensor_tensor(out=kh[d][:, cs], in0=tmp[:, cs], in1=kk[:, cs], op=ALU.mult),
                      reads=[K("tmp", k), K("kk", k)], writes=[K(KH, k)])


            per_block = []
            for k in range(NBK):
                lst = []
                blockbody(k, lambda f, *a, **kw: lst.append((f, a, kw)))
                per_block.append(lst)
            nst = max(len(l) for l in per_block)
            for j in range(nst):
                for k in range(NBK):
                    if j < len(per_block[k]):
                        f, a, kw = per_block[k][j]
                        f(*a, **kw)

        def pair_pre(d, pt):
            bks = sorted({blk_of(pt * 128), blk_of(pt * 128 + 127)})
            QT = [K("qt%d" % d, x) for x in bks]
            KT = [K("kt%d" % d, x) for x in bks]
            KH = [K("kh%d" % d, x) for x in bks]
            cols = slice(pt * 128, (pt + 1) * 128)
            pi, pk = newps()
            mm_group(fw, pst[pi][:, 0:128], [(kt[d][:, cols], qt[d][:, cols])], reads=KT + QT, writes=[pk])
            a = nxt("Am", 4)
            fw.op("dve", lambda e: e.tensor_tensor(out=Am[a][:], in0=pst[pi][:, 0:128], in1=masks[:, d, :], op=ALU.mult),
                  reads=[pk, "masks"], writes=[("Am", a)])
            pi2, pk2 = newps()
            mm_group(fw, pst[pi2][:, 0:128], [(kh[d][:, cols], identb[:])], reads=KH + ["identb"], writes=[pk2])
            kx = nxt("khat", 4)
            fw.op("act", lambda e: e.activation(out=khat[kx][:], in_=pst[pi2][:, 0:128], func=AF.Copy), reads=[pk2], writes=[("khat", kx)])
            return a, kx

        def half_step(d, pt, hf, a, kx):
            c = pt * 2 + hf
            bk = blk_of(c * CH)
            QT, DEC, SK, SBK = K("qt%d" % d, bk), K("dec%d" % d, bk), ("S", d), ("Sb", d)
            rows = slice(hf * 64, (hf + 1) * 64)
            tcols = slice(c * CH, (c + 1) * CH)
            pi3, pk3 = newps()
            mm_group(fw, pst[pi3][:, 0:CH], [(Vt[rows, pt, :], Am[a][rows, hf * 64:(hf + 1) * 64]), (Sb[d][:], qt[d][:, tcols])],
                     reads=["Vt", ("Am", a), SBK, QT], writes=[pk3])
            fw.op("dve", lambda e: e.tensor_tensor(out=oacc[:, tcols], in0=pst[pi3][:, 0:CH], in1=oacc[:, tcols], op=ALU.add),
                  reads=[pk3, "oacc"], writes=["oacc"])
            pi4, pk4 = newps()
            mm_group(fw, pst[pi4][:, 0:128], [(khat[kx][rows, :], Vt[rows, pt, :])], reads=[("khat", kx), "Vt"], writes=[pk4])
            fw.op("dve", lambda e: e.scalar_tensor_tensor(out=S[d][:], in0=S[d][:], scalar=dec[d][:, c:c + 1], in1=pst[pi4][:, 0:128],
                                                          op0=ALU.mult, op1=ALU.add), reads=[SK, DEC, pk4], writes=[SK])
            fw.op("act", lambda e: e.activation(out=Sb[d][:], in_=S[d][:], func=AF.Copy), reads=[SK], writes=[SBK])

        ptiles = [list(range(SEQ // 128, NPT)) + list(range(SEQ // 128)), list(range(NPT - 1, -1, -1))]
        for hd in range(2):
            for b in range(B):
                for k in range(NBK):
                    cs = slice(k * BW, (k + 1) * BW)
                    fw.dma("sp", qs[:, cs], dp[hd, 0, :, b, cs], writes=[K("qs", k)])
                    fw.dma("act", iE[:, cs], dp[hd, 3, :, b, cs], writes=[K("iE", k)])
                    fw.op("act", lambda e: e.activation(out=tmp[:, cs], in_=qs[:, cs], func=AF.Sigmoid), reads=[K("qs", k)], writes=[K("tmp", k)])
                    fw.op("pool", lambda e: e.tensor_tensor(out=qs[:, cs], in0=qs[:, cs], in1=tmp[:, cs], op=ALU.mult),
                          reads=[K("qs", k), K("tmp", k)], writes=[K("qs", k)])
                    fw.op("act", lambda e: e.activation(out=kh[1][:, cs], in_=iE[:, cs], func=AF.Copy), reads=[K("iE", k)], writes=[K("kh1", k)])
                for pt in range(NPT):
                    pi, pk = newps()
                    mm_group(fw, pst[pi][:, 0:128], [(kh[1][:, pt * 128:(pt + 1) * 128], identb[:])],
                             reads=[K("kh1", blk_of(pt * 128)), K("kh1", blk_of(pt * 128 + 127)), "identb"], writes=[pk])
                    if pt % 2 == 0:
                        fw.op("act", lambda e, pi=pi, pt=pt: e.activation(out=Vt[:, pt, :], in_=pst[pi][:, 0:128], func=AF.Copy), reads=[pk], writes=["Vt"])
                    else:
                        fw.op("dve", lambda e, pi=pi, pt=pt: e.tensor_copy(out=Vt[:, pt, :], in_=pst[pi][:, 0:128]), reads=[pk], writes=["Vt"])
                fw.op("pool", lambda e: e.memset(oacc[:], 0.0), reads=["oacc"], writes=["oacc"])
                for d in range(2):
                    setup(hd, b, d)
                    fw.op("dve", lambda e, d=d: e.memset(S[d][:], 0.0), reads=[("S", d)], writes=[("S", d)])
                    fw.op("dve", lambda e, d=d: e.memset(Sb[d][:], 0.0), reads=[("Sb", d)], writes=[("Sb", d)])
                for step in range(NPT):
                    pre = [pair_pre(d, ptiles[d][step]) for d in range(2)]
                    for hi in range(2):
                        for d in range(2):
                            hf = (0, 1)[hi] if d == 0 else (1, 0)[hi]
                            half_step(d, ptiles[d][step], hf, *pre[d])
                fw.dma("sp", kk[:], dp[hd, 4, :, b, :], writes=allk("kk"))
                fw.op("act", lambda e: e.activation(out=lf[:], in_=kk[:], func=AF.Sigmoid), reads=allk("kk"), writes=allk("lf"))
                fw.op("pool", lambda e: e.tensor_tensor(out=kk[:], in0=kk[:], in1=lf[:], op=ALU.mult), reads=allk("kk") + allk("lf"), writes=allk("kk"))
                for (t0, tw) in TL:
                    j = nxt("sqb", 2)
                    fw.op("act", lambda e, j=j, t0=t0, tw=tw: e.activation(out=sqb[j][:, 0:tw], in_=oacc[:, t0:t0 + tw], func=AF.Square),
                          reads=["oacc"], writes=[("sqb", j)])
                    pi, pk = newps()
                    mm_group(fw, pst[pi][:, 0:tw], [(ones_m[:], sqb[j][:, 0:tw])], reads=["ones_m", ("sqb", j)], writes=[pk])
                    r = nxt("rsb", 2)
                    fw.op("act", lambda e, pi=pi, r=r, tw=tw: e.activation(out=rsb[r][:, 0:tw], in_=pst[pi][:, 0:tw], func=AF.Ln, bias=c1[:, 1:2], scale=1.0 / 128),
                          reads=[pk, "c1"], writes=[("rsb", r)])
                    fw.op("act", lambda e, r=r, tw=tw: e.activation(out=rsb[r][:, 0:tw], in_=rsb[r][:, 0:tw], func=AF.Exp, scale=-0.5),
                          reads=[("rsb", r)], writes=[("rsb", r)])
                    fw.op("dve", lambda e, r=r, t0=t0, tw=tw: e.scalar_tensor_tensor(
                        out=G[:, t0:t0 + tw], in0=oacc[:, t0:t0 + tw], scalar=nwt[:, 0:1], in1=rsb[r][:, 0:tw], op0=ALU.mult, op1=ALU.mult),
                        reads=["oacc", "nwt", ("rsb", r)] + allk("G"), writes=allk("G"))
                fw.op("pool", lambda e: e.tensor_tensor(out=kh[0][:], in0=G[:], in1=kk[:], op=ALU.mult), reads=allk("G") + allk("kk") + allk("kh0"), writes=allk("kh0"))
                fw.dma("sp", dy[hd, :, b, :], kh[0][:], reads=allk("kh0"))
        fw.finish()
    return nc


def hg_consts():
    s = np.arange(128)[:, None]
    t = np.arange(128)[None, :]
    same = (s // CH) == (t // CH)
    mf = (same & (t >= s)).astype(np.float32)
    mb = (same & (t <= s)).astype(np.float32)
    return np.stack([mf, mb], 0), np.eye(128, dtype=np.float32)


def hg_host_inputs(pT, o, P):
    masks, ident = hg_consts()
    maps = []
    for k in range(NCORES):
        p5 = np.empty((2, 5, 128, B, TS), np.float32)
        lbl = np.empty((128, 2, DEPTH), np.float32)
        for j in range(2):
            hh = 2 * k + j
            for s in range(5):
                p5[j, s] = pT[s * D + hh * 128:s * D + (hh + 1) * 128]
            lbl[:, j, :] = P["hg_lb_logits"][:, hh * 128:(hh + 1) * 128].T
        maps.append({"p5": p5, "lbl": lbl, "nw": np.ascontiguousarray(P["hg_norm_w"][o].reshape(128, 1)),
                     "masks": masks, "ident": ident})
    return maps


def hg_host_gather(results):
    y = np.empty((D, B, TS), results[0]["yT"].dtype)
    for k in range(NCORES):
        o = results[k]["yT"]
        for j in range(2):
            hh = 2 * k + j
            y[hh * 128:(hh + 1) * 128] = o[j]
    return y


def kernel(**inp):
    P = {k: np.asarray(v, dtype=np.float32) for k, v in inp.items()}
    x, ctx = P["x"], P["ctx"]
    m = ada_host_gather(run(get_nc("ada", build_ada), ada_host_inputs(P["c"], P["c_ctx"], P["ada_w"], P["ada_b"])))
    xs = to_core_T(x, ctx)

    def tok_maps(l_post, l_pre, xs, ys):
        maps = []
        nmlp = vec_pk(P["norm_mlp_w"][l_post]) if l_post is not None else np.zeros((128, KC), np.float32)
        npre = vec_pk(P["norm_mix_w"][l_pre]) if l_pre < DEPTH else vec_pk(P["final_norm_w"])
        nrm = np.ascontiguousarray(np.stack([nmlp, npre], 1))
        for k in range(NCORES):
            b, r = core_tok(k)
            d = {"xT": xs[k], "adaL": ada_sets(m, l_post, l_pre, b), "adaC": ada_sets(m, l_post, l_pre, 2), "nrm": nrm}
            if l_post is not None:
                d["yT"] = ys[k]
                d["w_out"] = P["ab_w_out"][l_post // 2] if l_post % 2 == 0 else P["hg_w_out"][l_post // 2]
                d["w1"] = P["mlp_w1"][l_post]
                d["w2"] = P["mlp_w2"][l_post]
            if l_pre < DEPTH:
                d["w_in"] = P["ab_w_in"][l_pre // 2] if l_pre % 2 == 0 else P["hg_w_in"][l_pre // 2]
            maps.append(d)
        return maps

    res = run(get_nc("tok", build_tok, False, "ab"), tok_maps(None, 0, xs, None))
    for l in range(DEPTH):
        pT = tokT_to_full([r["pT"] for r in res])
        xs = [r["xoT"] for r in res]
        if l % 2 == 0:
            y = ab_host_gather(run(get_nc("ab", build_ab), ab_host_inputs(pT, l // 2, P)))
        else:
            y = hg_host_gather(run(get_nc("hg", build_hg, l), hg_host_inputs(pT, l // 2, P)))
        ys = full_to_tokT(y)
        pre = "final" if l + 1 == DEPTH else ("ab" if (l + 1) % 2 == 0 else "hg")
        res = run(get_nc("tok", build_tok, True, pre), tok_maps(l, l + 1, xs, ys))
    lat, _ = from_core_T([r["outT"] for r in res])
    return np.ascontiguousarray(lat.astype(np.float32))
```

```python
import numpy as np
from contextlib import ExitStack
import concourse.bass as bass
import concourse.mybir as mybir
from concourse.bass_utils import run_bass_kernel_spmd

F32 = mybir.dt.float32
BF16 = mybir.dt.bfloat16
AF = mybir.ActivationFunctionType
ALU = mybir.AluOpType
AX = mybir.AxisListType

D = 2048
KC = D // 128
B = 2
SEQ = 4096
CTX = 256
DEPTH = 4
DFF = 8192
NCORES = 8
NL = B * SEQ // NCORES
NCX = B * CTX // NCORES
NT = NL + NCX
EPS = 1e-6
D_AB_IN = 3136
D_HG_IN = 10240
TS = SEQ + CTX

SAME_ENG_SYNC = True


class Rec:
    def __init__(self):
        self.calls = []

    def __getattr__(self, name):
        def f(*a, **k):
            self.calls.append((name, a, k))
            return self
        return f


class FW:
    ENGS = ("pe", "dve", "act", "pool", "sp")

    def __init__(self, nc, es, n_dma_sems=48):
        self.nc = nc
        self.es = es
        self.eng = {"pe": nc.tensor, "dve": nc.vector, "act": nc.scalar, "pool": nc.gpsimd, "sp": nc.sync}
        self.sem = {e: es.enter_context(nc.semaphore("s_" + e)) for e in self.ENGS}
        self.cnt = {e: 0 for e in self.ENGS}
        self.prog = {e: [] for e in self.ENGS}
        self.seen = {e: {} for e in self.ENGS}
        self.dsem = [es.enter_context(nc.semaphore("d%d" % i)) for i in range(n_dma_sems)]
        self.dval = [0] * n_dma_sems
        self.drr = 0
        self.last_w = {}
        self.readers = {}
        self.semobj = {}
        self.n_ps = 0

    def sb(self, name, shape, dt):
        return self.es.enter_context(self.nc.sbuf_tensor(name, list(shape), dt))

    def ps(self, name, shape, dt=F32):
        return self.es.enter_context(self.nc.psum_tensor(name, list(shape), dt))

    def _collect(self, eng, reads, writes, nosame=False):
        toks = []
        for k in reads:
            t = self.last_w.get(k)
            if t is not None:
                toks.append(t)
        for k in writes:
            t = self.last_w.get(k)
            if t is not None:
                toks.append(t)
            toks.extend(self.readers.get(k, ()))
        need = {}
        for (sid, val) in toks:
            if ((not SAME_ENG_SYNC) or nosame) and sid == ("e", eng):
                continue
            if self.seen[eng].get(sid, 0) >= val:
                continue
            if need.get(sid, 0) < val:
                need[sid] = val
        for sid, val in need.items():
            self.seen[eng][sid] = val
        return list(need.items())

    def _commit(self, tok, reads, writes):
        for k in writes:
            self.last_w[k] = tok
            self.readers[k] = []
        for k in reads:
            if k in writes:
                continue
            self.readers.setdefault(k, []).append(tok)

    def _semof(self, sid):
        kind, x = sid
        return self.sem[x] if kind == "e" else self.dsem[x]

    def op(self, eng, fn, reads=(), writes=(), nosame=False):
        waits = self._collect(eng, reads, writes, nosame)
        self.cnt[eng] += 1
        tok = (("e", eng), self.cnt[eng])
        rec = Rec()
        fn(rec)
        calls = rec.calls

        def run(e, calls=calls):
            ins = None
            for (name, a, k) in calls:
                ins = getattr(e, name)(*a, **k)
            return ins
        self.prog[eng].append((waits, run, ("e", eng), 1))
        self._commit(tok, reads, writes)
        return tok

    def dma(self, eng, out, in_, reads=(), writes=(), **kw):
        waits = self._collect(eng, reads, writes)
        i = self.drr
        self.drr = (self.drr + 1) % len(self.dsem)
        sid = ("d", i)
        if self.dval[i] > 0 and self.seen[eng].get(sid, 0) < self.dval[i]:
            waits.append((sid, self.dval[i]))
            self.seen[eng][sid] = self.dval[i]
        self.dval[i] += 16
        tok = (sid, self.dval[i])
        self.prog[eng].append((waits, (lambda e, out=out, in_=in_, kw=kw: e.dma_start(out=out, in_=in_, **kw)), sid, 16))
        self._commit(tok, reads, writes)
        return tok

    def barrier(self):
        allw = [(("e", e), self.cnt[e]) for e in self.ENGS if self.cnt[e] > 0]
        allw += [(("d", i), v) for i, v in enumerate(self.dval) if v > 0]
        for e in self.ENGS:
            waits = []
            for (sid, val) in allw:
                if sid == ("e", e):
                    continue
                if self.seen[e].get(sid, 0) < val:
                    waits.append((sid, val))
                    self.seen[e][sid] = val
            if waits:
                self.prog[e].append((waits, None, None, 0))

    def finish(self, eng="sp"):
        waits = []
        for i, v in enumerate(self.dval):
            if v > 0:
                waits.append((("d", i), v))
        self.prog[eng].append((waits, None, None, 0))
        block = self.es.enter_context(self.nc.Block())

        def replay(name):
            def body(e):
                for (waits, fn, sid, inc) in self.prog[name]:
                    for (wsid, val) in waits:
                        e.wait_ge(self._semof(wsid), val)
                    if fn is not None:
                        ins = fn(e)
                        ins.then_inc(self._semof(sid), inc)
            return body

        block.tensor(replay("pe"))
        block.vector(replay("dve"))
        block.scalar(replay("act"))
        block.gpsimd(replay("pool"))
        block.sync(replay("sp"))


def mm_group(fw, out_ps, pairs, reads, writes):
    n = len(pairs)

    def fn(e):
        ins = None
        for i, (l, r) in enumerate(pairs):
            ins = e.matmul(out_ps, l, r, start=(i == 0), stop=(i == n - 1))
        return ins
    return fw.op("pe", fn, reads=reads, writes=writes)


ADA_COLS = 6 * D // NCORES
ADA_J = ADA_COLS // 128


def build_ada():
    nc = bass.Bass("TRN2", target_bir_lowering=False)
    cv = nc.dram_tensor("cv", [128, KC, 3], F32, kind="ExternalInput").ap()
    w = nc.dram_tensor("w", [DEPTH, D, ADA_COLS], F32, kind="ExternalInput").ap()
    bb = nc.dram_tensor("b", [128, DEPTH, ADA_J], F32, kind="ExternalInput").ap()
    out = nc.dram_tensor("out", [128, DEPTH, ADA_J, 3], F32, kind="ExternalOutput").ap()
    with ExitStack() as es:
        fw = FW(nc, es)
        cvt = fw.sb("cvt", [128, KC, 3], F32)
        sg = fw.sb("sg", [128, KC, 3], F32)
        sv = fw.sb("sv", [128, KC, 3], F32)
        bt = fw.sb("bt", [128, DEPTH, ADA_J], F32)
        ot = fw.sb("ot", [128, DEPTH, ADA_J, 3], F32)
        NB = 4
        wt = [fw.sb("wt%d" % i, [128, KC, 512], F32) for i in range(NB)]
        pst = [fw.ps("ps%d" % i, [128, 512], F32) for i in range(4)]
        fw.dma("sp", cvt[:], cv[:, :, :], writes=["cvt"])
        fw.dma("sp", bt[:], bb[:, :, :], writes=["bt"])
        fw.op("act", lambda e: e.activation(out=sg[:], in_=cvt[:], func=AF.Sigmoid), reads=["cvt"], writes=["sg"])
        fw.op("dve", lambda e: e.tensor_tensor(out=sv[:], in0=cvt[:], in1=sg[:], op=ALU.mult), reads=["cvt", "sg"], writes=["sv"])
        it = 0
        for l in range(DEPTH):
            for g in range(ADA_COLS // 512):
                wb = wt[it % NB]
                wk = "wt%d" % (it % NB)
                src = w[l].rearrange("(kc p) n -> p kc n", p=128)[:, :, g * 512:(g + 1) * 512]
                q = ("sp", "act", "pool")[it % 3]
                fw.dma(q, wb[:, 0:KC // 2, :], src[:, 0:KC // 2, :], writes=[wk])
                fw.dma(("act", "pool", "sp")[it % 3], wb[:, KC // 2:KC, :], src[:, KC // 2:KC, :], reads=[wk], writes=[wk])
                for jj in range(4):
                    j = g * 4 + jj
                    pi = (it * 4 + jj) % 4
                    pk = "ps%d" % pi
                    mm_group(fw, pst[pi][:, 0:3],
                             [(wb[:, kc, jj * 128:(jj + 1) * 128], sv[:, kc, :]) for kc in range(KC)],
                             reads=[wk, "sv"], writes=[pk])
                    fw.op("dve", lambda e, pi=pi, l=l, j=j: e.tensor_scalar(
                        out=ot[:, l, j, :], in0=pst[pi][:, 0:3], scalar1=bt[:, l, j:j + 1], scalar2=None, op0=ALU.add),
                        reads=[pk, "bt"], writes=["ot"])
                it += 1
        fw.dma("sp", out[:, :, :, :], ot[:], reads=["ot"])
        fw.finish()
    return nc


TT = [(0, 512), (512, 512), (1024, 64)]


def build_tok(post, pre):
    nin = {"ab": D_AB_IN, "hg": D_HG_IN, "final": 0}[pre]
    nc = bass.Bass("TRN2", target_bir_lowering=False)
    xin = nc.dram_tensor("xT", [D, NT], F32, kind="ExternalInput").ap()
    adaL = nc.dram_tensor("adaL", [128, 2, 6, KC], F32, kind="ExternalInput").ap()
    adaC = nc.dram_tensor("adaC", [128, 2, 6, KC], F32, kind="ExternalInput").ap()
    nrm = nc.dram_tensor("nrm", [128, 2, KC], F32, kind="ExternalInput").ap()
    if post:
        yin = nc.dram_tensor("yT", [D, NT], BF16, kind="ExternalInput").ap()
        w_out = nc.dram_tensor("w_out", [D, D], F32, kind="ExternalInput").ap()
        w1 = nc.dram_tensor("w1", [D, DFF], F32, kind="ExternalInput").ap()
        w2 = nc.dram_tensor("w2", [DFF, D], F32, kind="ExternalInput").ap()
    if nin:
        w_in = nc.dram_tensor("w_in", [D, nin], F32, kind="ExternalInput").ap()
        pout = nc.dram_tensor("pT", [nin, NT], F32, kind="ExternalOutput").ap()
        xout = nc.dram_tensor("xoT", [D, NT], F32, kind="ExternalOutput").ap()
    else:
        fout = nc.dram_tensor("outT", [D, NT], F32, kind="ExternalOutput").ap()

    with ExitStack() as es:
        fw = FW(nc, es)
        x = fw.sb("x", [128, KC, NT], F32)
        hT = fw.sb("hT", [128, KC, NT], BF16)
        h1 = fw.sb("h1", [128, KC, NT], BF16)
        NWB = 2
        wb = [fw.sb("wb%d" % i, [128, KC, 512], BF16) for i in range(NWB)]
        aL = fw.sb("aL", [128, 2, 6, KC], F32)
        aC = fw.sb("aC", [128, 2, 6, KC], F32)
        nw = fw.sb("nw", [128, 2, KC], F32)
        AL = fw.sb("AL", [128, 2, KC], F32)
        AC = fw.sb("AC", [128, 2, KC], F32)
        ones = fw.sb("ones", [128, 128], BF16)
        rstd = fw.sb("rstd", [128, NT], F32)
        tmpn = [fw.sb("tmpn%d" % i, [128, NT], F32) for i in range(2)]
        stg = [fw.sb("stg%d" % i, [128, NT], F32) for i in range(3)]
        rl = [fw.sb("rl%d" % i, [128, 512], F32) for i in range(2)]
        NPS = 8
        pst = [fw.ps("ps%d" % i, [128, 512], F32) for i in range(NPS)]
        st = {"ps": 0, "wb": 0, "tmpn": 0, "stg": 0, "rl": 0}

        def nxt(name, n):
            i = st[name] % n
            st[name] += 1
            return i

        xk = lambda kc: ("x", kc)
        hk = lambda kc: ("h", kc)
        h1k = lambda kc: ("h1", kc)

        if post:
            yv = yin.rearrange("(kc p) n -> p kc n", p=128)
            for kc in range(KC):
                fw.dma("sp" if kc % 2 == 0 else "act", hT[:, kc, :], yv[:, kc, :], writes=[hk(kc)])
        xv = xin.rearrange("(kc p) n -> p kc n", p=128)
        for kc in range(KC):
            fw.dma("sp" if kc % 2 == 0 else "act", x[:, kc, :], xv[:, kc, :], writes=[xk(kc)])
        fw.dma("act", aL[:], adaL[:, :, :, :], writes=["aL"])
        fw.dma("act", aC[:], adaC[:, :, :, :], writes=["aC"])
        fw.dma("act", nw[:], nrm[:, :, :], writes=["nw"])
        fw.op("dve", lambda e: e.memset(ones[:], 1.0), writes=["ones"])
        for (s, idx) in ((0, 4), (1, 1)):
            fw.op("dve", lambda e, s=s, idx=idx: e.scalar_tensor_tensor(
                out=AL[:, s, :], in0=aL[:, s, idx, :], scalar=1.0, in1=nw[:, s, :], op0=ALU.add, op1=ALU.mult),
                reads=["aL", "nw"], writes=["AL"])
            fw.op("dve", lambda e, s=s, idx=idx: e.scalar_tensor_tensor(
                out=AC[:, s, :], in0=aC[:, s, idx, :], scalar=1.0, in1=nw[:, s, :], op0=ALU.add, op1=ALU.mult),
                reads=["aC", "nw"], writes=["AC"])

        def load_w(src_ap, ncols):
            i = nxt("wb", NWB)
            fw.dma("pool", wb[i][:, :, 0:ncols], src_ap.rearrange("(kc p) n -> p kc n", p=128), writes=[("wb", i)])
            return i

        def proj(src, srckey, wsrc, ncols_total, evac):
            c0 = 0
            while c0 < ncols_total:
                gw = min(512, ncols_total - c0)
                wi = load_w(wsrc[:, c0:c0 + gw], gw)
                cc = 0
                while cc < gw:
                    cw = min(128, gw - cc)
                    c = (c0 + cc) // 128
                    for tt, (t0, tw) in enumerate(TT):
                        pi = nxt("ps", NPS)
                        pk = ("ps", pi)
                        mm_group(fw, pst[pi][0:cw, 0:tw],
                                 [(wb[wi][:, kc, cc:cc + cw], src[:, kc, t0:t0 + tw]) for kc in range(KC)],
                                 reads=[("wb", wi)] + [srckey(kc) for kc in range(KC)], writes=[pk])
                        evac(c, cw, tt, pst[pi][0:cw, 0:tw], pk)
                    cc += cw
                c0 += gw

        def norm_mod(s, shift_idx, out_dt_tile, outkey, final=False):
            for kc in range(KC):
                fw.op("act", lambda e, kc=kc: e.activation(out=h1[:, kc, :], in_=x[:, kc, :], func=AF.Square),
                      reads=[xk(kc)], writes=[h1k(kc)])
            for tt, (t0, tw) in enumerate(TT):
                pi = nxt("ps", NPS)
                pk = ("ps", pi)
                mm_group(fw, pst[pi][:, 0:tw], [(ones[:], h1[:, kc, t0:t0 + tw]) for kc in range(KC)],
                         reads=["ones"] + [h1k(kc) for kc in range(KC)], writes=[pk])
                fw.op("act", lambda e, pi=pi, t0=t0, tw=tw: e.activation(
                    out=rstd[:, t0:t0 + tw], in_=pst[pi][:, 0:tw], func=AF.Ln, bias=EPS_AP[:, 0:1], scale=1.0 / D),
                    reads=[pk, "eps"], writes=["rstd"])
            fw.op("act", lambda e: e.activation(out=rstd[:], in_=rstd[:], func=AF.Exp, scale=-0.5), reads=["rstd"], writes=["rstd"])
            for kc in range(KC):
                ti = nxt("tmpn", 2)
                tk = ("tmpn", ti)
                fw.op("dve", lambda e, kc=kc, ti=ti: e.tensor_tensor(out=tmpn[ti][:], in0=x[:, kc, :], in1=rstd[:], op=ALU.mult),
                      reads=[xk(kc), "rstd"], writes=[tk])
                if final:
                    si = nxt("stg", 3)
                    sk = ("stg", si)
                    fw.op("act", lambda e, kc=kc, ti=ti, si=si: e.activation(
                        out=stg[si][:], in_=tmpn[ti][:], func=AF.Identity, scale=nw[:, 1, kc:kc + 1]),
                        reads=[tk, "nw"], writes=[sk])
                    fw.dma("sp", fout[kc * 128:(kc + 1) * 128, :], stg[si][:], reads=[sk])
                else:
                    fw.op("act", lambda e, kc=kc, ti=ti: e.activation(
                        out=hT[:, kc, 0:NL], in_=tmpn[ti][:, 0:NL], func=AF.Identity,
                        bias=aL[:, s, shift_idx, kc:kc + 1], scale=AL[:, s, kc:kc + 1]),
                        reads=[tk, "aL", "AL"], writes=[hk(kc)])
                    fw.op("act", lambda e, kc=kc, ti=ti: e.activation(
                        out=hT[:, kc, NL:NT], in_=tmpn[ti][:, NL:NT], func=AF.Identity,
                        bias=aC[:, s, shift_idx, kc:kc + 1], scale=AC[:, s, kc:kc + 1]),
                        reads=[tk, "aC", "AC", hk(kc)], writes=[hk(kc)])

        EPS_AP = fw.sb("epsc", [128, 1], F32)
        fw.op("dve", lambda e: e.memset(EPS_AP[:], EPS), writes=["eps"])

        def resid_evac(gidx):
            def evac(c, cw, tt, ps_ap, pk):
                t0, tw = TT[tt]
                a = aL if tt < 2 else aC
                fw.op("dve", lambda e: e.scalar_tensor_tensor(
                    out=x[:, c, t0:t0 + tw], in0=ps_ap, scalar=a[:, 0, gidx, c:c + 1], in1=x[:, c, t0:t0 + tw],
                    op0=ALU.mult, op1=ALU.add),
                    reads=[pk, "aL", "aC", xk(c)], writes=[xk(c)])
            return evac

        if post:
            proj(hT, hk, w_out, D, resid_evac(2))
            norm_mod(0, 3, None, None)
            for q in range(4):
                def evac1(c, cw, tt, ps_ap, pk):
                    t0, tw = TT[tt]
                    ri = nxt("rl", 2)
                    rk = ("rl", ri)
                    fw.op("act", lambda e: e.activation(out=rl[ri][:, 0:tw], in_=ps_ap, func=AF.Relu),
                          reads=[pk], writes=[rk])
                    fw.op("dve", lambda e: e.tensor_tensor(out=h1[:, c, t0:t0 + tw], in0=rl[ri][:, 0:tw], in1=rl[ri][:, 0:tw], op=ALU.mult),
                          reads=[rk], writes=[h1k(c)])
                proj(hT, hk, w1[:, q * 2048:(q + 1) * 2048], 2048, evac1)
                proj(h1, h1k, w2[q * 2048:(q + 1) * 2048, :], D, resid_evac(5))

        if nin:
            xov = xout.rearrange("(kc p) n -> p kc n", p=128)
            for kc in range(KC):
                fw.dma("sp", xov[:, kc, :], x[:, kc, :], reads=[xk(kc)])
            norm_mod(1, 0, None, None)
            cur = {}

            def evac_p(c, cw, tt, ps_ap, pk):
                t0, tw = TT[tt]
                if tt == 0:
                    cur["si"] = nxt("stg", 3)
                si = cur["si"]
                sk = ("stg", si)
                eng = "act" if (c + tt) % 2 == 0 else "dve"
                if eng == "act":
                    fw.op("act", lambda e: e.activation(out=stg[si][0:cw, t0:t0 + tw], in_=ps_ap, func=AF.Copy),
                          reads=[pk], writes=[sk])
                else:
                    fw.op("dve", lambda e: e.tensor_copy(out=stg[si][0:cw, t0:t0 + tw], in_=ps_ap),
                          reads=[pk], writes=[sk])
                if tt == len(TT) - 1:
                    fw.dma("sp", pout[c * 128:c * 128 + cw, :], stg[si][0:cw, :], reads=[sk])
            proj(hT, hk, w_in, nin, evac_p)
        else:
            norm_mod(1, 0, None, None, final=True)
        fw.finish()
    return nc


def core_tok(k):
    return k // 4, k % 4


def to_core_T(lat, ctx):
    outs = []
    for k in range(NCORES):
        b, r = core_tok(k)
        a = np.concatenate([lat[b, r * NL:(r + 1) * NL], ctx[b, r * NCX:(r + 1) * NCX]], axis=0)
        outs.append(np.ascontiguousarray(a.T))
    return outs


def from_core_T(arrs):
    F = arrs[0].shape[0]
    lat = np.empty((B, SEQ, F), arrs[0].dtype)
    ctx = np.empty((B, CTX, F), arrs[0].dtype)
    for k in range(NCORES):
        b, r = core_tok(k)
        lat[b, r * NL:(r + 1) * NL] = arrs[k][:, :NL].T
        ctx[b, r * NCX:(r + 1) * NCX] = arrs[k][:, NL:].T
    return lat, ctx


def vec_pk(v):
    return np.ascontiguousarray(v.reshape(KC, 128).T)


def ada_sets(m, lpost, lpre, v):
    out = np.zeros((128, 2, 6, KC), np.float32)
    for s, l in enumerate((lpost, lpre)):
        if l is None or l < 0 or l >= DEPTH:
            continue
        out[:, s] = m[l, v].reshape(6, KC, 128).transpose(2, 0, 1)
    return out


def ada_host_inputs(c, c_ctx, ada_w, ada_b):
    cvec = np.stack([c[0], c[1], c_ctx], 0)
    cv = np.ascontiguousarray(cvec.reshape(3, KC, 128).transpose(2, 1, 0))
    maps = []
    for k in range(NCORES):
        w = np.ascontiguousarray(ada_w[:, :, k * ADA_COLS:(k + 1) * ADA_COLS])
        b = np.ascontiguousarray(ada_b[:, k * ADA_COLS:(k + 1) * ADA_COLS].reshape(DEPTH, ADA_J, 128).transpose(2, 0, 1))
        maps.append({"cv": cv, "w": w, "b": b})
    return maps


def ada_host_gather(results):
    m = np.zeros((DEPTH, 3, 6 * D), np.float32)
    for k in range(NCORES):
        o = results[k]["out"]
        m[:, :, k * ADA_COLS:(k + 1) * ADA_COLS] = o.transpose(1, 3, 2, 0).reshape(DEPTH, 3, ADA_COLS)
    return m


_NC_CACHE = {}


def get_nc(name, builder, *args):
    key = (name,) + tuple(args)
    if key not in _NC_CACHE:
        _NC_CACHE[key] = builder(*args)
    return _NC_CACHE[key]


def run(nc, maps):
    res = run_bass_kernel_spmd(nc, maps, core_ids=list(range(NCORES)))
    return res.results


ATTN_SCALE = 192.0 ** -0.5
TL = [(i * 512, 512) for i in range(SEQ // 512)] + [(SEQ, CTX)]
NKT = TS // 128
GELU_C = 1.5957691216057308


def build_ab():
    nc = bass.Bass("TRN2", target_bir_lowering=False)
    dgu = nc.dram_tensor("guT", [2, 128, B, TS], F32, kind="ExternalInput").ap()
    dcq = nc.dram_tensor("cqT", [512, B, TS], F32, kind="ExternalInput").ap()
    dckv = nc.dram_tensor("ckvT", [512, B, TS], F32, kind="ExternalInput").ap()
    dkr = nc.dram_tensor("krT", [64, B, TS], F32, kind="ExternalInput").ap()
    dvec = nc.dram_tensor("vecs", [128, 24], F32, kind="ExternalInput").ap()
    dwa = nc.dram_tensor("wax", [2, 2, 128, 128], F32, kind="ExternalInput").ap()
    dwuq = nc.dram_tensor("wuq", [512, 192], F32, kind="ExternalInput").ap()
    dwukv = nc.dram_tensor("wukv", [512, 256], F32, kind="ExternalInput").ap()
    dcs = nc.dram_tensor("cs", [2, 64, SEQ], F32, kind="ExternalInput").ap()
    drm = nc.dram_tensor("rm", [64, 64], F32, kind="ExternalInput").ap()
    dy = nc.dram_tensor("yT", [256, B, TS], BF16, kind="ExternalOutput").ap()

    with ExitStack() as es:
        fw = FW(nc, es)
        vec = fw.sb("vecsb", [128, 24], F32)
        c1 = fw.sb("c1", [128, 4], F32)
        nsp = fw.sb("nsp", [128, 2], F32)
        wax = fw.sb("waxb", [128, 4, 128], BF16)
        wuq = fw.sb("wuqb", [128, 4, 192], BF16)
        wukv = fw.sb("wukvb", [128, 4, 256], BF16)
        ones = fw.sb("ones", [128, 128], BF16)
        rm = fw.sb("rmsb", [64, 64], F32)
        NPS = 8
        pst = [fw.ps("ps%d" % i, [128, 512], F32) for i in range(NPS)]
        st = {}

        def nxt(name, n):
            i = st.get(name, 0) % n
            st[name] = st.get(name, 0) + 1
            return i

        fw.dma("sp", vec[:], dvec[:, :], writes=["vec"])
        fw.dma("sp", rm[:], drm[:, :], writes=["rm"])
        fw.dma("pool", wax[:], dwa.rearrange("a d i o -> i (a d) o"), writes=["wax"])
        fw.dma("pool", wuq[:], dwuq.rearrange("(kc p) n -> p kc n", p=128), writes=["wuq"])
        fw.dma("pool", wukv[:], dwukv.rearrange("(kc p) n -> p kc n", p=128), writes=["wukv"])
        fw.op("dve", lambda e: e.memset(ones[:], 1.0), writes=["ones"])
        fw.op("dve", lambda e: e.memset(c1[:, 0:1], 1.0), writes=["c1"])
        fw.op("dve", lambda e: e.memset(c1[:, 1:2], EPS), reads=["c1"], writes=["c1"])
        fw.op("act", lambda e: e.activation(out=nsp[:], in_=vec[:, 9:11], func=AF.Exp, scale=-1.0), reads=["vec"], writes=["nsp"])
        fw.op("act", lambda e: e.activation(out=nsp[:], in_=nsp[:], func=AF.Ln, bias=c1[:, 0:1], scale=1.0), reads=["nsp", "c1"], writes=["nsp"])
        fw.op("dve", lambda e: e.tensor_scalar(out=nsp[:], in0=nsp[:], scalar1=-8.0, scalar2=None, op0=ALU.mult), reads=["nsp"], writes=["nsp"])

        with ExitStack() as es1:
            names = ["tu", "tg", "uc", "t1", "t2", "t3", "hacc", "tgel"]
            T = {n: es1.enter_context(nc.sbuf_tensor("l_" + n, [128, TS], F32)) for n in names}
            ucb = es1.enter_context(nc.sbuf_tensor("l_ucb", [128, TS], BF16))
            BL = [(0, 1024), (1024, 2048), (2048, 3072), (3072, 4096), (SEQ, TS)]
            NBL = len(BL)

            def K(name, k):
                return (name, k)

            def stage_major(bodyfn):
                per_block = []
                for k in range(NBL):
                    lst = []
                    bodyfn(k, lambda f, *a, **kw: lst.append((f, a, kw)))
                    per_block.append(lst)
                for j in range(max(len(l) for l in per_block)):
                    for k in range(NBL):
                        if j < len(per_block[k]):
                            f, a_, kw = per_block[k][j]
                            f(*a_, **kw)

            for b in range(B):
                tu, tg, uc, t1, t2, t3, hacc, tgel = (T[n] for n in names)

                def conv_body(k, emit):
                    c0, cE = BL[k]
                    s0, s1 = (0, SEQ) if k < 4 else (SEQ, TS)
                    cs = slice(c0, cE)
                    emit(fw.dma, "sp", tg[:, cs], dgu[0, :, b, cs], writes=[K("tg", k)])
                    emit(fw.dma, "act", tu[:, cs], dgu[1, :, b, cs], writes=[K("tu", k)])
                    emit(fw.op, "pool", lambda e: e.tensor_tensor(out=tgel[:, cs], in0=tg[:, cs], in1=tg[:, cs], op=ALU.mult),
                         reads=[K("tg", k)], writes=[K("tgel", k)])
                    emit(fw.op, "pool", lambda e: e.tensor_scalar(out=tgel[:, cs], in0=tgel[:, cs], scalar1=0.044715, scalar2=1.0, op0=ALU.mult, op1=ALU.add),
                         reads=[K("tgel", k)], writes=[K("tgel", k)])
                    emit(fw.op, "pool", lambda e: e.tensor_tensor(out=tgel[:, cs], in0=tgel[:, cs], in1=tg[:, cs], op=ALU.mult),
                         reads=[K("tg", k), K("tgel", k)], writes=[K("tgel", k)])
                    emit(fw.op, "act", lambda e: e.activation(out=tgel[:, cs], in_=tgel[:, cs], func=AF.Sigmoid, scale=GELU_C),
                         reads=[K("tgel", k)], writes=[K("tgel", k)])
                    emit(fw.op, "pool", lambda e: e.tensor_tensor(out=tgel[:, cs], in0=tgel[:, cs], in1=tg[:, cs], op=ALU.mult),
                         reads=[K("tg", k), K("tgel", k)], writes=[K("tgel", k)])

                def conv_body2(k, emit):
                    c0, cE = BL[k]
                    s0, s1 = (0, SEQ) if k < 4 else (SEQ, TS)
                    cs = slice(c0, cE)
                    left = [K("tu", k - 1)] if c0 > s0 else []
                    right = [K("tu", k + 1)] if cE < s1 else []
                    emit(fw.op, "dve", lambda e: e.tensor_scalar(out=uc[:, cs], in0=tu[:, cs], scalar1=vec[:, 2:3], scalar2=vec[:, 4:5], op0=ALU.mult, op1=ALU.add),
                         reads=[K("tu", k), "vec"], writes=[K("uc", k)])
                    for (tap, sh) in ((0, 2), (1, 1)):
                        a0 = max(c0, s0 + sh)
                        emit(fw.op, "dve", lambda e, a0=a0, tap=tap, sh=sh: e.scalar_tensor_tensor(
                            out=uc[:, a0:cE], in0=tu[:, a0 - sh:cE - sh], scalar=vec[:, tap:tap + 1], in1=uc[:, a0:cE], op0=ALU.mult, op1=ALU.add),
                            reads=[K("tu", k), "vec", K("uc", k)] + left, writes=[K("uc", k)])
                    b1 = min(cE, s1 - 1)
                    emit(fw.op, "dve", lambda e: e.scalar_tensor_tensor(
                        out=uc[:, c0:b1], in0=tu[:, c0 + 1:b1 + 1], scalar=vec[:, 3:4], in1=uc[:, c0:b1], op0=ALU.mult, op1=ALU.add),
                        reads=[K("tu", k), "vec", K("uc", k)] + right, writes=[K("uc", k)])
                    emit(fw.op, "act", lambda e: e.activation(out=ucb[:, cs], in_=uc[:, cs], func=AF.Copy), reads=[K("uc", k)], writes=[K("ucb", k)])

                stage_major(conv_body)
                stage_major(conv_body2)
                for d in range(2):
                    def gate_body(k, emit):
                        c0, cE = BL[k]
                        cs = slice(c0, cE)
                        tiles = [(t0, min(512, cE - t0)) for t0 in range(c0, cE, 512)]
                        for (t0, tw) in tiles:
                            for (ax, dst, dn, bcol) in ((0, t1, "t1", 5 + d), (1, t2, "t2", 7 + d)):
                                def one(t0=t0, tw=tw, ax=ax, dst=dst, dn=dn, bcol=bcol):
                                    pi = nxt("ps", NPS)
                                    pk = ("ps", pi)
                                    mm_group(fw, pst[pi][:, 0:tw], [(wax[:, ax * 2 + d, :], ucb[:, t0:t0 + tw])],
                                             reads=["wax", K("ucb", k)], writes=[pk])
                                    fw.op("act", lambda e: e.activation(out=dst[:, t0:t0 + tw], in_=pst[pi][:, 0:tw], func=AF.Sigmoid,
                                                                        bias=vec[:, bcol:bcol + 1], scale=1.0),
                                          reads=[pk, "vec"], writes=[K(dn, k)])
                                emit(one)
                        emit(fw.op, "act", lambda e: e.activation(out=t1[:, cs], in_=t1[:, cs], func=AF.Exp, scale=nsp[:, d:d + 1]),
                             reads=[K("t1", k), "nsp"], writes=[K("t1", k)])
                        emit(fw.op, "pool", lambda e: e.tensor_tensor(out=t2[:, cs], in0=t2[:, cs], in1=uc[:, cs], op=ALU.mult),
                             reads=[K("t2", k), K("uc", k)], writes=[K("t2", k)])
                        emit(fw.op, "dve", lambda e: e.tensor_tensor(out=t3[:, cs], in0=t1[:, cs], in1=t1[:, cs], op=ALU.mult),
                             reads=[K("t1", k)], writes=[K("t3", k)])
                        emit(fw.op, "act", lambda e: e.activation(out=t3[:, cs], in_=t3[:, cs], func=AF.Sqrt, bias=c1[:, 0:1], scale=-1.0),
                             reads=[K("t3", k), "cE"], writes=[K("t3", k)])
                        emit(fw.op, "dve", lambda e: e.tensor_tensor(out=t2[:, cs], in0=t2[:, cs], in1=t3[:, cs], op=ALU.mult),
                             reads=[K("t2", k), K("t3", k)], writes=[K("t2", k)])
                    stage_major(gate_body)
                    dst = hacc if d == 0 else t3
                    dn = "hacc" if d == 0 else "t3"
                    order = [4, 0, 1, 2, 3] if d == 0 else [4, 3, 2, 1, 0]
                    prev = None
                    for k in order:
                        c0, cE = BL[k]
                        if d == 0:
                            init = 0.0 if prev is None else dst[:, BL[prev][1] - 1:BL[prev][1]]
                            o_ap, a_ap, b_ap = dst[:, c0:cE], t1[:, c0:cE], t2[:, c0:cE]
                        else:
                            init = 0.0 if prev is None else dst[:, BL[prev][0]:BL[prev][0] + 1]
                            o_ap, a_ap, b_ap = dst[:, c0:cE][:, ::-1], t1[:, c0:cE][:, ::-1], t2[:, c0:cE][:, ::-1]
                        fw.op("dve", lambda e, o_ap=o_ap, a_ap=a_ap, b_ap=b_ap, init=init: e.tensor_tensor_scan(
                            out=o_ap, data0=a_ap, data1=b_ap, initial=init, op0=ALU.mult, op1=ALU.add),
                            reads=[K("t1", k), K("t2", k)] + ([K(dn, prev)] if prev is not None else []), writes=[K(dn, k)])
                        if d == 1:
                            fw.op("pool", lambda e, c0=c0, cE=cE: e.tensor_tensor(out=hacc[:, c0:cE], in0=hacc[:, c0:cE], in1=t3[:, c0:cE], op=ALU.add),
                                  reads=[K("hacc", k), K("t3", k)], writes=[K("hacc", k)])
                        prev = k

                def out_body(k, emit):
                    c0, cE = BL[k]
                    cs = slice(c0, cE)
                    emit(fw.op, "dve", lambda e: e.tensor_tensor(out=ucb[:, cs], in0=tgel[:, cs], in1=hacc[:, cs], op=ALU.mult),
                         reads=[K("hacc", k), K("tgel", k), K("ucb", k)], writes=[K("ucb", k)])
                    emit(fw.dma, "sp", dy[0:128, b, cs], ucb[:, cs], reads=[K("ucb", k)])
                stage_major(out_body)
            fw.barrier()

        with ExitStack() as es2:
            def sb2(name, shape, dt):
                return es2.enter_context(nc.sbuf_tensor("m_" + name, list(shape), dt))
            cs = sb2("cssb", [64, 2, SEQ], F32)
            qTn = sb2("qTn", [128, TS], BF16)
            qTr = sb2("qTr", [64, TS], BF16)
            kTn = sb2("kTn", [128, TS], BF16)
            kTr = sb2("kTr", [64, TS], BF16)
            V = sb2("V", [128, NKT, 128], BF16)
            cqt = [sb2("cqt%d" % i, [128, 4, 512], F32) for i in range(2)]
            ckt = [sb2("ckt%d" % i, [128, 4, 512], F32) for i in range(2)]
            krt = [sb2("krt%d" % i, [64, 512], F32) for i in range(2)]
            sq = [sb2("sq%d" % i, [128, 4, 512], BF16) for i in range(2)]
            cqn = [sb2("cqn%d" % i, [128, 4, 512], BF16) for i in range(2)]
            ckn = [sb2("ckn%d" % i, [128, 4, 512], BF16) for i in range(2)]
            rs = [sb2("rs%d" % i, [128, 512], F32) for i in range(2)]
            qrs = [sb2("qrs%d" % i, [64, 512], F32) for i in range(2)]
            ra = [sb2("ra%d" % i, [64, 512], F32) for i in range(2)]
            rb = [sb2("rb%d" % i, [64, 512], F32) for i in range(2)]
            PT = [sb2("PT%d" % i, [128, 512], BF16) for i in range(6)]
            accD = [sb2("accD%d" % i, [128, 512], F32) for i in range(2)]
            accP = [sb2("accP%d" % i, [128, 512], F32) for i in range(2)]
            ones_f = sb2("ones_f", [128, 128], F32)
            fw.op("dve", lambda e: e.memset(ones_f[:], 1.0), writes=["ones_f"])
            rden = [sb2("rden%d" % i, [128, 512], F32) for i in range(2)]
            ob = [sb2("ob%d" % i, [128, 512], BF16) for i in range(2)]
            fw.dma("sp", cs[:], dcs.rearrange("a p t -> p a t"), writes=["cs"])

            def rope(src_ap, srck, t0, tw, dst_ap, dstk):
                pi = nxt("ps", NPS)
                pk = ("ps", pi)
                mm_group(fw, pst[pi][0:64, 0:tw], [(rm[:], src_ap)], reads=["rm", srck], writes=[pk])
                i = nxt("ra", 2)
                fw.op("dve", lambda e: e.tensor_tensor(out=ra[i][:, 0:tw], in0=src_ap, in1=cs[:, 0, t0:t0 + tw], op=ALU.mult),
                      reads=[srck, "cs"], writes=[("ra", i)])
                fw.op("dve", lambda e: e.tensor_tensor(out=rb[i][:, 0:tw], in0=pst[pi][0:64, 0:tw], in1=cs[:, 1, t0:t0 + tw], op=ALU.mult),
                      reads=[pk, "cs"], writes=[("rb", i)])
                fw.op("dve", lambda e: e.tensor_tensor(out=dst_ap, in0=ra[i][:, 0:tw], in1=rb[i][:, 0:tw], op=ALU.add),
                      reads=[("ra", i), ("rb", i)], writes=[dstk])

            for b in range(B):
                for (t0, tw) in TL:
                    is_ctx = t0 >= SEQ
                    i = nxt("tile", 2)
                    fw.dma("sp", cqt[i][:, :, 0:tw], dcq.rearrange("(kc p) b n -> p kc b n", p=128)[:, :, b, t0:t0 + tw], writes=[("cqt", i)])
                    fw.dma("act", ckt[i][:, :, 0:tw], dckv.rearrange("(kc p) b n -> p kc b n", p=128)[:, :, b, t0:t0 + tw], writes=[("ckt", i)])
                    fw.dma("sp", krt[i][:, 0:tw], dkr[:, b, t0:t0 + tw], writes=[("krt", i)])
                    for (src, srck, ncol, dst, dstk) in ((cqt[i], ("cqt", i), 11, cqn[i], ("cqn", i)), (ckt[i], ("ckt", i), 15, ckn[i], ("ckn", i))):
                        j = nxt("sq", 2)
                        fw.op("act", lambda e, src=src, j=j: e.activation(out=sq[j][:, :, 0:tw], in_=src[:, :, 0:tw], func=AF.Square),
                              reads=[srck], writes=[("sq", j)])
                        pi = nxt("ps", NPS)
                        pk = ("ps", pi)
                        mm_group(fw, pst[pi][:, 0:tw], [(ones[:], sq[j][:, kc, 0:tw]) for kc in range(4)],
                                 reads=["ones", ("sq", j)], writes=[pk])
                        r = nxt("rs", 2)
                        fw.op("act", lambda e, pi=pi, r=r: e.activation(out=rs[r][:, 0:tw], in_=pst[pi][:, 0:tw], func=AF.Ln,
                                                                      bias=c1[:, 1:2], scale=1.0 / 512), reads=[pk, "c1"], writes=[("rs", r)])
                        fw.op("act", lambda e, r=r: e.activation(out=rs[r][:, 0:tw], in_=rs[r][:, 0:tw], func=AF.Exp, scale=-0.5),
                              reads=[("rs", r)], writes=[("rs", r)])
                        for kc in range(4):
                            fw.op("dve", lambda e, src=src, dst=dst, kc=kc, r=r, ncol=ncol: e.scalar_tensor_tensor(
                                out=dst[:, kc, 0:tw], in0=src[:, kc, 0:tw], scalar=vec[:, ncol + kc:ncol + kc + 1], in1=rs[r][:, 0:tw],
                                op0=ALU.mult, op1=ALU.mult), reads=[srck, "vec", ("rs", r)], writes=[dstk])
                    pi = nxt("ps", NPS); pk = ("ps", pi)
                    mm_group(fw, pst[pi][:, 0:tw], [(wuq[:, kc, 0:128], cqn[i][:, kc, 0:tw]) for kc in range(4)],
                             reads=["wuq", ("cqn", i)], writes=[pk])
                    fw.op("act", lambda e, pi=pi: e.activation(out=qTn[:, t0:t0 + tw], in_=pst[pi][:, 0:tw], func=AF.Copy),
                          reads=[pk], writes=["qTn"])
                    pi = nxt("ps", NPS); pk = ("ps", pi)
                    mm_group(fw, pst[pi][0:64, 0:tw], [(wuq[:, kc, 128:192], cqn[i][:, kc, 0:tw]) for kc in range(4)],
                             reads=["wuq", ("cqn", i)], writes=[pk])
                    if is_ctx:
                        fw.op("act", lambda e, pi=pi: e.activation(out=qTr[:, t0:t0 + tw], in_=pst[pi][0:64, 0:tw], func=AF.Copy),
                              reads=[pk], writes=["qTr"])
                    else:
                        q = nxt("qrs", 2)
                        fw.op("act", lambda e, pi=pi, q=q: e.activation(out=qrs[q][:, 0:tw], in_=pst[pi][0:64, 0:tw], func=AF.Copy),
                              reads=[pk], writes=[("qrs", q)])
                        rope(qrs[q][:, 0:tw], ("qrs", q), t0, tw, qTr[:, t0:t0 + tw], "qTr")
                    pi = nxt("ps", NPS); pk = ("ps", pi)
                    mm_group(fw, pst[pi][:, 0:tw], [(wukv[:, kc, 0:128], ckn[i][:, kc, 0:tw]) for kc in range(4)],
                             reads=["wukv", ("ckn", i)], writes=[pk])
                    fw.op("act", lambda e, pi=pi: e.activation(out=kTn[:, t0:t0 + tw], in_=pst[pi][:, 0:tw], func=AF.Copy),
                          reads=[pk], writes=["kTn"])
                    for sub in range(tw // 128):
                        pi = nxt("ps", NPS); pk = ("ps", pi)
                        mm_group(fw, pst[pi][:, 0:128], [(ckn[i][:, kc, sub * 128:(sub + 1) * 128], wukv[:, kc, 128:256]) for kc in range(4)],
                                 reads=["wukv", ("ckn", i)], writes=[pk])
                        fw.op("dve", lambda e, pi=pi, sub=sub: e.tensor_copy(out=V[:, t0 // 128 + sub, :], in_=pst[pi][:, 0:128]),
                              reads=[pk], writes=["V"])
                    if is_ctx:
                        fw.op("act", lambda e, i=i: e.activation(out=kTr[:, t0:t0 + tw], in_=krt[i][:, 0:tw], func=AF.Copy),
                              reads=[("krt", i)], writes=["kTr"])
                    else:
                        rope(krt[i][:, 0:tw], ("krt", i), t0, tw, kTr[:, t0:t0 + tw], "kTr")
                SKEW = 2
                for (t0, tw) in TL:
                    is_ctx = t0 >= SEQ
                    kts = list(range(SEQ // 128, NKT)) if is_ctx else list(range(NKT))
                    oi = 4 + nxt("po", 2)
                    di = 6 + nxt("pd", 2)
                    ok, dk = ("ps", oi), ("ps", di)
                    ai = nxt("acc", 2)
                    aD, aP = accD[ai], accP[ai]
                    aDk, aPk = ("accD", ai), ("accP", ai)
                    pend = []

                    def consume(item):
                        n, kt, p = item
                        first, last = (n == 0), (n == len(kts) - 1)
                        fw.op("pe", lambda e: e.matmul(pst[oi][:, 0:tw], V[:, kt, :], PT[p][:, 0:tw], start=first, stop=last),
                              reads=["V", ("PT", p)], writes=[ok], nosame=not first)
                        if n % 2 == 0:
                            if n == 0:
                                fw.op("dve", lambda e: e.tensor_copy(out=aD[:, 0:tw], in_=PT[p][:, 0:tw]), reads=[("PT", p)], writes=[aDk])
                            else:
                                fw.op("dve", lambda e: e.tensor_tensor(out=aD[:, 0:tw], in0=aD[:, 0:tw], in1=PT[p][:, 0:tw], op=ALU.add),
                                      reads=[("PT", p), aDk], writes=[aDk])
                        else:
                            if n == 1:
                                fw.op("pool", lambda e: e.tensor_copy(out=aP[:, 0:tw], in_=PT[p][:, 0:tw]), reads=[("PT", p)], writes=[aPk])
                            else:
                                fw.op("pool", lambda e: e.tensor_tensor(out=aP[:, 0:tw], in0=aP[:, 0:tw], in1=PT[p][:, 0:tw], op=ALU.add),
                                      reads=[("PT", p), aPk], writes=[aPk])

                    for n, kt in enumerate(kts):
                        si = nxt("psS", 4)
                        sk = ("ps", si)
                        mm_group(fw, pst[si][:, 0:tw],
                                 [(kTn[:, kt * 128:(kt + 1) * 128], qTn[:, t0:t0 + tw]), (kTr[:, kt * 128:(kt + 1) * 128], qTr[:, t0:t0 + tw])],
                                 reads=["kTn", "qTn", "kTr", "qTr"], writes=[sk])
                        p = nxt("PT", 6)
                        fw.op("act", lambda e, si=si, p=p: e.activation(out=PT[p][:, 0:tw], in_=pst[si][:, 0:tw], func=AF.Exp, scale=ATTN_SCALE),
                              reads=[sk], writes=[("PT", p)])
                        pend.append((n, kt, p))
                        if len(pend) > SKEW:
                            consume(pend.pop(0))
                    while pend:
                        consume(pend.pop(0))
                    fw.op("dve", lambda e: e.tensor_tensor(out=aD[:, 0:tw], in0=aD[:, 0:tw], in1=aP[:, 0:tw], op=ALU.add),
                          reads=[aDk, aPk], writes=[aDk])
                    mm_group(fw, pst[di][:, 0:tw], [(ones_f[:], aD[:, 0:tw])], reads=["ones_f", aDk], writes=[dk])
                    r = nxt("rden", 2)
                    fw.op("dve", lambda e, r=r: e.reciprocal(out=rden[r][:, 0:tw], in_=pst[di][:, 0:tw]), reads=[dk], writes=[("rden", r)])
                    o = nxt("ob", 2)
                    fw.op("dve", lambda e, r=r, o=o: e.tensor_tensor(out=ob[o][:, 0:tw], in0=pst[oi][:, 0:tw], in1=rden[r][:, 0:tw], op=ALU.mult),
                          reads=[ok, ("rden", r)], writes=[("ob", o)])
                    fw.dma("sp", dy[128:256, b, t0:t0 + tw], ob[o][:, 0:tw], reads=[("ob", o)])
        fw.finish()
    return nc


def rope_consts():
    half = 32
    inv_freq = (10000.0 ** (-np.arange(0, half, 2, dtype=np.float32) / half)).astype(np.float32)
    t = np.arange(SEQ)
    row = (t // 64).astype(np.float32)
    col = (t % 64).astype(np.float32)
    ang_r = row[:, None] * inv_freq
    ang_c = col[:, None] * inv_freq
    ang = np.concatenate([ang_r, ang_r, ang_c, ang_c], axis=-1).astype(np.float32)
    cos = np.cos(ang).astype(np.float32)
    sin = np.sin(ang).astype(np.float32)
    sign = np.ones(64, np.float32)
    sign[0:16] = -1.0
    sign[32:48] = -1.0
    perm = np.concatenate([np.arange(16, 32), np.arange(0, 16), np.arange(48, 64), np.arange(32, 48)])
    rm = np.zeros((64, 64), np.float32)
    rm[perm, np.arange(64)] = 1.0
    cs = np.stack([cos.T, (sin * sign).T], 0)
    return np.ascontiguousarray(cs), rm


def ab_host_inputs(pT, e, P):
    cs, rm = rope_consts()
    maps = []
    cq = np.ascontiguousarray(pT[2048:2560])
    ckv = np.ascontiguousarray(pT[2560:3072])
    kr = np.ascontiguousarray(pT[3072:3136])
    for h in range(NCORES):
        sl = slice(h * 128, (h + 1) * 128)
        gu = np.stack([pT[h * 128:(h + 1) * 128], pT[1024 + h * 128:1024 + (h + 1) * 128]], 0)
        vec = np.zeros((128, 24), np.float32)
        vec[:, 0:4] = P["lru_conv_w"][e][:, sl].T
        vec[:, 4] = P["lru_conv_b"][e][sl]
        vec[:, 5:7] = P["lru_b_a"][e][:, sl].T
        vec[:, 7:9] = P["lru_b_x"][e][:, sl].T
        vec[:, 9:11] = P["lru_lambda"][e][:, sl].T
        vec[:, 11:15] = P["mla_q_norm_w"][e].reshape(4, 128).T
        vec[:, 15:19] = P["mla_kv_norm_w"][e].reshape(4, 128).T
        wax = np.stack([P["lru_w_a"][e][:, h], P["lru_w_x"][e][:, h]], 0)
        maps.append({"guT": np.ascontiguousarray(gu), "cqT": cq, "ckvT": ckv, "krT": kr, "vecs": vec,
                     "wax": np.ascontiguousarray(wax),
                     "wuq": np.ascontiguousarray(P["mla_w_uq"][e][:, h * 192:(h + 1) * 192]),
                     "wukv": np.ascontiguousarray(P["mla_w_ukv"][e][:, h * 256:(h + 1) * 256]),
                     "cs": cs, "rm": rm})
    return maps


def ab_host_gather(results):
    y = np.empty((D, B, TS), results[0]["yT"].dtype)
    for h in range(NCORES):
        o = results[h]["yT"]
        y[h * 128:(h + 1) * 128] = o[0:128]
        y[1024 + h * 128:1024 + (h + 1) * 128] = o[128:256]
    return y


def tokT_to_full(arrs):
    F = arrs[0].shape[0]
    full = np.empty((F, B, TS), arrs[0].dtype)
    for k in range(NCORES):
        b, r = core_tok(k)
        full[:, b, r * NL:(r + 1) * NL] = arrs[k][:, :NL]
        full[:, b, SEQ + r * NCX:SEQ + (r + 1) * NCX] = arrs[k][:, NL:]
    return full


def full_to_tokT(full):
    outs = []
    for k in range(NCORES):
        b, r = core_tok(k)
        outs.append(np.ascontiguousarray(np.concatenate(
            [full[:, b, r * NL:(r + 1) * NL], full[:, b, SEQ + r * NCX:SEQ + (r + 1) * NCX]], axis=1)))
    return outs


CH = 64
NCH = TS // CH
NPT = TS // 128


def build_hg(layer):
    nc = bass.Bass("TRN2", target_bir_lowering=False)
    dp = nc.dram_tensor("p5", [2, 5, 128, B, TS], F32, kind="ExternalInput").ap()
    dlb = nc.dram_tensor("lbl", [128, 2, DEPTH], F32, kind="ExternalInput").ap()
    dnw = nc.dram_tensor("nw", [128, 1], F32, kind="ExternalInput").ap()
    dmask = nc.dram_tensor("masks", [2, 128, 128], F32, kind="ExternalInput").ap()
    dident = nc.dram_tensor("ident", [128, 128], F32, kind="ExternalInput").ap()
    dy = nc.dram_tensor("yT", [2, 128, B, TS], BF16, kind="ExternalOutput").ap()

    with ExitStack() as es:
        fw = FW(nc, es)
        names = ["qs", "iE", "oacc", "kk", "lf", "G", "tmp"]
        T = {n: fw.sb("h_" + n, [128, TS], F32) for n in names}
        qs, iE, oacc, kk, lf, G, tmp = (T[n] for n in names)
        qt = [fw.sb("qt%d" % d, [128, TS], BF16) for d in range(2)]
        kt = [fw.sb("kt%d" % d, [128, TS], BF16) for d in range(2)]
        kh = [fw.sb("kh%d" % d, [128, TS], BF16) for d in range(2)]
        Vt = fw.sb("Vt", [128, NPT, 128], BF16)
        ones_b = fw.sb("ones_b", [128, 1088], BF16)
        ones_m = fw.sb("ones_m", [128, 128], BF16)
        masks = fw.sb("masks_sb", [128, 2, 128], F32)
        ident = fw.sb("ident_sb", [128, 128], F32)
        identb = fw.sb("identb_sb", [128, 128], BF16)
        lbl = fw.sb("lbl_sb", [128, 2, DEPTH], F32)
        lbs = fw.sb("lbs", [128, 8], F32)
        nwt = fw.sb("nwt", [128, 1], F32)
        c1 = fw.sb("c1h", [128, 2], F32)
        dec = [fw.sb("dec%d" % d, [128, NCH], F32) for d in range(2)]
        S = [fw.sb("S%d" % d, [128, 128], F32) for d in range(2)]
        Sb = [fw.sb("Sb%d" % d, [128, 128], BF16) for d in range(2)]
        Am = [fw.sb("Am%d" % i, [128, 128], BF16) for i in range(4)]
        khat = [fw.sb("khat%d" % i, [128, 128], BF16) for i in range(4)]
        sqb = [fw.sb("sqb%d" % i, [128, 512], BF16) for i in range(2)]
        rsb = [fw.sb("rsb%d" % i, [128, 512], F32) for i in range(2)]
        NPS = 8
        pst = [fw.ps("ps%d" % i, [128, 512], F32) for i in range(NPS)]
        st = {}

        def nxt(name, n):
            i = st.get(name, 0) % n
            st[name] = st.get(name, 0) + 1
            return i

        def newps():
            pi = nxt("ps", NPS)
            return pi, ("ps", pi)

        fw.dma("sp", masks[:], dmask.rearrange("a s t -> s a t"), writes=["masks"])
        fw.dma("sp", ident[:], dident[:, :], writes=["ident"])
        fw.dma("sp", lbl[:], dlb[:, :, :], writes=["lbl"])
        fw.dma("sp", nwt[:], dnw[:, :], writes=["nwt"])
        fw.op("act", lambda e: e.activation(out=identb[:], in_=ident[:], func=AF.Copy), reads=["ident"], writes=["identb"])
        fw.op("dve", lambda e: e.memset(ones_b[:], 1.0), writes=["ones_b"])
        fw.op("dve", lambda e: e.memset(ones_m[:], 1.0), writes=["ones_m"])
        fw.op("dve", lambda e: e.memset(c1[:, 0:1], 1.0), writes=["c1"])
        fw.op("dve", lambda e: e.memset(c1[:, 1:2], EPS), reads=["c1"], writes=["c1"])
        fw.op("act", lambda e: e.activation(out=lbl[:], in_=lbl[:], func=AF.Exp), reads=["lbl"], writes=["lbl"])
        fw.op("dve", lambda e: e.tensor_reduce(out=lbs[:, 0:2], in_=lbl[:], axis=AX.X, op=ALU.add), reads=["lbl"], writes=["lbs"])
        fw.op("dve", lambda e: e.tensor_reduce(out=lbs[:, 2:4], in_=lbl[:, :, 1:layer + 1], axis=AX.X, op=ALU.add), reads=["lbl", "lbs"], writes=["lbs"])
        fw.op("dve", lambda e: e.reciprocal(out=lbs[:, 0:2], in_=lbs[:, 0:2]), reads=["lbs"], writes=["lbs"])
        fw.op("dve", lambda e: e.tensor_tensor(out=lbs[:, 4:6], in0=lbs[:, 2:4], in1=lbs[:, 0:2], op=ALU.mult), reads=["lbs"], writes=["lbs"])
        fw.op("dve", lambda e: e.tensor_scalar(out=lbs[:, 6:8], in0=lbs[:, 4:6], scalar1=-1.0, scalar2=1.0, op0=ALU.mult, op1=ALU.add),
              reads=["lbs"], writes=["lbs"])

        def v3(t):
            return t[:].rearrange("p (c j) -> p c j", j=CH)

        NBK = 4
        BW = TS // NBK
        CPB = BW // CH

        def K(name, k):
            return (name, k)

        def allk(name):
            return [(name, k) for k in range(NBK)]

        def blk_of(col):
            return col // BW

        def setup(hd, b, d):
            QT, KT, KH, DEC = "qt%d" % d, "kt%d" % d, "kh%d" % d, "dec%d" % d
            G3, E3, tmp3 = v3(G), v3(iE), v3(tmp)
            last_col = CH - 1 if d == 0 else 0
            stages = []

            def blockbody(k, emit):
                c0 = k * BW
                cs = slice(c0, c0 + BW)
                ch = slice(k * CPB, (k + 1) * CPB)
                emit(fw.dma, "sp" if k % 2 == 0 else "act", kk[:, cs], dp[hd, 1 + d, :, b, cs], writes=[K("kk", k)])
                emit(fw.op, "act", lambda e: e.activation(out=kk[:, cs], in_=kk[:, cs], func=AF.Sigmoid), reads=[K("kk", k)], writes=[K("kk", k)])
                emit(fw.op, "pool", lambda e: e.tensor_scalar(out=kk[:, cs], in0=kk[:, cs], scalar1=lbs[:, 6 + hd:7 + hd], scalar2=lbs[:, 4 + hd:5 + hd],
                                                        op0=ALU.mult, op1=ALU.add), reads=[K("kk", k), "lbs"], writes=[K("kk", k)])
                emit(fw.op, "act", lambda e: e.activation(out=lf[:, cs], in_=kk[:, cs], func=AF.Ln), reads=[K("kk", k)], writes=[K("lf", k)])
                emit(fw.op, "pool", lambda e: e.tensor_scalar(out=kk[:, cs], in0=kk[:, cs], scalar1=-1.0, scalar2=1.0, op0=ALU.mult, op1=ALU.add),
                      reads=[K("kk", k)], writes=[K("kk", k)])
                init = 0.0 if k == 0 else G[:, c0 - 1:c0]
                emit(fw.op, "dve", lambda e: e.tensor_tensor_scan(out=G[:, cs], data0=ones_b[:, 0:BW], data1=lf[:, cs], initial=init,
                                                            op0=ALU.mult, op1=ALU.add),
                      reads=["ones_b", K("lf", k)] + ([K("G", k - 1)] if k > 0 else []), writes=[K("G", k)])
                if d == 0:
                    if k == 0:
                        emit(fw.op, "dve", lambda e: e.tensor_copy(out=E3[:, 0:1, :], in_=G3[:, 0:1, :]), reads=[K("G", 0)], writes=[K("iE", 0)])
                        emit(fw.op, "dve", lambda e: e.tensor_tensor(out=E3[:, 1:CPB, :], in0=G3[:, 1:CPB, :],
                                                               in1=G3[:, 0:CPB - 1, CH - 1:CH].broadcast_to([128, CPB - 1, CH]), op=ALU.subtract),
                              reads=[K("G", 0), K("iE", 0)], writes=[K("iE", 0)])
                    else:
                        emit(fw.op, "dve", lambda e: e.tensor_tensor(out=E3[:, ch, :], in0=G3[:, ch, :],
                                                               in1=G3[:, k * CPB - 1:(k + 1) * CPB - 1, CH - 1:CH].broadcast_to([128, CPB, CH]), op=ALU.subtract),
                              reads=[K("G", k), K("G", k - 1)], writes=[K("iE", k)])
                else:
                    emit(fw.op, "dve", lambda e: e.tensor_tensor(out=E3[:, ch, :], in0=G3[:, ch, CH - 1:CH].broadcast_to([128, CPB, CH]),
                                                           in1=G3[:, ch, :], op=ALU.subtract), reads=[K("G", k)], writes=[K("iE", k)])
                    emit(fw.op, "pool", lambda e: e.tensor_tensor(out=iE[:, cs], in0=iE[:, cs], in1=lf[:, cs], op=ALU.add),
                          reads=[K("lf", k), K("iE", k)], writes=[K("iE", k)])
                emit(fw.op, "act", lambda e: e.activation(out=dec[d][:, ch], in_=E3[:, ch, last_col], func=AF.Exp), reads=[K("iE", k)], writes=[K(DEC, k)])
                emit(fw.op, "act", lambda e: e.activation(out=tmp[:, cs], in_=iE[:, cs], func=AF.Exp), reads=[K("iE", k)], writes=[K("tmp", k)])
                emit(fw.op, "dve", lambda e: e.tensor_tensor(out=qt[d][:, cs], in0=qs[:, cs], in1=tmp[:, cs], op=ALU.mult),
                      reads=[K("qs", k), K("tmp", k)], writes=[K(QT, k)])
                emit(fw.op, "act", lambda e: e.activation(out=tmp[:, cs], in_=iE[:, cs], func=AF.Exp, scale=-1.0), reads=[K("iE", k)], writes=[K("tmp", k)])
                emit(fw.op, "pool", lambda e: e.tensor_tensor(out=kt[d][:, cs], in0=kk[:, cs], in1=tmp[:, cs], op=ALU.mult),
                      reads=[K("kk", k), K("tmp", k)], writes=[K(KT, k)])
                emit(fw.op, "dve", lambda e: e.tensor_tensor(out=tmp3[:, ch, :], in0=E3[:, ch, last_col:last_col + 1].broadcast_to([128, CPB, CH]),
                                                       in1=E3[:, ch, :], op=ALU.subtract), reads=[K("iE", k)], writes=[K("tmp", k)])
                emit(fw.op, "act", lambda e: e.activation(out=tmp[:, cs], in_=tmp[:, cs], func=AF.Exp), reads=[K("tmp", k)], writes=[K("tmp", k)])
                emit(fw.op, "dve", lambda e: e.tensor_tensor(out=kh[d][:, cs], in0=tmp[:, cs], in1=kk[:, cs], op=ALU.mult),
                      reads=[K("tmp", k), K("kk", k)], writes=[K(KH, k)])


            per_block = []
            for k in range(NBK):
                lst = []
                blockbody(k, lambda f, *a, **kw: lst.append((f, a, kw)))
                per_block.append(lst)
            nst = max(len(l) for l in per_block)
            for j in range(nst):
                for k in range(NBK):
                    if j < len(per_block[k]):
                        f, a, kw = per_block[k][j]
                        f(*a, **kw)

        def pair_pre(d, pt):
            bks = sorted({blk_of(pt * 128), blk_of(pt * 128 + 127)})
            QT = [K("qt%d" % d, x) for x in bks]
            KT = [K("kt%d" % d, x) for x in bks]
            KH = [K("kh%d" % d, x) for x in bks]
            cols = slice(pt * 128, (pt + 1) * 128)
            pi, pk = newps()
            mm_group(fw, pst[pi][:, 0:128], [(kt[d][:, cols], qt[d][:, cols])], reads=KT + QT, writes=[pk])
            a = nxt("Am", 4)
            fw.op("dve", lambda e: e.tensor_tensor(out=Am[a][:], in0=pst[pi][:, 0:128], in1=masks[:, d, :], op=ALU.mult),
                  reads=[pk, "masks"], writes=[("Am", a)])
            pi2, pk2 = newps()
            mm_group(fw, pst[pi2][:, 0:128], [(kh[d][:, cols], identb[:])], reads=KH + ["identb"], writes=[pk2])
            kx = nxt("khat", 4)
            fw.op("act", lambda e: e.activation(out=khat[kx][:], in_=pst[pi2][:, 0:128], func=AF.Copy), reads=[pk2], writes=[("khat", kx)])
            return a, kx

        def half_step(d, pt, hf, a, kx):
            c = pt * 2 + hf
            bk = blk_of(c * CH)
            QT, DEC, SK, SBK = K("qt%d" % d, bk), K("dec%d" % d, bk), ("S", d), ("Sb", d)
            rows = slice(hf * 64, (hf + 1) * 64)
            tcols = slice(c * CH, (c + 1) * CH)
            pi3, pk3 = newps()
            mm_group(fw, pst[pi3][:, 0:CH], [(Vt[rows, pt, :], Am[a][rows, hf * 64:(hf + 1) * 64]), (Sb[d][:], qt[d][:, tcols])],
                     reads=["Vt", ("Am", a), SBK, QT], writes=[pk3])
            fw.op("dve", lambda e: e.tensor_tensor(out=oacc[:, tcols], in0=pst[pi3][:, 0:CH], in1=oacc[:, tcols], op=ALU.add),
                  reads=[pk3, "oacc"], writes=["oacc"])
            pi4, pk4 = newps()
            mm_group(fw, pst[pi4][:, 0:128], [(khat[kx][rows, :], Vt[rows, pt, :])], reads=[("khat", kx), "Vt"], writes=[pk4])
            fw.op("dve", lambda e: e.scalar_tensor_tensor(out=S[d][:], in0=S[d][:], scalar=dec[d][:, c:c + 1], in1=pst[pi4][:, 0:128],
                                                          op0=ALU.mult, op1=ALU.add), reads=[SK, DEC, pk4], writes=[SK])
            fw.op("act", lambda e: e.activation(out=Sb[d][:], in_=S[d][:], func=AF.Copy), reads=[SK], writes=[SBK])

        ptiles = [list(range(SEQ // 128, NPT)) + list(range(SEQ // 128)), list(range(NPT - 1, -1, -1))]
        for hd in range(2):
            for b in range(B):
                for k in range(NBK):
                    cs = slice(k * BW, (k + 1) * BW)
                    fw.dma("sp", qs[:, cs], dp[hd, 0, :, b, cs], writes=[K("qs", k)])
                    fw.dma("act", iE[:, cs], dp[hd, 3, :, b, cs], writes=[K("iE", k)])
                    fw.op("act", lambda e: e.activation(out=tmp[:, cs], in_=qs[:, cs], func=AF.Sigmoid), reads=[K("qs", k)], writes=[K("tmp", k)])
                    fw.op("pool", lambda e: e.tensor_tensor(out=qs[:, cs], in0=qs[:, cs], in1=tmp[:, cs], op=ALU.mult),
                          reads=[K("qs", k), K("tmp", k)], writes=[K("qs", k)])
                    fw.op("act", lambda e: e.activation(out=kh[1][:, cs], in_=iE[:, cs], func=AF.Copy), reads=[K("iE", k)], writes=[K("kh1", k)])
                for pt in range(NPT):
                    pi, pk = newps()
                    mm_group(fw, pst[pi][:, 0:128], [(kh[1][:, pt * 128:(pt + 1) * 128], identb[:])],
                             reads=[K("kh1", blk_of(pt * 128)), K("kh1", blk_of(pt * 128 + 127)), "identb"], writes=[pk])
                    if pt % 2 == 0:
                        fw.op("act", lambda e, pi=pi, pt=pt: e.activation(out=Vt[:, pt, :], in_=pst[pi][:, 0:128], func=AF.Copy), reads=[pk], writes=["Vt"])
                    else:
                        fw.op("dve", lambda e, pi=pi, pt=pt: e.tensor_copy(out=Vt[:, pt, :], in_=pst[pi][:, 0:128]), reads=[pk], writes=["Vt"])
                fw.op("pool", lambda e: e.memset(oacc[:], 0.0), reads=["oacc"], writes=["oacc"])
                for d in range(2):
                    setup(hd, b, d)
                    fw.op("dve", lambda e, d=d: e.memset(S[d][:], 0.0), reads=[("S", d)], writes=[("S", d)])
                    fw.op("dve", lambda e, d=d: e.memset(Sb[d][:], 0.0), reads=[("Sb", d)], writes=[("Sb", d)])
                pre = [pair_pre(d, ptiles[d][0]) for d in range(2)]
                for step in range(NPT):
                    nxt_pre = None
                    for hi in range(2):
                        for d in range(2):
                            hf = (0, 1)[hi] if d == 0 else (1, 0)[hi]
                            half_step(d, ptiles[d][step], hf, *pre[d])
                        if hi == 0 and step + 1 < NPT:
                            nxt_pre = [pair_pre(d, ptiles[d][step + 1]) for d in range(2)]
                    pre = nxt_pre
                fw.dma("sp", kk[:], dp[hd, 4, :, b, :], writes=allk("kk"))
                fw.op("act", lambda e: e.activation(out=lf[:], in_=kk[:], func=AF.Sigmoid), reads=allk("kk"), writes=allk("lf"))
                fw.op("pool", lambda e: e.tensor_tensor(out=kk[:], in0=kk[:], in1=lf[:], op=ALU.mult), reads=allk("kk") + allk("lf"), writes=allk("kk"))
                for (t0, tw) in TL:
                    j = nxt("sqb", 2)
                    fw.op("act", lambda e, j=j, t0=t0, tw=tw: e.activation(out=sqb[j][:, 0:tw], in_=oacc[:, t0:t0 + tw], func=AF.Square),
                          reads=["oacc"], writes=[("sqb", j)])
                    pi, pk = newps()
                    mm_group(fw, pst[pi][:, 0:tw], [(ones_m[:], sqb[j][:, 0:tw])], reads=["ones_m", ("sqb", j)], writes=[pk])
                    r = nxt("rsb", 2)
                    fw.op("act", lambda e, pi=pi, r=r, tw=tw: e.activation(out=rsb[r][:, 0:tw], in_=pst[pi][:, 0:tw], func=AF.Ln, bias=c1[:, 1:2], scale=1.0 / 128),
                          reads=[pk, "c1"], writes=[("rsb", r)])
                    fw.op("act", lambda e, r=r, tw=tw: e.activation(out=rsb[r][:, 0:tw], in_=rsb[r][:, 0:tw], func=AF.Exp, scale=-0.5),
                          reads=[("rsb", r)], writes=[("rsb", r)])
                    fw.op("dve", lambda e, r=r, t0=t0, tw=tw: e.scalar_tensor_tensor(
                        out=G[:, t0:t0 + tw], in0=oacc[:, t0:t0 + tw], scalar=nwt[:, 0:1], in1=rsb[r][:, 0:tw], op0=ALU.mult, op1=ALU.mult),
                        reads=["oacc", "nwt", ("rsb", r)] + allk("G"), writes=allk("G"))
                fw.op("pool", lambda e: e.tensor_tensor(out=kh[0][:], in0=G[:], in1=kk[:], op=ALU.mult), reads=allk("G") + allk("kk") + allk("kh0"), writes=allk("kh0"))
                fw.dma("sp", dy[hd, :, b, :], kh[0][:], reads=allk("kh0"))
        fw.finish()
    return nc


def hg_consts():
    s = np.arange(128)[:, None]
    t = np.arange(128)[None, :]
    same = (s // CH) == (t // CH)
    mf = (same & (t >= s)).astype(np.float32)
    mb = (same & (t <= s)).astype(np.float32)
    return np.stack([mf, mb], 0), np.eye(128, dtype=np.float32)


def hg_host_inputs(pT, o, P):
    masks, ident = hg_consts()
    maps = []
    for k in range(NCORES):
        p5 = np.empty((2, 5, 128, B, TS), np.float32)
        lbl = np.empty((128, 2, DEPTH), np.float32)
        for j in range(2):
            hh = 2 * k + j
            for s in range(5):
                p5[j, s] = pT[s * D + hh * 128:s * D + (hh + 1) * 128]
            lbl[:, j, :] = P["hg_lb_logits"][:, hh * 128:(hh + 1) * 128].T
        maps.append({"p5": p5, "lbl": lbl, "nw": np.ascontiguousarray(P["hg_norm_w"][o].reshape(128, 1)),
                     "masks": masks, "ident": ident})
    return maps


def hg_host_gather(results):
    y = np.empty((D, B, TS), results[0]["yT"].dtype)
    for k in range(NCORES):
        o = results[k]["yT"]
        for j in range(2):
            hh = 2 * k + j
            y[hh * 128:(hh + 1) * 128] = o[j]
    return y


def kernel(**inp):
    P = {k: np.asarray(v, dtype=np.float32) for k, v in inp.items()}
    x, ctx = P["x"], P["ctx"]
    m = ada_host_gather(run(get_nc("ada", build_ada), ada_host_inputs(P["c"], P["c_ctx"], P["ada_w"], P["ada_b"])))
    xs = to_core_T(x, ctx)

    def tok_maps(l_post, l_pre, xs, ys):
        maps = []
        nmlp = vec_pk(P["norm_mlp_w"][l_post]) if l_post is not None else np.zeros((128, KC), np.float32)
        npre = vec_pk(P["norm_mix_w"][l_pre]) if l_pre < DEPTH else vec_pk(P["final_norm_w"])
        nrm = np.ascontiguousarray(np.stack([nmlp, npre], 1))
        for k in range(NCORES):
            b, r = core_tok(k)
            d = {"xT": xs[k], "adaL": ada_sets(m, l_post, l_pre, b), "adaC": ada_sets(m, l_post, l_pre, 2), "nrm": nrm}
            if l_post is not None:
                d["yT"] = ys[k]
                d["w_out"] = P["ab_w_out"][l_post // 2] if l_post % 2 == 0 else P["hg_w_out"][l_post // 2]
                d["w1"] = P["mlp_w1"][l_post]
                d["w2"] = P["mlp_w2"][l_post]
            if l_pre < DEPTH:
                d["w_in"] = P["ab_w_in"][l_pre // 2] if l_pre % 2 == 0 else P["hg_w_in"][l_pre // 2]
            maps.append(d)
        return maps

    res = run(get_nc("tok", build_tok, False, "ab"), tok_maps(None, 0, xs, None))
    for l in range(DEPTH):
        pT = tokT_to_full([r["pT"] for r in res])
        xs = [r["xoT"] for r in res]
        if l % 2 == 0:
            y = ab_host_gather(run(get_nc("ab", build_ab), ab_host_inputs(pT, l // 2, P)))
        else:
            y = hg_host_gather(run(get_nc("hg", build_hg, l), hg_host_inputs(pT, l // 2, P)))
        ys = full_to_tokT(y)
        pre = "final" if l + 1 == DEPTH else ("ab" if (l + 1) % 2 == 0 else "hg")
        res = run(get_nc("tok", build_tok, True, pre), tok_maps(l, l + 1, xs, ys))
    lat, _ = from_core_T([r["outT"] for r in res])
    return np.ascontiguousarray(lat.astype(np.float32))
```
